# Optimizing a Trainium2 kernel written in Bass

```python
import math
import jax
import jax.numpy as jnp
from jax import lax
import numpy as np

D_MODEL = 1024
BATCH = 16
SEQ = 2048
DEPTH = 2

GRID_W = 64
CTX_LEN = 256
EPS = 1e-6
F32 = jnp.float32

E_HY = 512
HY_SHORT = 3
HY_EMB_BANDS = 8
HY_EMB = 1 + 2 * HY_EMB_BANDS
HY_ORDER = 64
HY_DECAY_TARGET = 1e-2
HY_FAST_DECAY = 0.3
HY_SLOW_DECAY = 1.5

E_LRU = 512
LRU_BLOCKS = 8
LRU_BLOCK_DIM = E_LRU // LRU_BLOCKS
LRU_CONV = 4
LRU_C = 8.0

HG_HEADS = 4
HG_DK = 128
HG_DV = 128
E_HG = HG_HEADS * HG_DV
HG_CHUNK = 64

D_FF = 4 * D_MODEL
N_BRANCH = 3

IN_SPLITS = (3 * E_HY, E_LRU, E_LRU, HG_HEADS * HG_DK, HG_HEADS * HG_DK, HG_HEADS * HG_DK, E_HG, E_HG, N_BRANCH * D_MODEL)
IN_WIDTH = sum(IN_SPLITS)

kernel_name = 'hybrid_hyena_rglru_hgrn2_dit_prefix'


def rmsnorm(x, g):
    xf = x.astype(F32)
    y = xf * lax.rsqrt(jnp.mean(xf * xf, axis=-1, keepdims=True) + EPS)
    return (y * g.astype(F32)).astype(x.dtype)


def modulate(x, shift, scale):
    return x * (1 + scale) + shift


def dwconv(x, w, b, pad):
    y = lax.conv_general_dilated(x, w[:, None, :].astype(x.dtype), window_strides=(1,), padding=[pad],
                                 dimension_numbers=('NWC', 'WIO', 'NWC'), feature_group_count=x.shape[-1])
    return y + b


def grid_transpose(t, n_rows, n_cols):
    bsz, length, width = t.shape
    return t.reshape(bsz, n_rows, n_cols, width).transpose(0, 2, 1, 3).reshape(bsz, length, width)


def _keep(t):
    return t


def _flip(t):
    return t[:, ::-1]


def hyena_filter(length, w1, b1, w2, b2, w3, freq):
    pos = jnp.arange(length, dtype=F32)
    t = pos / max(length - 1, 1)
    bands = jnp.linspace(1e-4, HY_EMB_BANDS - 1, HY_EMB_BANDS, dtype=F32)
    ang = bands[None, :] * (2.0 * math.pi * pos / length)[:, None]
    z = jnp.concatenate([t[:, None], jnp.cos(ang), -jnp.sin(ang)], axis=-1)
    h = jnp.sin(freq[0].astype(F32) * (z @ w1.astype(F32) + b1.astype(F32)))
    h = jnp.sin(freq[1].astype(F32) * (h @ w2.astype(F32) + b2.astype(F32)))
    h = h @ w3.astype(F32)
    deltas = jnp.abs(jnp.linspace(math.log(HY_DECAY_TARGET) / HY_SLOW_DECAY,
                                  math.log(HY_DECAY_TARGET) / HY_FAST_DECAY, E_HY, dtype=F32))
    decay = jnp.exp(-t[:, None] * deltas[None, :])
    h_fwd = h[:, :E_HY] * decay
    h_bwd = h[:, E_HY:] * decay
    return jnp.concatenate([h_fwd, jnp.zeros((1, E_HY), F32), h_bwd[1:][::-1]], axis=0)


def hyena_mixer(p_hy, short_w, short_b, w1, b1, w2, b2, w3, freq, skip):
    length = p_hy.shape[1]
    z = dwconv(p_hy, short_w, short_b, (1, 1))
    v, x0, x1 = jnp.split(z, 3, axis=-1)
    k_circ = hyena_filter(length, w1, b1, w2, b2, w3, freq)
    u = (v * x1).astype(F32)
    uf = jnp.fft.rfft(u, n=2 * length, axis=1)
    kf = jnp.fft.rfft(k_circ, axis=0)
    y = jnp.fft.irfft(uf * kf[None], n=2 * length, axis=1)[:, :length] + u * skip.astype(F32)
    return (x0.astype(F32) * y).astype(p_hy.dtype)


def _lin_combine(left, right):
    a_l, b_l = left
    a_r, b_r = right
    return a_l * a_r, a_r * b_l + b_r


def rglru_scan(xb, conv_w, conv_b, wr, br, wi, bi, lam, h0):
    bsz, length, _ = xb.shape
    xc = dwconv(xb, conv_w, conv_b, (LRU_CONV - 1, 0)).astype(F32)
    xh = xc.reshape(bsz, length, LRU_BLOCKS, LRU_BLOCK_DIM)
    r = jax.nn.sigmoid(jnp.einsum('blhi,hij->blhj', xh, wr.astype(F32)).reshape(bsz, length, E_LRU) + br.astype(F32))
    i = jax.nn.sigmoid(jnp.einsum('blhi,hij->blhj', xh, wi.astype(F32)).reshape(bsz, length, E_LRU) + bi.astype(F32))
    log_a = -LRU_C * r * jax.nn.softplus(-lam.astype(F32))
    a = jnp.exp(log_a)
    b = jnp.sqrt(-jnp.expm1(2.0 * log_a)) * (i * xc)
    b = b.at[:, 0].add(a[:, 0] * h0)
    _, h = lax.associative_scan(_lin_combine, (a, b), axis=1)
    return h, h[:, -1]


def rglru_mixer(x_ctx, g_ctx, x_lat, g_lat, conv_w, conv_b, wr, br, wi, bi, lam, need_ctx):
    bsz = x_lat.shape[0]
    h_ctx_dirs, h_lat_dirs = [], []
    for d in range(2):
        order = _flip if d == 1 else _keep
        args = (conv_w[d], conv_b[d], wr[d], br[d], wi[d], bi[d], lam[d])
        h_c, s_c = rglru_scan(order(x_ctx), *args, jnp.zeros((bsz, E_LRU), F32))
        h_l, _ = rglru_scan(order(x_lat), *args, s_c)
        h_lat_dirs.append(order(h_l))
        if need_ctx:
            h_ctx_dirs.append(order(h_c))
    y_lat = ((h_lat_dirs[0] + h_lat_dirs[1]) * jax.nn.gelu(g_lat.astype(F32))).astype(x_lat.dtype)
    y_ctx = ((h_ctx_dirs[0] + h_ctx_dirs[1]) * jax.nn.gelu(g_ctx.astype(F32))).astype(x_ctx.dtype) if need_ctx else None
    return y_ctx, y_lat


def hgrn2_chunked(q, k, v, log_f, s0, need_out):
    bsz, length = q.shape[:2]
    n = length // HG_CHUNK

    def chunks(t):
        return t.reshape(bsz, n, HG_CHUNK, HG_HEADS, t.shape[-1])

    q, k, v, log_f = chunks(q), chunks(k), chunks(v), chunks(log_f)
    b = jnp.cumsum(log_f, axis=2)
    b_last = b[:, :, -1:]
    kv = jnp.einsum('bnshk,bnshv->bnhkv', k * jnp.exp(b_last - b), v)
    decay = jnp.exp(b_last[:, :, 0])

    def step(s, inp):
        dec, kv_c = inp
        return dec[..., None] * s + kv_c, s

    s_final, s_start = lax.scan(step, s0, (jnp.moveaxis(decay, 1, 0), jnp.moveaxis(kv, 1, 0)))
    if not need_out:
        return None, s_final
    s_start = jnp.moveaxis(s_start, 0, 1)
    ref = b[:, :, HG_CHUNK // 2:HG_CHUNK // 2 + 1]
    scores = jnp.einsum('bnthk,bnshk->bnhts', q * jnp.exp(b - ref), k * jnp.exp(ref - b))
    lower_tri = jnp.tril(jnp.ones((HG_CHUNK, HG_CHUNK), dtype=bool))
    scores = jnp.where(lower_tri, scores, 0.0)
    o = jnp.einsum('bnhts,bnshv->bnthv', scores, v) + jnp.einsum('bnthk,bnhkv->bnthv', q * jnp.exp(b), s_start)
    return o.reshape(bsz, length, HG_HEADS, HG_DV), s_final


def _hgrn2_inputs(q, f_fwd, f_bwd, i, lb):
    def heads(t):
        return t.reshape(t.shape[0], t.shape[1], HG_HEADS, -1)

    qh = jax.nn.silu(heads(q.astype(F32)))
    vh = heads(i.astype(F32))
    per_dir = []
    for f_logit in (f_fwd, f_bwd):
        f = lb + (1.0 - lb) * jax.nn.sigmoid(f_logit.astype(F32))
        per_dir.append((heads(1.0 - f), heads(jnp.log(f))))
    return qh, vh, per_dir


def _hgrn2_out(o, g, norm_g):
    bsz, length = o.shape[:2]
    o = rmsnorm(o, norm_g).reshape(bsz, length, E_HG)
    return (o * jax.nn.silu(g.astype(F32))).astype(g.dtype)


def hgrn2_mixer(ctx_proj, lat_proj, lb, norm_g, need_ctx):
    q_c, v_c, dirs_c = _hgrn2_inputs(*ctx_proj[:4], lb)
    q_l, v_l, dirs_l = _hgrn2_inputs(*lat_proj[:4], lb)
    bsz = q_l.shape[0]
    s0 = jnp.zeros((bsz, HG_HEADS, HG_DK, HG_DV), F32)
    o_ctx, o_lat = [], []
    for d in range(2):
        order = _flip if d == 1 else _keep
        (k_c, lf_c), (k_l, lf_l) = dirs_c[d], dirs_l[d]
        oc, s_ctx = hgrn2_chunked(order(q_c), order(k_c), order(v_c), order(lf_c), s0, need_ctx)
        ol, _ = hgrn2_chunked(order(q_l), order(k_l), order(v_l), order(lf_l), s_ctx, True)
        o_lat.append(order(ol))
        if need_ctx:
            o_ctx.append(order(oc))
    y_lat = _hgrn2_out(o_lat[0] + o_lat[1], lat_proj[4], norm_g)
    y_ctx = _hgrn2_out(o_ctx[0] + o_ctx[1], ctx_proj[4], norm_g) if need_ctx else None
    return y_ctx, y_lat


def merge_branches(gates, y_hy, y_lru, y_hg, w_proj_hy, w_proj_lru, w_proj_hg, w_out):
    g_hy, g_lru, g_hg = jnp.split(jax.nn.sigmoid(gates), N_BRANCH, axis=-1)
    m = g_hy * (y_hy @ w_proj_hy) + g_lru * (y_lru @ w_proj_lru) + g_hg * (y_hg @ w_proj_hg)
    return m @ w_out


def sq_relu_mlp(x, w1, w2):
    return jnp.square(jax.nn.relu(x @ w1)) @ w2


def setup_inputs(seed: int = 0) -> dict:
    key = jax.random.key(seed)
    ks = iter(jax.random.split(key, 40))

    def nrm(shape, scale):
        return jax.random.normal(next(ks), shape, F32) * scale

    a0 = jax.random.uniform(next(ks), (DEPTH, 2, E_LRU), F32, minval=0.9, maxval=0.999)
    p = a0 ** (1.0 / LRU_C)
    return {
        'x': nrm((BATCH, SEQ, D_MODEL), 1.0),
        'c': nrm((BATCH, D_MODEL), 1.0),
        'ctx': nrm((BATCH, CTX_LEN, D_MODEL), 1.0),
        'c_ctx': nrm((D_MODEL,), 1.0),
        'w_ada': nrm((DEPTH, D_MODEL, 6 * D_MODEL), 0.5 * D_MODEL ** -0.5),
        'b_ada': nrm((DEPTH, 6 * D_MODEL), 0.02),
        'norm_gains': 1.0 + nrm((DEPTH, 4, D_MODEL), 0.05),
        'w_in': nrm((DEPTH, D_MODEL, IN_WIDTH), D_MODEL ** -0.5),
        'hy_short_w': nrm((DEPTH, HY_SHORT, 3 * E_HY), HY_SHORT ** -0.5),
        'hy_short_b': nrm((DEPTH, 3 * E_HY), 0.02),
        'hy_ff_w1': nrm((DEPTH, HY_EMB, HY_ORDER), HY_EMB ** -0.5),
        'hy_ff_b1': nrm((DEPTH, HY_ORDER), 0.1),
        'hy_ff_w2': nrm((DEPTH, HY_ORDER, HY_ORDER), HY_ORDER ** -0.5),
        'hy_ff_b2': nrm((DEPTH, HY_ORDER), 0.1),
        'hy_ff_w3': nrm((DEPTH, HY_ORDER, 2 * E_HY), 0.05 * HY_ORDER ** -0.5),
        'hy_freq': 1.0 + nrm((DEPTH, 2, HY_ORDER), 0.05),
        'hy_skip': nrm((DEPTH, E_HY), 0.1),
        'lru_conv_w': nrm((DEPTH, 2, LRU_CONV, E_LRU), LRU_CONV ** -0.5),
        'lru_conv_b': nrm((DEPTH, 2, E_LRU), 0.02),
        'lru_wr': nrm((DEPTH, 2, LRU_BLOCKS, LRU_BLOCK_DIM, LRU_BLOCK_DIM), LRU_BLOCK_DIM ** -0.5),
        'lru_br': nrm((DEPTH, 2, E_LRU), 0.02),
        'lru_wi': nrm((DEPTH, 2, LRU_BLOCKS, LRU_BLOCK_DIM, LRU_BLOCK_DIM), LRU_BLOCK_DIM ** -0.5),
        'lru_bi': nrm((DEPTH, 2, E_LRU), 0.02),
        'lru_lambda': jnp.log(p) - jnp.log1p(-p),
        'hg_lower_bounds': nrm((DEPTH, HG_HEADS * HG_DK), 0.1),
        'hg_norm_g': 1.0 + nrm((DEPTH, HG_DV), 0.05),
        'w_proj_hy': nrm((DEPTH, E_HY, D_MODEL), E_HY ** -0.5),
        'w_proj_lru': nrm((DEPTH, E_LRU, D_MODEL), E_LRU ** -0.5),
        'w_proj_hg': nrm((DEPTH, E_HG, D_MODEL), E_HG ** -0.5),
        'w_out': nrm((DEPTH, D_MODEL, D_MODEL), D_MODEL ** -0.5),
        'w_mlp1': nrm((DEPTH, D_MODEL, D_FF), D_MODEL ** -0.5),
        'w_mlp2': nrm((DEPTH, D_FF, D_MODEL), D_FF ** -0.5),
    }


def reference(x, c, ctx, c_ctx, w_ada, b_ada, norm_gains, w_in, hy_short_w, hy_short_b, hy_ff_w1, hy_ff_b1,
              hy_ff_w2, hy_ff_b2, hy_ff_w3, hy_freq, hy_skip, lru_conv_w, lru_conv_b, lru_wr, lru_br, lru_wi,
              lru_bi, lru_lambda, hg_lower_bounds, hg_norm_g, w_proj_hy, w_proj_lru, w_proj_hg, w_out,
              w_mlp1, w_mlp2):
    rows = x.shape[1] // GRID_W
    offs = [int(o) for o in np.cumsum(IN_SPLITS)[:-1]]
    lb_all = jnp.cumsum(jax.nn.softmax(hg_lower_bounds.astype(F32), axis=0), axis=0)
    lb_all = lb_all - lb_all[0]
    h_lat, h_ctx = x, ctx
    for layer in range(DEPTH):
        need_ctx = layer < DEPTH - 1
        col_major = layer % 2 == 1
        ada_lat = jax.nn.silu(c) @ w_ada[layer] + b_ada[layer]
        ada_ctx = jax.nn.silu(c_ctx) @ w_ada[layer] + b_ada[layer]
        sh1, sc1, g1, sh2, sc2, g2 = jnp.split(ada_lat[:, None, :], 6, axis=-1)
        sh1c, sc1c, g1c, sh2c, sc2c, g2c = jnp.split(ada_ctx, 6, axis=-1)
        gains = norm_gains[layer]

        u_lat = modulate(rmsnorm(h_lat, gains[0]), sh1, sc1)
        u_ctx = modulate(rmsnorm(h_ctx, gains[0]), sh1c, sc1c)
        if col_major:
            u_lat = grid_transpose(u_lat, rows, GRID_W)
        p_lat = jnp.split(u_lat @ w_in[layer], offs, axis=-1)
        p_ctx = jnp.split(u_ctx @ w_in[layer], offs, axis=-1)
        hy_args = (hy_short_w[layer], hy_short_b[layer], hy_ff_w1[layer], hy_ff_b1[layer], hy_ff_w2[layer],
                   hy_ff_b2[layer], hy_ff_w3[layer], hy_freq[layer], hy_skip[layer])
        lru_args = (lru_conv_w[layer], lru_conv_b[layer], lru_wr[layer], lru_br[layer], lru_wi[layer],
                    lru_bi[layer], lru_lambda[layer])
        proj_args = (w_proj_hy[layer], w_proj_lru[layer], w_proj_hg[layer], w_out[layer])
        y_hy_lat = hyena_mixer(p_lat[0], *hy_args)
        y_lru_ctx, y_lru_lat = rglru_mixer(p_ctx[1], p_ctx[2], p_lat[1], p_lat[2], *lru_args, need_ctx)
        y_hg_ctx, y_hg_lat = hgrn2_mixer(p_ctx[3:8], p_lat[3:8], lb_all[layer], hg_norm_g[layer], need_ctx)
        m_lat = merge_branches(p_lat[8], y_hy_lat, y_lru_lat, y_hg_lat, *proj_args)
        if col_major:
            m_lat = grid_transpose(m_lat, GRID_W, rows)
        h_lat = h_lat + g1 * rmsnorm(m_lat, gains[1])
        if need_ctx:
            y_hy_ctx = hyena_mixer(p_ctx[0], *hy_args)
            m_ctx = merge_branches(p_ctx[8], y_hy_ctx, y_lru_ctx, y_hg_ctx, *proj_args)
            h_ctx = h_ctx + g1c * rmsnorm(m_ctx, gains[1])

        f_lat = sq_relu_mlp(modulate(rmsnorm(h_lat, gains[2]), sh2, sc2), w_mlp1[layer], w_mlp2[layer])
        h_lat = h_lat + g2 * rmsnorm(f_lat, gains[3])
        if need_ctx:
            f_ctx = sq_relu_mlp(modulate(rmsnorm(h_ctx, gains[2]), sh2c, sc2c), w_mlp1[layer], w_mlp2[layer])
            h_ctx = h_ctx + g2c * rmsnorm(f_ctx, gains[3])
    return h_lat
```

```python
import math
from contextlib import ExitStack
import numpy as np
import ml_dtypes
import concourse.bass as bass
import concourse.mybir as mybir
from concourse.bass_utils import run_bass_kernel_spmd

F32 = mybir.dt.float32
BF16 = mybir.dt.bfloat16
AF = mybir.ActivationFunctionType
ALU = mybir.AluOpType

NCORES = 8
NSEQ = 2
D = 1024
KC = 8
SEQ = 2048
CTX = 256
T = SEQ + CTX
DEPTH = 2
GRID_W = 64
EPS = 1e-6
E_HY = 512
INW = 8192
DFF = 4096
PADL = 3
PT = T + 12
TB = [(0, 256), (256, 768), (768, 1280), (1280, 1792), (1792, 2304)]
C_HY, C_LX, C_LG, C_Q, C_FF, C_FB, C_I, C_OG, C_MG = 0, 1536, 2048, 2560, 3072, 3584, 4096, 4608, 5120


def poff(t):
    return t + PADL if t < CTX else t + 3 * PADL


class Ev:
    __slots__ = ("q", "sem", "val", "dma")

    def __init__(self, q, sem, val, dma=False):
        self.q, self.sem, self.val, self.dma = q, sem, val, dma


class Buf:
    __slots__ = ("name", "w", "r")

    def __init__(self, name=""):
        self.name, self.w, self.r = name, None, []


class Tile:
    def __init__(self, t, name):
        self.t, self.name, self.bufs = t, name, {}

    def b(self, *key):
        v = self.bufs.get(key)
        if v is None:
            v = self.bufs[key] = Buf(f"{self.name}{key}")
        return v

    def __getitem__(self, k):
        return self.t[k]


class Q:
    LIM = 3000

    def __init__(self, kern, name, eng, ndma=0):
        self.k, self.name, self.eng = kern, name, eng
        self.sem = kern.new_sem(name)
        self.cnt = 0
        self.seen = {}
        self.last = None
        self.slots = [[kern.new_sem(f"{name}d{i}"), 0] for i in range(ndma)]
        self.nxt = 0
        self.pend = []

    def wait(self, ev):
        key = id(ev.sem)
        if self.seen.get(key, 0) >= ev.val:
            return
        self.eng.wait_ge(ev.sem, ev.val)
        self.seen[key] = ev.val
        self.k.nwait += 1

    def signal(self, ins):
        if self.cnt >= Q.LIM:
            self.sem = self.k.new_sem(self.name + "x")
            self.cnt = 0
        self.cnt += 1
        ins.then_inc(self.sem, 1)
        self.last = Ev(self, self.sem, self.cnt)
        return self.last

    def dma_signal(self, fn):
        slot = self.slots[self.nxt]
        self.nxt = (self.nxt + 1) % len(self.slots)
        if slot[1] > 0:
            self.wait(Ev(self, slot[0], slot[1], True))
        ins = fn()
        slot[1] += 16
        ins.then_inc(slot[0], 16)
        return Ev(self, slot[0], slot[1], True)


class Kern:
    def __init__(self):
        self.nc = bass.Bass("TRN2", target_bir_lowering=False)
        self.stack = ExitStack()
        self.nwait = 0
        self.nins = 0
        self.sems = []
        nc = self.nc
        self.PE = Q(self, "pe", nc.tensor)
        self.ACT = Q(self, "act", nc.scalar)
        self.DVE = Q(self, "dve", nc.vector)
        self.POOL = Q(self, "pool", nc.gpsimd, ndma=2)
        self.SP = Q(self, "sp", nc.sync, ndma=16)
        self.queues = [self.PE, self.ACT, self.DVE, self.POOL, self.SP]
        self.banks = []
        for i in range(8):
            t = nc.alloc_psum_tensor(f"bank{i}", [128, 512], F32)
            self.banks.append(Tile(t, f"bank{i}"))
        self.bank_i = 0
        self.dbg = []

    def new_sem(self, name):
        s = self.stack.enter_context(self.nc.semaphore(f"s_{name}_{len(self.sems)}"))
        self.sems.append(s)
        return s

    def bank(self):
        b = self.banks[self.bank_i]
        self.bank_i = (self.bank_i + 1) % 8
        return b

    def _deps(self, q, r, w):
        for b in r:
            if b.w is not None:
                self._dep(q, b.w, 0)
        for b in w:
            if b.w is not None:
                self._dep(q, b.w, 1)
            for e in b.r:
                self._dep(q, e, 2)

    def _dep(self, q, ev, kind):
        if ev.q is q and not ev.dma:
            if q is self.PE or kind == 2:
                return
        q.wait(ev)

    def _record(self, ev, r, w):
        for b in r:
            if not ev.dma:
                b.r = [e for e in b.r if e.dma or e.q is not ev.q]
            b.r.append(ev)
        for b in w:
            b.w = ev
            b.r = []

    def op(self, q, fn, r=(), w=(), sig=True):
        self._deps(q, r, w)
        ins = fn()
        self.nins += 1
        if sig:
            ev = q.signal(ins)
            for (pr, pw) in q.pend:
                self._record(ev, pr, pw)
            q.pend = []
            self._record(ev, r, w)
        else:
            q.pend.append((list(r), list(w)))
        return ins

    def dma(self, q, out, in_, r=(), w=(), **kw):
        assert not q.pend
        self._deps(q, r, w)
        ev = q.dma_signal(lambda: q.eng.dma_start(out=out, in_=in_, **kw))
        self.nins += 1
        self._record(ev, r, w)

    def barrier(self):
        evs = []
        for q in self.queues:
            assert not q.pend, q.name
            if q.last is not None:
                evs.append(q.last)
            for s in q.slots:
                if s[1] > 0:
                    evs.append(Ev(q, s[0], s[1], True))
        for q in self.queues:
            for e in evs:
                if e.q is q and not e.dma:
                    continue
                q.wait(e)

    def tile(self, name, shape, dtype, stack=None):
        st = stack if stack is not None else self.stack
        self.ntile = getattr(self, "ntile", 0) + 1
        name = f"{name}_{self.ntile}"
        t = st.enter_context(self.nc.sbuf_tensor(name, list(shape), dtype))
        return Tile(t, name)

    class Scope:
        def __init__(self, k):
            self.k = k
            self.st = ExitStack()

        def __enter__(self):
            self.st.__enter__()
            return self

        def tile(self, name, shape, dtype):
            return self.k.tile(name, shape, dtype, self.st)

        def __exit__(self, *a):
            self.k.barrier()
            return self.st.__exit__(*a)

    def scope(self):
        return Kern.Scope(self)

    def mm(self, out, lhsT, rhs, start, stop, r=(), w=(), sig=False):
        nc = self.nc
        return self.op(self.PE, lambda: nc.tensor.matmul(out, lhsT, rhs, start=start, stop=stop), r, w, sig)

    def tr(self, out, in_, ident, r=(), w=(), sig=True):
        nc = self.nc
        return self.op(self.PE, lambda: nc.tensor.transpose(out, in_, ident), r, w, sig)

    def act(self, out, in_, func, r=(), w=(), bias=None, scale=None):
        nc = self.nc
        kw = {}
        if bias is not None:
            kw["bias"] = bias
        if scale is not None:
            kw["scale"] = scale
        return self.op(self.ACT, lambda: nc.scalar.activation(out=out, in_=in_, func=func, **kw), r, w)

    def tt(self, q, out, in0, in1, op, r=(), w=()):
        return self.op(q, lambda: q.eng.tensor_tensor(out=out, in0=in0, in1=in1, op=op), r, w)

    def ts(self, q, out, in0, s1, op0, s2=None, op1=None, r=(), w=()):
        if op1 is None:
            return self.op(q, lambda: q.eng.tensor_scalar(out=out, in0=in0, scalar1=s1, scalar2=None, op0=op0), r, w)
        return self.op(q, lambda: q.eng.tensor_scalar(out=out, in0=in0, scalar1=s1, scalar2=s2, op0=op0, op1=op1), r, w)

    def stt(self, out, in0, scalar, in1, op0, op1, r=(), w=()):
        nc = self.nc
        return self.op(self.DVE, lambda: nc.vector.scalar_tensor_tensor(out=out, in0=in0, scalar=scalar, in1=in1, op0=op0, op1=op1), r, w)

    def cp(self, q, out, in_, r=(), w=()):
        if q is self.ACT:
            return self.op(q, lambda: q.eng.copy(out=out, in_=in_), r, w)
        return self.op(q, lambda: q.eng.tensor_copy(out=out, in_=in_), r, w)

    def memset(self, q, ap, val, w=()):
        return self.op(q, lambda: q.eng.memset(ap, val), (), w)


_CONSTS = None


def make_consts():
    global _CONSTS
    if _CONSTS is not None:
        return _CONSTS
    bf = ml_dtypes.bfloat16
    c = {}
    c["ident_f"] = np.eye(128, dtype=np.float32)
    c["ident_b"] = np.eye(128, dtype=np.float32).astype(bf)
    c["ones_d"] = np.full((128, 128), 1.0 / D, np.float32).astype(bf)
    c["ones_v"] = np.full((128, 128), 1.0 / 128, np.float32).astype(bf)
    c["onesrow"] = np.ones((128, 512), np.float32).astype(bf)

    def dft(L):
        N = 2 * L
        s = np.arange(L, dtype=np.float64)[:, None]
        f = np.arange(L, dtype=np.float64)[None, :]
        ang = 2.0 * np.pi * np.mod((2 * f + 1) * s, 2 * N) / (2 * N)
        return np.cos(ang), np.sin(ang)

    cs, sn = dft(SEQ)
    fw = np.stack([cs, sn], 0).reshape(2, 16, 128, 16, 128)
    c["fwdL"] = np.ascontiguousarray(fw.transpose(3, 2, 1, 0, 4)).astype(bf)
    iv = np.stack([cs.T, sn.T], 0).reshape(2, 16, 128, SEQ)
    c["invL"] = np.ascontiguousarray(iv.transpose(1, 2, 0, 3)).astype(bf)
    cs, sn = dft(CTX)
    fw = np.stack([cs, sn], 0).reshape(2, 2, 128, CTX)
    c["fwdC"] = np.ascontiguousarray(fw.transpose(2, 1, 0, 3)).astype(bf)
    iv = np.stack([cs.T, sn.T], 0).reshape(2, 2, 128, CTX)
    c["invC"] = np.ascontiguousarray(iv.transpose(2, 1, 0, 3)).astype(bf)

    def zemb(L):
        pos = np.arange(L, dtype=np.float32)
        t = pos / np.float32(max(L - 1, 1))
        bands = np.linspace(1e-4, 7, 8, dtype=np.float32)
        ang = bands[None, :] * (np.float32(2.0 * math.pi) * pos / np.float32(L))[:, None]
        z = np.concatenate([t[:, None], np.cos(ang), -np.sin(ang)], -1).astype(np.float32)
        return np.ascontiguousarray(z.T), t

    c["zL"], tl = zemb(SEQ)
    c["zC"], tc = zemb(CTX)
    negt = np.zeros((128, 18), np.float32)
    negt[:, :16] = -tl.reshape(16, 128).T
    negt[:, 16:] = -tc.reshape(2, 128).T
    c["negt"] = negt
    deltas = np.abs(np.linspace(math.log(1e-2) / 1.5, math.log(1e-2) / 0.3, E_HY, dtype=np.float32))
    c["delta"] = np.ascontiguousarray(np.broadcast_to(deltas[None, :], (128, E_HY))).astype(np.float32)
    s = np.arange(128)[:, None]
    t = np.arange(128)[None, :]
    same = (s // 64) == (t // 64)
    c["maskF"] = (same & (s <= t)).astype(np.uint16)
    c["maskB"] = (same & (s >= t)).astype(np.uint16)
    _CONSTS = c
    return c


CONST_DT = {"ident_f": F32, "ident_b": BF16, "ones_d": BF16, "ones_v": BF16, "onesrow": BF16, "fwdL": BF16,
            "invL": BF16, "fwdC": BF16, "invC": BF16, "zL": F32, "zC": F32, "negt": F32, "delta": F32,
            "maskF": mybir.dt.uint16, "maskB": mybir.dt.uint16}

IN_SHAPES = {
    "x": [NSEQ, SEQ, D], "c": [NSEQ, D], "ctx": [NSEQ, CTX, D], "c_ctx": [D],
    "w_ada": [DEPTH, D, 6 * D], "b_ada": [DEPTH, 6 * D], "norm_gains": [DEPTH, 4, D], "w_in": [DEPTH, D, INW],
    "hy_short_w": [DEPTH, 3, 1536], "hy_short_b": [DEPTH, 1536], "hy_ff_w1": [DEPTH, 17, 64],
    "hy_ff_b1": [DEPTH, 64], "hy_ff_w2": [DEPTH, 64, 64], "hy_ff_b2": [DEPTH, 64], "hy_ff_w3": [DEPTH, 64, 1024],
    "hy_freq": [DEPTH, 2, 64], "hy_skip": [DEPTH, 512], "lru_conv_w": [DEPTH, 2, 4, 512],
    "lru_conv_b": [DEPTH, 2, 512], "lru_wr": [DEPTH, 2, 8, 64, 64], "lru_br": [DEPTH, 2, 512],
    "lru_wi": [DEPTH, 2, 8, 64, 64], "lru_bi": [DEPTH, 2, 512], "lru_lambda": [DEPTH, 2, 512],
    "hg_lower_bounds": [DEPTH, 512], "hg_norm_g": [DEPTH, 128], "w_proj_hy": [DEPTH, 512, D],
    "w_proj_lru": [DEPTH, 512, D], "w_proj_hg": [DEPTH, 512, D], "w_out": [DEPTH, D, D],
    "w_mlp1": [DEPTH, D, DFF], "w_mlp2": [DEPTH, DFF, D],
}


VEC_LIST = [
    ("b_ada", 6144), ("norm_gains", 4096), ("hy_short_w", 4608), ("hy_short_b", 1536), ("lru_conv_w", 4096),
    ("lru_conv_b", 1024), ("lru_br", 1024), ("lru_bi", 1024), ("lru_lambda", 1024), ("hg_lower_bounds", 512),
    ("hg_norm_g", 128), ("hy_ff_b1", 64), ("hy_ff_b2", 64), ("hy_freq0", 64), ("hy_freq1", 64),
]


class Net:
    def __init__(self, nseq=NSEQ, depth=DEPTH, stage="full", dbg=()):
        self.k = k = Kern()
        self.nc = nc = k.nc
        self.nseq, self.depth, self.stage = nseq, depth, stage
        self.dbg_names = dbg
        self.dbg_out = {}
        self.inp = {}
        for name, shp in IN_SHAPES.items():
            shp = list(shp)
            if name in ("x", "c", "ctx"):
                shp[0] = nseq
            self.inp[name] = nc.dram_tensor(name, shp, F32, kind="ExternalInput").ap()
        self.cst = {}
        for name, arr in make_consts().items():
            self.cst[name] = nc.dram_tensor("k_" + name, list(arr.shape), CONST_DT[name], kind="ExternalInput").ap()
        self.out = nc.dram_tensor("out", [nseq, SEQ, D], F32, kind="ExternalOutput").ap()

        def scratch(name, shape, dt=BF16):
            return Tile(nc.dram_tensor(name, list(shape), dt, kind="Internal").ap(), name)

        self.WB = {
            "w_in": scratch("wb_in", [DEPTH, D, INW]), "w_ada": scratch("wb_ada", [DEPTH, D, 6 * D]),
            "w_proj_hy": scratch("wb_phy", [DEPTH, 512, D]), "w_proj_lru": scratch("wb_plru", [DEPTH, 512, D]),
            "w_proj_hg": scratch("wb_phg", [DEPTH, 512, D]), "w_out": scratch("wb_out", [DEPTH, D, D]),
            "w_mlp1": scratch("wb_m1", [DEPTH, D, DFF]), "w_mlp2": scratch("wb_m2", [DEPTH, DFF, D]),
        }
        self.KS = scratch("ks", [DEPTH, 16, 128, 2, 512])
        self.KSC = scratch("ksc", [2, 128, 2, 512])
        self.YP = scratch("yp", [3, 128, 4, T])

    def dump(self, name, tile_ap, shape, dtype=F32):
        if name not in self.dbg_names:
            return
        k, nc = self.k, self.nc
        k.barrier()
        d = nc.dram_tensor("dbg_" + name, list(shape), dtype, kind="ExternalOutput").ap()
        k.dma(k.SP, d, tile_ap)
        k.barrier()
        self.dbg_out[name] = "dbg_" + name

    def precast(self):
        k = self.k
        order = []
        for l in range(self.depth):
            order.append(("w_ada", l))
        for l in range(self.depth):
            for n in ("w_in", "w_proj_hy", "w_proj_hg", "w_proj_lru", "w_out", "w_mlp1", "w_mlp2"):
                order.append((n, l))
        for (n, l) in order:
            src = self.inp[n][l]
            dst = self.WB[n]
            k.dma(k.POOL, dst[l], src, w=[dst.b(l)], max_dma_last_dim=4096)

    def consts(self):
        k, nc = self.k, self.nc
        C = {}
        for name in ("ident_f", "ident_b", "ones_d", "ones_v", "onesrow", "fwdC", "invC", "maskF", "maskB"):
            shp = list(make_consts()[name].shape)
            t = k.tile("c_" + name, shp, CONST_DT[name])
            k.dma(k.SP, t[:], self.cst[name], w=[t.b()])
            C[name] = t
        self.C = C

    def vecbank(self):
        k, nc = self.k, self.nc
        rows = []
        for l in range(self.depth):
            for name, ln in VEC_LIST:
                if name == "hy_freq0":
                    src = self.inp["hy_freq"][l, 0]
                elif name == "hy_freq1":
                    src = self.inp["hy_freq"][l, 1]
                else:
                    src = self.inp[name][l]
                    if len(src.shape) > 1:
                        src = src.flatten() if hasattr(src, "flatten") else src
                rows.append(((name, l), src, ln))
        for i in range(self.nseq):
            rows.append((("c", i), self.inp["c"][i], D))
        rows.append((("c_ctx", 0), self.inp["c_ctx"], D))
        place = {}
        tile_i, r = 0, 0
        for key, src, ln in rows:
            nr = (ln + 127) // 128
            if r + nr > 128:
                tile_i, r = tile_i + 1, 0
            place[key] = (tile_i, r, nr, ln)
            r += nr
        ntile = tile_i + 1
        self.VB = VB = k.tile("VB", [128, ntile, 128], F32)
        with k.scope() as sc:
            RT = sc.tile("rowtile", [128, ntile, 128], F32)
            k.memset(k.DVE, RT[:], 0.0, w=[RT.b()])
            for key, src, ln in rows:
                ti, r0, nr, _ = place[key]
                if ln >= 128:
                    k.dma(k.SP, RT[r0:r0 + nr, ti, :], src.rearrange("(r c) -> r c", c=128), w=[RT.b()])
                else:
                    k.dma(k.SP, RT[r0:r0 + 1, ti, 0:ln], src.rearrange("(r c) -> r c", r=1), w=[RT.b()])
            for ti in range(ntile):
                bk = k.bank()
                k.tr(bk[:, 0:128], RT[:, ti, :], self.C["ident_f"][:], r=[RT.b(), self.C["ident_f"].b()], w=[bk.b()])
                k.cp(k.DVE, VB[:, ti, :], bk[:, 0:128], r=[bk.b()], w=[VB.b()])
        self.place = place

    def vec(self, name, l, i0=0, n=1):
        ti, r0, nr, ln = self.place[(name, l)]
        return self.VB[:, ti, r0 + i0:r0 + i0 + n]

    def derive(self):
        k, nc, C = self.k, self.nc, self.C
        V, A, P = k.DVE, k.ACT, k.POOL
        VBb = self.VB.b()
        with k.scope() as sc:
            scT = sc.tile("scT", [128, 8, 3], BF16)
            for i in range(3):
                src = self.vec("c", min(i, self.nseq - 1), 0, 8) if i < 2 else self.vec("c_ctx", 0, 0, 8)
                k.act(scT[:, :, i], src, AF.Silu, r=[VBb], w=[scT.b()])
            WA = [sc.tile(f"wada{i}", [128, 8, 512], BF16) for i in range(2)]
            for l in range(self.depth):
                ADA = sc.tile(f"ADA{l}", [128, 48, 3], F32)
                bk = k.bank()
                wsrc = self.WB["w_ada"]
                wv = wsrc.t[l].rearrange("(kc p) n -> p kc n", p=128)
                for g in range(12):
                    wt = WA[g % 2]
                    k.dma(k.SP, wt[:], wv[:, :, g * 512:(g + 1) * 512], r=[wsrc.b(l)], w=[wt.b()])
                    for jj in range(4):
                        j = g * 4 + jj
                        for kc in range(8):
                            k.mm(bk[:, j * 3:j * 3 + 3], wt[:, kc, jj * 128:(jj + 1) * 128], scT[:, kc, :],
                                 kc == 0, kc == 7, r=[wt.b(), scT.b()], w=[bk.b()], sig=(kc == 7))
                bada = self.vec("b_ada", l, 0, 48)
                k.tt(V, ADA[:], bk[:, 0:144].rearrange("p (j c) -> p j c", c=3),
                     bada.unsqueeze(2).to_broadcast([128, 48, 3]), ALU.add, r=[bk.b(), VBb], w=[ADA.b()])
                self.dump(f"ada{l}", ADA[:], [128, 48, 3])
                M = self.MOD[l]
                g = [self.vec("norm_gains", l, 8 * i, 8).unsqueeze(2).to_broadcast([128, 8, 3]) for i in range(4)]
                tmp = sc.tile(f"modtmp{l}", [128, 8, 3], F32)
                k.cp(V, M[:, 0], ADA[:, 0:8, :], r=[ADA.b()], w=[M.b()])
                k.ts(V, tmp[:], ADA[:, 8:16, :], 1.0, ALU.add, r=[ADA.b()], w=[tmp.b()])
                k.tt(V, M[:, 1], tmp[:], g[0], ALU.mult, r=[tmp.b(), VBb], w=[M.b()])
                k.tt(V, M[:, 2], ADA[:, 16:24, :], g[1], ALU.mult, r=[ADA.b(), VBb], w=[M.b()])
                k.cp(V, M[:, 3], ADA[:, 24:32, :], r=[ADA.b()], w=[M.b()])
                tmp2 = sc.tile(f"modtmp2{l}", [128, 8, 3], F32)
                k.ts(V, tmp2[:], ADA[:, 32:40, :], 1.0, ALU.add, r=[ADA.b()], w=[tmp2.b()])
                k.tt(V, M[:, 4], tmp2[:], g[2], ALU.mult, r=[tmp2.b(), VBb], w=[M.b()])
                k.tt(V, M[:, 5], ADA[:, 40:48, :], g[3], ALU.mult, r=[ADA.b(), VBb], w=[M.b()])
            for l in range(self.depth):
                e = sc.tile(f"c1e{l}", [128, 8], F32)
                k.act(e[:], self.vec("lru_lambda", l, 0, 8), AF.Exp, r=[VBb], w=[e.b()], scale=-1.0)
                k.ts(V, e[:], e[:], 1.0, ALU.add, r=[e.b()], w=[e.b()])
                k.act(e[:], e[:], AF.Ln, r=[e.b()], w=[e.b()])
                k.ts(V, self.C1[:, l, 0, :], e[:], -8.0, ALU.mult, r=[e.b()], w=[self.C1.b()])
                k.ts(V, self.C1[:, l, 1, :], e[:], -16.0, ALU.mult, r=[e.b()], w=[self.C1.b()])
            k.memset(V, self.LB[:, 0, 0, :], 0.0, w=[self.LB.b()])
            k.memset(V, self.LB[:, 0, 1, :], 1.0, w=[self.LB.b()])
            k.memset(V, self.LB[:, 0, 2, :], -1.0, w=[self.LB.b()])
            if self.depth > 1:
                dlt = sc.tile("lbd", [128, 4], F32)
                k.tt(V, dlt[:], self.vec("hg_lower_bounds", 1, 0, 4), self.vec("hg_lower_bounds", 0, 0, 4),
                     ALU.subtract, r=[VBb], w=[dlt.b()])
                k.act(self.LB[:, 1, 0, :], dlt[:], AF.Sigmoid, r=[dlt.b()], w=[self.LB.b()])
                k.act(self.LB[:, 1, 1, :], dlt[:], AF.Sigmoid, r=[dlt.b()], w=[self.LB.b()], scale=-1.0)
                k.ts(V, self.LB[:, 1, 2, :], self.LB[:, 1, 1, :], -1.0, ALU.mult, r=[self.LB.b()], w=[self.LB.b()])
            for l in range(self.depth):
                k.tt(V, self.FB[:, l, 0:1], self.vec("hy_freq0", l), self.vec("hy_ff_b1", l), ALU.mult, r=[VBb], w=[self.FB.b()])
                k.tt(V, self.FB[:, l, 1:2], self.vec("hy_freq1", l), self.vec("hy_ff_b2", l), ALU.mult, r=[VBb], w=[self.FB.b()])
            skr = sc.tile("skraw", [128, self.depth, 512], F32)
            for l in range(self.depth):
                k.dma(k.SP, skr[:, l, :], self.inp["hy_skip"][l].partition_broadcast(128), w=[skr.b()])
                k.ts(V, self.SK[:, l, :], skr[:, l, :], 2.0 / (2 * SEQ), ALU.mult, r=[skr.b()], w=[self.SK.b()])
            k.ts(V, self.SK[:, self.depth, :], skr[:, 0, :], 2.0 / (2 * CTX), ALU.mult, r=[skr.b()], w=[self.SK.b()])

    def wbd(self, l, d, ri, cc):
        return self.WBD[:, (d * 2 + ri) * 4 + cc, :]

    def build_wbd(self, sc, l):
        k = self.k
        self.WBD = sc.tile("WBD", [128, 16, 128], BF16)
        with k.scope() as sc2:
            stg = sc2.tile("wbdstage", [128, 16, 128], F32)
            k.memset(k.POOL, stg[:], 0.0, w=[stg.b()])
            for d in range(2):
                for ri, nm in enumerate(("lru_wr", "lru_wi")):
                    for cc in range(4):
                        idx = (d * 2 + ri) * 4 + cc
                        for h in range(2):
                            k.dma(k.SP, stg[h * 64:(h + 1) * 64, idx, h * 64:(h + 1) * 64],
                                  self.inp[nm][l, d, 2 * cc + h], w=[stg.b()])
            k.cp(k.DVE, self.WBD[:], stg[:], r=[stg.b()], w=[self.WBD.b()])

    def filt(self, l, ctx):
        k, nc, C = self.k, self.nc, self.C
        V, A, P = k.DVE, k.ACT, k.POOL
        VBb = self.VB.b()
        L = CTX if ctx else SEQ
        nt = L // 128
        blocks = [(0, 256)] if ctx else [(i * 512, (i + 1) * 512) for i in range(4)]
        sk = self.SK[:, self.depth if ctx else l, :]
        scale = 2.0 / (2 * L)
        PI = math.pi
        with k.scope() as sc:
            zT = sc.tile("zT", [17, L], F32)
            k.dma(k.SP, zT[:], self.cst["zC" if ctx else "zL"], w=[zT.b()])
            w1 = sc.tile("fw1", [17, 64], F32)
            w2 = sc.tile("fw2", [64, 64], F32)
            w3 = sc.tile("fw3", [64, 1024], F32)
            k.dma(k.SP, w1[:], self.inp["hy_ff_w1"][l], w=[w1.b()])
            k.dma(k.SP, w2[:], self.inp["hy_ff_w2"][l], w=[w2.b()])
            k.dma(k.SP, w3[:], self.inp["hy_ff_w3"][l], w=[w3.b()])
            h1 = sc.tile("fh1", [64, L], F32)
            h2 = sc.tile("fh2", [64, L], F32)
            arg = sc.tile("farg", [64, 512], F32)
            m1 = sc.tile("fm1", [64, 512], F32)
            for stage in range(2):
                wt, src, dst = (w1, zT, h1) if stage == 0 else (w2, h1, h2)
                kk = 17 if stage == 0 else 64
                fr = self.vec("hy_freq0" if stage == 0 else "hy_freq1", l)[0:64, :]
                fb = self.FB[0:64, l, stage:stage + 1]
                for (b0, b1) in blocks:
                    n = b1 - b0
                    bk = k.bank()
                    k.mm(bk[0:64, 0:n], wt[0:kk, :], src[0:kk, b0:b1], True, True, r=[wt.b(), src.b()], w=[bk.b()], sig=True)
                    k.ts(V, arg[:, 0:n], bk[0:64, 0:n], fr, ALU.mult, fb, ALU.add, r=[bk.b(), VBb, self.FB.b()], w=[arg.b()])
                    k.ts(V, m1[:, 0:n], arg[:, 0:n], -PI, ALU.is_lt, 2 * PI, ALU.mult, r=[arg.b()], w=[m1.b()])
                    k.tt(V, m1[:, 0:n], m1[:, 0:n], arg[:, 0:n], ALU.add, r=[m1.b(), arg.b()], w=[m1.b()])
                    k.ts(V, arg[:, 0:n], arg[:, 0:n], PI, ALU.is_gt, -2 * PI, ALU.mult, r=[arg.b()], w=[arg.b()])
                    k.tt(V, arg[:, 0:n], arg[:, 0:n], m1[:, 0:n], ALU.add, r=[m1.b(), arg.b()], w=[arg.b()])
                    k.act(dst[:, b0:b1], arg[:, 0:n], AF.Sin, r=[arg.b()], w=[dst.b()])
            self.dump(f"fh2_{l}_{int(ctx)}", h2[:], [64, L])
            hs = sc.tile("fhs", [128, nt, 512], BF16)
            hd = sc.tile("fhd", [128, nt, 512], BF16)
            dec = sc.tile("fdec", [128, 512], F32)
            hf = sc.tile("fhf", [128, 512], F32)
            hb = sc.tile("fhb", [128, 512], F32)
            for jt in range(nt):
                b0, b1 = k.bank(), k.bank()
                k.mm(b0[:, :], h2[:, jt * 128:(jt + 1) * 128], w3[:, 0:512], True, True, r=[h2.b(), w3.b()], w=[b0.b()], sig=True)
                k.mm(b1[:, :], h2[:, jt * 128:(jt + 1) * 128], w3[:, 512:1024], True, True, r=[h2.b(), w3.b()], w=[b1.b()], sig=True)
                col = (16 if ctx else 0) + jt
                k.act(dec[:], C["delta"][:], AF.Exp, r=[C["delta"].b(), C["negt"].b()], w=[dec.b()], scale=C["negt"][:, col:col + 1])
                k.tt(V, hf[:], b0[:, :], dec[:], ALU.mult, r=[b0.b(), dec.b()], w=[hf.b()])
                k.tt(V, hb[:], b1[:, :], dec[:], ALU.mult, r=[b1.b(), dec.b()], w=[hb.b()])
                if jt == 0:
                    k.memset(V, hb[0:1, :], 0.0, w=[hb.b()])
                k.tt(V, hs[:, jt, :], hf[:], hb[:], ALU.add, r=[hf.b(), hb.b()], w=[hs.b(jt)])
                k.tt(V, hd[:, jt, :], hb[:], hf[:], ALU.subtract, r=[hf.b(), hb.b()], w=[hd.b(jt)])
            FW = None if ctx else [sc.tile(f"ffw{i}", [128, 16, 2, 128], BF16) for i in range(2)]
            KT = [sc.tile(f"fkt{i}", [128, 2, 512], BF16) for i in range(2)]
            dst = self.KSC if ctx else self.KS
            for ft in range(nt):
                if ctx:
                    fw = lambda jt, cs: C["fwdC"][:, jt, cs, ft * 128:(ft + 1) * 128]
                    fwb = C["fwdC"].b()
                else:
                    fwt = FW[ft % 2]
                    k.dma(k.SP, fwt[:], self.cst["fwdL"][ft], w=[fwt.b()])
                    fw = lambda jt, cs: fwt[:, jt, cs, :]
                    fwb = fwt.b()
                bA, bB = k.bank(), k.bank()
                for jt in range(nt):
                    k.mm(bA[:, :], fw(jt, 0), hs[:, jt, :], jt == 0, jt == nt - 1, r=[fwb, hs.b(jt)], w=[bA.b()], sig=(jt == nt - 1))
                for jt in range(nt):
                    k.mm(bB[:, :], fw(jt, 1), hd[:, jt, :], jt == 0, jt == nt - 1, r=[fwb, hd.b(jt)], w=[bB.b()], sig=(jt == nt - 1))
                kt = KT[ft % 2]
                k.stt(kt[:, 0, :], bA[:, :], scale, sk, ALU.mult, ALU.add, r=[bA.b(), self.SK.b()], w=[kt.b()])
                k.act(kt[:, 1, :], bB[:, :], AF.Copy, r=[bB.b()], w=[kt.b()], scale=scale)
                if ctx:
                    k.dma(k.SP, dst.t[ft], kt[:], r=[kt.b()], w=[dst.b(ft)])
                else:
                    k.dma(k.SP, dst.t[l, ft], kt[:], r=[kt.b()], w=[dst.b(l, ft)])

    def resident(self):
        k = self.k
        self.EPSC = k.tile("epsc", [128, 1], F32)
        k.memset(k.DVE, self.EPSC[:], EPS, w=[self.EPSC.b()])
        self.MOD = [k.tile(f"MOD{l}", [128, 6, 8, 3], F32) for l in range(self.depth)]
        self.C1 = k.tile("C1", [128, self.depth, 2, 8], F32)
        self.LB = k.tile("LB", [128, self.depth, 3, 4], F32)

    def load_x(self, s):
        k, C = self.k, self.C
        X = self.X
        with k.scope() as sc:
            XT = [sc.tile(f"xt{i}", [128, D], F32) for i in range(3)]
            for tt in range(T // 128):
                xt = XT[tt % 3]
                src = self.inp["ctx"][s, tt * 128:(tt + 1) * 128, :] if tt < 2 else self.inp["x"][s, (tt - 2) * 128:(tt - 1) * 128, :]
                k.dma(k.SP, xt[:], src, w=[xt.b()])
                tbi = self.tbi(tt * 128)
                for h in range(2):
                    bk = k.bank()
                    for j in range(4):
                        kc = h * 4 + j
                        k.tr(bk[:, j * 128:(j + 1) * 128], xt[:, kc * 128:(kc + 1) * 128], C["ident_f"][:],
                             r=[xt.b(), C["ident_f"].b()], w=[bk.b()], sig=(j == 3))
                    q = k.ACT if h == 0 else k.DVE
                    k.cp(q, X[:, h * 4:(h + 1) * 4, tt * 128:(tt + 1) * 128], bk[:, :].rearrange("p (a b) -> p a b", b=128),
                         r=[bk.b()], w=[X.b(kc2, tbi) for kc2 in range(h * 4, h * 4 + 4)])

    @staticmethod
    def tbi(t):
        for i, (a, b) in enumerate(TB):
            if a <= t < b:
                return i
        raise ValueError(t)

    def rstd_of(self, sc, src3, rbufs, n, ones, nk=8, tag="", sq=None, sqb=None):
        k = self.k
        if sq is None:
            sq = sc.tile("sq" + tag, [128, nk, 512], BF16)
        if sqb is None:
            sqb = [sq.b()]
        k.act(sq[:, :, 0:n], src3, AF.Square, r=rbufs, w=sqb)
        bk = k.bank()
        for kc in range(nk):
            k.mm(bk[:, 0:n], ones[:], sq[:, kc, 0:n], kc == 0, kc == nk - 1, r=sqb + [ones.b()], w=[bk.b()], sig=(kc == nk - 1))
        rs = sc.tile("rstd" + tag, [128, 512], F32)
        k.act(rs[:, 0:n], bk[:, 0:n], AF.Ln, r=[bk.b(), self.EPSC.b()], w=[rs.b()], bias=self.EPSC[:, 0:1])
        k.act(rs[:, 0:n], rs[:, 0:n], AF.Exp, r=[rs.b()], w=[rs.b()], scale=-0.5)
        return rs

    def phase_a(self, s, l):
        k, C = self.k, self.C
        X, U, M = self.X, self.U, self.MOD[l]
        with k.scope() as sc:
            for tbi, (t0, t1) in enumerate(TB):
                n = t1 - t0
                col = 2 if tbi == 0 else s
                with k.scope() as sc2:
                    rs = self.rstd_of(sc2, X[:, :, t0:t1], [X.b(kc, tbi) for kc in range(8)], n, C["ones_d"])
                    tmp = [sc2.tile(f"natmp{i}", [128, 512], F32) for i in range(2)]
                    for kc in range(8):
                        tp = tmp[kc % 2]
                        k.stt(tp[:, 0:n], X[:, kc, t0:t1], M[:, 1, kc, col:col + 1], rs[:, 0:n], ALU.mult, ALU.mult,
                              r=[X.b(kc, tbi), M.b(), rs.b()], w=[tp.b()])
                        k.act(U[:, kc, t0:t1], tp[:, 0:n], AF.Identity, r=[tp.b(), M.b()], w=[U.b(kc, tbi)],
                              bias=M[:, 0, kc, col:col + 1])

    def win_tile(self, wt, l, c0, ncol=128):
        k = self.k
        src = self.WB["w_in"]
        k.dma(k.SP, wt[:, :, 0:ncol], src.t[l].rearrange("(kc p) n -> p kc n", p=128)[:, :, c0:c0 + ncol],
              r=[src.b(l)], w=[wt.b()])

    def proj_fm(self, wt, tbi, bk, ncol0=0):
        k, U = self.k, self.U
        t0, t1 = TB[tbi]
        n = t1 - t0
        for kc in range(8):
            k.mm(bk[:, 0:n], wt[:, kc, ncol0:ncol0 + 128], U[:, kc, t0:t1], kc == 0, kc == 7,
                 r=[wt.b(), U.b(kc, tbi)], w=[bk.b()], sig=(kc == 7))
        return n

    def hyena(self, s, l):
        k, C, nc = self.k, self.C, self.nc
        V, A, P = k.DVE, k.ACT, k.POOL
        VBb = self.VB.b()
        need_ctx = l < self.depth - 1
        with k.scope() as sc:
            Utok = sc.tile("Utok", [128, 18, 512], BF16)
            with k.scope() as sc2:
                P1 = [sc2.tile(f"hyP{i}", [128, PT], F32) for i in range(1)]
                acc = sc2.tile("hyacc", [128, PT], F32)
                Zv = sc2.tile("hyZv", [128, PT], F32)
                ubf = [sc2.tile(f"hyu{i}", [128, PT], BF16) for i in range(1)]
                X0S = [sc2.tile(f"hyx0s{i}", [128, T], BF16) for i in range(2)]
                WT = [sc2.tile(f"hyw{i}", [128, 8, 128], BF16) for i in range(3)]
                for p in P1:
                    k.memset(P, p[:], 0.0, w=[p.b()])
                segs = [(0, CTX), (CTX, T)]
                it = 0
                for cc in range(4):
                    ub = ubf[0]
                    x0s = X0S[cc % 2]
                    for comp in (0, 2, 1):
                        wt = WT[it % 3]
                        p1 = P1[0]
                        it += 1
                        self.win_tile(wt, l, C_HY + comp * 512 + cc * 128)
                        for tbi, (t0, t1) in enumerate(TB):
                            bk = k.bank()
                            n = self.proj_fm(wt, tbi, bk)
                            k.cp(A, p1[:, poff(t0):poff(t0) + n], bk[:, 0:n], r=[bk.b()], w=[p1.b()])
                        ch = comp * 4 + cc
                        w0, w1, w2 = (self.vec("hy_short_w", l, kk * 12 + ch) for kk in range(3))
                        bb = self.vec("hy_short_b", l, ch)
                        for (a0, a1) in segs:
                            q0, q1 = poff(a0), poff(a0) + (a1 - a0)
                            k.act(acc[:, q0:q1], p1[:, q0:q1], AF.Identity, r=[p1.b(), VBb], w=[acc.b()], bias=bb, scale=w1)
                            k.stt(acc[:, q0:q1], p1[:, q0 - 1:q1 - 1], w0, acc[:, q0:q1], ALU.mult, ALU.add,
                                  r=[p1.b(), acc.b(), VBb], w=[acc.b()])
                            if comp == 0:
                                k.stt(Zv[:, q0:q1], p1[:, q0 + 1:q1 + 1], w2, acc[:, q0:q1], ALU.mult, ALU.add,
                                      r=[p1.b(), acc.b(), VBb], w=[Zv.b()])
                            elif comp == 2:
                                k.stt(acc[:, q0:q1], p1[:, q0 + 1:q1 + 1], w2, acc[:, q0:q1], ALU.mult, ALU.add,
                                      r=[p1.b(), acc.b(), VBb], w=[acc.b()])
                                k.tt(V, ub[:, q0:q1], acc[:, q0:q1], Zv[:, q0:q1], ALU.mult, r=[acc.b(), Zv.b()], w=[ub.b()])
                            else:
                                k.stt(x0s[:, a0:a1], p1[:, q0 + 1:q1 + 1], w2, acc[:, q0:q1], ALU.mult, ALU.add,
                                      r=[p1.b(), acc.b(), VBb], w=[x0s.b()])
                    k.dma(k.SP, self.YP.t[0, :, cc, :], x0s[:], r=[x0s.b()], w=[self.YP.b(0, cc, i) for i in range(5)])
                    for g0 in range(0, 18, 8):
                        g1 = min(18, g0 + 8)
                        bk = k.bank()
                        bv = bk[:, :].bitcast(BF16)
                        for tt in range(g0, g1):
                            k.tr(bv[:, (tt - g0) * 128:(tt - g0 + 1) * 128], ub[:, poff(tt * 128):poff(tt * 128) + 128], C["ident_b"][:],
                                 r=[ub.b(), C["ident_b"].b()], w=[bk.b()], sig=(tt == g1 - 1))
                        k.cp(A, Utok[:, g0:g1, cc * 128:(cc + 1) * 128],
                             bv[:, 0:(g1 - g0) * 128].rearrange("p (a b) -> p a b", b=128), r=[bk.b()], w=[Utok.b(cc)])
            self.dump("utok", Utok[:], [128, 18, 512], BF16)
            ut_bufs = [Utok.b(cc) for cc in range(4)]
            YP = self.YP
            if need_ctx:
                with k.scope() as sc4:
                    KT = [sc4.tile(f"hykt{i}", [128, 2, 512], BF16) for i in range(2)]
                    AB = [sc4.tile(f"hyab{i}", [128, 2, 512], BF16) for i in range(2)]
                    TM = [sc4.tile(f"hytm{i}", [128, 512], F32) for i in range(4)]
                    Yc = sc4.tile("hyYc", [128, 2, 2, 512], BF16)
                    x0c = sc4.tile("hyx0c", [128, 4, CTX], BF16)
                    k.dma(k.SP, x0c[:], YP.t[0, :, :, 0:CTX], r=[YP.b(0, cc, 0) for cc in range(4)], w=[x0c.b()])
                    for ft in range(2):
                        kt, ab = KT[ft % 2], AB[ft % 2]
                        k.dma(k.SP, kt[:], self.KSC.t[ft], r=[self.KSC.b(ft)], w=[kt.b()])
                        bA, bB = k.bank(), k.bank()
                        for cs, bk in ((0, bA), (1, bB)):
                            for st in range(2):
                                k.mm(bk[:, :], C["fwdC"][:, st, cs, ft * 128:(ft + 1) * 128], Utok[:, st, :], st == 0, st == 1,
                                     r=[C["fwdC"].b()] + ut_bufs, w=[bk.b()], sig=(st == 1))
                        k.cp(A, ab[:, 0, :], bA[:, :], r=[bA.b()], w=[ab.b(0)])
                        k.cp(A, ab[:, 1, :], bB[:, :], r=[bB.b()], w=[ab.b(1)])
                        self.spec_mul(Yc[:, ft, 0, :], Yc[:, ft, 1, :], ab[:, 0, :], ab[:, 1, :], kt[:, 0, :], kt[:, 1, :],
                                      [t[:] for t in TM], [ab.b(0), ab.b(1), kt.b()], TM, [Yc.b(ft)])
                    for cc in range(4):
                        bk = k.bank()
                        i = 0
                        for ft in range(2):
                            for cs in range(2):
                                k.mm(bk[:, 0:CTX], Yc[:, ft, cs, cc * 128:(cc + 1) * 128], C["invC"][:, ft, cs, :], i == 0, i == 3,
                                     r=[Yc.b(ft), C["invC"].b()], w=[bk.b()], sig=(i == 3))
                                i += 1
                        k.tt(V, x0c[:, cc, :], bk[:, 0:CTX], x0c[:, cc, :], ALU.mult, r=[bk.b(), x0c.b()], w=[x0c.b()])
                    k.dma(k.SP, YP.t[0, :, :, 0:CTX], x0c[:], r=[x0c.b()], w=[YP.b(0, cc, 0) for cc in range(4)])
            with k.scope() as sc3:
                DB = [sc3.tile(f"hydb{i}", [128, 4096], BF16) for i in range(2)]
                KT = [sc3.tile(f"hykt{i}", [128, 2, 512], BF16) for i in range(2)]
                AB = [sc3.tile(f"hyab{i}", [128, 2, 256], BF16) for i in range(2)]
                TM = [sc3.tile(f"hytm{i}", [128, 256], F32) for i in range(4)]
                Yf = sc3.tile("hyYf", [128, 16, 2, 256], BF16)
                XB = [sc3.tile(f"hyxb{i}", [128, 512], BF16) for i in range(4)]
                dbi = 0
                xbi = 0
                for half in range(2):
                    h0 = half * 256
                    for ft in range(16):
                        fwt = DB[dbi % 2]
                        dbi += 1
                        fw = fwt[:, :].rearrange("p (a b c) -> p a b c", a=16, b=2)
                        kt, ab = KT[ft % 2], AB[ft % 2]
                        k.dma(k.SP, fwt[:, :], self.cst["fwdL"][ft].rearrange("p a b c -> p (a b c)"), w=[fwt.b()])
                        k.dma(k.SP, kt[:], self.KS.t[l, ft], r=[self.KS.b(l, ft)], w=[kt.b()])
                        bk = k.bank()
                        for cs in range(2):
                            for st in range(16):
                                k.mm(bk[:, cs * 256:(cs + 1) * 256], fw[:, st, cs, :], Utok[:, 2 + st, h0:h0 + 256], st == 0, st == 15,
                                     r=[fwt.b(), Utok.b(half * 2), Utok.b(half * 2 + 1)], w=[bk.b()], sig=(st == 15))
                        k.cp(A, ab[:, :, :], bk[:, :].rearrange("p (a b) -> p a b", a=2), r=[bk.b()], w=[ab.b()])
                        self.spec_mul(Yf[:, ft, 0, :], Yf[:, ft, 1, :], ab[:, 0, :], ab[:, 1, :], kt[:, 0, h0:h0 + 256], kt[:, 1, h0:h0 + 256],
                                      [t[:] for t in TM], [ab.b(), kt.b()], TM, [Yf.b(ft)])
                    bks = [[k.bank() for tb in range(4)] for c2 in range(2)]
                    for ft in range(16):
                        ivt = DB[dbi % 2]
                        dbi += 1
                        iv = ivt[:, :].rearrange("p (a b) -> p a b", a=2)
                        k.dma(k.SP, ivt[:, :], self.cst["invL"][ft].rearrange("p a b -> p (a b)"), w=[ivt.b()])
                        for c2 in range(2):
                            for tb in range(4):
                                bk = bks[c2][tb]
                                for cs in range(2):
                                    k.mm(bk[:, :], Yf[:, ft, cs, c2 * 128:(c2 + 1) * 128], iv[:, cs, tb * 512:(tb + 1) * 512],
                                         ft == 0 and cs == 0, ft == 15 and cs == 1, r=[Yf.b(ft), ivt.b()], w=[bk.b()],
                                         sig=(cs == 1 and tb == 3 and c2 == 1))
                    for c2 in range(2):
                        cc = half * 2 + c2
                        for tb in range(4):
                            t0 = CTX + tb * 512
                            xb = XB[xbi % 4]
                            xbi += 1
                            k.dma(k.SP, xb[:], YP.t[0, :, cc, t0:t0 + 512], r=[YP.b(0, cc, tb + 1)], w=[xb.b()])
                            k.tt(V, xb[:], bks[c2][tb][:, :], xb[:], ALU.mult, r=[bks[c2][tb].b(), xb.b()], w=[xb.b()])
                            k.dma(k.SP, YP.t[0, :, cc, t0:t0 + 512], xb[:], r=[xb.b()], w=[YP.b(0, cc, tb + 1)])

    def lru(self, s, l):
        k, C = self.k, self.C
        V, A, P = k.DVE, k.ACT, k.POOL
        VBb = self.VB.b()
        YP = self.YP
        with k.scope() as sc:
            self.build_wbd(sc, l)
            xp = sc.tile("lrxp", [128, PT], F32)
            k.memset(P, xp[:], 0.0, w=[xp.b()])
            hsum = sc.tile("lrhs", [128, T], F32)
            xc = sc.tile("lrxc", [128, T], F32)
            xcb = sc.tile("lrxcb", [128, T], BF16)
            R = sc.tile("lrR", [128, T], F32)
            I = sc.tile("lrI", [128, T], F32)
            Aa = sc.tile("lrA", [128, T], F32)
            YS = [sc.tile(f"lrys{i}", [128, T], BF16) for i in range(2)]
            GG = [sc.tile(f"lrgg{i}", [128, 512], BF16) for i in range(2)]
            WX = [sc.tile(f"lrwx{i}", [128, 8, 128], BF16) for i in range(2)]
            WG = [sc.tile(f"lrwg{i}", [128, 8, 128], BF16) for i in range(2)]
            segs = [(0, CTX), (CTX, T)]
            for cc in range(4):
                wx, wg, ys = WX[cc % 2], WG[cc % 2], YS[cc % 2]
                self.win_tile(wx, l, C_LX + cc * 128)
                self.win_tile(wg, l, C_LG + cc * 128)
                for tbi, (t0, t1) in enumerate(TB):
                    bk = k.bank()
                    n = self.proj_fm(wx, tbi, bk)
                    k.cp(A, xp[:, poff(t0):poff(t0) + n], bk[:, 0:n], r=[bk.b()], w=[xp.b()])
                for d in range(2):
                    sg = -1 if d == 0 else 1
                    ch = d * 4 + cc
                    wk = [self.vec("lru_conv_w", l, (d * 4 + kk) * 4 + cc) for kk in range(4)]
                    bb = self.vec("lru_conv_b", l, ch)
                    for (a0, a1) in segs:
                        q0, q1 = poff(a0), poff(a0) + (a1 - a0)
                        k.act(xc[:, a0:a1], xp[:, q0:q1], AF.Identity, r=[xp.b(), VBb], w=[xc.b()], bias=bb, scale=wk[3])
                        for j in (1, 2, 3):
                            k.stt(xc[:, a0:a1], xp[:, q0 + sg * j:q1 + sg * j], wk[3 - j], xc[:, a0:a1], ALU.mult, ALU.add,
                                  r=[xp.b(), xc.b(), VBb], w=[xc.b()])
                    k.cp(P, xcb[:], xc[:], r=[xc.b()], w=[xcb.b()])
                    for tbi, (t0, t1) in enumerate(TB):
                        n = t1 - t0
                        b0, b1 = k.bank(), k.bank()
                        k.mm(b0[:, 0:n], self.wbd(l, d, 0, cc), xcb[:, t0:t1], True, True, r=[self.WBD.b(), xcb.b()], w=[b0.b()], sig=True)
                        k.mm(b1[:, 0:n], self.wbd(l, d, 1, cc), xcb[:, t0:t1], True, True, r=[self.WBD.b(), xcb.b()], w=[b1.b()], sig=True)
                        k.act(R[:, t0:t1], b0[:, 0:n], AF.Sigmoid, r=[b0.b(), VBb], w=[R.b()], bias=self.vec("lru_br", l, ch))
                        k.act(I[:, t0:t1], b1[:, 0:n], AF.Sigmoid, r=[b1.b(), VBb], w=[I.b()], bias=self.vec("lru_bi", l, ch))
                    k.act(Aa[:], R[:], AF.Exp, r=[R.b(), self.C1.b()], w=[Aa.b()], scale=self.C1[:, l, 0, ch:ch + 1])
                    k.act(R[:], R[:], AF.Exp, r=[R.b(), self.C1.b()], w=[R.b()], scale=self.C1[:, l, 1, ch:ch + 1])
                    k.act(R[:], R[:], AF.Sqrt, r=[R.b()], w=[R.b()], bias=1.0, scale=-1.0)
                    k.tt(P, I[:], I[:], xc[:], ALU.mult, r=[I.b(), xc.b()], w=[I.b()])
                    k.tt(V, I[:], I[:], R[:], ALU.mult, r=[I.b(), R.b()], w=[I.b()])
                    if d == 0:
                        k.op(V, lambda: V.eng.tensor_tensor_scan(out=hsum[:], data0=Aa[:], data1=I[:], initial=0.0, op0=ALU.mult, op1=ALU.add),
                             r=[Aa.b(), I.b()], w=[hsum.b()])
                    else:
                        k.op(V, lambda: V.eng.tensor_tensor_scan(out=I[:, 0:CTX][:, ::-1], data0=Aa[:, 0:CTX][:, ::-1], data1=I[:, 0:CTX][:, ::-1],
                                                                 initial=0.0, op0=ALU.mult, op1=ALU.add), r=[Aa.b(), I.b()], w=[I.b()])
                        k.op(V, lambda: V.eng.tensor_tensor_scan(out=I[:, CTX:T][:, ::-1], data0=Aa[:, CTX:T][:, ::-1], data1=I[:, CTX:T][:, ::-1],
                                                                 initial=I[:, 0:1], op0=ALU.mult, op1=ALU.add), r=[Aa.b(), I.b()], w=[I.b()])
                        k.tt(P, hsum[:], hsum[:], I[:], ALU.add, r=[hsum.b(), I.b()], w=[hsum.b()])
                for tbi, (t0, t1) in enumerate(TB):
                    n = t1 - t0
                    bk = k.bank()
                    self.proj_fm(wg, tbi, bk)
                    gg = GG[tbi % 2]
                    k.act(gg[:, 0:n], bk[:, 0:n], AF.Gelu_apprx_tanh, r=[bk.b()], w=[gg.b()])
                    k.tt(V, ys[:, t0:t1], hsum[:, t0:t1], gg[:, 0:n], ALU.mult, r=[hsum.b(), gg.b()], w=[ys.b()])
                k.dma(k.SP, YP.t[1, :, cc, :], ys[:], r=[ys.b()], w=[YP.b(1, cc, i) for i in range(5)])

    def hgrn(self, s, l):
        k, C = self.k, self.C
        V, A, P = k.DVE, k.ACT, k.POOL
        VBb = self.VB.b()
        YP, U = self.YP, self.U
        NCH = T // 64
        with k.scope() as sc:
            qb = sc.tile("hgq", [128, T], BF16)
            Vtok = sc.tile("hgV", [128, 18, 128], BF16)
            O = sc.tile("hgO", [128, T], F32)
            Bc = sc.tile("hgB", [128, T], F32)
            kkb = sc.tile("hgkk", [128, T], BF16)
            CH = sc.tile("hgCH", [128, 6, NCH], F32)
            S = sc.tile("hgS", [128, 128], F32)
            SB = [sc.tile(f"hgSb{i}", [128, 128], BF16) for i in range(4)]
            Dt = sc.tile("hgD", [128, 512], F32)
            E1 = sc.tile("hgE1", [128, 512], F32)
            E2 = sc.tile("hgE2", [128, 512], F32)
            QT = [sc.tile(f"hgqt{i}", [128, 512], BF16) for i in range(2)]
            KT = [sc.tile(f"hgkt{i}", [128, 512], BF16) for i in range(2)]
            QH = [sc.tile(f"hgqh{i}", [128, 512], BF16) for i in range(2)]
            KH = [sc.tile(f"hgkh{i}", [128, 512], BF16) for i in range(2)]
            PM = [sc.tile(f"hgpm{i}", [128, 128], BF16) for i in range(3)]
            KK = [sc.tile(f"hgkhk{i}", [128, 128], BF16) for i in range(3)]
            ys = sc.tile("hgys", [128, T], BF16)
            SG = [sc.tile(f"hgsg{i}", [128, 512], F32) for i in range(2)]
            TMP = [sc.tile(f"hgtmp{i}", [128, 512], F32) for i in range(2)]
            WT = [sc.tile(f"hgw{i}", [128, 8, 128], BF16) for i in range(6)]
            wi_ = 0
            blk_i = 0
            pr_i = 0
            for hd in range(4):
                ws = {}
                for nm, c0 in (("q", C_Q), ("ff", C_FF), ("fb", C_FB), ("i", C_I), ("og", C_OG)):
                    ws[nm] = WT[wi_ % 6]
                    wi_ += 1
                    self.win_tile(ws[nm], l, c0 + hd * 128)
                for tbi, (t0, t1) in enumerate(TB):
                    bk = k.bank()
                    n = self.proj_fm(ws["q"], tbi, bk)
                    k.act(qb[:, t0:t1], bk[:, 0:n], AF.Silu, r=[bk.b()], w=[qb.b()])
                for g0 in range(0, 18, 4):
                    g1 = min(18, g0 + 4)
                    bk = k.bank()
                    for tt in range(g0, g1):
                        tbi = self.tbi(tt * 128)
                        for kc in range(8):
                            k.mm(bk[:, (tt - g0) * 128:(tt - g0 + 1) * 128], U[:, kc, tt * 128:(tt + 1) * 128], ws["i"][:, kc, :],
                                 kc == 0, kc == 7, r=[U.b(kc, tbi), ws["i"].b()], w=[bk.b()], sig=(kc == 7))
                    k.cp(A, Vtok[:, g0:g1, :], bk[:, 0:(g1 - g0) * 128].rearrange("p (a b) -> p a b", b=128), r=[bk.b()], w=[Vtok.b()])
                if getattr(self, "hg_cut", 0) == 1:
                    return
                for d in range(2):
                    wf = ws["ff"] if d == 0 else ws["fb"]
                    for tbi, (t0, t1) in enumerate(TB):
                        bk = k.bank()
                        n = self.proj_fm(wf, tbi, bk)
                        k.act(Bc[:, t0:t1], bk[:, 0:n], AF.Sigmoid, r=[bk.b()], w=[Bc.b()])
                    lbv, omlv, nomlv = (self.LB[:, l, i, hd:hd + 1] for i in range(3))
                    k.ts(V, kkb[:], Bc[:], nomlv, ALU.mult, omlv, ALU.add, r=[Bc.b(), self.LB.b()], w=[kkb.b()])
                    k.ts(V, Bc[:], Bc[:], omlv, ALU.mult, lbv, ALU.add, r=[Bc.b(), self.LB.b()], w=[Bc.b()])
                    k.act(Bc[:], Bc[:], AF.Ln, r=[Bc.b()], w=[Bc.b()])
                    ones = C["onesrow"]
                    if d == 0:
                        order = [0, 1, 2, 3, 4]
                    else:
                        order = [0, 4, 3, 2, 1]
                    prev = None
                    for tbi in order:
                        t0, t1 = TB[tbi]
                        n = t1 - t0
                        seg = Bc[:, t0:t1] if d == 0 else Bc[:, t0:t1][:, ::-1]
                        init = 0.0 if prev is None else prev
                        k.op(V, lambda seg=seg, n=n, init=init: V.eng.tensor_tensor_scan(out=seg, data0=ones[:, 0:n], data1=seg, initial=init,
                                                                                      op0=ALU.mult, op1=ALU.add), r=[Bc.b(), ones.b()], w=[Bc.b()])
                        prev = Bc[:, t1 - 1:t1] if d == 0 else Bc[:, t0:t0 + 1]
                    if d == 0:
                        k.cp(V, CH[:, 0, :], Bc[:, 32::64], r=[Bc.b()], w=[CH.b()])
                        k.cp(V, CH[:, 1, :], Bc[:, 63::64], r=[Bc.b()], w=[CH.b()])
                        k.memset(V, CH[:, 2, 0:1], 0.0, w=[CH.b()])
                        k.cp(V, CH[:, 2, 1:NCH], CH[:, 1, 0:NCH - 1], r=[CH.b()], w=[CH.b()])
                    else:
                        k.cp(V, CH[:, 0, :], Bc[:, 31::64], r=[Bc.b()], w=[CH.b()])
                        k.cp(V, CH[:, 1, :], Bc[:, 0::64], r=[Bc.b()], w=[CH.b()])
                        k.cp(V, CH[:, 2, 0:NCH - 1], CH[:, 1, 1:NCH], r=[CH.b()], w=[CH.b()])
                        k.memset(V, CH[:, 2, 3:4], 0.0, w=[CH.b()])
                        k.cp(V, CH[:, 2, NCH - 1:NCH], CH[:, 1, 0:1], r=[CH.b()], w=[CH.b()])
                    k.tt(V, CH[:, 3, :], CH[:, 0, :], CH[:, 2, :], ALU.subtract, r=[CH.b()], w=[CH.b()])
                    k.tt(V, CH[:, 4, :], CH[:, 1, :], CH[:, 0, :], ALU.subtract, r=[CH.b()], w=[CH.b()])
                    k.tt(V, CH[:, 5, :], CH[:, 1, :], CH[:, 2, :], ALU.subtract, r=[CH.b()], w=[CH.b()])
                    k.act(CH[:, 3:6, :], CH[:, 3:6, :], AF.Exp, r=[CH.b()], w=[CH.b()])
                    if getattr(self, "hg_cut", 0) == 2:
                        return
                    k.memset(V, S[:], 0.0, w=[S.b()])
                    sbi = 0
                    sb_cur = SB[sbi % 4]
                    k.memset(V, sb_cur[:], 0.0, w=[sb_cur.b()])
                    mask = C["maskF"] if d == 0 else C["maskB"]
                    for tbi in order:
                        t0, t1 = TB[tbi]
                        n = t1 - t0
                        c0, nch = t0 // 64, n // 64
                        qt, kt_, qh, kh = QT[blk_i % 2], KT[blk_i % 2], QH[blk_i % 2], KH[blk_i % 2]
                        blk_i += 1

                        def bc(row):
                            return CH[:, row, c0:c0 + nch].unsqueeze(2).to_broadcast([128, nch, 64])

                        def v3(ap):
                            return ap.rearrange("p (a b) -> p a b", b=64)
                        k.tt(V, v3(Dt[:, 0:n]), v3(Bc[:, t0:t1]), bc(0), ALU.subtract, r=[Bc.b(), CH.b()], w=[Dt.b()])
                        k.act(E1[:, 0:n], Dt[:, 0:n], AF.Exp, r=[Dt.b()], w=[E1.b()])
                        k.act(E2[:, 0:n], Dt[:, 0:n], AF.Exp, r=[Dt.b()], w=[E2.b()], scale=-1.0)
                        k.tt(V, qt[:, 0:n], qb[:, t0:t1], E1[:, 0:n], ALU.mult, r=[qb.b(), E1.b()], w=[qt.b()])
                        k.tt(P, kt_[:, 0:n], kkb[:, t0:t1], E2[:, 0:n], ALU.mult, r=[kkb.b(), E2.b()], w=[kt_.b()])
                        k.tt(P, v3(E1[:, 0:n]), v3(E1[:, 0:n]), bc(3), ALU.mult, r=[E1.b(), CH.b()], w=[E1.b()])
                        k.tt(V, v3(E2[:, 0:n]), v3(E2[:, 0:n]), bc(4), ALU.mult, r=[E2.b(), CH.b()], w=[E2.b()])
                        k.tt(V, qh[:, 0:n], qb[:, t0:t1], E1[:, 0:n], ALU.mult, r=[qb.b(), E1.b()], w=[qh.b()])
                        k.tt(P, kh[:, 0:n], kkb[:, t0:t1], E2[:, 0:n], ALU.mult, r=[kkb.b(), E2.b()], w=[kh.b()])
                        if getattr(self, "hg_cut", 0) == 3:
                            return
                        npair = n // 128
                        pairs = list(range(npair)) if d == 0 else list(range(npair - 1, -1, -1))
                        for j in pairs:
                            o = j * 128
                            tp = t0 + o
                            tt = tp // 128
                            pm, kk_ = PM[pr_i % 3], KK[pr_i % 3]
                            pr_i += 1
                            b_sc, b_tr, b_kv, b_kv2, b_o = k.bank(), k.bank(), k.bank(), k.bank(), k.bank()
                            kvb = [b_kv, b_kv2]
                            k.mm(b_sc[:, 0:128], kt_[:, o:o + 128], qt[:, o:o + 128], True, True, r=[kt_.b(), qt.b()], w=[b_sc.b()], sig=True)
                            if getattr(self, "hg_cut", 0) == 7:
                                return
                            if getattr(self, "hg_var", 0) == 1:
                                sct = TMP[0]
                                k.cp(A, sct[:, 0:128], b_sc[:, 0:128], r=[b_sc.b()], w=[sct.b()])
                                k.tt(V, pm[:], sct[:, 0:128], mask[:], ALU.mult, r=[sct.b(), mask.b()], w=[pm.b()])
                            elif getattr(self, "hg_var", 0) == 2:
                                k.cp(V, pm[:], b_sc[:, 0:128], r=[b_sc.b()], w=[pm.b()])
                            else:
                                k.memset(P, pm[:], 0.0, w=[pm.b()])
                                k.op(V, lambda pm=pm, b_sc=b_sc: V.eng.copy_predicated(out=pm[:], mask=mask[:], data=b_sc[:, 0:128]),
                                     r=[b_sc.b(), mask.b(), pm.b()], w=[pm.b()])
                            if getattr(self, "hg_cut", 0) == 6:
                                return
                            btv = b_tr[:, :].bitcast(BF16)
                            k.tr(btv[:, 0:128], kh[:, o:o + 128], C["ident_b"][:], r=[kh.b(), C["ident_b"].b()], w=[b_tr.b()])
                            k.cp(A, kk_[:], btv[:, 0:128], r=[b_tr.b()], w=[kk_.b()])
                            if getattr(self, "hg_cut", 0) == 4:
                                return
                            k.mm(b_kv[:, 0:128], kk_[0:64, :], Vtok[0:64, tt, :], True, True, r=[kk_.b(), Vtok.b()], w=[b_kv.b()], sig=True)
                            k.mm(b_kv2[:, 0:128], kk_[64:128, :], Vtok[64:128, tt, :], True, True, r=[kk_.b(), Vtok.b()], w=[b_kv2.b()], sig=True)
                            halves = [0, 1] if d == 0 else [1, 0]
                            sb_start = {}
                            for hh in halves:
                                ch = (tp // 64) + hh
                                sb_start[hh] = sb_cur
                                k.stt(S[:], S[:], CH[:, 5, ch:ch + 1], kvb[hh][:, 0:128], ALU.mult, ALU.add,
                                      r=[S.b(), CH.b(), kvb[hh].b()], w=[S.b()])
                                sbi += 1
                                sb_cur = SB[sbi % 4]
                                k.cp(A, sb_cur[:], S[:], r=[S.b()], w=[sb_cur.b()])
                            if getattr(self, "hg_cut", 0) == 5:
                                return
                            k.mm(b_o[:, 0:128], Vtok[:, tt, :], pm[:], True, False, r=[Vtok.b(), pm.b()], w=[b_o.b()], sig=False)
                            k.mm(b_o[:, 0:64], sb_start[0][:], qh[:, o:o + 64], False, False, r=[sb_start[0].b(), qh.b()], w=[b_o.b()], sig=False)
                            k.mm(b_o[:, 64:128], sb_start[1][:], qh[:, o + 64:o + 128], False, True, r=[sb_start[1].b(), qh.b()], w=[b_o.b()], sig=True)
                            if d == 0:
                                k.cp(A, O[:, tp:tp + 128], b_o[:, 0:128], r=[b_o.b()], w=[O.b(tt)])
                            else:
                                k.tt(V, O[:, tp:tp + 128], b_o[:, 0:128], O[:, tp:tp + 128], ALU.add, r=[b_o.b(), O.b(tt)], w=[O.b(tt)])
                ng = self.vec("hg_norm_g", l)
                for tbi, (t0, t1) in enumerate(TB):
                    n = t1 - t0
                    with k.scope() as sc2:
                        obufs = [O.b(tt) for tt in range(t0 // 128, t1 // 128)]
                        rs = self.rstd_of(sc2, O[:, t0:t1].unsqueeze(1), obufs, n, C["ones_v"], nk=1, tag="hg")
                        bk = k.bank()
                        self.proj_fm(ws["og"], tbi, bk)
                        sg, tmp = SG[tbi % 2], TMP[tbi % 2]
                        k.act(sg[:, 0:n], bk[:, 0:n], AF.Silu, r=[bk.b()], w=[sg.b()])
                        k.stt(tmp[:, 0:n], O[:, t0:t1], ng, rs[:, 0:n], ALU.mult, ALU.mult, r=obufs + [VBb, rs.b()], w=[tmp.b()])
                        k.tt(V, ys[:, t0:t1], tmp[:, 0:n], sg[:, 0:n], ALU.mult, r=[tmp.b(), sg.b()], w=[ys.b()])
                k.dma(k.SP, YP.t[2, :, hd, :], ys[:], r=[ys.b()], w=[YP.b(2, hd, i) for i in range(5)])

    def w_tile(self, wt, name, l, c0, nkc, ncol=128, k0=0):
        k = self.k
        src = self.WB[name]
        k.dma(k.SP, wt[:, 0:nkc, 0:ncol],
              src.t[l].rearrange("(kc p) n -> p kc n", p=128)[:, k0:k0 + nkc, c0:c0 + ncol], r=[src.b(l)], w=[wt.b()])

    def merge(self, s, l):
        k, C = self.k, self.C
        V, A, P = k.DVE, k.ACT, k.POOL
        X, U, M, YP = self.X, self.U, self.MOD[l], self.YP
        last = l == self.depth - 1
        pnames = ("w_proj_hy", "w_proj_lru", "w_proj_hg")
        with k.scope() as sc:
            YB = [sc.tile(f"mgy{i}", [128, 4, 512], BF16) for i in range(3)]
            mbf = sc.tile("mgm", [128, 8, 512], BF16)
            MO = sc.tile("mgmo", [128, 8, 512], F32)
            G = [sc.tile(f"mgg{i}", [128, 512], F32) for i in range(2)]
            macc = sc.tile("mgacc", [128, 512], F32)
            T2 = [sc.tile(f"mgt{i}", [128, 512], F32) for i in range(2)]
            WG = [sc.tile(f"mgwg{i}", [128, 8, 128], BF16) for i in range(6)]
            WP = [sc.tile(f"mgwp{i}", [128, 4, 128], BF16) for i in range(6)]
            WO = [sc.tile(f"mgwo{i}", [128, 8, 128], BF16) for i in range(2)]
            it = 0
            for tbi, (t0, t1) in enumerate(TB):
                if last and tbi == 0:
                    continue
                n = t1 - t0
                col = 2 if tbi == 0 else s
                for br in range(3):
                    k.dma(k.SP, YB[br][:, :, 0:n], YP.t[br, :, :, t0:t1], r=[YP.b(br, cc, tbi) for cc in range(4)], w=[YB[br].b()])
                for oc in range(8):
                    for br in range(3):
                        wg, wp = WG[it % 6], WP[it % 6]
                        g, t2 = G[it % 2], T2[it % 2]
                        it += 1
                        self.win_tile(wg, l, C_MG + br * 1024 + oc * 128)
                        self.w_tile(wp, pnames[br], l, oc * 128, 4)
                        bp, bg = k.bank(), k.bank()
                        for kc in range(4):
                            k.mm(bp[:, 0:n], wp[:, kc, :], YB[br][:, kc, 0:n], kc == 0, kc == 3, r=[wp.b(), YB[br].b()], w=[bp.b()], sig=(kc == 3))
                        self.proj_fm(wg, tbi, bg)
                        k.act(g[:, 0:n], bg[:, 0:n], AF.Sigmoid, r=[bg.b()], w=[g.b()])
                        if br == 0:
                            k.tt(V, macc[:, 0:n], bp[:, 0:n], g[:, 0:n], ALU.mult, r=[bp.b(), g.b()], w=[macc.b()])
                        elif br == 1:
                            k.tt(V, t2[:, 0:n], bp[:, 0:n], g[:, 0:n], ALU.mult, r=[bp.b(), g.b()], w=[t2.b()])
                            k.tt(P, macc[:, 0:n], macc[:, 0:n], t2[:, 0:n], ALU.add, r=[macc.b(), t2.b()], w=[macc.b()])
                        else:
                            k.tt(V, t2[:, 0:n], bp[:, 0:n], g[:, 0:n], ALU.mult, r=[bp.b(), g.b()], w=[t2.b()])
                            k.tt(P, mbf[:, oc, 0:n], macc[:, 0:n], t2[:, 0:n], ALU.add, r=[macc.b(), t2.b()], w=[mbf.b(oc)])
                for oc in range(8):
                    wo = WO[oc % 2]
                    self.w_tile(wo, "w_out", l, oc * 128, 8)
                    bk = k.bank()
                    for kc in range(8):
                        k.mm(bk[:, 0:n], wo[:, kc, :], mbf[:, kc, 0:n], kc == 0, kc == 7, r=[wo.b(), mbf.b(kc)], w=[bk.b()], sig=(kc == 7))
                    k.cp(A, MO[:, oc, 0:n], bk[:, 0:n], r=[bk.b()], w=[MO.b(oc)])
                self.resid_update(sc, MO, mbf, n, tbi, t0, t1, M, 2, col, sqb=[mbf.b(oc) for oc in range(8)])

    def resid_update(self, sc, MO, sqt, n, tbi, t0, t1, M, gi, col, sqb=None):
        k, C = self.k, self.C
        X = self.X
        with k.scope() as sc2:
            rs = self.rstd_of(sc2, MO[:, 0:8, 0:n], [MO.b(kc) for kc in range(8)], n, C["ones_d"], sq=sqt, sqb=sqb)
            tmp = [sc2.tile(f"rutmp{i}", [128, 512], F32) for i in range(2)]
            for kc in range(8):
                tp = tmp[kc % 2]
                k.stt(tp[:, 0:n], MO[:, kc, 0:n], M[:, gi, kc, col:col + 1], rs[:, 0:n], ALU.mult, ALU.mult,
                      r=[MO.b(kc), M.b(), rs.b()], w=[tp.b()])
                k.tt(k.POOL, X[:, kc, t0:t1], X[:, kc, t0:t1], tp[:, 0:n], ALU.add, r=[X.b(kc, tbi), tp.b()], w=[X.b(kc, tbi)])

    def mlp(self, s, l):
        k, C = self.k, self.C
        V, A, P = k.DVE, k.ACT, k.POOL
        X, U, M = self.X, self.U, self.MOD[l]
        last = l == self.depth - 1
        with k.scope() as sc:
            H = sc.tile("mlH", [128, 32, 512], BF16)
            MO = Tile(H.t[:, 0:16, :].rearrange("p a b -> p (a b)").bitcast(F32).rearrange("p (a b) -> p a b", b=512), "mlMO")
            MO.b = lambda *key: H.b("mo")
            sqt = sc.tile("mlsq", [128, 8, 512], BF16)
            SQ = [sc.tile(f"mlsqr{i}", [128, 512], F32) for i in range(2)]
            W1 = [sc.tile(f"mlw1{i}", [128, 8, 512], BF16) for i in range(2)]
            W2 = [sc.tile(f"mlw2{i}", [128, 2, 1024], BF16) for i in range(2)]
            hb_all = [H.b(j) for j in range(32)] + [H.b("mo")]
            for tbi, (t0, t1) in enumerate(TB):
                if last and tbi == 0:
                    continue
                n = t1 - t0
                col = 2 if tbi == 0 else s
                with k.scope() as sc2:
                    rs = self.rstd_of(sc2, X[:, :, t0:t1], [X.b(kc, tbi) for kc in range(8)], n, C["ones_d"], sq=sqt)
                    tmp = [sc2.tile(f"mltmp{i}", [128, 512], F32) for i in range(2)]
                    for kc in range(8):
                        tp = tmp[kc % 2]
                        k.stt(tp[:, 0:n], X[:, kc, t0:t1], M[:, 4, kc, col:col + 1], rs[:, 0:n], ALU.mult, ALU.mult,
                              r=[X.b(kc, tbi), M.b(), rs.b()], w=[tp.b()])
                        k.act(U[:, kc, t0:t1], tp[:, 0:n], AF.Identity, r=[tp.b(), M.b()], w=[U.b(kc, tbi)], bias=M[:, 3, kc, col:col + 1])
                for jg in range(8):
                    w1 = W1[jg % 2]
                    self.w_tile(w1, "w_mlp1", l, jg * 512, 8, ncol=512)
                    for jj in range(4):
                        j = jg * 4 + jj
                        bk = k.bank()
                        for kc in range(8):
                            k.mm(bk[:, 0:n], w1[:, kc, jj * 128:(jj + 1) * 128], U[:, kc, t0:t1], kc == 0, kc == 7,
                                 r=[w1.b(), U.b(kc, tbi)], w=[bk.b()], sig=(kc == 7))
                        sq = SQ[j % 2]
                        k.act(sq[:, 0:n], bk[:, 0:n], AF.Square, r=[bk.b()], w=[sq.b()])
                        k.stt(H[:, j, 0:n], bk[:, 0:n], 0.0, sq[:, 0:n], ALU.is_gt, ALU.mult, r=[bk.b(), sq.b()], w=[H.b(j), H.b("mo")])
                bks = [k.bank() for _ in range(8)]
                for jg in range(16):
                    w2 = W2[jg % 2]
                    self.w_tile(w2, "w_mlp2", l, 0, 2, ncol=1024, k0=jg * 2)
                    for jj in range(2):
                        j = jg * 2 + jj
                        for oc in range(8):
                            k.mm(bks[oc][:, 0:n], w2[:, jj, oc * 128:(oc + 1) * 128], H[:, j, 0:n], j == 0, j == 31,
                                 r=[w2.b(), H.b(j)], w=[bks[oc].b()], sig=(oc == 7 and jj == 1))
                for oc in range(8):
                    k.cp(A if oc % 2 == 0 else V, MO[:, oc, 0:n], bks[oc][:, 0:n], r=[bks[oc].b()], w=hb_all)
                self.resid_update(sc, MO, sqt, n, tbi, t0, t1, M, 5, col)

    def permute(self, fwd):
        k = self.k
        X = self.X
        engs = [k.ACT, k.DVE, k.POOL]
        with k.scope() as sc:
            TMPS = [sc.tile(f"pmt{i}", [128, SEQ], F32) for i in range(2)]
            for kc in range(8):
                tmp = TMPS[kc % 2]
                xb = [X.b(kc, tbi) for tbi in range(1, 5)]
                src = X[:, kc, CTX:T]
                if fwd:
                    v = src.rearrange("p (r c) -> p c r", c=GRID_W)
                    tv = tmp[:, :].rearrange("p (c r) -> p c r", c=GRID_W)
                else:
                    v = src.rearrange("p (c r) -> p r c", c=GRID_W)
                    tv = tmp[:, :].rearrange("p (r c) -> p r c", c=GRID_W)
                k.cp(engs[kc % 3], tv, v, r=xb, w=[tmp.b()])
                k.cp(engs[(kc + 1) % 3], X[:, kc, CTX:T], tmp[:, :], r=[tmp.b()], w=xb)

    def store_out(self, s):
        k, C = self.k, self.C
        X = self.X
        with k.scope() as sc:
            OT = [sc.tile(f"ot{i}", [128, D], F32) for i in range(3)]
            for tt in range(2, T // 128):
                ot = OT[tt % 3]
                tbi = self.tbi(tt * 128)
                for h in range(2):
                    bk = k.bank()
                    for j in range(4):
                        kc = h * 4 + j
                        k.tr(bk[:, j * 128:(j + 1) * 128], X[:, kc, tt * 128:(tt + 1) * 128], C["ident_f"][:],
                             r=[X.b(kc, tbi), C["ident_f"].b()], w=[bk.b()], sig=(j == 3))
                    k.cp(k.ACT if h == 0 else k.DVE, ot[:, h * 512:(h + 1) * 512], bk[:, :], r=[bk.b()], w=[ot.b(h)])
                k.dma(k.SP, self.out[s, (tt - 2) * 128:(tt - 1) * 128, :], ot[:], r=[ot.b(0), ot.b(1)], w=[])

    def layer(self, s, l):
        import os
        stop = os.environ.get("KSTOP", "")
        self.phase_a(s, l)
        if stop == f"a{l}":
            return True
        self.hyena(s, l)
        if stop == f"hy{l}":
            return True
        self.hgrn(s, l)
        if stop == f"hg{l}":
            return True
        self.lru(s, l)
        if stop == f"lru{l}":
            return True
        self.merge(s, l)
        if stop == f"mg{l}":
            return True
        self.mlp(s, l)
        if stop == f"mlp{l}":
            return True
        return False

    def build_full(self):
        self.prologue()
        for s in range(self.nseq):
            self.load_x(s)
            for l in range(self.depth):
                if l % 2 == 1:
                    self.permute(True)
                if self.layer(s, l):
                    self.store_out(s)
                    self.finish()
                    return
                if l % 2 == 1:
                    self.permute(False)
            self.store_out(s)
        self.finish()

    def spec_mul(self, yre, yim, A_, B_, Kre, Kim, tm, rb, TM, wb):
        k = self.k
        V, P = k.DVE, k.POOL
        k.tt(V, tm[0], A_, Kre, ALU.mult, r=rb, w=[TM[0].b()])
        k.tt(V, tm[1], B_, Kim, ALU.mult, r=rb, w=[TM[1].b()])
        k.tt(V, yre, tm[0], tm[1], ALU.add, r=[TM[0].b(), TM[1].b()], w=wb)
        k.tt(P, tm[2], B_, Kre, ALU.mult, r=rb, w=[TM[2].b()])
        k.tt(P, tm[3], A_, Kim, ALU.mult, r=rb, w=[TM[3].b()])
        k.tt(P, yim, tm[2], tm[3], ALU.subtract, r=[TM[2].b(), TM[3].b()], w=wb)

    def prologue(self):
        k = self.k
        self.precast()
        self.consts()
        self.resident()
        self.vecbank()
        with k.scope() as psc:
            self.FB = psc.tile("FB", [128, self.depth, 2], F32)
            self.SK = psc.tile("SK", [128, self.depth + 1, 512], F32)
            for name in ("negt", "delta"):
                shp = list(make_consts()[name].shape)
                t = psc.tile("c_" + name, shp, CONST_DT[name])
                k.dma(k.SP, t[:], self.cst[name], w=[t.b()])
                self.C[name] = t
            self.derive()
            for l in range(self.depth):
                self.filt(l, False)
            self.filt(0, True)
        self.X = k.tile("X", [128, 8, T], F32)
        self.U = k.tile("U", [128, 8, T], BF16)

    def finish(self):
        k = self.k
        k.barrier()
        k.stack.close()


def make_in_maps(inputs, ncores=NCORES, nseq=NSEQ, j=0):
    cst = make_consts()
    maps = []
    for i in range(ncores):
        m = {}
        for name in IN_SHAPES:
            a = np.asarray(inputs[name])
            if name in ("x", "c", "ctx"):
                a = a[NSEQ * i + j:NSEQ * i + j + nseq]
            m[name] = np.ascontiguousarray(a, dtype=np.float32)
        for name, arr in cst.items():
            m["k_" + name] = arr
        maps.append(m)
    return maps


_NET = None
NLAUNCH = 2


def kernel(**inputs):
    global _NET
    nseq = NSEQ // NLAUNCH
    if _NET is None:
        net = Net(nseq=nseq)
        net.build_full()
        _NET = net
    net = _NET
    outs = []
    for j in range(NLAUNCH):
        maps = make_in_maps(inputs, NCORES, nseq, j * nseq)
        res = run_bass_kernel_spmd(net.nc, maps, core_ids=list(range(NCORES)))
        outs.append([np.asarray(r["out"]) for r in res.results])
    full = np.zeros((NCORES * NSEQ, SEQ, D), np.float32)
    for j in range(NLAUNCH):
        for i in range(NCORES):
            full[NSEQ * i + j * nseq:NSEQ * i + (j + 1) * nseq] = outs[j][i]
    return full
```

```python
import math
from contextlib import ExitStack
import numpy as np
import ml_dtypes
import concourse.bass as bass
import concourse.mybir as mybir
from concourse.bass_utils import run_bass_kernel_spmd

F32 = mybir.dt.float32
BF16 = mybir.dt.bfloat16
AF = mybir.ActivationFunctionType
ALU = mybir.AluOpType

NCORES = 8
NSEQ = 2
D = 1024
KC = 8
SEQ = 2048
CTX = 256
T = SEQ + CTX
DEPTH = 2
GRID_W = 64
EPS = 1e-6
E_HY = 512
INW = 8192
DFF = 4096
PADL = 3
PT = T + 12
TB = [(0, 256), (256, 768), (768, 1280), (1280, 1792), (1792, 2304)]
C_HY, C_LX, C_LG, C_Q, C_FF, C_FB, C_I, C_OG, C_MG = 0, 1536, 2048, 2560, 3072, 3584, 4096, 4608, 5120


def poff(t):
    return t + PADL if t < CTX else t + 3 * PADL


class Ev:
    __slots__ = ("q", "sem", "val", "dma")

    def __init__(self, q, sem, val, dma=False):
        self.q, self.sem, self.val, self.dma = q, sem, val, dma


class Buf:
    __slots__ = ("name", "w", "r")

    def __init__(self, name=""):
        self.name, self.w, self.r = name, None, []


class Tile:
    def __init__(self, t, name):
        self.t, self.name, self.bufs = t, name, {}

    def b(self, *key):
        v = self.bufs.get(key)
        if v is None:
            v = self.bufs[key] = Buf(f"{self.name}{key}")
        return v

    def __getitem__(self, k):
        return self.t[k]


class Q:
    LIM = 3000

    def __init__(self, kern, name, eng, ndma=0):
        self.k, self.name, self.eng = kern, name, eng
        self.sem = kern.new_sem(name)
        self.cnt = 0
        self.seen = {}
        self.last = None
        self.slots = [[kern.new_sem(f"{name}d{i}"), 0] for i in range(ndma)]
        self.nxt = 0
        self.pend = []
        self.collect = None

    def wait(self, ev):
        key = id(ev.sem)
        if self.seen.get(key, 0) >= ev.val:
            return
        self.seen[key] = ev.val
        if self.collect is not None:
            self.collect = [e for e in self.collect if e.sem is not ev.sem] + [ev]
            return
        self.eng.wait_ge(ev.sem, ev.val)
        self.k.nwait += 1

    def flush(self, ins_fn):
        ws, self.collect = self.collect, None
        for e in ws[:-1]:
            self.eng.wait_ge(e.sem, e.val)
            self.k.nwait += 1
        ins = ins_fn()
        if ws:
            ins._wait_ge(ws[-1].sem, ws[-1].val)
        return ins

    def signal(self, ins):
        if self.cnt >= Q.LIM:
            self.sem = self.k.new_sem(self.name + "x")
            self.cnt = 0
        self.cnt += 1
        ins.then_inc(self.sem, 1)
        self.last = Ev(self, self.sem, self.cnt)
        return self.last

    def dma_signal(self, fn):
        slot = self.slots[self.nxt]
        self.nxt = (self.nxt + 1) % len(self.slots)
        if slot[1] > 0:
            self.wait(Ev(self, slot[0], slot[1], True))
        ins = self.flush(fn)
        slot[1] += 16
        ins.then_inc(slot[0], 16)
        return Ev(self, slot[0], slot[1], True)


class Kern:
    def __init__(self):
        self.nc = bass.Bass("TRN2", target_bir_lowering=False)
        self.stack = ExitStack()
        self.nwait = 0
        self.nins = 0
        self.sems = []
        nc = self.nc
        self.PE = Q(self, "pe", nc.tensor)
        self.ACT = Q(self, "act", nc.scalar)
        self.DVE = Q(self, "dve", nc.vector)
        self.POOL = Q(self, "pool", nc.gpsimd, ndma=2)
        self.SP = Q(self, "sp", nc.sync, ndma=16)
        self.queues = [self.PE, self.ACT, self.DVE, self.POOL, self.SP]
        self.banks = []
        for i in range(8):
            t = nc.alloc_psum_tensor(f"bank{i}", [128, 512], F32)
            self.banks.append(Tile(t, f"bank{i}"))
        self.bank_i = 0
        self.dbg = []

    def new_sem(self, name):
        s = self.stack.enter_context(self.nc.semaphore(f"s_{name}_{len(self.sems)}"))
        self.sems.append(s)
        return s

    def bank(self):
        b = self.banks[self.bank_i]
        self.bank_i = (self.bank_i + 1) % 8
        return b

    def _deps(self, q, r, w):
        for b in r:
            if b.w is not None:
                self._dep(q, b.w, 0)
        for b in w:
            if b.w is not None:
                self._dep(q, b.w, 1)
            for e in b.r:
                self._dep(q, e, 2)

    def _dep(self, q, ev, kind):
        if ev.q is q and not ev.dma:
            if q is self.PE or kind == 2:
                return
        q.wait(ev)

    def _record(self, ev, r, w):
        for b in r:
            if not ev.dma:
                b.r = [e for e in b.r if e.dma or e.q is not ev.q]
            b.r.append(ev)
        for b in w:
            b.w = ev
            b.r = []

    def op(self, q, fn, r=(), w=(), sig=True):
        q.collect = []
        self._deps(q, r, w)
        ins = q.flush(fn)
        self.nins += 1
        if sig:
            ev = q.signal(ins)
            for (pr, pw) in q.pend:
                self._record(ev, pr, pw)
            q.pend = []
            self._record(ev, r, w)
        else:
            q.pend.append((list(r), list(w)))
        return ins

    def dma(self, q, out, in_, r=(), w=(), **kw):
        assert not q.pend
        q.collect = []
        self._deps(q, r, w)
        ev = q.dma_signal(lambda: q.eng.dma_start(out=out, in_=in_, **kw))
        self.nins += 1
        self._record(ev, r, w)

    def barrier(self):
        evs = []
        for q in self.queues:
            assert not q.pend, q.name
            if q.last is not None:
                evs.append(q.last)
            for s in q.slots:
                if s[1] > 0:
                    evs.append(Ev(q, s[0], s[1], True))
        for q in self.queues:
            for e in evs:
                if e.q is q and not e.dma:
                    continue
                q.wait(e)

    def tile(self, name, shape, dtype, stack=None):
        st = stack if stack is not None else self.stack
        self.ntile = getattr(self, "ntile", 0) + 1
        name = f"{name}_{self.ntile}"
        t = st.enter_context(self.nc.sbuf_tensor(name, list(shape), dtype))
        return Tile(t, name)

    class Scope:
        def __init__(self, k):
            self.k = k
            self.st = ExitStack()

        def __enter__(self):
            self.st.__enter__()
            return self

        def tile(self, name, shape, dtype):
            return self.k.tile(name, shape, dtype, self.st)

        def __exit__(self, *a):
            self.k.barrier()
            return self.st.__exit__(*a)

    def scope(self):
        return Kern.Scope(self)

    def mm(self, out, lhsT, rhs, start, stop, r=(), w=(), sig=False):
        nc = self.nc
        return self.op(self.PE, lambda: nc.tensor.matmul(out, lhsT, rhs, start=start, stop=stop), r, w, sig)

    def tr(self, out, in_, ident, r=(), w=(), sig=True):
        nc = self.nc
        return self.op(self.PE, lambda: nc.tensor.transpose(out, in_, ident), r, w, sig)

    def act(self, out, in_, func, r=(), w=(), bias=None, scale=None):
        nc = self.nc
        kw = {}
        if bias is not None:
            kw["bias"] = bias
        if scale is not None:
            kw["scale"] = scale
        return self.op(self.ACT, lambda: nc.scalar.activation(out=out, in_=in_, func=func, **kw), r, w)

    def tt(self, q, out, in0, in1, op, r=(), w=()):
        return self.op(q, lambda: q.eng.tensor_tensor(out=out, in0=in0, in1=in1, op=op), r, w)

    def ts(self, q, out, in0, s1, op0, s2=None, op1=None, r=(), w=()):
        if op1 is None:
            return self.op(q, lambda: q.eng.tensor_scalar(out=out, in0=in0, scalar1=s1, scalar2=None, op0=op0), r, w)
        return self.op(q, lambda: q.eng.tensor_scalar(out=out, in0=in0, scalar1=s1, scalar2=s2, op0=op0, op1=op1), r, w)

    def stt(self, out, in0, scalar, in1, op0, op1, r=(), w=()):
        nc = self.nc
        return self.op(self.DVE, lambda: nc.vector.scalar_tensor_tensor(out=out, in0=in0, scalar=scalar, in1=in1, op0=op0, op1=op1), r, w)

    def cp(self, q, out, in_, r=(), w=()):
        if q is self.ACT:
            return self.op(q, lambda: q.eng.copy(out=out, in_=in_), r, w)
        return self.op(q, lambda: q.eng.tensor_copy(out=out, in_=in_), r, w)

    def memset(self, q, ap, val, w=()):
        return self.op(q, lambda: q.eng.memset(ap, val), (), w)


_CONSTS = None


def make_consts():
    global _CONSTS
    if _CONSTS is not None:
        return _CONSTS
    bf = ml_dtypes.bfloat16
    c = {}
    c["ident_f"] = np.eye(128, dtype=np.float32)
    c["ident_b"] = np.eye(128, dtype=np.float32).astype(bf)
    c["ones_d"] = np.full((128, 128), 1.0 / D, np.float32).astype(bf)
    c["ones_v"] = np.full((128, 128), 1.0 / 128, np.float32).astype(bf)
    c["onesrow"] = np.ones((128, 512), np.float32).astype(bf)

    def dft(L):
        N = 2 * L
        s = np.arange(L, dtype=np.float64)[:, None]
        f = np.arange(L, dtype=np.float64)[None, :]
        ang = 2.0 * np.pi * np.mod((2 * f + 1) * s, 2 * N) / (2 * N)
        return np.cos(ang), np.sin(ang)

    cs, sn = dft(SEQ)
    fw = np.stack([cs, sn], 0).reshape(2, 16, 128, 16, 128)
    c["fwdL"] = np.ascontiguousarray(fw.transpose(3, 2, 1, 0, 4)).astype(bf)
    iv = np.stack([cs.T, sn.T], 0).reshape(2, 16, 128, SEQ)
    c["invL"] = np.ascontiguousarray(iv.transpose(1, 2, 0, 3)).astype(bf)
    cs, sn = dft(CTX)
    fw = np.stack([cs, sn], 0).reshape(2, 2, 128, CTX)
    c["fwdC"] = np.ascontiguousarray(fw.transpose(2, 1, 0, 3)).astype(bf)
    iv = np.stack([cs.T, sn.T], 0).reshape(2, 2, 128, CTX)
    c["invC"] = np.ascontiguousarray(iv.transpose(2, 1, 0, 3)).astype(bf)

    def zemb(L):
        pos = np.arange(L, dtype=np.float32)
        t = pos / np.float32(max(L - 1, 1))
        bands = np.linspace(1e-4, 7, 8, dtype=np.float32)
        ang = bands[None, :] * (np.float32(2.0 * math.pi) * pos / np.float32(L))[:, None]
        z = np.concatenate([t[:, None], np.cos(ang), -np.sin(ang)], -1).astype(np.float32)
        return np.ascontiguousarray(z.T), t

    c["zL"], tl = zemb(SEQ)
    c["zC"], tc = zemb(CTX)
    negt = np.zeros((128, 18), np.float32)
    negt[:, :16] = -tl.reshape(16, 128).T
    negt[:, 16:] = -tc.reshape(2, 128).T
    c["negt"] = negt
    deltas = np.abs(np.linspace(math.log(1e-2) / 1.5, math.log(1e-2) / 0.3, E_HY, dtype=np.float32))
    c["delta"] = np.ascontiguousarray(np.broadcast_to(deltas[None, :], (128, E_HY))).astype(np.float32)
    s = np.arange(128)[:, None]
    t = np.arange(128)[None, :]
    same = (s // 64) == (t // 64)
    c["maskF"] = (same & (s <= t)).astype(np.uint16)
    c["maskB"] = (same & (s >= t)).astype(np.uint16)
    _CONSTS = c
    return c


CONST_DT = {"ident_f": F32, "ident_b": BF16, "ones_d": BF16, "ones_v": BF16, "onesrow": BF16, "fwdL": BF16,
            "invL": BF16, "fwdC": BF16, "invC": BF16, "zL": F32, "zC": F32, "negt": F32, "delta": F32,
            "maskF": mybir.dt.uint16, "maskB": mybir.dt.uint16}

IN_SHAPES = {
    "x": [NSEQ, SEQ, D], "c": [NSEQ, D], "ctx": [NSEQ, CTX, D], "c_ctx": [D],
    "w_ada": [DEPTH, D, 6 * D], "b_ada": [DEPTH, 6 * D], "norm_gains": [DEPTH, 4, D], "w_in": [DEPTH, D, INW],
    "hy_short_w": [DEPTH, 3, 1536], "hy_short_b": [DEPTH, 1536], "hy_ff_w1": [DEPTH, 17, 64],
    "hy_ff_b1": [DEPTH, 64], "hy_ff_w2": [DEPTH, 64, 64], "hy_ff_b2": [DEPTH, 64], "hy_ff_w3": [DEPTH, 64, 1024],
    "hy_freq": [DEPTH, 2, 64], "hy_skip": [DEPTH, 512], "lru_conv_w": [DEPTH, 2, 4, 512],
    "lru_conv_b": [DEPTH, 2, 512], "lru_wr": [DEPTH, 2, 8, 64, 64], "lru_br": [DEPTH, 2, 512],
    "lru_wi": [DEPTH, 2, 8, 64, 64], "lru_bi": [DEPTH, 2, 512], "lru_lambda": [DEPTH, 2, 512],
    "hg_lower_bounds": [DEPTH, 512], "hg_norm_g": [DEPTH, 128], "w_proj_hy": [DEPTH, 512, D],
    "w_proj_lru": [DEPTH, 512, D], "w_proj_hg": [DEPTH, 512, D], "w_out": [DEPTH, D, D],
    "w_mlp1": [DEPTH, D, DFF], "w_mlp2": [DEPTH, DFF, D],
}


VEC_LIST = [
    ("b_ada", 6144), ("norm_gains", 4096), ("hy_short_w", 4608), ("hy_short_b", 1536), ("lru_conv_w", 4096),
    ("lru_conv_b", 1024), ("lru_br", 1024), ("lru_bi", 1024), ("lru_lambda", 1024), ("hg_lower_bounds", 512),
    ("hg_norm_g", 128), ("hy_ff_b1", 64), ("hy_ff_b2", 64), ("hy_freq0", 64), ("hy_freq1", 64),
]


class Net:
    def __init__(self, nseq=NSEQ, depth=DEPTH, stage="full", dbg=()):
        self.k = k = Kern()
        self.nc = nc = k.nc
        self.nseq, self.depth, self.stage = nseq, depth, stage
        self.dbg_names = dbg
        self.dbg_out = {}
        self.inp = {}
        for name, shp in IN_SHAPES.items():
            shp = list(shp)
            if name in ("x", "c", "ctx"):
                shp[0] = nseq
            self.inp[name] = nc.dram_tensor(name, shp, F32, kind="ExternalInput").ap()
        self.cst = {}
        for name, arr in make_consts().items():
            self.cst[name] = nc.dram_tensor("k_" + name, list(arr.shape), CONST_DT[name], kind="ExternalInput").ap()
        self.out = nc.dram_tensor("out", [nseq, SEQ, D], F32, kind="ExternalOutput").ap()

        def scratch(name, shape, dt=BF16):
            return Tile(nc.dram_tensor(name, list(shape), dt, kind="Internal").ap(), name)

        self.WB = {
            "w_in": scratch("wb_in", [DEPTH, D, INW]), "w_ada": scratch("wb_ada", [DEPTH, D, 6 * D]),
            "w_proj_hy": scratch("wb_phy", [DEPTH, 512, D]), "w_proj_lru": scratch("wb_plru", [DEPTH, 512, D]),
            "w_proj_hg": scratch("wb_phg", [DEPTH, 512, D]), "w_out": scratch("wb_out", [DEPTH, D, D]),
            "w_mlp1": scratch("wb_m1", [DEPTH, D, DFF]), "w_mlp2": scratch("wb_m2", [DEPTH, DFF, D]),
        }
        self.KS = scratch("ks", [DEPTH, 16, 128, 2, 512])
        self.KSC = scratch("ksc", [2, 128, 2, 512])
        self.YP = scratch("yp", [3, 128, 4, T])

    def dump(self, name, tile_ap, shape, dtype=F32):
        if name not in self.dbg_names:
            return
        k, nc = self.k, self.nc
        k.barrier()
        d = nc.dram_tensor("dbg_" + name, list(shape), dtype, kind="ExternalOutput").ap()
        k.dma(k.SP, d, tile_ap)
        k.barrier()
        self.dbg_out[name] = "dbg_" + name

    def precast(self):
        k = self.k
        order = []
        for l in range(self.depth):
            order.append(("w_ada", l))
        for l in range(self.depth):
            for n in ("w_in", "w_proj_hy", "w_proj_hg", "w_proj_lru", "w_out", "w_mlp1", "w_mlp2"):
                order.append((n, l))
        for (n, l) in order:
            src = self.inp[n][l]
            dst = self.WB[n]
            k.dma(k.POOL, dst[l], src, w=[dst.b(l)], max_dma_last_dim=4096)

    def consts(self):
        k, nc = self.k, self.nc
        C = {}
        for name in ("ident_f", "ident_b", "ones_d", "ones_v", "onesrow", "fwdC", "invC", "maskF", "maskB"):
            shp = list(make_consts()[name].shape)
            t = k.tile("c_" + name, shp, CONST_DT[name])
            k.dma(k.SP, t[:], self.cst[name], w=[t.b()])
            C[name] = t
        self.C = C

    def vecbank(self):
        k, nc = self.k, self.nc
        rows = []
        for l in range(self.depth):
            for name, ln in VEC_LIST:
                if name == "hy_freq0":
                    src = self.inp["hy_freq"][l, 0]
                elif name == "hy_freq1":
                    src = self.inp["hy_freq"][l, 1]
                else:
                    src = self.inp[name][l]
                    if len(src.shape) > 1:
                        src = src.flatten() if hasattr(src, "flatten") else src
                rows.append(((name, l), src, ln))
        for i in range(self.nseq):
            rows.append((("c", i), self.inp["c"][i], D))
        rows.append((("c_ctx", 0), self.inp["c_ctx"], D))
        place = {}
        tile_i, r = 0, 0
        for key, src, ln in rows:
            nr = (ln + 127) // 128
            if r + nr > 128:
                tile_i, r = tile_i + 1, 0
            place[key] = (tile_i, r, nr, ln)
            r += nr
        ntile = tile_i + 1
        self.VB = VB = k.tile("VB", [128, ntile, 128], F32)
        with k.scope() as sc:
            RT = sc.tile("rowtile", [128, ntile, 128], F32)
            k.memset(k.DVE, RT[:], 0.0, w=[RT.b()])
            for key, src, ln in rows:
                ti, r0, nr, _ = place[key]
                if ln >= 128:
                    k.dma(k.SP, RT[r0:r0 + nr, ti, :], src.rearrange("(r c) -> r c", c=128), w=[RT.b()])
                else:
                    k.dma(k.SP, RT[r0:r0 + 1, ti, 0:ln], src.rearrange("(r c) -> r c", r=1), w=[RT.b()])
            for ti in range(ntile):
                bk = k.bank()
                k.tr(bk[:, 0:128], RT[:, ti, :], self.C["ident_f"][:], r=[RT.b(), self.C["ident_f"].b()], w=[bk.b()])
                k.cp(k.DVE, VB[:, ti, :], bk[:, 0:128], r=[bk.b()], w=[VB.b()])
        self.place = place

    def vec(self, name, l, i0=0, n=1):
        ti, r0, nr, ln = self.place[(name, l)]
        return self.VB[:, ti, r0 + i0:r0 + i0 + n]

    def derive(self):
        k, nc, C = self.k, self.nc, self.C
        V, A, P = k.DVE, k.ACT, k.POOL
        VBb = self.VB.b()
        with k.scope() as sc:
            scT = sc.tile("scT", [128, 8, 3], BF16)
            for i in range(3):
                src = self.vec("c", min(i, self.nseq - 1), 0, 8) if i < 2 else self.vec("c_ctx", 0, 0, 8)
                k.act(scT[:, :, i], src, AF.Silu, r=[VBb], w=[scT.b()])
            WA = [sc.tile(f"wada{i}", [128, 8, 512], BF16) for i in range(2)]
            for l in range(self.depth):
                ADA = sc.tile(f"ADA{l}", [128, 48, 3], F32)
                bk = k.bank()
                wsrc = self.WB["w_ada"]
                wv = wsrc.t[l].rearrange("(kc p) n -> p kc n", p=128)
                for g in range(12):
                    wt = WA[g % 2]
                    k.dma(k.SP, wt[:], wv[:, :, g * 512:(g + 1) * 512], r=[wsrc.b(l)], w=[wt.b()])
                    for jj in range(4):
                        j = g * 4 + jj
                        for kc in range(8):
                            k.mm(bk[:, j * 3:j * 3 + 3], wt[:, kc, jj * 128:(jj + 1) * 128], scT[:, kc, :],
                                 kc == 0, kc == 7, r=[wt.b(), scT.b()], w=[bk.b()], sig=(kc == 7))
                bada = self.vec("b_ada", l, 0, 48)
                k.tt(V, ADA[:], bk[:, 0:144].rearrange("p (j c) -> p j c", c=3),
                     bada.unsqueeze(2).to_broadcast([128, 48, 3]), ALU.add, r=[bk.b(), VBb], w=[ADA.b()])
                self.dump(f"ada{l}", ADA[:], [128, 48, 3])
                M = self.MOD[l]
                g = [self.vec("norm_gains", l, 8 * i, 8).unsqueeze(2).to_broadcast([128, 8, 3]) for i in range(4)]
                tmp = sc.tile(f"modtmp{l}", [128, 8, 3], F32)
                k.cp(V, M[:, 0], ADA[:, 0:8, :], r=[ADA.b()], w=[M.b()])
                k.ts(V, tmp[:], ADA[:, 8:16, :], 1.0, ALU.add, r=[ADA.b()], w=[tmp.b()])
                k.tt(V, M[:, 1], tmp[:], g[0], ALU.mult, r=[tmp.b(), VBb], w=[M.b()])
                k.tt(V, M[:, 2], ADA[:, 16:24, :], g[1], ALU.mult, r=[ADA.b(), VBb], w=[M.b()])
                k.cp(V, M[:, 3], ADA[:, 24:32, :], r=[ADA.b()], w=[M.b()])
                tmp2 = sc.tile(f"modtmp2{l}", [128, 8, 3], F32)
                k.ts(V, tmp2[:], ADA[:, 32:40, :], 1.0, ALU.add, r=[ADA.b()], w=[tmp2.b()])
                k.tt(V, M[:, 4], tmp2[:], g[2], ALU.mult, r=[tmp2.b(), VBb], w=[M.b()])
                k.tt(V, M[:, 5], ADA[:, 40:48, :], g[3], ALU.mult, r=[ADA.b(), VBb], w=[M.b()])
            for l in range(self.depth):
                e = sc.tile(f"c1e{l}", [128, 8], F32)
                k.act(e[:], self.vec("lru_lambda", l, 0, 8), AF.Exp, r=[VBb], w=[e.b()], scale=-1.0)
                k.ts(V, e[:], e[:], 1.0, ALU.add, r=[e.b()], w=[e.b()])
                k.act(e[:], e[:], AF.Ln, r=[e.b()], w=[e.b()])
                k.ts(V, self.C1[:, l, 0, :], e[:], -8.0, ALU.mult, r=[e.b()], w=[self.C1.b()])
                k.ts(V, self.C1[:, l, 1, :], e[:], -16.0, ALU.mult, r=[e.b()], w=[self.C1.b()])
            k.memset(V, self.LB[:, 0, 0, :], 0.0, w=[self.LB.b()])
            k.memset(V, self.LB[:, 0, 1, :], 1.0, w=[self.LB.b()])
            k.memset(V, self.LB[:, 0, 2, :], -1.0, w=[self.LB.b()])
            if self.depth > 1:
                dlt = sc.tile("lbd", [128, 4], F32)
                k.tt(V, dlt[:], self.vec("hg_lower_bounds", 1, 0, 4), self.vec("hg_lower_bounds", 0, 0, 4),
                     ALU.subtract, r=[VBb], w=[dlt.b()])
                k.act(self.LB[:, 1, 0, :], dlt[:], AF.Sigmoid, r=[dlt.b()], w=[self.LB.b()])
                k.act(self.LB[:, 1, 1, :], dlt[:], AF.Sigmoid, r=[dlt.b()], w=[self.LB.b()], scale=-1.0)
                k.ts(V, self.LB[:, 1, 2, :], self.LB[:, 1, 1, :], -1.0, ALU.mult, r=[self.LB.b()], w=[self.LB.b()])
            for l in range(self.depth):
                k.tt(V, self.FB[:, l, 0:1], self.vec("hy_freq0", l), self.vec("hy_ff_b1", l), ALU.mult, r=[VBb], w=[self.FB.b()])
                k.tt(V, self.FB[:, l, 1:2], self.vec("hy_freq1", l), self.vec("hy_ff_b2", l), ALU.mult, r=[VBb], w=[self.FB.b()])
            skr = sc.tile("skraw", [128, self.depth, 512], F32)
            for l in range(self.depth):
                k.dma(k.SP, skr[:, l, :], self.inp["hy_skip"][l].partition_broadcast(128), w=[skr.b()])
                k.ts(V, self.SK[:, l, :], skr[:, l, :], 2.0 / (2 * SEQ), ALU.mult, r=[skr.b()], w=[self.SK.b()])
            k.ts(V, self.SK[:, self.depth, :], skr[:, 0, :], 2.0 / (2 * CTX), ALU.mult, r=[skr.b()], w=[self.SK.b()])

    def wbd(self, l, d, ri, cc):
        return self.WBD[:, (d * 2 + ri) * 4 + cc, :]

    def build_wbd(self, sc, l):
        k = self.k
        self.WBD = sc.tile("WBD", [128, 16, 128], BF16)
        with k.scope() as sc2:
            stg = sc2.tile("wbdstage", [128, 16, 128], F32)
            k.memset(k.POOL, stg[:], 0.0, w=[stg.b()])
            for d in range(2):
                for ri, nm in enumerate(("lru_wr", "lru_wi")):
                    for cc in range(4):
                        idx = (d * 2 + ri) * 4 + cc
                        for h in range(2):
                            k.dma(k.SP, stg[h * 64:(h + 1) * 64, idx, h * 64:(h + 1) * 64],
                                  self.inp[nm][l, d, 2 * cc + h], w=[stg.b()])
            k.cp(k.DVE, self.WBD[:], stg[:], r=[stg.b()], w=[self.WBD.b()])

    def filt(self, l, ctx):
        k, nc, C = self.k, self.nc, self.C
        V, A, P = k.DVE, k.ACT, k.POOL
        VBb = self.VB.b()
        L = CTX if ctx else SEQ
        nt = L // 128
        blocks = [(0, 256)] if ctx else [(i * 512, (i + 1) * 512) for i in range(4)]
        sk = self.SK[:, self.depth if ctx else l, :]
        scale = 2.0 / (2 * L)
        PI = math.pi
        with k.scope() as sc:
            zT = sc.tile("zT", [17, L], F32)
            k.dma(k.SP, zT[:], self.cst["zC" if ctx else "zL"], w=[zT.b()])
            w1 = sc.tile("fw1", [17, 64], F32)
            w2 = sc.tile("fw2", [64, 64], F32)
            w3 = sc.tile("fw3", [64, 1024], F32)
            k.dma(k.SP, w1[:], self.inp["hy_ff_w1"][l], w=[w1.b()])
            k.dma(k.SP, w2[:], self.inp["hy_ff_w2"][l], w=[w2.b()])
            k.dma(k.SP, w3[:], self.inp["hy_ff_w3"][l], w=[w3.b()])
            h1 = sc.tile("fh1", [64, L], F32)
            h2 = sc.tile("fh2", [64, L], F32)
            arg = sc.tile("farg", [64, 512], F32)
            m1 = sc.tile("fm1", [64, 512], F32)
            for stage in range(2):
                wt, src, dst = (w1, zT, h1) if stage == 0 else (w2, h1, h2)
                kk = 17 if stage == 0 else 64
                fr = self.vec("hy_freq0" if stage == 0 else "hy_freq1", l)[0:64, :]
                fb = self.FB[0:64, l, stage:stage + 1]
                for (b0, b1) in blocks:
                    n = b1 - b0
                    bk = k.bank()
                    k.mm(bk[0:64, 0:n], wt[0:kk, :], src[0:kk, b0:b1], True, True, r=[wt.b(), src.b()], w=[bk.b()], sig=True)
                    k.ts(V, arg[:, 0:n], bk[0:64, 0:n], fr, ALU.mult, fb, ALU.add, r=[bk.b(), VBb, self.FB.b()], w=[arg.b()])
                    k.ts(V, m1[:, 0:n], arg[:, 0:n], -PI, ALU.is_lt, 2 * PI, ALU.mult, r=[arg.b()], w=[m1.b()])
                    k.tt(V, m1[:, 0:n], m1[:, 0:n], arg[:, 0:n], ALU.add, r=[m1.b(), arg.b()], w=[m1.b()])
                    k.ts(V, arg[:, 0:n], arg[:, 0:n], PI, ALU.is_gt, -2 * PI, ALU.mult, r=[arg.b()], w=[arg.b()])
                    k.tt(V, arg[:, 0:n], arg[:, 0:n], m1[:, 0:n], ALU.add, r=[m1.b(), arg.b()], w=[arg.b()])
                    k.act(dst[:, b0:b1], arg[:, 0:n], AF.Sin, r=[arg.b()], w=[dst.b()])
            self.dump(f"fh2_{l}_{int(ctx)}", h2[:], [64, L])
            hs = sc.tile("fhs", [128, nt, 512], BF16)
            hd = sc.tile("fhd", [128, nt, 512], BF16)
            dec = sc.tile("fdec", [128, 512], F32)
            hf = sc.tile("fhf", [128, 512], F32)
            hb = sc.tile("fhb", [128, 512], F32)
            for jt in range(nt):
                b0, b1 = k.bank(), k.bank()
                k.mm(b0[:, :], h2[:, jt * 128:(jt + 1) * 128], w3[:, 0:512], True, True, r=[h2.b(), w3.b()], w=[b0.b()], sig=True)
                k.mm(b1[:, :], h2[:, jt * 128:(jt + 1) * 128], w3[:, 512:1024], True, True, r=[h2.b(), w3.b()], w=[b1.b()], sig=True)
                col = (16 if ctx else 0) + jt
                k.act(dec[:], C["delta"][:], AF.Exp, r=[C["delta"].b(), C["negt"].b()], w=[dec.b()], scale=C["negt"][:, col:col + 1])
                k.tt(V, hf[:], b0[:, :], dec[:], ALU.mult, r=[b0.b(), dec.b()], w=[hf.b()])
                k.tt(V, hb[:], b1[:, :], dec[:], ALU.mult, r=[b1.b(), dec.b()], w=[hb.b()])
                if jt == 0:
                    k.memset(V, hb[0:1, :], 0.0, w=[hb.b()])
                k.tt(V, hs[:, jt, :], hf[:], hb[:], ALU.add, r=[hf.b(), hb.b()], w=[hs.b(jt)])
                k.tt(V, hd[:, jt, :], hb[:], hf[:], ALU.subtract, r=[hf.b(), hb.b()], w=[hd.b(jt)])
            FW = None if ctx else [sc.tile(f"ffw{i}", [128, 16, 2, 128], BF16) for i in range(2)]
            KT = [sc.tile(f"fkt{i}", [128, 2, 512], BF16) for i in range(2)]
            dst = self.KSC if ctx else self.KS
            for ft in range(nt):
                if ctx:
                    fw = lambda jt, cs: C["fwdC"][:, jt, cs, ft * 128:(ft + 1) * 128]
                    fwb = C["fwdC"].b()
                else:
                    fwt = FW[ft % 2]
                    k.dma(k.SP, fwt[:], self.cst["fwdL"][ft], w=[fwt.b()])
                    fw = lambda jt, cs: fwt[:, jt, cs, :]
                    fwb = fwt.b()
                bA, bB = k.bank(), k.bank()
                for jt in range(nt):
                    k.mm(bA[:, :], fw(jt, 0), hs[:, jt, :], jt == 0, jt == nt - 1, r=[fwb, hs.b(jt)], w=[bA.b()], sig=(jt == nt - 1))
                for jt in range(nt):
                    k.mm(bB[:, :], fw(jt, 1), hd[:, jt, :], jt == 0, jt == nt - 1, r=[fwb, hd.b(jt)], w=[bB.b()], sig=(jt == nt - 1))
                kt = KT[ft % 2]
                k.stt(kt[:, 0, :], bA[:, :], scale, sk, ALU.mult, ALU.add, r=[bA.b(), self.SK.b()], w=[kt.b()])
                k.act(kt[:, 1, :], bB[:, :], AF.Copy, r=[bB.b()], w=[kt.b()], scale=scale)
                if ctx:
                    k.dma(k.SP, dst.t[ft], kt[:], r=[kt.b()], w=[dst.b(ft)])
                else:
                    k.dma(k.SP, dst.t[l, ft], kt[:], r=[kt.b()], w=[dst.b(l, ft)])

    def resident(self):
        k = self.k
        self.EPSC = k.tile("epsc", [128, 1], F32)
        k.memset(k.DVE, self.EPSC[:], EPS, w=[self.EPSC.b()])
        self.MOD = [k.tile(f"MOD{l}", [128, 6, 8, 3], F32) for l in range(self.depth)]
        self.C1 = k.tile("C1", [128, self.depth, 2, 8], F32)
        self.LB = k.tile("LB", [128, self.depth, 3, 4], F32)

    def load_x(self, s):
        k, C = self.k, self.C
        X = self.X
        with k.scope() as sc:
            XT = [sc.tile(f"xt{i}", [128, D], F32) for i in range(3)]
            for tt in range(T // 128):
                xt = XT[tt % 3]
                src = self.inp["ctx"][s, tt * 128:(tt + 1) * 128, :] if tt < 2 else self.inp["x"][s, (tt - 2) * 128:(tt - 1) * 128, :]
                k.dma(k.SP, xt[:], src, w=[xt.b()])
                tbi = self.tbi(tt * 128)
                for h in range(2):
                    bk = k.bank()
                    for j in range(4):
                        kc = h * 4 + j
                        k.tr(bk[:, j * 128:(j + 1) * 128], xt[:, kc * 128:(kc + 1) * 128], C["ident_f"][:],
                             r=[xt.b(), C["ident_f"].b()], w=[bk.b()], sig=(j == 3))
                    q = k.ACT if h == 0 else k.DVE
                    k.cp(q, X[:, h * 4:(h + 1) * 4, tt * 128:(tt + 1) * 128], bk[:, :].rearrange("p (a b) -> p a b", b=128),
                         r=[bk.b()], w=[X.b(kc2, tbi) for kc2 in range(h * 4, h * 4 + 4)])

    @staticmethod
    def tbi(t):
        for i, (a, b) in enumerate(TB):
            if a <= t < b:
                return i
        raise ValueError(t)

    def rstd_of(self, sc, src3, rbufs, n, ones, nk=8, tag="", sq=None, sqb=None):
        k = self.k
        if sq is None:
            sq = sc.tile("sq" + tag, [128, nk, 512], BF16)
        if sqb is None:
            sqb = [sq.b()]
        k.act(sq[:, :, 0:n], src3, AF.Square, r=rbufs, w=sqb)
        bk = k.bank()
        for kc in range(nk):
            k.mm(bk[:, 0:n], ones[:], sq[:, kc, 0:n], kc == 0, kc == nk - 1, r=sqb + [ones.b()], w=[bk.b()], sig=(kc == nk - 1))
        rs = sc.tile("rstd" + tag, [128, 512], F32)
        k.act(rs[:, 0:n], bk[:, 0:n], AF.Ln, r=[bk.b(), self.EPSC.b()], w=[rs.b()], bias=self.EPSC[:, 0:1])
        k.act(rs[:, 0:n], rs[:, 0:n], AF.Exp, r=[rs.b()], w=[rs.b()], scale=-0.5)
        return rs

    def phase_a(self, s, l):
        k, C = self.k, self.C
        X, U, M = self.X, self.U, self.MOD[l]
        with k.scope() as sc:
            for tbi, (t0, t1) in enumerate(TB):
                n = t1 - t0
                col = 2 if tbi == 0 else s
                with k.scope() as sc2:
                    rs = self.rstd_of(sc2, X[:, :, t0:t1], [X.b(kc, tbi) for kc in range(8)], n, C["ones_d"])
                    tmp = [sc2.tile(f"natmp{i}", [128, 512], F32) for i in range(2)]
                    for kc in range(8):
                        tp = tmp[kc % 2]
                        k.stt(tp[:, 0:n], X[:, kc, t0:t1], M[:, 1, kc, col:col + 1], rs[:, 0:n], ALU.mult, ALU.mult,
                              r=[X.b(kc, tbi), M.b(), rs.b()], w=[tp.b()])
                        k.act(U[:, kc, t0:t1], tp[:, 0:n], AF.Identity, r=[tp.b(), M.b()], w=[U.b(kc, tbi)],
                              bias=M[:, 0, kc, col:col + 1])

    def win_tile(self, wt, l, c0, ncol=128):
        k = self.k
        src = self.WB["w_in"]
        k.dma(k.SP, wt[:, :, 0:ncol], src.t[l].rearrange("(kc p) n -> p kc n", p=128)[:, :, c0:c0 + ncol],
              r=[src.b(l)], w=[wt.b()])

    def proj_fm(self, wt, tbi, bk, ncol0=0):
        k, U = self.k, self.U
        t0, t1 = TB[tbi]
        n = t1 - t0
        for kc in range(8):
            k.mm(bk[:, 0:n], wt[:, kc, ncol0:ncol0 + 128], U[:, kc, t0:t1], kc == 0, kc == 7,
                 r=[wt.b(), U.b(kc, tbi)], w=[bk.b()], sig=(kc == 7))
        return n

    def hyena(self, s, l):
        k, C, nc = self.k, self.C, self.nc
        V, A, P = k.DVE, k.ACT, k.POOL
        VBb = self.VB.b()
        need_ctx = l < self.depth - 1
        with k.scope() as sc:
            Utok = sc.tile("Utok", [128, 18, 512], BF16)
            with k.scope() as sc2:
                P1 = [sc2.tile(f"hyP{i}", [128, PT], F32) for i in range(1)]
                acc = sc2.tile("hyacc", [128, PT], F32)
                Zv = sc2.tile("hyZv", [128, PT], F32)
                ubf = [sc2.tile(f"hyu{i}", [128, PT], BF16) for i in range(1)]
                X0S = [sc2.tile(f"hyx0s{i}", [128, T], BF16) for i in range(2)]
                WT = [sc2.tile(f"hyw{i}", [128, 8, 128], BF16) for i in range(3)]
                for p in P1:
                    k.memset(P, p[:], 0.0, w=[p.b()])
                segs = [(0, CTX), (CTX, T)]
                it = 0
                for cc in range(4):
                    ub = ubf[0]
                    x0s = X0S[cc % 2]
                    for comp in (0, 2, 1):
                        wt = WT[it % 3]
                        p1 = P1[0]
                        it += 1
                        self.win_tile(wt, l, C_HY + comp * 512 + cc * 128)
                        for tbi, (t0, t1) in enumerate(TB):
                            bk = k.bank()
                            n = self.proj_fm(wt, tbi, bk)
                            k.cp(A, p1[:, poff(t0):poff(t0) + n], bk[:, 0:n], r=[bk.b()], w=[p1.b()])
                        ch = comp * 4 + cc
                        w0, w1, w2 = (self.vec("hy_short_w", l, kk * 12 + ch) for kk in range(3))
                        bb = self.vec("hy_short_b", l, ch)
                        for (a0, a1) in segs:
                            q0, q1 = poff(a0), poff(a0) + (a1 - a0)
                            k.act(acc[:, q0:q1], p1[:, q0:q1], AF.Identity, r=[p1.b(), VBb], w=[acc.b()], bias=bb, scale=w1)
                            k.stt(acc[:, q0:q1], p1[:, q0 - 1:q1 - 1], w0, acc[:, q0:q1], ALU.mult, ALU.add,
                                  r=[p1.b(), acc.b(), VBb], w=[acc.b()])
                            if comp == 0:
                                k.stt(Zv[:, q0:q1], p1[:, q0 + 1:q1 + 1], w2, acc[:, q0:q1], ALU.mult, ALU.add,
                                      r=[p1.b(), acc.b(), VBb], w=[Zv.b()])
                            elif comp == 2:
                                k.stt(acc[:, q0:q1], p1[:, q0 + 1:q1 + 1], w2, acc[:, q0:q1], ALU.mult, ALU.add,
                                      r=[p1.b(), acc.b(), VBb], w=[acc.b()])
                                k.tt(V, ub[:, q0:q1], acc[:, q0:q1], Zv[:, q0:q1], ALU.mult, r=[acc.b(), Zv.b()], w=[ub.b()])
                            else:
                                k.stt(x0s[:, a0:a1], p1[:, q0 + 1:q1 + 1], w2, acc[:, q0:q1], ALU.mult, ALU.add,
                                      r=[p1.b(), acc.b(), VBb], w=[x0s.b()])
                    k.dma(k.SP, self.YP.t[0, :, cc, :], x0s[:], r=[x0s.b()], w=[self.YP.b(0, cc, i) for i in range(5)])
                    for g0 in range(0, 18, 8):
                        g1 = min(18, g0 + 8)
                        bk = k.bank()
                        bv = bk[:, :].bitcast(BF16)
                        for tt in range(g0, g1):
                            k.tr(bv[:, (tt - g0) * 128:(tt - g0 + 1) * 128], ub[:, poff(tt * 128):poff(tt * 128) + 128], C["ident_b"][:],
                                 r=[ub.b(), C["ident_b"].b()], w=[bk.b()], sig=(tt == g1 - 1))
                        k.cp(A, Utok[:, g0:g1, cc * 128:(cc + 1) * 128],
                             bv[:, 0:(g1 - g0) * 128].rearrange("p (a b) -> p a b", b=128), r=[bk.b()], w=[Utok.b(cc)])
            self.dump("utok", Utok[:], [128, 18, 512], BF16)
            ut_bufs = [Utok.b(cc) for cc in range(4)]
            YP = self.YP
            if need_ctx:
                with k.scope() as sc4:
                    KT = [sc4.tile(f"hykt{i}", [128, 2, 512], BF16) for i in range(2)]
                    AB = [sc4.tile(f"hyab{i}", [128, 2, 512], BF16) for i in range(2)]
                    TM = [sc4.tile(f"hytm{i}", [128, 512], F32) for i in range(4)]
                    Yc = sc4.tile("hyYc", [128, 2, 2, 512], BF16)
                    x0c = sc4.tile("hyx0c", [128, 4, CTX], BF16)
                    k.dma(k.SP, x0c[:], YP.t[0, :, :, 0:CTX], r=[YP.b(0, cc, 0) for cc in range(4)], w=[x0c.b()])
                    for ft in range(2):
                        kt, ab = KT[ft % 2], AB[ft % 2]
                        k.dma(k.SP, kt[:], self.KSC.t[ft], r=[self.KSC.b(ft)], w=[kt.b()])
                        bA, bB = k.bank(), k.bank()
                        for cs, bk in ((0, bA), (1, bB)):
                            for st in range(2):
                                k.mm(bk[:, :], C["fwdC"][:, st, cs, ft * 128:(ft + 1) * 128], Utok[:, st, :], st == 0, st == 1,
                                     r=[C["fwdC"].b()] + ut_bufs, w=[bk.b()], sig=(st == 1))
                        k.cp(A, ab[:, 0, :], bA[:, :], r=[bA.b()], w=[ab.b(0)])
                        k.cp(A, ab[:, 1, :], bB[:, :], r=[bB.b()], w=[ab.b(1)])
                        self.spec_mul(Yc[:, ft, 0, :], Yc[:, ft, 1, :], ab[:, 0, :], ab[:, 1, :], kt[:, 0, :], kt[:, 1, :],
                                      [t[:] for t in TM], [ab.b(0), ab.b(1), kt.b()], TM, [Yc.b(ft)])
                    for cc in range(4):
                        bk = k.bank()
                        i = 0
                        for ft in range(2):
                            for cs in range(2):
                                k.mm(bk[:, 0:CTX], Yc[:, ft, cs, cc * 128:(cc + 1) * 128], C["invC"][:, ft, cs, :], i == 0, i == 3,
                                     r=[Yc.b(ft), C["invC"].b()], w=[bk.b()], sig=(i == 3))
                                i += 1
                        k.tt(V, x0c[:, cc, :], bk[:, 0:CTX], x0c[:, cc, :], ALU.mult, r=[bk.b(), x0c.b()], w=[x0c.b()])
                    k.dma(k.SP, YP.t[0, :, :, 0:CTX], x0c[:], r=[x0c.b()], w=[YP.b(0, cc, 0) for cc in range(4)])
            with k.scope() as sc3:
                DB = [sc3.tile(f"hydb{i}", [128, 4096], BF16) for i in range(2)]
                KT = [sc3.tile(f"hykt{i}", [128, 2, 512], BF16) for i in range(2)]
                AB = [sc3.tile(f"hyab{i}", [128, 2, 256], BF16) for i in range(2)]
                TM = [sc3.tile(f"hytm{i}", [128, 256], F32) for i in range(4)]
                Yf = sc3.tile("hyYf", [128, 16, 2, 256], BF16)
                XB = [sc3.tile(f"hyxb{i}", [128, 512], BF16) for i in range(4)]
                dbi = 0
                xbi = 0
                for half in range(2):
                    h0 = half * 256
                    for ft in range(16):
                        fwt = DB[dbi % 2]
                        dbi += 1
                        fw = fwt[:, :].rearrange("p (a b c) -> p a b c", a=16, b=2)
                        kt, ab = KT[ft % 2], AB[ft % 2]
                        k.dma(k.SP, fwt[:, :], self.cst["fwdL"][ft].rearrange("p a b c -> p (a b c)"), w=[fwt.b()])
                        k.dma(k.SP, kt[:], self.KS.t[l, ft], r=[self.KS.b(l, ft)], w=[kt.b()])
                        bk = k.bank()
                        for cs in range(2):
                            for st in range(16):
                                k.mm(bk[:, cs * 256:(cs + 1) * 256], fw[:, st, cs, :], Utok[:, 2 + st, h0:h0 + 256], st == 0, st == 15,
                                     r=[fwt.b(), Utok.b(half * 2), Utok.b(half * 2 + 1)], w=[bk.b()], sig=(st == 15))
                        k.cp(A, ab[:, :, :], bk[:, :].rearrange("p (a b) -> p a b", a=2), r=[bk.b()], w=[ab.b()])
                        self.spec_mul(Yf[:, ft, 0, :], Yf[:, ft, 1, :], ab[:, 0, :], ab[:, 1, :], kt[:, 0, h0:h0 + 256], kt[:, 1, h0:h0 + 256],
                                      [t[:] for t in TM], [ab.b(), kt.b()], TM, [Yf.b(ft)])
                    bks = [[k.bank() for tb in range(4)] for c2 in range(2)]
                    for ft in range(16):
                        ivt = DB[dbi % 2]
                        dbi += 1
                        iv = ivt[:, :].rearrange("p (a b) -> p a b", a=2)
                        k.dma(k.SP, ivt[:, :], self.cst["invL"][ft].rearrange("p a b -> p (a b)"), w=[ivt.b()])
                        for c2 in range(2):
                            for tb in range(4):
                                bk = bks[c2][tb]
                                for cs in range(2):
                                    k.mm(bk[:, :], Yf[:, ft, cs, c2 * 128:(c2 + 1) * 128], iv[:, cs, tb * 512:(tb + 1) * 512],
                                         ft == 0 and cs == 0, ft == 15 and cs == 1, r=[Yf.b(ft), ivt.b()], w=[bk.b()],
                                         sig=(cs == 1 and tb == 3 and c2 == 1))
                    for c2 in range(2):
                        cc = half * 2 + c2
                        for tb in range(4):
                            t0 = CTX + tb * 512
                            xb = XB[xbi % 4]
                            xbi += 1
                            k.dma(k.SP, xb[:], YP.t[0, :, cc, t0:t0 + 512], r=[YP.b(0, cc, tb + 1)], w=[xb.b()])
                            k.tt(V, xb[:], bks[c2][tb][:, :], xb[:], ALU.mult, r=[bks[c2][tb].b(), xb.b()], w=[xb.b()])
                            k.dma(k.SP, YP.t[0, :, cc, t0:t0 + 512], xb[:], r=[xb.b()], w=[YP.b(0, cc, tb + 1)])

    def lru(self, s, l):
        k, C = self.k, self.C
        V, A, P = k.DVE, k.ACT, k.POOL
        VBb = self.VB.b()
        YP = self.YP
        with k.scope() as sc:
            self.build_wbd(sc, l)
            xp = sc.tile("lrxp", [128, PT], F32)
            k.memset(P, xp[:], 0.0, w=[xp.b()])
            hsum = sc.tile("lrhs", [128, T], F32)
            xc = sc.tile("lrxc", [128, T], F32)
            xcb = sc.tile("lrxcb", [128, T], BF16)
            R = sc.tile("lrR", [128, T], F32)
            I = sc.tile("lrI", [128, T], F32)
            Aa = sc.tile("lrA", [128, T], F32)
            YS = [sc.tile(f"lrys{i}", [128, T], BF16) for i in range(2)]
            GG = [sc.tile(f"lrgg{i}", [128, 512], BF16) for i in range(2)]
            WX = [sc.tile(f"lrwx{i}", [128, 8, 128], BF16) for i in range(2)]
            WG = [sc.tile(f"lrwg{i}", [128, 8, 128], BF16) for i in range(2)]
            segs = [(0, CTX), (CTX, T)]
            for cc in range(4):
                wx, wg, ys = WX[cc % 2], WG[cc % 2], YS[cc % 2]
                self.win_tile(wx, l, C_LX + cc * 128)
                self.win_tile(wg, l, C_LG + cc * 128)
                for tbi, (t0, t1) in enumerate(TB):
                    bk = k.bank()
                    n = self.proj_fm(wx, tbi, bk)
                    k.cp(A, xp[:, poff(t0):poff(t0) + n], bk[:, 0:n], r=[bk.b()], w=[xp.b()])
                for d in range(2):
                    sg = -1 if d == 0 else 1
                    ch = d * 4 + cc
                    wk = [self.vec("lru_conv_w", l, (d * 4 + kk) * 4 + cc) for kk in range(4)]
                    bb = self.vec("lru_conv_b", l, ch)
                    for (a0, a1) in segs:
                        q0, q1 = poff(a0), poff(a0) + (a1 - a0)
                        k.act(xc[:, a0:a1], xp[:, q0:q1], AF.Identity, r=[xp.b(), VBb], w=[xc.b()], bias=bb, scale=wk[3])
                        for j in (1, 2, 3):
                            k.stt(xc[:, a0:a1], xp[:, q0 + sg * j:q1 + sg * j], wk[3 - j], xc[:, a0:a1], ALU.mult, ALU.add,
                                  r=[xp.b(), xc.b(), VBb], w=[xc.b()])
                    k.cp(P, xcb[:], xc[:], r=[xc.b()], w=[xcb.b()])
                    for tbi, (t0, t1) in enumerate(TB):
                        n = t1 - t0
                        b0, b1 = k.bank(), k.bank()
                        k.mm(b0[:, 0:n], self.wbd(l, d, 0, cc), xcb[:, t0:t1], True, True, r=[self.WBD.b(), xcb.b()], w=[b0.b()], sig=True)
                        k.mm(b1[:, 0:n], self.wbd(l, d, 1, cc), xcb[:, t0:t1], True, True, r=[self.WBD.b(), xcb.b()], w=[b1.b()], sig=True)
                        k.act(R[:, t0:t1], b0[:, 0:n], AF.Sigmoid, r=[b0.b(), VBb], w=[R.b()], bias=self.vec("lru_br", l, ch))
                        k.act(I[:, t0:t1], b1[:, 0:n], AF.Sigmoid, r=[b1.b(), VBb], w=[I.b()], bias=self.vec("lru_bi", l, ch))
                    k.act(Aa[:], R[:], AF.Exp, r=[R.b(), self.C1.b()], w=[Aa.b()], scale=self.C1[:, l, 0, ch:ch + 1])
                    k.act(R[:], R[:], AF.Exp, r=[R.b(), self.C1.b()], w=[R.b()], scale=self.C1[:, l, 1, ch:ch + 1])
                    k.act(R[:], R[:], AF.Sqrt, r=[R.b()], w=[R.b()], bias=1.0, scale=-1.0)
                    k.tt(P, I[:], I[:], xc[:], ALU.mult, r=[I.b(), xc.b()], w=[I.b()])
                    k.tt(V, I[:], I[:], R[:], ALU.mult, r=[I.b(), R.b()], w=[I.b()])
                    if d == 0:
                        k.op(V, lambda: V.eng.tensor_tensor_scan(out=hsum[:], data0=Aa[:], data1=I[:], initial=0.0, op0=ALU.mult, op1=ALU.add),
                             r=[Aa.b(), I.b()], w=[hsum.b()])
                    else:
                        k.op(V, lambda: V.eng.tensor_tensor_scan(out=I[:, 0:CTX][:, ::-1], data0=Aa[:, 0:CTX][:, ::-1], data1=I[:, 0:CTX][:, ::-1],
                                                                 initial=0.0, op0=ALU.mult, op1=ALU.add), r=[Aa.b(), I.b()], w=[I.b()])
                        k.op(V, lambda: V.eng.tensor_tensor_scan(out=I[:, CTX:T][:, ::-1], data0=Aa[:, CTX:T][:, ::-1], data1=I[:, CTX:T][:, ::-1],
                                                                 initial=I[:, 0:1], op0=ALU.mult, op1=ALU.add), r=[Aa.b(), I.b()], w=[I.b()])
                        k.tt(P, hsum[:], hsum[:], I[:], ALU.add, r=[hsum.b(), I.b()], w=[hsum.b()])
                for tbi, (t0, t1) in enumerate(TB):
                    n = t1 - t0
                    bk = k.bank()
                    self.proj_fm(wg, tbi, bk)
                    gg = GG[tbi % 2]
                    k.act(gg[:, 0:n], bk[:, 0:n], AF.Gelu_apprx_tanh, r=[bk.b()], w=[gg.b()])
                    k.tt(V, ys[:, t0:t1], hsum[:, t0:t1], gg[:, 0:n], ALU.mult, r=[hsum.b(), gg.b()], w=[ys.b()])
                k.dma(k.SP, YP.t[1, :, cc, :], ys[:], r=[ys.b()], w=[YP.b(1, cc, i) for i in range(5)])

    def hgrn(self, s, l):
        k, C = self.k, self.C
        V, A, P = k.DVE, k.ACT, k.POOL
        VBb = self.VB.b()
        YP, U = self.YP, self.U
        NCH = T // 64
        with k.scope() as sc:
            qb = sc.tile("hgq", [128, T], BF16)
            Vtok = sc.tile("hgV", [128, 18, 128], BF16)
            O = sc.tile("hgO", [128, T], F32)
            Bc = sc.tile("hgB", [128, T], F32)
            kkb = sc.tile("hgkk", [128, T], BF16)
            CH = sc.tile("hgCH", [128, 6, NCH], F32)
            S = sc.tile("hgS", [128, 128], F32)
            SB = [sc.tile(f"hgSb{i}", [128, 128], BF16) for i in range(4)]
            Dt = sc.tile("hgD", [128, 512], F32)
            E1 = sc.tile("hgE1", [128, 512], F32)
            E2 = sc.tile("hgE2", [128, 512], F32)
            QT = [sc.tile(f"hgqt{i}", [128, 512], BF16) for i in range(2)]
            KT = [sc.tile(f"hgkt{i}", [128, 512], BF16) for i in range(2)]
            QH = [sc.tile(f"hgqh{i}", [128, 512], BF16) for i in range(2)]
            KH = [sc.tile(f"hgkh{i}", [128, 512], BF16) for i in range(2)]
            PM = [sc.tile(f"hgpm{i}", [128, 128], BF16) for i in range(3)]
            KK = [sc.tile(f"hgkhk{i}", [128, 128], BF16) for i in range(3)]
            ys = sc.tile("hgys", [128, T], BF16)
            SG = [sc.tile(f"hgsg{i}", [128, 512], F32) for i in range(2)]
            TMP = [sc.tile(f"hgtmp{i}", [128, 512], F32) for i in range(2)]
            WT = [sc.tile(f"hgw{i}", [128, 8, 128], BF16) for i in range(6)]
            wi_ = 0
            blk_i = 0
            pr_i = 0
            for hd in range(4):
                ws = {}
                for nm, c0 in (("q", C_Q), ("ff", C_FF), ("fb", C_FB), ("i", C_I), ("og", C_OG)):
                    ws[nm] = WT[wi_ % 6]
                    wi_ += 1
                    self.win_tile(ws[nm], l, c0 + hd * 128)
                for tbi, (t0, t1) in enumerate(TB):
                    bk = k.bank()
                    n = self.proj_fm(ws["q"], tbi, bk)
                    k.act(qb[:, t0:t1], bk[:, 0:n], AF.Silu, r=[bk.b()], w=[qb.b()])
                for g0 in range(0, 18, 4):
                    g1 = min(18, g0 + 4)
                    bk = k.bank()
                    for tt in range(g0, g1):
                        tbi = self.tbi(tt * 128)
                        for kc in range(8):
                            k.mm(bk[:, (tt - g0) * 128:(tt - g0 + 1) * 128], U[:, kc, tt * 128:(tt + 1) * 128], ws["i"][:, kc, :],
                                 kc == 0, kc == 7, r=[U.b(kc, tbi), ws["i"].b()], w=[bk.b()], sig=(kc == 7))
                    k.cp(A, Vtok[:, g0:g1, :], bk[:, 0:(g1 - g0) * 128].rearrange("p (a b) -> p a b", b=128), r=[bk.b()], w=[Vtok.b()])
                if getattr(self, "hg_cut", 0) == 1:
                    return
                for d in range(2):
                    wf = ws["ff"] if d == 0 else ws["fb"]
                    for tbi, (t0, t1) in enumerate(TB):
                        bk = k.bank()
                        n = self.proj_fm(wf, tbi, bk)
                        k.act(Bc[:, t0:t1], bk[:, 0:n], AF.Sigmoid, r=[bk.b()], w=[Bc.b()])
                    lbv, omlv, nomlv = (self.LB[:, l, i, hd:hd + 1] for i in range(3))
                    k.ts(V, kkb[:], Bc[:], nomlv, ALU.mult, omlv, ALU.add, r=[Bc.b(), self.LB.b()], w=[kkb.b()])
                    k.ts(V, Bc[:], Bc[:], omlv, ALU.mult, lbv, ALU.add, r=[Bc.b(), self.LB.b()], w=[Bc.b()])
                    k.act(Bc[:], Bc[:], AF.Ln, r=[Bc.b()], w=[Bc.b()])
                    ones = C["onesrow"]
                    if d == 0:
                        order = [0, 1, 2, 3, 4]
                    else:
                        order = [0, 4, 3, 2, 1]
                    prev = None
                    for tbi in order:
                        t0, t1 = TB[tbi]
                        n = t1 - t0
                        seg = Bc[:, t0:t1] if d == 0 else Bc[:, t0:t1][:, ::-1]
                        init = 0.0 if prev is None else prev
                        k.op(V, lambda seg=seg, n=n, init=init: V.eng.tensor_tensor_scan(out=seg, data0=ones[:, 0:n], data1=seg, initial=init,
                                                                                      op0=ALU.mult, op1=ALU.add), r=[Bc.b(), ones.b()], w=[Bc.b()])
                        prev = Bc[:, t1 - 1:t1] if d == 0 else Bc[:, t0:t0 + 1]
                    if d == 0:
                        k.cp(V, CH[:, 0, :], Bc[:, 32::64], r=[Bc.b()], w=[CH.b()])
                        k.cp(V, CH[:, 1, :], Bc[:, 63::64], r=[Bc.b()], w=[CH.b()])
                        k.memset(V, CH[:, 2, 0:1], 0.0, w=[CH.b()])
                        k.cp(V, CH[:, 2, 1:NCH], CH[:, 1, 0:NCH - 1], r=[CH.b()], w=[CH.b()])
                    else:
                        k.cp(V, CH[:, 0, :], Bc[:, 31::64], r=[Bc.b()], w=[CH.b()])
                        k.cp(V, CH[:, 1, :], Bc[:, 0::64], r=[Bc.b()], w=[CH.b()])
                        k.cp(V, CH[:, 2, 0:NCH - 1], CH[:, 1, 1:NCH], r=[CH.b()], w=[CH.b()])
                        k.memset(V, CH[:, 2, 3:4], 0.0, w=[CH.b()])
                        k.cp(V, CH[:, 2, NCH - 1:NCH], CH[:, 1, 0:1], r=[CH.b()], w=[CH.b()])
                    k.tt(V, CH[:, 3, :], CH[:, 0, :], CH[:, 2, :], ALU.subtract, r=[CH.b()], w=[CH.b()])
                    k.tt(V, CH[:, 4, :], CH[:, 1, :], CH[:, 0, :], ALU.subtract, r=[CH.b()], w=[CH.b()])
                    k.tt(V, CH[:, 5, :], CH[:, 1, :], CH[:, 2, :], ALU.subtract, r=[CH.b()], w=[CH.b()])
                    k.act(CH[:, 3:6, :], CH[:, 3:6, :], AF.Exp, r=[CH.b()], w=[CH.b()])
                    if getattr(self, "hg_cut", 0) == 2:
                        return
                    k.memset(V, S[:], 0.0, w=[S.b()])
                    sbi = 0
                    sb_cur = SB[sbi % 4]
                    k.memset(V, sb_cur[:], 0.0, w=[sb_cur.b()])
                    mask = C["maskF"] if d == 0 else C["maskB"]
                    for tbi in order:
                        t0, t1 = TB[tbi]
                        n = t1 - t0
                        c0, nch = t0 // 64, n // 64
                        qt, kt_, qh, kh = QT[blk_i % 2], KT[blk_i % 2], QH[blk_i % 2], KH[blk_i % 2]
                        blk_i += 1

                        def bc(row):
                            return CH[:, row, c0:c0 + nch].unsqueeze(2).to_broadcast([128, nch, 64])

                        def v3(ap):
                            return ap.rearrange("p (a b) -> p a b", b=64)
                        k.tt(V, v3(Dt[:, 0:n]), v3(Bc[:, t0:t1]), bc(0), ALU.subtract, r=[Bc.b(), CH.b()], w=[Dt.b()])
                        k.act(E1[:, 0:n], Dt[:, 0:n], AF.Exp, r=[Dt.b()], w=[E1.b()])
                        k.act(E2[:, 0:n], Dt[:, 0:n], AF.Exp, r=[Dt.b()], w=[E2.b()], scale=-1.0)
                        k.tt(V, qt[:, 0:n], qb[:, t0:t1], E1[:, 0:n], ALU.mult, r=[qb.b(), E1.b()], w=[qt.b()])
                        k.tt(P, kt_[:, 0:n], kkb[:, t0:t1], E2[:, 0:n], ALU.mult, r=[kkb.b(), E2.b()], w=[kt_.b()])
                        k.tt(P, v3(E1[:, 0:n]), v3(E1[:, 0:n]), bc(3), ALU.mult, r=[E1.b(), CH.b()], w=[E1.b()])
                        k.tt(V, v3(E2[:, 0:n]), v3(E2[:, 0:n]), bc(4), ALU.mult, r=[E2.b(), CH.b()], w=[E2.b()])
                        k.tt(V, qh[:, 0:n], qb[:, t0:t1], E1[:, 0:n], ALU.mult, r=[qb.b(), E1.b()], w=[qh.b()])
                        k.tt(P, kh[:, 0:n], kkb[:, t0:t1], E2[:, 0:n], ALU.mult, r=[kkb.b(), E2.b()], w=[kh.b()])
                        if getattr(self, "hg_cut", 0) == 3:
                            return
                        npair = n // 128
                        pairs = list(range(npair)) if d == 0 else list(range(npair - 1, -1, -1))
                        for j in pairs:
                            o = j * 128
                            tp = t0 + o
                            tt = tp // 128
                            pm, kk_ = PM[pr_i % 3], KK[pr_i % 3]
                            pr_i += 1
                            b_sc, b_tr, b_kv, b_kv2, b_o = k.bank(), k.bank(), k.bank(), k.bank(), k.bank()
                            kvb = [b_kv, b_kv2]
                            k.mm(b_sc[:, 0:128], kt_[:, o:o + 128], qt[:, o:o + 128], True, True, r=[kt_.b(), qt.b()], w=[b_sc.b()], sig=True)
                            if getattr(self, "hg_cut", 0) == 7:
                                return
                            if getattr(self, "hg_var", 0) == 1:
                                sct = TMP[0]
                                k.cp(A, sct[:, 0:128], b_sc[:, 0:128], r=[b_sc.b()], w=[sct.b()])
                                k.tt(V, pm[:], sct[:, 0:128], mask[:], ALU.mult, r=[sct.b(), mask.b()], w=[pm.b()])
                            elif getattr(self, "hg_var", 0) == 2:
                                k.cp(V, pm[:], b_sc[:, 0:128], r=[b_sc.b()], w=[pm.b()])
                            else:
                                k.memset(P, pm[:], 0.0, w=[pm.b()])
                                k.op(V, lambda pm=pm, b_sc=b_sc: V.eng.copy_predicated(out=pm[:], mask=mask[:], data=b_sc[:, 0:128]),
                                     r=[b_sc.b(), mask.b(), pm.b()], w=[pm.b()])
                            if getattr(self, "hg_cut", 0) == 6:
                                return
                            btv = b_tr[:, :].bitcast(BF16)
                            k.tr(btv[:, 0:128], kh[:, o:o + 128], C["ident_b"][:], r=[kh.b(), C["ident_b"].b()], w=[b_tr.b()])
                            k.cp(A, kk_[:], btv[:, 0:128], r=[b_tr.b()], w=[kk_.b()])
                            if getattr(self, "hg_cut", 0) == 4:
                                return
                            k.mm(b_kv[:, 0:128], kk_[0:64, :], Vtok[0:64, tt, :], True, True, r=[kk_.b(), Vtok.b()], w=[b_kv.b()], sig=True)
                            k.mm(b_kv2[:, 0:128], kk_[64:128, :], Vtok[64:128, tt, :], True, True, r=[kk_.b(), Vtok.b()], w=[b_kv2.b()], sig=True)
                            halves = [0, 1] if d == 0 else [1, 0]
                            sb_start = {}
                            for hh in halves:
                                ch = (tp // 64) + hh
                                sb_start[hh] = sb_cur
                                k.stt(S[:], S[:], CH[:, 5, ch:ch + 1], kvb[hh][:, 0:128], ALU.mult, ALU.add,
                                      r=[S.b(), CH.b(), kvb[hh].b()], w=[S.b()])
                                sbi += 1
                                sb_cur = SB[sbi % 4]
                                k.cp(A, sb_cur[:], S[:], r=[S.b()], w=[sb_cur.b()])
                            if getattr(self, "hg_cut", 0) == 5:
                                return
                            k.mm(b_o[:, 0:128], Vtok[:, tt, :], pm[:], True, False, r=[Vtok.b(), pm.b()], w=[b_o.b()], sig=False)
                            k.mm(b_o[:, 0:64], sb_start[0][:], qh[:, o:o + 64], False, False, r=[sb_start[0].b(), qh.b()], w=[b_o.b()], sig=False)
                            k.mm(b_o[:, 64:128], sb_start[1][:], qh[:, o + 64:o + 128], False, True, r=[sb_start[1].b(), qh.b()], w=[b_o.b()], sig=True)
                            if d == 0:
                                k.cp(A, O[:, tp:tp + 128], b_o[:, 0:128], r=[b_o.b()], w=[O.b(tt)])
                            else:
                                k.tt(V, O[:, tp:tp + 128], b_o[:, 0:128], O[:, tp:tp + 128], ALU.add, r=[b_o.b(), O.b(tt)], w=[O.b(tt)])
                ng = self.vec("hg_norm_g", l)
                for tbi, (t0, t1) in enumerate(TB):
                    n = t1 - t0
                    with k.scope() as sc2:
                        obufs = [O.b(tt) for tt in range(t0 // 128, t1 // 128)]
                        rs = self.rstd_of(sc2, O[:, t0:t1].unsqueeze(1), obufs, n, C["ones_v"], nk=1, tag="hg")
                        bk = k.bank()
                        self.proj_fm(ws["og"], tbi, bk)
                        sg, tmp = SG[tbi % 2], TMP[tbi % 2]
                        k.act(sg[:, 0:n], bk[:, 0:n], AF.Silu, r=[bk.b()], w=[sg.b()])
                        k.stt(tmp[:, 0:n], O[:, t0:t1], ng, rs[:, 0:n], ALU.mult, ALU.mult, r=obufs + [VBb, rs.b()], w=[tmp.b()])
                        k.tt(V, ys[:, t0:t1], tmp[:, 0:n], sg[:, 0:n], ALU.mult, r=[tmp.b(), sg.b()], w=[ys.b()])
                k.dma(k.SP, YP.t[2, :, hd, :], ys[:], r=[ys.b()], w=[YP.b(2, hd, i) for i in range(5)])

    def w_tile(self, wt, name, l, c0, nkc, ncol=128, k0=0):
        k = self.k
        src = self.WB[name]
        k.dma(k.SP, wt[:, 0:nkc, 0:ncol],
              src.t[l].rearrange("(kc p) n -> p kc n", p=128)[:, k0:k0 + nkc, c0:c0 + ncol], r=[src.b(l)], w=[wt.b()])

    def merge(self, s, l):
        k, C = self.k, self.C
        V, A, P = k.DVE, k.ACT, k.POOL
        X, U, M, YP = self.X, self.U, self.MOD[l], self.YP
        last = l == self.depth - 1
        pnames = ("w_proj_hy", "w_proj_lru", "w_proj_hg")
        with k.scope() as sc:
            YB = [sc.tile(f"mgy{i}", [128, 4, 512], BF16) for i in range(3)]
            mbf = sc.tile("mgm", [128, 8, 512], BF16)
            MO = sc.tile("mgmo", [128, 8, 512], F32)
            G = [sc.tile(f"mgg{i}", [128, 512], F32) for i in range(2)]
            macc = sc.tile("mgacc", [128, 512], F32)
            T2 = [sc.tile(f"mgt{i}", [128, 512], F32) for i in range(2)]
            WG = [sc.tile(f"mgwg{i}", [128, 8, 128], BF16) for i in range(6)]
            WP = [sc.tile(f"mgwp{i}", [128, 4, 128], BF16) for i in range(6)]
            WO = [sc.tile(f"mgwo{i}", [128, 8, 128], BF16) for i in range(2)]
            it = 0
            for tbi, (t0, t1) in enumerate(TB):
                if last and tbi == 0:
                    continue
                n = t1 - t0
                col = 2 if tbi == 0 else s
                for br in range(3):
                    k.dma(k.SP, YB[br][:, :, 0:n], YP.t[br, :, :, t0:t1], r=[YP.b(br, cc, tbi) for cc in range(4)], w=[YB[br].b()])
                for oc in range(8):
                    for br in range(3):
                        wg, wp = WG[it % 6], WP[it % 6]
                        g, t2 = G[it % 2], T2[it % 2]
                        it += 1
                        self.win_tile(wg, l, C_MG + br * 1024 + oc * 128)
                        self.w_tile(wp, pnames[br], l, oc * 128, 4)
                        bp, bg = k.bank(), k.bank()
                        for kc in range(4):
                            k.mm(bp[:, 0:n], wp[:, kc, :], YB[br][:, kc, 0:n], kc == 0, kc == 3, r=[wp.b(), YB[br].b()], w=[bp.b()], sig=(kc == 3))
                        self.proj_fm(wg, tbi, bg)
                        k.act(g[:, 0:n], bg[:, 0:n], AF.Sigmoid, r=[bg.b()], w=[g.b()])
                        if br == 0:
                            k.tt(V, macc[:, 0:n], bp[:, 0:n], g[:, 0:n], ALU.mult, r=[bp.b(), g.b()], w=[macc.b()])
                        elif br == 1:
                            k.tt(V, t2[:, 0:n], bp[:, 0:n], g[:, 0:n], ALU.mult, r=[bp.b(), g.b()], w=[t2.b()])
                            k.tt(P, macc[:, 0:n], macc[:, 0:n], t2[:, 0:n], ALU.add, r=[macc.b(), t2.b()], w=[macc.b()])
                        else:
                            k.tt(V, t2[:, 0:n], bp[:, 0:n], g[:, 0:n], ALU.mult, r=[bp.b(), g.b()], w=[t2.b()])
                            k.tt(P, mbf[:, oc, 0:n], macc[:, 0:n], t2[:, 0:n], ALU.add, r=[macc.b(), t2.b()], w=[mbf.b(oc)])
                for oc in range(8):
                    wo = WO[oc % 2]
                    self.w_tile(wo, "w_out", l, oc * 128, 8)
                    bk = k.bank()
                    for kc in range(8):
                        k.mm(bk[:, 0:n], wo[:, kc, :], mbf[:, kc, 0:n], kc == 0, kc == 7, r=[wo.b(), mbf.b(kc)], w=[bk.b()], sig=(kc == 7))
                    k.cp(A, MO[:, oc, 0:n], bk[:, 0:n], r=[bk.b()], w=[MO.b(oc)])
                self.resid_update(sc, MO, mbf, n, tbi, t0, t1, M, 2, col, sqb=[mbf.b(oc) for oc in range(8)])

    def resid_update(self, sc, MO, sqt, n, tbi, t0, t1, M, gi, col, sqb=None):
        k, C = self.k, self.C
        X = self.X
        with k.scope() as sc2:
            rs = self.rstd_of(sc2, MO[:, 0:8, 0:n], [MO.b(kc) for kc in range(8)], n, C["ones_d"], sq=sqt, sqb=sqb)
            tmp = [sc2.tile(f"rutmp{i}", [128, 512], F32) for i in range(2)]
            for kc in range(8):
                tp = tmp[kc % 2]
                k.stt(tp[:, 0:n], MO[:, kc, 0:n], M[:, gi, kc, col:col + 1], rs[:, 0:n], ALU.mult, ALU.mult,
                      r=[MO.b(kc), M.b(), rs.b()], w=[tp.b()])
                k.tt(k.POOL, X[:, kc, t0:t1], X[:, kc, t0:t1], tp[:, 0:n], ALU.add, r=[X.b(kc, tbi), tp.b()], w=[X.b(kc, tbi)])

    def mlp(self, s, l):
        k, C = self.k, self.C
        V, A, P = k.DVE, k.ACT, k.POOL
        X, U, M = self.X, self.U, self.MOD[l]
        last = l == self.depth - 1
        with k.scope() as sc:
            H = sc.tile("mlH", [128, 32, 512], BF16)
            MO = Tile(H.t[:, 0:16, :].rearrange("p a b -> p (a b)").bitcast(F32).rearrange("p (a b) -> p a b", b=512), "mlMO")
            MO.b = lambda *key: H.b("mo")
            sqt = sc.tile("mlsq", [128, 8, 512], BF16)
            SQ = [sc.tile(f"mlsqr{i}", [128, 512], F32) for i in range(2)]
            W1 = [sc.tile(f"mlw1{i}", [128, 8, 512], BF16) for i in range(2)]
            W2 = [sc.tile(f"mlw2{i}", [128, 2, 1024], BF16) for i in range(2)]
            hb_all = [H.b(j) for j in range(32)] + [H.b("mo")]
            for tbi, (t0, t1) in enumerate(TB):
                if last and tbi == 0:
                    continue
                n = t1 - t0
                col = 2 if tbi == 0 else s
                with k.scope() as sc2:
                    rs = self.rstd_of(sc2, X[:, :, t0:t1], [X.b(kc, tbi) for kc in range(8)], n, C["ones_d"], sq=sqt)
                    tmp = [sc2.tile(f"mltmp{i}", [128, 512], F32) for i in range(2)]
                    for kc in range(8):
                        tp = tmp[kc % 2]
                        k.stt(tp[:, 0:n], X[:, kc, t0:t1], M[:, 4, kc, col:col + 1], rs[:, 0:n], ALU.mult, ALU.mult,
                              r=[X.b(kc, tbi), M.b(), rs.b()], w=[tp.b()])
                        k.act(U[:, kc, t0:t1], tp[:, 0:n], AF.Identity, r=[tp.b(), M.b()], w=[U.b(kc, tbi)], bias=M[:, 3, kc, col:col + 1])
                for jg in range(8):
                    w1 = W1[jg % 2]
                    self.w_tile(w1, "w_mlp1", l, jg * 512, 8, ncol=512)
                    for jj in range(4):
                        j = jg * 4 + jj
                        bk = k.bank()
                        for kc in range(8):
                            k.mm(bk[:, 0:n], w1[:, kc, jj * 128:(jj + 1) * 128], U[:, kc, t0:t1], kc == 0, kc == 7,
                                 r=[w1.b(), U.b(kc, tbi)], w=[bk.b()], sig=(kc == 7))
                        sq = SQ[j % 2]
                        k.act(sq[:, 0:n], bk[:, 0:n], AF.Square, r=[bk.b()], w=[sq.b()])
                        k.stt(H[:, j, 0:n], bk[:, 0:n], 0.0, sq[:, 0:n], ALU.is_gt, ALU.mult, r=[bk.b(), sq.b()], w=[H.b(j), H.b("mo")])
                bks = [k.bank() for _ in range(8)]
                for jg in range(16):
                    w2 = W2[jg % 2]
                    self.w_tile(w2, "w_mlp2", l, 0, 2, ncol=1024, k0=jg * 2)
                    for jj in range(2):
                        j = jg * 2 + jj
                        for oc in range(8):
                            k.mm(bks[oc][:, 0:n], w2[:, jj, oc * 128:(oc + 1) * 128], H[:, j, 0:n], j == 0, j == 31,
                                 r=[w2.b(), H.b(j)], w=[bks[oc].b()], sig=(oc == 7 and jj == 1))
                for oc in range(8):
                    k.cp(A if oc % 2 == 0 else V, MO[:, oc, 0:n], bks[oc][:, 0:n], r=[bks[oc].b()], w=hb_all)
                self.resid_update(sc, MO, sqt, n, tbi, t0, t1, M, 5, col)

    def permute(self, fwd):
        k = self.k
        X = self.X
        engs = [k.ACT, k.DVE, k.POOL]
        with k.scope() as sc:
            TMPS = [sc.tile(f"pmt{i}", [128, SEQ], F32) for i in range(2)]
            for kc in range(8):
                tmp = TMPS[kc % 2]
                xb = [X.b(kc, tbi) for tbi in range(1, 5)]
                src = X[:, kc, CTX:T]
                if fwd:
                    v = src.rearrange("p (r c) -> p c r", c=GRID_W)
                    tv = tmp[:, :].rearrange("p (c r) -> p c r", c=GRID_W)
                else:
                    v = src.rearrange("p (c r) -> p r c", c=GRID_W)
                    tv = tmp[:, :].rearrange("p (r c) -> p r c", c=GRID_W)
                k.cp(engs[kc % 3], tv, v, r=xb, w=[tmp.b()])
                k.cp(engs[(kc + 1) % 3], X[:, kc, CTX:T], tmp[:, :], r=[tmp.b()], w=xb)

    def store_out(self, s):
        k, C = self.k, self.C
        X = self.X
        with k.scope() as sc:
            OT = [sc.tile(f"ot{i}", [128, D], F32) for i in range(3)]
            for tt in range(2, T // 128):
                ot = OT[tt % 3]
                tbi = self.tbi(tt * 128)
                for h in range(2):
                    bk = k.bank()
                    for j in range(4):
                        kc = h * 4 + j
                        k.tr(bk[:, j * 128:(j + 1) * 128], X[:, kc, tt * 128:(tt + 1) * 128], C["ident_f"][:],
                             r=[X.b(kc, tbi), C["ident_f"].b()], w=[bk.b()], sig=(j == 3))
                    k.cp(k.ACT if h == 0 else k.DVE, ot[:, h * 512:(h + 1) * 512], bk[:, :], r=[bk.b()], w=[ot.b(h)])
                k.dma(k.SP, self.out[s, (tt - 2) * 128:(tt - 1) * 128, :], ot[:], r=[ot.b(0), ot.b(1)], w=[])

    def layer(self, s, l):
        import os
        stop = os.environ.get("KSTOP", "")
        self.phase_a(s, l)
        if stop == f"a{l}":
            return True
        self.hyena(s, l)
        if stop == f"hy{l}":
            return True
        self.hgrn(s, l)
        if stop == f"hg{l}":
            return True
        self.lru(s, l)
        if stop == f"lru{l}":
            return True
        self.merge(s, l)
        if stop == f"mg{l}":
            return True
        self.mlp(s, l)
        if stop == f"mlp{l}":
            return True
        return False

    def build_full(self):
        self.prologue()
        for s in range(self.nseq):
            self.load_x(s)
            for l in range(self.depth):
                if l % 2 == 1:
                    self.permute(True)
                if self.layer(s, l):
                    self.store_out(s)
                    self.finish()
                    return
                if l % 2 == 1:
                    self.permute(False)
            self.store_out(s)
        self.finish()

    def spec_mul(self, yre, yim, A_, B_, Kre, Kim, tm, rb, TM, wb):
        k = self.k
        V, P = k.DVE, k.POOL
        k.tt(V, tm[0], A_, Kre, ALU.mult, r=rb, w=[TM[0].b()])
        k.tt(V, tm[1], B_, Kim, ALU.mult, r=rb, w=[TM[1].b()])
        k.tt(V, yre, tm[0], tm[1], ALU.add, r=[TM[0].b(), TM[1].b()], w=wb)
        k.tt(P, tm[2], B_, Kre, ALU.mult, r=rb, w=[TM[2].b()])
        k.tt(P, tm[3], A_, Kim, ALU.mult, r=rb, w=[TM[3].b()])
        k.tt(P, yim, tm[2], tm[3], ALU.subtract, r=[TM[2].b(), TM[3].b()], w=wb)

    def prologue(self):
        k = self.k
        self.precast()
        self.consts()
        self.resident()
        self.vecbank()
        with k.scope() as psc:
            self.FB = psc.tile("FB", [128, self.depth, 2], F32)
            self.SK = psc.tile("SK", [128, self.depth + 1, 512], F32)
            for name in ("negt", "delta"):
                shp = list(make_consts()[name].shape)
                t = psc.tile("c_" + name, shp, CONST_DT[name])
                k.dma(k.SP, t[:], self.cst[name], w=[t.b()])
                self.C[name] = t
            self.derive()
            for l in range(self.depth):
                self.filt(l, False)
            self.filt(0, True)
        self.X = k.tile("X", [128, 8, T], F32)
        self.U = k.tile("U", [128, 8, T], BF16)

    def finish(self):
        k = self.k
        k.barrier()
        k.stack.close()


def make_in_maps(inputs, ncores=NCORES, nseq=NSEQ, j=0):
    cst = make_consts()
    maps = []
    for i in range(ncores):
        m = {}
        for name in IN_SHAPES:
            a = np.asarray(inputs[name])
            if name in ("x", "c", "ctx"):
                a = a[NSEQ * i + j:NSEQ * i + j + nseq]
            m[name] = np.ascontiguousarray(a, dtype=np.float32)
        for name, arr in cst.items():
            m["k_" + name] = arr
        maps.append(m)
    return maps


_NET = None
NLAUNCH = 1


def kernel(**inputs):
    global _NET
    nseq = NSEQ // NLAUNCH
    if _NET is None:
        net = Net(nseq=nseq)
        net.build_full()
        _NET = net
    net = _NET
    outs = []
    for j in range(NLAUNCH):
        maps = make_in_maps(inputs, NCORES, nseq, j * nseq)
        res = run_bass_kernel_spmd(net.nc, maps, core_ids=list(range(NCORES)))
        outs.append([np.asarray(r["out"]) for r in res.results])
    full = np.zeros((NCORES * NSEQ, SEQ, D), np.float32)
    for j in range(NLAUNCH):
        for i in range(NCORES):
            full[NSEQ * i + j * nseq:NSEQ * i + (j + 1) * nseq] = outs[j][i]
    return full
```

```python
import math
from contextlib import ExitStack
import numpy as np
import ml_dtypes
import concourse.bass as bass
import concourse.mybir as mybir
from concourse.bass_utils import run_bass_kernel_spmd

F32 = mybir.dt.float32
BF16 = mybir.dt.bfloat16
AF = mybir.ActivationFunctionType
ALU = mybir.AluOpType

NCORES = 8
NSEQ = 2
D = 1024
KC = 8
SEQ = 2048
CTX = 256
T = SEQ + CTX
DEPTH = 2
GRID_W = 64
EPS = 1e-6
E_HY = 512
INW = 8192
DFF = 4096
PADL = 3
PT = T + 12
TB = [(0, 256), (256, 768), (768, 1280), (1280, 1792), (1792, 2304)]
C_HY, C_LX, C_LG, C_Q, C_FF, C_FB, C_I, C_OG, C_MG = 0, 1536, 2048, 2560, 3072, 3584, 4096, 4608, 5120


def poff(t):
    return t + PADL if t < CTX else t + 3 * PADL


class Ev:
    __slots__ = ("q", "sem", "val", "dma")

    def __init__(self, q, sem, val, dma=False):
        self.q, self.sem, self.val, self.dma = q, sem, val, dma


class Buf:
    __slots__ = ("name", "w", "r")

    def __init__(self, name=""):
        self.name, self.w, self.r = name, None, []


class Tile:
    def __init__(self, t, name):
        self.t, self.name, self.bufs = t, name, {}

    def b(self, *key):
        v = self.bufs.get(key)
        if v is None:
            v = self.bufs[key] = Buf(f"{self.name}{key}")
        return v

    def __getitem__(self, k):
        return self.t[k]


class Q:
    LIM = 3000

    def __init__(self, kern, name, eng, ndma=0):
        self.k, self.name, self.eng = kern, name, eng
        self.sem = kern.new_sem(name)
        self.cnt = 0
        self.seen = {}
        self.last = None
        self.slots = [[kern.new_sem(f"{name}d{i}"), 0] for i in range(ndma)]
        self.nxt = 0
        self.pend = []
        self.collect = None

    def wait(self, ev):
        key = id(ev.sem)
        if self.seen.get(key, 0) >= ev.val:
            return
        self.seen[key] = ev.val
        if self.collect is not None:
            self.collect = [e for e in self.collect if e.sem is not ev.sem] + [ev]
            return
        self.eng.wait_ge(ev.sem, ev.val)
        self.k.nwait += 1

    def flush(self, ins_fn):
        ws, self.collect = self.collect, None
        for e in ws[:-1]:
            self.eng.wait_ge(e.sem, e.val)
            self.k.nwait += 1
        ins = ins_fn()
        if ws:
            ins._wait_ge(ws[-1].sem, ws[-1].val)
        return ins

    def signal(self, ins):
        if self.cnt >= Q.LIM:
            self.sem = self.k.new_sem(self.name + "x")
            self.cnt = 0
        self.cnt += 1
        ins.then_inc(self.sem, 1)
        self.last = Ev(self, self.sem, self.cnt)
        return self.last

    def dma_signal(self, fn):
        slot = self.slots[self.nxt]
        self.nxt = (self.nxt + 1) % len(self.slots)
        if slot[1] > 0:
            self.wait(Ev(self, slot[0], slot[1], True))
        ins = self.flush(fn)
        slot[1] += 16
        ins.then_inc(slot[0], 16)
        return Ev(self, slot[0], slot[1], True)


class Kern:
    def __init__(self):
        self.nc = bass.Bass("TRN2", target_bir_lowering=False)
        self.stack = ExitStack()
        self.nwait = 0
        self.nins = 0
        self.sems = []
        nc = self.nc
        self.PE = Q(self, "pe", nc.tensor)
        self.ACT = Q(self, "act", nc.scalar)
        self.DVE = Q(self, "dve", nc.vector)
        self.POOL = Q(self, "pool", nc.gpsimd, ndma=2)
        self.SP = Q(self, "sp", nc.sync, ndma=16)
        self.queues = [self.PE, self.ACT, self.DVE, self.POOL, self.SP]
        self.banks = []
        for i in range(8):
            t = nc.alloc_psum_tensor(f"bank{i}", [128, 512], F32)
            self.banks.append(Tile(t, f"bank{i}"))
        self.bank_i = 0
        self.dbg = []

    def new_sem(self, name):
        s = self.stack.enter_context(self.nc.semaphore(f"s_{name}_{len(self.sems)}"))
        self.sems.append(s)
        return s

    def bank(self):
        b = self.banks[self.bank_i]
        self.bank_i = (self.bank_i + 1) % 8
        return b

    def _deps(self, q, r, w):
        for b in r:
            if b.w is not None:
                self._dep(q, b.w, 0)
        for b in w:
            if b.w is not None:
                self._dep(q, b.w, 1)
            for e in b.r:
                self._dep(q, e, 2)

    def _dep(self, q, ev, kind):
        if ev.q is q and not ev.dma:
            if q is self.PE or kind == 2:
                return
        q.wait(ev)

    def _record(self, ev, r, w):
        for b in r:
            if not ev.dma:
                b.r = [e for e in b.r if e.dma or e.q is not ev.q]
            b.r.append(ev)
        for b in w:
            b.w = ev
            b.r = []

    def op(self, q, fn, r=(), w=(), sig=True):
        q.collect = []
        self._deps(q, r, w)
        ins = q.flush(fn)
        self.nins += 1
        if sig:
            ev = q.signal(ins)
            for (pr, pw) in q.pend:
                self._record(ev, pr, pw)
            q.pend = []
            self._record(ev, r, w)
        else:
            q.pend.append((list(r), list(w)))
        return ins

    def dma(self, q, out, in_, r=(), w=(), **kw):
        assert not q.pend
        q.collect = []
        self._deps(q, r, w)
        ev = q.dma_signal(lambda: q.eng.dma_start(out=out, in_=in_, **kw))
        self.nins += 1
        self._record(ev, r, w)

    def barrier(self):
        evs = []
        for q in self.queues:
            assert not q.pend, q.name
            if q.last is not None:
                evs.append(q.last)
            for s in q.slots:
                if s[1] > 0:
                    evs.append(Ev(q, s[0], s[1], True))
        for q in self.queues:
            for e in evs:
                if e.q is q and not e.dma:
                    continue
                q.wait(e)

    def tile(self, name, shape, dtype, stack=None):
        st = stack if stack is not None else self.stack
        self.ntile = getattr(self, "ntile", 0) + 1
        name = f"{name}_{self.ntile}"
        t = st.enter_context(self.nc.sbuf_tensor(name, list(shape), dtype))
        return Tile(t, name)

    class Scope:
        def __init__(self, k):
            self.k = k
            self.st = ExitStack()
            self.cache = {}

        def ctile(self, name, shape, dtype):
            key = (name, tuple(shape), str(dtype))
            t = self.cache.get(key)
            if t is None:
                t = self.cache[key] = self.tile(name, shape, dtype)
            return t

        def __enter__(self):
            self.st.__enter__()
            return self

        def tile(self, name, shape, dtype):
            return self.k.tile(name, shape, dtype, self.st)

        def __exit__(self, *a):
            self.k.barrier()
            return self.st.__exit__(*a)

    def scope(self):
        return Kern.Scope(self)

    def mm(self, out, lhsT, rhs, start, stop, r=(), w=(), sig=False):
        nc = self.nc
        return self.op(self.PE, lambda: nc.tensor.matmul(out, lhsT, rhs, start=start, stop=stop), r, w, sig)

    def tr(self, out, in_, ident, r=(), w=(), sig=True):
        nc = self.nc
        return self.op(self.PE, lambda: nc.tensor.transpose(out, in_, ident), r, w, sig)

    def act(self, out, in_, func, r=(), w=(), bias=None, scale=None):
        nc = self.nc
        kw = {}
        if bias is not None:
            kw["bias"] = bias
        if scale is not None:
            kw["scale"] = scale
        return self.op(self.ACT, lambda: nc.scalar.activation(out=out, in_=in_, func=func, **kw), r, w)

    def tt(self, q, out, in0, in1, op, r=(), w=()):
        return self.op(q, lambda: q.eng.tensor_tensor(out=out, in0=in0, in1=in1, op=op), r, w)

    def ts(self, q, out, in0, s1, op0, s2=None, op1=None, r=(), w=()):
        if op1 is None:
            return self.op(q, lambda: q.eng.tensor_scalar(out=out, in0=in0, scalar1=s1, scalar2=None, op0=op0), r, w)
        return self.op(q, lambda: q.eng.tensor_scalar(out=out, in0=in0, scalar1=s1, scalar2=s2, op0=op0, op1=op1), r, w)

    def stt(self, out, in0, scalar, in1, op0, op1, r=(), w=()):
        nc = self.nc
        return self.op(self.DVE, lambda: nc.vector.scalar_tensor_tensor(out=out, in0=in0, scalar=scalar, in1=in1, op0=op0, op1=op1), r, w)

    def cp(self, q, out, in_, r=(), w=()):
        if q is self.ACT:
            return self.op(q, lambda: q.eng.copy(out=out, in_=in_), r, w)
        return self.op(q, lambda: q.eng.tensor_copy(out=out, in_=in_), r, w)

    def memset(self, q, ap, val, w=()):
        return self.op(q, lambda: q.eng.memset(ap, val), (), w)


_CONSTS = None


def make_consts():
    global _CONSTS
    if _CONSTS is not None:
        return _CONSTS
    bf = ml_dtypes.bfloat16
    c = {}
    c["ident_f"] = np.eye(128, dtype=np.float32)
    c["ident_b"] = np.eye(128, dtype=np.float32).astype(bf)
    c["ones_d"] = np.full((128, 128), 1.0 / D, np.float32).astype(bf)
    c["ones_v"] = np.full((128, 128), 1.0 / 128, np.float32).astype(bf)
    c["onesrow"] = np.ones((128, 512), np.float32).astype(bf)

    def dft(L):
        N = 2 * L
        s = np.arange(L, dtype=np.float64)[:, None]
        f = np.arange(L, dtype=np.float64)[None, :]
        ang = 2.0 * np.pi * np.mod((2 * f + 1) * s, 2 * N) / (2 * N)
        return np.cos(ang), np.sin(ang)

    cs, sn = dft(SEQ)
    fw = np.stack([cs, sn], 0).reshape(2, 16, 128, 16, 128)
    c["fwdL"] = np.ascontiguousarray(fw.transpose(3, 2, 1, 0, 4)).astype(bf)
    iv = np.stack([cs.T, sn.T], 0).reshape(2, 16, 128, SEQ)
    c["invL"] = np.ascontiguousarray(iv.transpose(1, 2, 0, 3)).astype(bf)
    cs, sn = dft(CTX)
    fw = np.stack([cs, sn], 0).reshape(2, 2, 128, CTX)
    c["fwdC"] = np.ascontiguousarray(fw.transpose(2, 1, 0, 3)).astype(bf)
    iv = np.stack([cs.T, sn.T], 0).reshape(2, 2, 128, CTX)
    c["invC"] = np.ascontiguousarray(iv.transpose(2, 1, 0, 3)).astype(bf)

    def zemb(L):
        pos = np.arange(L, dtype=np.float32)
        t = pos / np.float32(max(L - 1, 1))
        bands = np.linspace(1e-4, 7, 8, dtype=np.float32)
        ang = bands[None, :] * (np.float32(2.0 * math.pi) * pos / np.float32(L))[:, None]
        z = np.concatenate([t[:, None], np.cos(ang), -np.sin(ang)], -1).astype(np.float32)
        return np.ascontiguousarray(z.T), t

    c["zL"], tl = zemb(SEQ)
    c["zC"], tc = zemb(CTX)
    negt = np.zeros((128, 18), np.float32)
    negt[:, :16] = -tl.reshape(16, 128).T
    negt[:, 16:] = -tc.reshape(2, 128).T
    c["negt"] = negt
    deltas = np.abs(np.linspace(math.log(1e-2) / 1.5, math.log(1e-2) / 0.3, E_HY, dtype=np.float32))
    c["delta"] = np.ascontiguousarray(np.broadcast_to(deltas[None, :], (128, E_HY))).astype(np.float32)
    s = np.arange(128)[:, None]
    t = np.arange(128)[None, :]
    same = (s // 64) == (t // 64)
    c["maskF"] = (same & (s <= t)).astype(np.uint16)
    c["maskB"] = (same & (s >= t)).astype(np.uint16)
    _CONSTS = c
    return c


CONST_DT = {"ident_f": F32, "ident_b": BF16, "ones_d": BF16, "ones_v": BF16, "onesrow": BF16, "fwdL": BF16,
            "invL": BF16, "fwdC": BF16, "invC": BF16, "zL": F32, "zC": F32, "negt": F32, "delta": F32,
            "maskF": mybir.dt.uint16, "maskB": mybir.dt.uint16}

IN_SHAPES = {
    "x": [NSEQ, SEQ, D], "c": [NSEQ, D], "ctx": [NSEQ, CTX, D], "c_ctx": [D],
    "w_ada": [DEPTH, D, 6 * D], "b_ada": [DEPTH, 6 * D], "norm_gains": [DEPTH, 4, D], "w_in": [DEPTH, D, INW],
    "hy_short_w": [DEPTH, 3, 1536], "hy_short_b": [DEPTH, 1536], "hy_ff_w1": [DEPTH, 17, 64],
    "hy_ff_b1": [DEPTH, 64], "hy_ff_w2": [DEPTH, 64, 64], "hy_ff_b2": [DEPTH, 64], "hy_ff_w3": [DEPTH, 64, 1024],
    "hy_freq": [DEPTH, 2, 64], "hy_skip": [DEPTH, 512], "lru_conv_w": [DEPTH, 2, 4, 512],
    "lru_conv_b": [DEPTH, 2, 512], "lru_wr": [DEPTH, 2, 8, 64, 64], "lru_br": [DEPTH, 2, 512],
    "lru_wi": [DEPTH, 2, 8, 64, 64], "lru_bi": [DEPTH, 2, 512], "lru_lambda": [DEPTH, 2, 512],
    "hg_lower_bounds": [DEPTH, 512], "hg_norm_g": [DEPTH, 128], "w_proj_hy": [DEPTH, 512, D],
    "w_proj_lru": [DEPTH, 512, D], "w_proj_hg": [DEPTH, 512, D], "w_out": [DEPTH, D, D],
    "w_mlp1": [DEPTH, D, DFF], "w_mlp2": [DEPTH, DFF, D],
}


VEC_LIST = [
    ("b_ada", 6144), ("norm_gains", 4096), ("hy_short_w", 4608), ("hy_short_b", 1536), ("lru_conv_w", 4096),
    ("lru_conv_b", 1024), ("lru_br", 1024), ("lru_bi", 1024), ("lru_lambda", 1024), ("hg_lower_bounds", 512),
    ("hg_norm_g", 128), ("hy_ff_b1", 64), ("hy_ff_b2", 64), ("hy_freq0", 64), ("hy_freq1", 64),
]


class Net:
    def __init__(self, nseq=NSEQ, depth=DEPTH, stage="full", dbg=()):
        self.k = k = Kern()
        self.nc = nc = k.nc
        self.nseq, self.depth, self.stage = nseq, depth, stage
        self.dbg_names = dbg
        self.dbg_out = {}
        self.inp = {}
        for name, shp in IN_SHAPES.items():
            shp = list(shp)
            if name in ("x", "c", "ctx"):
                shp[0] = nseq
            self.inp[name] = nc.dram_tensor(name, shp, F32, kind="ExternalInput").ap()
        self.cst = {}
        for name, arr in make_consts().items():
            self.cst[name] = nc.dram_tensor("k_" + name, list(arr.shape), CONST_DT[name], kind="ExternalInput").ap()
        self.out = nc.dram_tensor("out", [nseq, SEQ, D], F32, kind="ExternalOutput").ap()

        def scratch(name, shape, dt=BF16):
            return Tile(nc.dram_tensor(name, list(shape), dt, kind="Internal").ap(), name)

        self.WB = {
            "w_in": scratch("wb_in", [DEPTH, D, INW]), "w_ada": scratch("wb_ada", [DEPTH, D, 6 * D]),
            "w_proj_hy": scratch("wb_phy", [DEPTH, 512, D]), "w_proj_lru": scratch("wb_plru", [DEPTH, 512, D]),
            "w_proj_hg": scratch("wb_phg", [DEPTH, 512, D]), "w_out": scratch("wb_out", [DEPTH, D, D]),
            "w_mlp1": scratch("wb_m1", [DEPTH, D, DFF]), "w_mlp2": scratch("wb_m2", [DEPTH, DFF, D]),
        }
        self.KS = scratch("ks", [DEPTH, 16, 128, 2, 512])
        self.KSC = scratch("ksc", [2, 128, 2, 512])
        self.YP = scratch("yp", [3, 128, 4, T])

    def dump(self, name, tile_ap, shape, dtype=F32):
        if name not in self.dbg_names:
            return
        k, nc = self.k, self.nc
        k.barrier()
        d = nc.dram_tensor("dbg_" + name, list(shape), dtype, kind="ExternalOutput").ap()
        k.dma(k.SP, d, tile_ap)
        k.barrier()
        self.dbg_out[name] = "dbg_" + name

    def precast(self):
        k = self.k
        order = []
        for l in range(self.depth):
            order.append(("w_ada", l))
        for l in range(self.depth):
            for n in ("w_in", "w_proj_hy", "w_proj_hg", "w_proj_lru", "w_out", "w_mlp1", "w_mlp2"):
                order.append((n, l))
        for (n, l) in order:
            src = self.inp[n][l]
            dst = self.WB[n]
            k.dma(k.POOL, dst[l], src, w=[dst.b(l)], max_dma_last_dim=4096)

    def consts(self):
        k, nc = self.k, self.nc
        C = {}
        for name in ("ident_f", "ident_b", "ones_d", "ones_v", "onesrow", "fwdC", "invC", "maskF", "maskB"):
            shp = list(make_consts()[name].shape)
            t = k.tile("c_" + name, shp, CONST_DT[name])
            k.dma(k.SP, t[:], self.cst[name], w=[t.b()])
            C[name] = t
        self.C = C

    def vecbank(self):
        k, nc = self.k, self.nc
        rows = []
        for l in range(self.depth):
            for name, ln in VEC_LIST:
                if name == "hy_freq0":
                    src = self.inp["hy_freq"][l, 0]
                elif name == "hy_freq1":
                    src = self.inp["hy_freq"][l, 1]
                else:
                    src = self.inp[name][l]
                    if len(src.shape) > 1:
                        src = src.flatten() if hasattr(src, "flatten") else src
                rows.append(((name, l), src, ln))
        for i in range(self.nseq):
            rows.append((("c", i), self.inp["c"][i], D))
        rows.append((("c_ctx", 0), self.inp["c_ctx"], D))
        place = {}
        tile_i, r = 0, 0
        for key, src, ln in rows:
            nr = (ln + 127) // 128
            if r + nr > 128:
                tile_i, r = tile_i + 1, 0
            place[key] = (tile_i, r, nr, ln)
            r += nr
        ntile = tile_i + 1
        self.VB = VB = k.tile("VB", [128, ntile, 128], F32)
        with k.scope() as sc:
            RT = sc.tile("rowtile", [128, ntile, 128], F32)
            k.memset(k.DVE, RT[:], 0.0, w=[RT.b()])
            for key, src, ln in rows:
                ti, r0, nr, _ = place[key]
                if ln >= 128:
                    k.dma(k.SP, RT[r0:r0 + nr, ti, :], src.rearrange("(r c) -> r c", c=128), w=[RT.b()])
                else:
                    k.dma(k.SP, RT[r0:r0 + 1, ti, 0:ln], src.rearrange("(r c) -> r c", r=1), w=[RT.b()])
            for ti in range(ntile):
                bk = k.bank()
                k.tr(bk[:, 0:128], RT[:, ti, :], self.C["ident_f"][:], r=[RT.b(), self.C["ident_f"].b()], w=[bk.b()])
                k.cp(k.DVE, VB[:, ti, :], bk[:, 0:128], r=[bk.b()], w=[VB.b()])
        self.place = place

    def vec(self, name, l, i0=0, n=1):
        ti, r0, nr, ln = self.place[(name, l)]
        return self.VB[:, ti, r0 + i0:r0 + i0 + n]

    def derive(self):
        k, nc, C = self.k, self.nc, self.C
        V, A, P = k.DVE, k.ACT, k.POOL
        VBb = self.VB.b()
        with k.scope() as sc:
            scT = sc.tile("scT", [128, 8, 3], BF16)
            for i in range(3):
                src = self.vec("c", min(i, self.nseq - 1), 0, 8) if i < 2 else self.vec("c_ctx", 0, 0, 8)
                k.act(scT[:, :, i], src, AF.Silu, r=[VBb], w=[scT.b()])
            WA = [sc.tile(f"wada{i}", [128, 8, 512], BF16) for i in range(2)]
            for l in range(self.depth):
                ADA = sc.tile(f"ADA{l}", [128, 48, 3], F32)
                bk = k.bank()
                wsrc = self.WB["w_ada"]
                wv = wsrc.t[l].rearrange("(kc p) n -> p kc n", p=128)
                for g in range(12):
                    wt = WA[g % 2]
                    k.dma(k.SP, wt[:], wv[:, :, g * 512:(g + 1) * 512], r=[wsrc.b(l)], w=[wt.b()])
                    for jj in range(4):
                        j = g * 4 + jj
                        for kc in range(8):
                            k.mm(bk[:, j * 3:j * 3 + 3], wt[:, kc, jj * 128:(jj + 1) * 128], scT[:, kc, :],
                                 kc == 0, kc == 7, r=[wt.b(), scT.b()], w=[bk.b()], sig=(kc == 7))
                bada = self.vec("b_ada", l, 0, 48)
                k.tt(V, ADA[:], bk[:, 0:144].rearrange("p (j c) -> p j c", c=3),
                     bada.unsqueeze(2).to_broadcast([128, 48, 3]), ALU.add, r=[bk.b(), VBb], w=[ADA.b()])
                self.dump(f"ada{l}", ADA[:], [128, 48, 3])
                M = self.MOD[l]
                g = [self.vec("norm_gains", l, 8 * i, 8).unsqueeze(2).to_broadcast([128, 8, 3]) for i in range(4)]
                tmp = sc.tile(f"modtmp{l}", [128, 8, 3], F32)
                k.cp(V, M[:, 0], ADA[:, 0:8, :], r=[ADA.b()], w=[M.b()])
                k.ts(V, tmp[:], ADA[:, 8:16, :], 1.0, ALU.add, r=[ADA.b()], w=[tmp.b()])
                k.tt(V, M[:, 1], tmp[:], g[0], ALU.mult, r=[tmp.b(), VBb], w=[M.b()])
                k.tt(V, M[:, 2], ADA[:, 16:24, :], g[1], ALU.mult, r=[ADA.b(), VBb], w=[M.b()])
                k.cp(V, M[:, 3], ADA[:, 24:32, :], r=[ADA.b()], w=[M.b()])
                tmp2 = sc.tile(f"modtmp2{l}", [128, 8, 3], F32)
                k.ts(V, tmp2[:], ADA[:, 32:40, :], 1.0, ALU.add, r=[ADA.b()], w=[tmp2.b()])
                k.tt(V, M[:, 4], tmp2[:], g[2], ALU.mult, r=[tmp2.b(), VBb], w=[M.b()])
                k.tt(V, M[:, 5], ADA[:, 40:48, :], g[3], ALU.mult, r=[ADA.b(), VBb], w=[M.b()])
            for l in range(self.depth):
                e = sc.tile(f"c1e{l}", [128, 8], F32)
                k.act(e[:], self.vec("lru_lambda", l, 0, 8), AF.Exp, r=[VBb], w=[e.b()], scale=-1.0)
                k.ts(V, e[:], e[:], 1.0, ALU.add, r=[e.b()], w=[e.b()])
                k.act(e[:], e[:], AF.Ln, r=[e.b()], w=[e.b()])
                k.ts(V, self.C1[:, l, 0, :], e[:], -8.0, ALU.mult, r=[e.b()], w=[self.C1.b()])
                k.ts(V, self.C1[:, l, 1, :], e[:], -16.0, ALU.mult, r=[e.b()], w=[self.C1.b()])
            k.memset(V, self.LB[:, 0, 0, :], 0.0, w=[self.LB.b()])
            k.memset(V, self.LB[:, 0, 1, :], 1.0, w=[self.LB.b()])
            k.memset(V, self.LB[:, 0, 2, :], -1.0, w=[self.LB.b()])
            if self.depth > 1:
                dlt = sc.tile("lbd", [128, 4], F32)
                k.tt(V, dlt[:], self.vec("hg_lower_bounds", 1, 0, 4), self.vec("hg_lower_bounds", 0, 0, 4),
                     ALU.subtract, r=[VBb], w=[dlt.b()])
                k.act(self.LB[:, 1, 0, :], dlt[:], AF.Sigmoid, r=[dlt.b()], w=[self.LB.b()])
                k.act(self.LB[:, 1, 1, :], dlt[:], AF.Sigmoid, r=[dlt.b()], w=[self.LB.b()], scale=-1.0)
                k.ts(V, self.LB[:, 1, 2, :], self.LB[:, 1, 1, :], -1.0, ALU.mult, r=[self.LB.b()], w=[self.LB.b()])
            for l in range(self.depth):
                k.tt(V, self.FB[:, l, 0:1], self.vec("hy_freq0", l), self.vec("hy_ff_b1", l), ALU.mult, r=[VBb], w=[self.FB.b()])
                k.tt(V, self.FB[:, l, 1:2], self.vec("hy_freq1", l), self.vec("hy_ff_b2", l), ALU.mult, r=[VBb], w=[self.FB.b()])
            skr = sc.tile("skraw", [128, self.depth, 512], F32)
            for l in range(self.depth):
                k.dma(k.SP, skr[:, l, :], self.inp["hy_skip"][l].partition_broadcast(128), w=[skr.b()])
                k.ts(V, self.SK[:, l, :], skr[:, l, :], 2.0 / (2 * SEQ), ALU.mult, r=[skr.b()], w=[self.SK.b()])
            k.ts(V, self.SK[:, self.depth, :], skr[:, 0, :], 2.0 / (2 * CTX), ALU.mult, r=[skr.b()], w=[self.SK.b()])

    def wbd(self, l, d, ri, cc):
        return self.WBD[:, (d * 2 + ri) * 4 + cc, :]

    def build_wbd(self, sc, l):
        k = self.k
        self.WBD = sc.tile("WBD", [128, 16, 128], BF16)
        with k.scope() as sc2:
            stg = sc2.tile("wbdstage", [128, 16, 128], F32)
            k.memset(k.POOL, stg[:], 0.0, w=[stg.b()])
            for d in range(2):
                for ri, nm in enumerate(("lru_wr", "lru_wi")):
                    for cc in range(4):
                        idx = (d * 2 + ri) * 4 + cc
                        for h in range(2):
                            k.dma(k.SP, stg[h * 64:(h + 1) * 64, idx, h * 64:(h + 1) * 64],
                                  self.inp[nm][l, d, 2 * cc + h], w=[stg.b()])
            k.cp(k.DVE, self.WBD[:], stg[:], r=[stg.b()], w=[self.WBD.b()])

    def filt(self, l, ctx):
        k, nc, C = self.k, self.nc, self.C
        V, A, P = k.DVE, k.ACT, k.POOL
        VBb = self.VB.b()
        L = CTX if ctx else SEQ
        nt = L // 128
        blocks = [(0, 256)] if ctx else [(i * 512, (i + 1) * 512) for i in range(4)]
        sk = self.SK[:, self.depth if ctx else l, :]
        scale = 2.0 / (2 * L)
        PI = math.pi
        with k.scope() as sc:
            zT = sc.tile("zT", [17, L], F32)
            k.dma(k.SP, zT[:], self.cst["zC" if ctx else "zL"], w=[zT.b()])
            w1 = sc.tile("fw1", [17, 64], F32)
            w2 = sc.tile("fw2", [64, 64], F32)
            w3 = sc.tile("fw3", [64, 1024], F32)
            k.dma(k.SP, w1[:], self.inp["hy_ff_w1"][l], w=[w1.b()])
            k.dma(k.SP, w2[:], self.inp["hy_ff_w2"][l], w=[w2.b()])
            k.dma(k.SP, w3[:], self.inp["hy_ff_w3"][l], w=[w3.b()])
            h1 = sc.tile("fh1", [64, L], F32)
            h2 = sc.tile("fh2", [64, L], F32)
            arg = sc.tile("farg", [64, 512], F32)
            m1 = sc.tile("fm1", [64, 512], F32)
            for stage in range(2):
                wt, src, dst = (w1, zT, h1) if stage == 0 else (w2, h1, h2)
                kk = 17 if stage == 0 else 64
                fr = self.vec("hy_freq0" if stage == 0 else "hy_freq1", l)[0:64, :]
                fb = self.FB[0:64, l, stage:stage + 1]
                for (b0, b1) in blocks:
                    n = b1 - b0
                    bk = k.bank()
                    k.mm(bk[0:64, 0:n], wt[0:kk, :], src[0:kk, b0:b1], True, True, r=[wt.b(), src.b()], w=[bk.b()], sig=True)
                    k.ts(V, arg[:, 0:n], bk[0:64, 0:n], fr, ALU.mult, fb, ALU.add, r=[bk.b(), VBb, self.FB.b()], w=[arg.b()])
                    k.ts(V, m1[:, 0:n], arg[:, 0:n], -PI, ALU.is_lt, 2 * PI, ALU.mult, r=[arg.b()], w=[m1.b()])
                    k.tt(V, m1[:, 0:n], m1[:, 0:n], arg[:, 0:n], ALU.add, r=[m1.b(), arg.b()], w=[m1.b()])
                    k.ts(V, arg[:, 0:n], arg[:, 0:n], PI, ALU.is_gt, -2 * PI, ALU.mult, r=[arg.b()], w=[arg.b()])
                    k.tt(V, arg[:, 0:n], arg[:, 0:n], m1[:, 0:n], ALU.add, r=[m1.b(), arg.b()], w=[arg.b()])
                    k.act(dst[:, b0:b1], arg[:, 0:n], AF.Sin, r=[arg.b()], w=[dst.b()])
            self.dump(f"fh2_{l}_{int(ctx)}", h2[:], [64, L])
            hs = sc.tile("fhs", [128, nt, 512], BF16)
            hd = sc.tile("fhd", [128, nt, 512], BF16)
            dec = sc.tile("fdec", [128, 512], F32)
            hf = sc.tile("fhf", [128, 512], F32)
            hb = sc.tile("fhb", [128, 512], F32)
            for jt in range(nt):
                b0, b1 = k.bank(), k.bank()
                k.mm(b0[:, :], h2[:, jt * 128:(jt + 1) * 128], w3[:, 0:512], True, True, r=[h2.b(), w3.b()], w=[b0.b()], sig=True)
                k.mm(b1[:, :], h2[:, jt * 128:(jt + 1) * 128], w3[:, 512:1024], True, True, r=[h2.b(), w3.b()], w=[b1.b()], sig=True)
                col = (16 if ctx else 0) + jt
                k.act(dec[:], C["delta"][:], AF.Exp, r=[C["delta"].b(), C["negt"].b()], w=[dec.b()], scale=C["negt"][:, col:col + 1])
                k.tt(V, hf[:], b0[:, :], dec[:], ALU.mult, r=[b0.b(), dec.b()], w=[hf.b()])
                k.tt(V, hb[:], b1[:, :], dec[:], ALU.mult, r=[b1.b(), dec.b()], w=[hb.b()])
                if jt == 0:
                    k.memset(V, hb[0:1, :], 0.0, w=[hb.b()])
                k.tt(V, hs[:, jt, :], hf[:], hb[:], ALU.add, r=[hf.b(), hb.b()], w=[hs.b(jt)])
                k.tt(V, hd[:, jt, :], hb[:], hf[:], ALU.subtract, r=[hf.b(), hb.b()], w=[hd.b(jt)])
            FW = None if ctx else [sc.tile(f"ffw{i}", [128, 16, 2, 128], BF16) for i in range(2)]
            KT = [sc.tile(f"fkt{i}", [128, 2, 512], BF16) for i in range(2)]
            dst = self.KSC if ctx else self.KS
            for ft in range(nt):
                if ctx:
                    fw = lambda jt, cs: C["fwdC"][:, jt, cs, ft * 128:(ft + 1) * 128]
                    fwb = C["fwdC"].b()
                else:
                    fwt = FW[ft % 2]
                    k.dma(k.SP, fwt[:], self.cst["fwdL"][ft], w=[fwt.b()])
                    fw = lambda jt, cs: fwt[:, jt, cs, :]
                    fwb = fwt.b()
                bA, bB = k.bank(), k.bank()
                for jt in range(nt):
                    k.mm(bA[:, :], fw(jt, 0), hs[:, jt, :], jt == 0, jt == nt - 1, r=[fwb, hs.b(jt)], w=[bA.b()], sig=(jt == nt - 1))
                for jt in range(nt):
                    k.mm(bB[:, :], fw(jt, 1), hd[:, jt, :], jt == 0, jt == nt - 1, r=[fwb, hd.b(jt)], w=[bB.b()], sig=(jt == nt - 1))
                kt = KT[ft % 2]
                k.stt(kt[:, 0, :], bA[:, :], scale, sk, ALU.mult, ALU.add, r=[bA.b(), self.SK.b()], w=[kt.b()])
                k.act(kt[:, 1, :], bB[:, :], AF.Copy, r=[bB.b()], w=[kt.b()], scale=scale)
                if ctx:
                    k.dma(k.SP, dst.t[ft], kt[:], r=[kt.b()], w=[dst.b(ft)])
                else:
                    k.dma(k.SP, dst.t[l, ft], kt[:], r=[kt.b()], w=[dst.b(l, ft)])

    def resident(self):
        k = self.k
        self.EPSC = k.tile("epsc", [128, 1], F32)
        k.memset(k.DVE, self.EPSC[:], EPS, w=[self.EPSC.b()])
        self.MOD = [k.tile(f"MOD{l}", [128, 6, 8, 3], F32) for l in range(self.depth)]
        self.C1 = k.tile("C1", [128, self.depth, 2, 8], F32)
        self.LB = k.tile("LB", [128, self.depth, 3, 4], F32)

    def load_x(self, s):
        k, C = self.k, self.C
        X = self.X
        with k.scope() as sc:
            XT = [sc.tile(f"xt{i}", [128, D], F32) for i in range(3)]
            for tt in range(T // 128):
                xt = XT[tt % 3]
                src = self.inp["ctx"][s, tt * 128:(tt + 1) * 128, :] if tt < 2 else self.inp["x"][s, (tt - 2) * 128:(tt - 1) * 128, :]
                k.dma(k.SP, xt[:], src, w=[xt.b()])
                tbi = self.tbi(tt * 128)
                for h in range(2):
                    bk = k.bank()
                    for j in range(4):
                        kc = h * 4 + j
                        k.tr(bk[:, j * 128:(j + 1) * 128], xt[:, kc * 128:(kc + 1) * 128], C["ident_f"][:],
                             r=[xt.b(), C["ident_f"].b()], w=[bk.b()], sig=(j == 3))
                    q = k.ACT if h == 0 else k.DVE
                    k.cp(q, X[:, h * 4:(h + 1) * 4, tt * 128:(tt + 1) * 128], bk[:, :].rearrange("p (a b) -> p a b", b=128),
                         r=[bk.b()], w=[X.b(kc2, tbi) for kc2 in range(h * 4, h * 4 + 4)])

    @staticmethod
    def tbi(t):
        for i, (a, b) in enumerate(TB):
            if a <= t < b:
                return i
        raise ValueError(t)

    def rstd_of(self, sc, src3, rbufs, n, ones, nk=8, tag="", sq=None, sqb=None):
        k = self.k
        if sq is None:
            sq = sc.ctile("sq" + tag, [128, nk, 512], BF16)
        if sqb is None:
            sqb = [sq.b()]
        k.act(sq[:, :, 0:n], src3, AF.Square, r=rbufs, w=sqb)
        bk = k.bank()
        for kc in range(nk):
            k.mm(bk[:, 0:n], ones[:], sq[:, kc, 0:n], kc == 0, kc == nk - 1, r=sqb + [ones.b()], w=[bk.b()], sig=(kc == nk - 1))
        rs = sc.ctile("rstd" + tag, [128, 512], F32)
        k.act(rs[:, 0:n], bk[:, 0:n], AF.Ln, r=[bk.b(), self.EPSC.b()], w=[rs.b()], bias=self.EPSC[:, 0:1])
        k.act(rs[:, 0:n], rs[:, 0:n], AF.Exp, r=[rs.b()], w=[rs.b()], scale=-0.5)
        return rs

    def phase_a(self, s, l):
        k, C = self.k, self.C
        X, U, M = self.X, self.U, self.MOD[l]
        with k.scope() as sc:
            for tbi, (t0, t1) in enumerate(TB):
                n = t1 - t0
                col = 2 if tbi == 0 else s
                if True:
                    sc2 = sc
                    rs = self.rstd_of(sc2, X[:, :, t0:t1], [X.b(kc, tbi) for kc in range(8)], n, C["ones_d"])
                    tmp = [sc2.ctile(f"natmp{i}", [128, 512], F32) for i in range(2)]
                    for kc in range(8):
                        tp = tmp[kc % 2]
                        k.stt(tp[:, 0:n], X[:, kc, t0:t1], M[:, 1, kc, col:col + 1], rs[:, 0:n], ALU.mult, ALU.mult,
                              r=[X.b(kc, tbi), M.b(), rs.b()], w=[tp.b()])
                        k.act(U[:, kc, t0:t1], tp[:, 0:n], AF.Identity, r=[tp.b(), M.b()], w=[U.b(kc, tbi)],
                              bias=M[:, 0, kc, col:col + 1])

    def win_tile(self, wt, l, c0, ncol=128):
        k = self.k
        src = self.WB["w_in"]
        k.dma(k.SP, wt[:, :, 0:ncol], src.t[l].rearrange("(kc p) n -> p kc n", p=128)[:, :, c0:c0 + ncol],
              r=[src.b(l)], w=[wt.b()])

    def proj_fm(self, wt, tbi, bk, ncol0=0):
        k, U = self.k, self.U
        t0, t1 = TB[tbi]
        n = t1 - t0
        for kc in range(8):
            k.mm(bk[:, 0:n], wt[:, kc, ncol0:ncol0 + 128], U[:, kc, t0:t1], kc == 0, kc == 7,
                 r=[wt.b(), U.b(kc, tbi)], w=[bk.b()], sig=(kc == 7))
        return n

    def hyena(self, s, l):
        k, C, nc = self.k, self.C, self.nc
        V, A, P = k.DVE, k.ACT, k.POOL
        VBb = self.VB.b()
        need_ctx = l < self.depth - 1
        with k.scope() as sc:
            Utok = sc.tile("Utok", [128, 18, 512], BF16)
            with k.scope() as sc2:
                P1 = [sc2.tile(f"hyP{i}", [128, PT], F32) for i in range(1)]
                acc = sc2.tile("hyacc", [128, PT], F32)
                Zv = sc2.tile("hyZv", [128, PT], F32)
                ubf = [sc2.tile(f"hyu{i}", [128, PT], BF16) for i in range(1)]
                X0S = [sc2.tile(f"hyx0s{i}", [128, T], BF16) for i in range(2)]
                WT = [sc2.tile(f"hyw{i}", [128, 8, 128], BF16) for i in range(3)]
                for p in P1:
                    k.memset(P, p[:], 0.0, w=[p.b()])
                segs = [(0, CTX), (CTX, T)]
                it = 0
                for cc in range(4):
                    ub = ubf[0]
                    x0s = X0S[cc % 2]
                    for comp in (0, 2, 1):
                        wt = WT[it % 3]
                        p1 = P1[0]
                        it += 1
                        self.win_tile(wt, l, C_HY + comp * 512 + cc * 128)
                        for tbi, (t0, t1) in enumerate(TB):
                            bk = k.bank()
                            n = self.proj_fm(wt, tbi, bk)
                            k.cp(A, p1[:, poff(t0):poff(t0) + n], bk[:, 0:n], r=[bk.b()], w=[p1.b()])
                        ch = comp * 4 + cc
                        w0, w1, w2 = (self.vec("hy_short_w", l, kk * 12 + ch) for kk in range(3))
                        bb = self.vec("hy_short_b", l, ch)
                        for (a0, a1) in segs:
                            q0, q1 = poff(a0), poff(a0) + (a1 - a0)
                            k.act(acc[:, q0:q1], p1[:, q0:q1], AF.Identity, r=[p1.b(), VBb], w=[acc.b()], bias=bb, scale=w1)
                            k.stt(acc[:, q0:q1], p1[:, q0 - 1:q1 - 1], w0, acc[:, q0:q1], ALU.mult, ALU.add,
                                  r=[p1.b(), acc.b(), VBb], w=[acc.b()])
                            if comp == 0:
                                k.stt(Zv[:, q0:q1], p1[:, q0 + 1:q1 + 1], w2, acc[:, q0:q1], ALU.mult, ALU.add,
                                      r=[p1.b(), acc.b(), VBb], w=[Zv.b()])
                            elif comp == 2:
                                k.stt(acc[:, q0:q1], p1[:, q0 + 1:q1 + 1], w2, acc[:, q0:q1], ALU.mult, ALU.add,
                                      r=[p1.b(), acc.b(), VBb], w=[acc.b()])
                                k.tt(V, ub[:, q0:q1], acc[:, q0:q1], Zv[:, q0:q1], ALU.mult, r=[acc.b(), Zv.b()], w=[ub.b()])
                            else:
                                k.stt(x0s[:, a0:a1], p1[:, q0 + 1:q1 + 1], w2, acc[:, q0:q1], ALU.mult, ALU.add,
                                      r=[p1.b(), acc.b(), VBb], w=[x0s.b()])
                    k.dma(k.SP, self.YP.t[0, :, cc, :], x0s[:], r=[x0s.b()], w=[self.YP.b(0, cc, i) for i in range(5)])
                    for g0 in range(0, 18, 8):
                        g1 = min(18, g0 + 8)
                        bk = k.bank()
                        bv = bk[:, :].bitcast(BF16)
                        for tt in range(g0, g1):
                            k.tr(bv[:, (tt - g0) * 128:(tt - g0 + 1) * 128], ub[:, poff(tt * 128):poff(tt * 128) + 128], C["ident_b"][:],
                                 r=[ub.b(), C["ident_b"].b()], w=[bk.b()], sig=(tt == g1 - 1))
                        k.cp(A, Utok[:, g0:g1, cc * 128:(cc + 1) * 128],
                             bv[:, 0:(g1 - g0) * 128].rearrange("p (a b) -> p a b", b=128), r=[bk.b()], w=[Utok.b(cc)])
            self.dump("utok", Utok[:], [128, 18, 512], BF16)
            ut_bufs = [Utok.b(cc) for cc in range(4)]
            YP = self.YP
            if need_ctx:
                with k.scope() as sc4:
                    KT = [sc4.tile(f"hykt{i}", [128, 2, 512], BF16) for i in range(2)]
                    AB = [sc4.tile(f"hyab{i}", [128, 2, 512], BF16) for i in range(2)]
                    TM = [sc4.tile(f"hytm{i}", [128, 512], F32) for i in range(4)]
                    Yc = sc4.tile("hyYc", [128, 2, 2, 512], BF16)
                    x0c = sc4.tile("hyx0c", [128, 4, CTX], BF16)
                    k.dma(k.SP, x0c[:], YP.t[0, :, :, 0:CTX], r=[YP.b(0, cc, 0) for cc in range(4)], w=[x0c.b()])
                    for ft in range(2):
                        kt, ab = KT[ft % 2], AB[ft % 2]
                        k.dma(k.SP, kt[:], self.KSC.t[ft], r=[self.KSC.b(ft)], w=[kt.b()])
                        bA, bB = k.bank(), k.bank()
                        for cs, bk in ((0, bA), (1, bB)):
                            for st in range(2):
                                k.mm(bk[:, :], C["fwdC"][:, st, cs, ft * 128:(ft + 1) * 128], Utok[:, st, :], st == 0, st == 1,
                                     r=[C["fwdC"].b()] + ut_bufs, w=[bk.b()], sig=(st == 1))
                        k.cp(A, ab[:, 0, :], bA[:, :], r=[bA.b()], w=[ab.b(0)])
                        k.cp(A, ab[:, 1, :], bB[:, :], r=[bB.b()], w=[ab.b(1)])
                        self.spec_mul(Yc[:, ft, 0, :], Yc[:, ft, 1, :], ab[:, 0, :], ab[:, 1, :], kt[:, 0, :], kt[:, 1, :],
                                      [t[:] for t in TM], [ab.b(0), ab.b(1), kt.b()], TM, [Yc.b(ft)])
                    for cc in range(4):
                        bk = k.bank()
                        i = 0
                        for ft in range(2):
                            for cs in range(2):
                                k.mm(bk[:, 0:CTX], Yc[:, ft, cs, cc * 128:(cc + 1) * 128], C["invC"][:, ft, cs, :], i == 0, i == 3,
                                     r=[Yc.b(ft), C["invC"].b()], w=[bk.b()], sig=(i == 3))
                                i += 1
                        k.tt(V, x0c[:, cc, :], bk[:, 0:CTX], x0c[:, cc, :], ALU.mult, r=[bk.b(), x0c.b()], w=[x0c.b()])
                    k.dma(k.SP, YP.t[0, :, :, 0:CTX], x0c[:], r=[x0c.b()], w=[YP.b(0, cc, 0) for cc in range(4)])
            with k.scope() as sc3:
                DB = [sc3.tile(f"hydb{i}", [128, 4096], BF16) for i in range(2)]
                KT = [sc3.tile(f"hykt{i}", [128, 2, 512], BF16) for i in range(2)]
                AB = [sc3.tile(f"hyab{i}", [128, 2, 256], BF16) for i in range(2)]
                TM = [sc3.tile(f"hytm{i}", [128, 256], F32) for i in range(4)]
                Yf = sc3.tile("hyYf", [128, 16, 2, 256], BF16)
                XB = [sc3.tile(f"hyxb{i}", [128, 512], BF16) for i in range(4)]
                dbi = 0
                xbi = 0
                for half in range(2):
                    h0 = half * 256
                    for ft in range(16):
                        fwt = DB[dbi % 2]
                        dbi += 1
                        fw = fwt[:, :].rearrange("p (a b c) -> p a b c", a=16, b=2)
                        kt, ab = KT[ft % 2], AB[ft % 2]
                        k.dma(k.SP, fwt[:, :], self.cst["fwdL"][ft].rearrange("p a b c -> p (a b c)"), w=[fwt.b()])
                        k.dma(k.SP, kt[:], self.KS.t[l, ft], r=[self.KS.b(l, ft)], w=[kt.b()])
                        bk = k.bank()
                        for cs in range(2):
                            for st in range(16):
                                k.mm(bk[:, cs * 256:(cs + 1) * 256], fw[:, st, cs, :], Utok[:, 2 + st, h0:h0 + 256], st == 0, st == 15,
                                     r=[fwt.b(), Utok.b(half * 2), Utok.b(half * 2 + 1)], w=[bk.b()], sig=(st == 15))
                        k.cp(A, ab[:, :, :], bk[:, :].rearrange("p (a b) -> p a b", a=2), r=[bk.b()], w=[ab.b()])
                        self.spec_mul(Yf[:, ft, 0, :], Yf[:, ft, 1, :], ab[:, 0, :], ab[:, 1, :], kt[:, 0, h0:h0 + 256], kt[:, 1, h0:h0 + 256],
                                      [t[:] for t in TM], [ab.b(), kt.b()], TM, [Yf.b(ft)])
                    bks = [[k.bank() for tb in range(4)] for c2 in range(2)]
                    for ft in range(16):
                        ivt = DB[dbi % 2]
                        dbi += 1
                        iv = ivt[:, :].rearrange("p (a b) -> p a b", a=2)
                        k.dma(k.SP, ivt[:, :], self.cst["invL"][ft].rearrange("p a b -> p (a b)"), w=[ivt.b()])
                        for c2 in range(2):
                            for tb in range(4):
                                bk = bks[c2][tb]
                                for cs in range(2):
                                    k.mm(bk[:, :], Yf[:, ft, cs, c2 * 128:(c2 + 1) * 128], iv[:, cs, tb * 512:(tb + 1) * 512],
                                         ft == 0 and cs == 0, ft == 15 and cs == 1, r=[Yf.b(ft), ivt.b()], w=[bk.b()],
                                         sig=(cs == 1 and tb == 3 and c2 == 1))
                    for c2 in range(2):
                        cc = half * 2 + c2
                        for tb in range(4):
                            t0 = CTX + tb * 512
                            xb = XB[xbi % 4]
                            xbi += 1
                            k.dma(k.SP, xb[:], YP.t[0, :, cc, t0:t0 + 512], r=[YP.b(0, cc, tb + 1)], w=[xb.b()])
                            k.tt(V, xb[:], bks[c2][tb][:, :], xb[:], ALU.mult, r=[bks[c2][tb].b(), xb.b()], w=[xb.b()])
                            k.dma(k.SP, YP.t[0, :, cc, t0:t0 + 512], xb[:], r=[xb.b()], w=[YP.b(0, cc, tb + 1)])

    def lru(self, s, l):
        k, C = self.k, self.C
        V, A, P = k.DVE, k.ACT, k.POOL
        VBb = self.VB.b()
        YP = self.YP
        with k.scope() as sc:
            self.build_wbd(sc, l)
            xp = sc.tile("lrxp", [128, PT], F32)
            k.memset(P, xp[:], 0.0, w=[xp.b()])
            hsum = sc.tile("lrhs", [128, T], F32)
            xc = sc.tile("lrxc", [128, T], F32)
            xcb = sc.tile("lrxcb", [128, T], BF16)
            R = sc.tile("lrR", [128, T], F32)
            I = sc.tile("lrI", [128, T], F32)
            Aa = sc.tile("lrA", [128, T], F32)
            YS = [sc.tile(f"lrys{i}", [128, T], BF16) for i in range(2)]
            GG = [sc.tile(f"lrgg{i}", [128, 512], BF16) for i in range(2)]
            WX = [sc.tile(f"lrwx{i}", [128, 8, 128], BF16) for i in range(2)]
            WG = [sc.tile(f"lrwg{i}", [128, 8, 128], BF16) for i in range(2)]
            segs = [(0, CTX), (CTX, T)]
            for cc in range(4):
                wx, wg, ys = WX[cc % 2], WG[cc % 2], YS[cc % 2]
                self.win_tile(wx, l, C_LX + cc * 128)
                self.win_tile(wg, l, C_LG + cc * 128)
                for tbi, (t0, t1) in enumerate(TB):
                    bk = k.bank()
                    n = self.proj_fm(wx, tbi, bk)
                    k.cp(A, xp[:, poff(t0):poff(t0) + n], bk[:, 0:n], r=[bk.b()], w=[xp.b()])
                for d in range(2):
                    sg = -1 if d == 0 else 1
                    ch = d * 4 + cc
                    wk = [self.vec("lru_conv_w", l, (d * 4 + kk) * 4 + cc) for kk in range(4)]
                    bb = self.vec("lru_conv_b", l, ch)
                    for (a0, a1) in segs:
                        q0, q1 = poff(a0), poff(a0) + (a1 - a0)
                        k.act(xc[:, a0:a1], xp[:, q0:q1], AF.Identity, r=[xp.b(), VBb], w=[xc.b()], bias=bb, scale=wk[3])
                        for j in (1, 2, 3):
                            k.stt(xc[:, a0:a1], xp[:, q0 + sg * j:q1 + sg * j], wk[3 - j], xc[:, a0:a1], ALU.mult, ALU.add,
                                  r=[xp.b(), xc.b(), VBb], w=[xc.b()])
                    k.cp(P, xcb[:], xc[:], r=[xc.b()], w=[xcb.b()])
                    for tbi, (t0, t1) in enumerate(TB):
                        n = t1 - t0
                        b0, b1 = k.bank(), k.bank()
                        k.mm(b0[:, 0:n], self.wbd(l, d, 0, cc), xcb[:, t0:t1], True, True, r=[self.WBD.b(), xcb.b()], w=[b0.b()], sig=True)
                        k.mm(b1[:, 0:n], self.wbd(l, d, 1, cc), xcb[:, t0:t1], True, True, r=[self.WBD.b(), xcb.b()], w=[b1.b()], sig=True)
                        k.act(R[:, t0:t1], b0[:, 0:n], AF.Sigmoid, r=[b0.b(), VBb], w=[R.b()], bias=self.vec("lru_br", l, ch))
                        k.act(I[:, t0:t1], b1[:, 0:n], AF.Sigmoid, r=[b1.b(), VBb], w=[I.b()], bias=self.vec("lru_bi", l, ch))
                    k.act(Aa[:], R[:], AF.Exp, r=[R.b(), self.C1.b()], w=[Aa.b()], scale=self.C1[:, l, 0, ch:ch + 1])
                    k.act(R[:], R[:], AF.Exp, r=[R.b(), self.C1.b()], w=[R.b()], scale=self.C1[:, l, 1, ch:ch + 1])
                    k.act(R[:], R[:], AF.Sqrt, r=[R.b()], w=[R.b()], bias=1.0, scale=-1.0)
                    k.tt(P, I[:], I[:], xc[:], ALU.mult, r=[I.b(), xc.b()], w=[I.b()])
                    k.tt(V, I[:], I[:], R[:], ALU.mult, r=[I.b(), R.b()], w=[I.b()])
                    if d == 0:
                        k.op(V, lambda: V.eng.tensor_tensor_scan(out=hsum[:], data0=Aa[:], data1=I[:], initial=0.0, op0=ALU.mult, op1=ALU.add),
                             r=[Aa.b(), I.b()], w=[hsum.b()])
                    else:
                        k.op(V, lambda: V.eng.tensor_tensor_scan(out=I[:, 0:CTX][:, ::-1], data0=Aa[:, 0:CTX][:, ::-1], data1=I[:, 0:CTX][:, ::-1],
                                                                 initial=0.0, op0=ALU.mult, op1=ALU.add), r=[Aa.b(), I.b()], w=[I.b()])
                        k.op(V, lambda: V.eng.tensor_tensor_scan(out=I[:, CTX:T][:, ::-1], data0=Aa[:, CTX:T][:, ::-1], data1=I[:, CTX:T][:, ::-1],
                                                                 initial=I[:, 0:1], op0=ALU.mult, op1=ALU.add), r=[Aa.b(), I.b()], w=[I.b()])
                        k.tt(P, hsum[:], hsum[:], I[:], ALU.add, r=[hsum.b(), I.b()], w=[hsum.b()])
                for tbi, (t0, t1) in enumerate(TB):
                    n = t1 - t0
                    bk = k.bank()
                    self.proj_fm(wg, tbi, bk)
                    gg = GG[tbi % 2]
                    k.act(gg[:, 0:n], bk[:, 0:n], AF.Gelu_apprx_tanh, r=[bk.b()], w=[gg.b()])
                    k.tt(V, ys[:, t0:t1], hsum[:, t0:t1], gg[:, 0:n], ALU.mult, r=[hsum.b(), gg.b()], w=[ys.b()])
                k.dma(k.SP, YP.t[1, :, cc, :], ys[:], r=[ys.b()], w=[YP.b(1, cc, i) for i in range(5)])

    def hgrn(self, s, l):
        k, C = self.k, self.C
        V, A, P = k.DVE, k.ACT, k.POOL
        VBb = self.VB.b()
        YP, U = self.YP, self.U
        NCH = T // 64
        with k.scope() as sc:
            qb = sc.tile("hgq", [128, T], BF16)
            Vtok = sc.tile("hgV", [128, 18, 128], BF16)
            O = sc.tile("hgO", [128, T], F32)
            Bc = sc.tile("hgB", [128, T], F32)
            kkb = sc.tile("hgkk", [128, T], BF16)
            CH = sc.tile("hgCH", [128, 6, NCH], F32)
            S = sc.tile("hgS", [128, 128], F32)
            SB = [sc.tile(f"hgSb{i}", [128, 128], BF16) for i in range(4)]
            Dt = sc.tile("hgD", [128, 512], F32)
            E1 = sc.tile("hgE1", [128, 512], F32)
            E2 = sc.tile("hgE2", [128, 512], F32)
            QT = [sc.tile(f"hgqt{i}", [128, 512], BF16) for i in range(2)]
            KT = [sc.tile(f"hgkt{i}", [128, 512], BF16) for i in range(2)]
            QH = [sc.tile(f"hgqh{i}", [128, 512], BF16) for i in range(2)]
            KH = [sc.tile(f"hgkh{i}", [128, 512], BF16) for i in range(2)]
            PM = [sc.tile(f"hgpm{i}", [128, 128], BF16) for i in range(3)]
            KK = [sc.tile(f"hgkhk{i}", [128, 128], BF16) for i in range(3)]
            ys = sc.tile("hgys", [128, T], BF16)
            SG = [sc.tile(f"hgsg{i}", [128, 512], F32) for i in range(2)]
            TMP = [sc.tile(f"hgtmp{i}", [128, 512], F32) for i in range(2)]
            WT = [sc.tile(f"hgw{i}", [128, 8, 128], BF16) for i in range(6)]
            wi_ = 0
            blk_i = 0
            pr_i = 0
            for hd in range(4):
                ws = {}
                for nm, c0 in (("q", C_Q), ("ff", C_FF), ("fb", C_FB), ("i", C_I), ("og", C_OG)):
                    ws[nm] = WT[wi_ % 6]
                    wi_ += 1
                    self.win_tile(ws[nm], l, c0 + hd * 128)
                for tbi, (t0, t1) in enumerate(TB):
                    bk = k.bank()
                    n = self.proj_fm(ws["q"], tbi, bk)
                    k.act(qb[:, t0:t1], bk[:, 0:n], AF.Silu, r=[bk.b()], w=[qb.b()])
                for g0 in range(0, 18, 4):
                    g1 = min(18, g0 + 4)
                    bk = k.bank()
                    for tt in range(g0, g1):
                        tbi = self.tbi(tt * 128)
                        for kc in range(8):
                            k.mm(bk[:, (tt - g0) * 128:(tt - g0 + 1) * 128], U[:, kc, tt * 128:(tt + 1) * 128], ws["i"][:, kc, :],
                                 kc == 0, kc == 7, r=[U.b(kc, tbi), ws["i"].b()], w=[bk.b()], sig=(kc == 7))
                    k.cp(A, Vtok[:, g0:g1, :], bk[:, 0:(g1 - g0) * 128].rearrange("p (a b) -> p a b", b=128), r=[bk.b()], w=[Vtok.b()])
                if getattr(self, "hg_cut", 0) == 1:
                    return
                for d in range(2):
                    wf = ws["ff"] if d == 0 else ws["fb"]
                    for tbi, (t0, t1) in enumerate(TB):
                        bk = k.bank()
                        n = self.proj_fm(wf, tbi, bk)
                        k.act(Bc[:, t0:t1], bk[:, 0:n], AF.Sigmoid, r=[bk.b()], w=[Bc.b()])
                    lbv, omlv, nomlv = (self.LB[:, l, i, hd:hd + 1] for i in range(3))
                    k.ts(V, kkb[:], Bc[:], nomlv, ALU.mult, omlv, ALU.add, r=[Bc.b(), self.LB.b()], w=[kkb.b()])
                    k.ts(V, Bc[:], Bc[:], omlv, ALU.mult, lbv, ALU.add, r=[Bc.b(), self.LB.b()], w=[Bc.b()])
                    k.act(Bc[:], Bc[:], AF.Ln, r=[Bc.b()], w=[Bc.b()])
                    ones = C["onesrow"]
                    if d == 0:
                        order = [0, 1, 2, 3, 4]
                    else:
                        order = [0, 4, 3, 2, 1]
                    prev = None
                    for tbi in order:
                        t0, t1 = TB[tbi]
                        n = t1 - t0
                        seg = Bc[:, t0:t1] if d == 0 else Bc[:, t0:t1][:, ::-1]
                        init = 0.0 if prev is None else prev
                        k.op(V, lambda seg=seg, n=n, init=init: V.eng.tensor_tensor_scan(out=seg, data0=ones[:, 0:n], data1=seg, initial=init,
                                                                                      op0=ALU.mult, op1=ALU.add), r=[Bc.b(), ones.b()], w=[Bc.b()])
                        prev = Bc[:, t1 - 1:t1] if d == 0 else Bc[:, t0:t0 + 1]
                    if d == 0:
                        k.cp(V, CH[:, 0, :], Bc[:, 32::64], r=[Bc.b()], w=[CH.b()])
                        k.cp(V, CH[:, 1, :], Bc[:, 63::64], r=[Bc.b()], w=[CH.b()])
                        k.memset(V, CH[:, 2, 0:1], 0.0, w=[CH.b()])
                        k.cp(V, CH[:, 2, 1:NCH], CH[:, 1, 0:NCH - 1], r=[CH.b()], w=[CH.b()])
                    else:
                        k.cp(V, CH[:, 0, :], Bc[:, 31::64], r=[Bc.b()], w=[CH.b()])
                        k.cp(V, CH[:, 1, :], Bc[:, 0::64], r=[Bc.b()], w=[CH.b()])
                        k.cp(V, CH[:, 2, 0:NCH - 1], CH[:, 1, 1:NCH], r=[CH.b()], w=[CH.b()])
                        k.memset(V, CH[:, 2, 3:4], 0.0, w=[CH.b()])
                        k.cp(V, CH[:, 2, NCH - 1:NCH], CH[:, 1, 0:1], r=[CH.b()], w=[CH.b()])
                    k.tt(V, CH[:, 3, :], CH[:, 0, :], CH[:, 2, :], ALU.subtract, r=[CH.b()], w=[CH.b()])
                    k.tt(V, CH[:, 4, :], CH[:, 1, :], CH[:, 0, :], ALU.subtract, r=[CH.b()], w=[CH.b()])
                    k.tt(V, CH[:, 5, :], CH[:, 1, :], CH[:, 2, :], ALU.subtract, r=[CH.b()], w=[CH.b()])
                    k.act(CH[:, 3:6, :], CH[:, 3:6, :], AF.Exp, r=[CH.b()], w=[CH.b()])
                    if getattr(self, "hg_cut", 0) == 2:
                        return
                    k.memset(V, S[:], 0.0, w=[S.b()])
                    sbi = 0
                    sb_cur = SB[sbi % 4]
                    k.memset(V, sb_cur[:], 0.0, w=[sb_cur.b()])
                    mask = C["maskF"] if d == 0 else C["maskB"]
                    for tbi in order:
                        t0, t1 = TB[tbi]
                        n = t1 - t0
                        c0, nch = t0 // 64, n // 64
                        qt, kt_, qh, kh = QT[blk_i % 2], KT[blk_i % 2], QH[blk_i % 2], KH[blk_i % 2]
                        blk_i += 1

                        def bc(row):
                            return CH[:, row, c0:c0 + nch].unsqueeze(2).to_broadcast([128, nch, 64])

                        def v3(ap):
                            return ap.rearrange("p (a b) -> p a b", b=64)
                        k.tt(V, v3(Dt[:, 0:n]), v3(Bc[:, t0:t1]), bc(0), ALU.subtract, r=[Bc.b(), CH.b()], w=[Dt.b()])
                        k.act(E1[:, 0:n], Dt[:, 0:n], AF.Exp, r=[Dt.b()], w=[E1.b()])
                        k.act(E2[:, 0:n], Dt[:, 0:n], AF.Exp, r=[Dt.b()], w=[E2.b()], scale=-1.0)
                        k.tt(V, qt[:, 0:n], qb[:, t0:t1], E1[:, 0:n], ALU.mult, r=[qb.b(), E1.b()], w=[qt.b()])
                        k.tt(P, kt_[:, 0:n], kkb[:, t0:t1], E2[:, 0:n], ALU.mult, r=[kkb.b(), E2.b()], w=[kt_.b()])
                        k.tt(P, v3(E1[:, 0:n]), v3(E1[:, 0:n]), bc(3), ALU.mult, r=[E1.b(), CH.b()], w=[E1.b()])
                        k.tt(V, v3(E2[:, 0:n]), v3(E2[:, 0:n]), bc(4), ALU.mult, r=[E2.b(), CH.b()], w=[E2.b()])
                        k.tt(V, qh[:, 0:n], qb[:, t0:t1], E1[:, 0:n], ALU.mult, r=[qb.b(), E1.b()], w=[qh.b()])
                        k.tt(P, kh[:, 0:n], kkb[:, t0:t1], E2[:, 0:n], ALU.mult, r=[kkb.b(), E2.b()], w=[kh.b()])
                        if getattr(self, "hg_cut", 0) == 3:
                            return
                        npair = n // 128
                        pairs = list(range(npair)) if d == 0 else list(range(npair - 1, -1, -1))
                        def make_pair(j, t0=t0, qt=qt, kt_=kt_, qh=qh, kh=kh):
                            nonlocal pr_i
                            o = j * 128
                            tp = t0 + o
                            tt = tp // 128
                            pm, kk_ = PM[pr_i % 3], KK[pr_i % 3]
                            pr_i += 1
                            st = {}

                            def front():
                                b_sc, b_tr, b_kv, b_kv2 = k.bank(), k.bank(), k.bank(), k.bank()
                                st["kvb"] = [b_kv, b_kv2]
                                k.mm(b_sc[:, 0:128], kt_[:, o:o + 128], qt[:, o:o + 128], True, True, r=[kt_.b(), qt.b()], w=[b_sc.b()], sig=True)
                                k.memset(P, pm[:], 0.0, w=[pm.b()])
                                k.op(V, lambda: V.eng.copy_predicated(out=pm[:], mask=mask[:], data=b_sc[:, 0:128]),
                                     r=[b_sc.b(), mask.b(), pm.b()], w=[pm.b()])
                                btv = b_tr[:, :].bitcast(BF16)
                                k.tr(btv[:, 0:128], kh[:, o:o + 128], C["ident_b"][:], r=[kh.b(), C["ident_b"].b()], w=[b_tr.b()])
                                k.cp(A, kk_[:], btv[:, 0:128], r=[b_tr.b()], w=[kk_.b()])
                                k.mm(b_kv[:, 0:128], kk_[0:64, :], Vtok[0:64, tt, :], True, True, r=[kk_.b(), Vtok.b()], w=[b_kv.b()], sig=True)
                                k.mm(b_kv2[:, 0:128], kk_[64:128, :], Vtok[64:128, tt, :], True, True, r=[kk_.b(), Vtok.b()], w=[b_kv2.b()], sig=True)

                            def back():
                                nonlocal sbi, sb_cur
                                kvb = st["kvb"]
                                halves = [0, 1] if d == 0 else [1, 0]
                                sb_start = {}
                                for hh in halves:
                                    ch = (tp // 64) + hh
                                    sb_start[hh] = sb_cur
                                    k.stt(S[:], S[:], CH[:, 5, ch:ch + 1], kvb[hh][:, 0:128], ALU.mult, ALU.add,
                                          r=[S.b(), CH.b(), kvb[hh].b()], w=[S.b()])
                                    sbi += 1
                                    sb_cur = SB[sbi % 4]
                                    k.cp(A, sb_cur[:], S[:], r=[S.b()], w=[sb_cur.b()])
                                b_o = k.bank()
                                k.mm(b_o[:, 0:128], Vtok[:, tt, :], pm[:], True, False, r=[Vtok.b(), pm.b()], w=[b_o.b()], sig=False)
                                k.mm(b_o[:, 0:64], sb_start[0][:], qh[:, o:o + 64], False, False, r=[sb_start[0].b(), qh.b()], w=[b_o.b()], sig=False)
                                k.mm(b_o[:, 64:128], sb_start[1][:], qh[:, o + 64:o + 128], False, True, r=[sb_start[1].b(), qh.b()], w=[b_o.b()], sig=True)
                                if d == 0:
                                    k.cp(A, O[:, tp:tp + 128], b_o[:, 0:128], r=[b_o.b()], w=[O.b(tt)])
                                else:
                                    k.tt(V, O[:, tp:tp + 128], b_o[:, 0:128], O[:, tp:tp + 128], ALU.add, r=[b_o.b(), O.b(tt)], w=[O.b(tt)])
                            return front, back
                        prs = [make_pair(j) for j in pairs]
                        prs[0][0]()
                        for pi in range(len(prs)):
                            if pi + 1 < len(prs):
                                prs[pi + 1][0]()
                            prs[pi][1]()
                ng = self.vec("hg_norm_g", l)
                for tbi, (t0, t1) in enumerate(TB):
                    n = t1 - t0
                    if True:
                        sc2 = sc
                        obufs = [O.b(tt) for tt in range(t0 // 128, t1 // 128)]
                        rs = self.rstd_of(sc2, O[:, t0:t1].unsqueeze(1), obufs, n, C["ones_v"], nk=1, tag="hg")
                        bk = k.bank()
                        self.proj_fm(ws["og"], tbi, bk)
                        sg, tmp = SG[tbi % 2], TMP[tbi % 2]
                        k.act(sg[:, 0:n], bk[:, 0:n], AF.Silu, r=[bk.b()], w=[sg.b()])
                        k.stt(tmp[:, 0:n], O[:, t0:t1], ng, rs[:, 0:n], ALU.mult, ALU.mult, r=obufs + [VBb, rs.b()], w=[tmp.b()])
                        k.tt(V, ys[:, t0:t1], tmp[:, 0:n], sg[:, 0:n], ALU.mult, r=[tmp.b(), sg.b()], w=[ys.b()])
                k.dma(k.SP, YP.t[2, :, hd, :], ys[:], r=[ys.b()], w=[YP.b(2, hd, i) for i in range(5)])

    def w_tile(self, wt, name, l, c0, nkc, ncol=128, k0=0):
        k = self.k
        src = self.WB[name]
        k.dma(k.SP, wt[:, 0:nkc, 0:ncol],
              src.t[l].rearrange("(kc p) n -> p kc n", p=128)[:, k0:k0 + nkc, c0:c0 + ncol], r=[src.b(l)], w=[wt.b()])

    def merge(self, s, l):
        k, C = self.k, self.C
        V, A, P = k.DVE, k.ACT, k.POOL
        X, U, M, YP = self.X, self.U, self.MOD[l], self.YP
        last = l == self.depth - 1
        pnames = ("w_proj_hy", "w_proj_lru", "w_proj_hg")
        with k.scope() as sc:
            YB = [sc.tile(f"mgy{i}", [128, 4, 512], BF16) for i in range(3)]
            mbf = sc.tile("mgm", [128, 8, 512], BF16)
            MO = sc.tile("mgmo", [128, 8, 512], F32)
            G = [sc.tile(f"mgg{i}", [128, 512], F32) for i in range(2)]
            macc = sc.tile("mgacc", [128, 512], F32)
            T2 = [sc.tile(f"mgt{i}", [128, 512], F32) for i in range(2)]
            WG = [sc.tile(f"mgwg{i}", [128, 8, 128], BF16) for i in range(6)]
            WP = [sc.tile(f"mgwp{i}", [128, 4, 128], BF16) for i in range(6)]
            WO = [sc.tile(f"mgwo{i}", [128, 8, 128], BF16) for i in range(2)]
            it = 0
            for tbi, (t0, t1) in enumerate(TB):
                if last and tbi == 0:
                    continue
                n = t1 - t0
                col = 2 if tbi == 0 else s
                for br in range(3):
                    k.dma(k.SP, YB[br][:, :, 0:n], YP.t[br, :, :, t0:t1], r=[YP.b(br, cc, tbi) for cc in range(4)], w=[YB[br].b()])
                for oc in range(8):
                    for br in range(3):
                        wg, wp = WG[it % 6], WP[it % 6]
                        g, t2 = G[it % 2], T2[it % 2]
                        it += 1
                        self.win_tile(wg, l, C_MG + br * 1024 + oc * 128)
                        self.w_tile(wp, pnames[br], l, oc * 128, 4)
                        bp, bg = k.bank(), k.bank()
                        for kc in range(4):
                            k.mm(bp[:, 0:n], wp[:, kc, :], YB[br][:, kc, 0:n], kc == 0, kc == 3, r=[wp.b(), YB[br].b()], w=[bp.b()], sig=(kc == 3))
                        self.proj_fm(wg, tbi, bg)
                        k.act(g[:, 0:n], bg[:, 0:n], AF.Sigmoid, r=[bg.b()], w=[g.b()])
                        if br == 0:
                            k.tt(V, macc[:, 0:n], bp[:, 0:n], g[:, 0:n], ALU.mult, r=[bp.b(), g.b()], w=[macc.b()])
                        elif br == 1:
                            k.tt(V, t2[:, 0:n], bp[:, 0:n], g[:, 0:n], ALU.mult, r=[bp.b(), g.b()], w=[t2.b()])
                            k.tt(P, macc[:, 0:n], macc[:, 0:n], t2[:, 0:n], ALU.add, r=[macc.b(), t2.b()], w=[macc.b()])
                        else:
                            k.tt(V, t2[:, 0:n], bp[:, 0:n], g[:, 0:n], ALU.mult, r=[bp.b(), g.b()], w=[t2.b()])
                            k.tt(P, mbf[:, oc, 0:n], macc[:, 0:n], t2[:, 0:n], ALU.add, r=[macc.b(), t2.b()], w=[mbf.b(oc)])
                for oc in range(8):
                    wo = WO[oc % 2]
                    self.w_tile(wo, "w_out", l, oc * 128, 8)
                    bk = k.bank()
                    for kc in range(8):
                        k.mm(bk[:, 0:n], wo[:, kc, :], mbf[:, kc, 0:n], kc == 0, kc == 7, r=[wo.b(), mbf.b(kc)], w=[bk.b()], sig=(kc == 7))
                    k.cp(A, MO[:, oc, 0:n], bk[:, 0:n], r=[bk.b()], w=[MO.b(oc)])
                self.resid_update(sc, MO, mbf, n, tbi, t0, t1, M, 2, col, sqb=[mbf.b(oc) for oc in range(8)])

    def resid_update(self, sc, MO, sqt, n, tbi, t0, t1, M, gi, col, sqb=None):
        k, C = self.k, self.C
        X = self.X
        if True:
            sc2 = sc
            rs = self.rstd_of(sc2, MO[:, 0:8, 0:n], [MO.b(kc) for kc in range(8)], n, C["ones_d"], sq=sqt, sqb=sqb)
            tmp = [sc2.ctile(f"rutmp{i}", [128, 512], F32) for i in range(2)]
            for kc in range(8):
                tp = tmp[kc % 2]
                k.stt(tp[:, 0:n], MO[:, kc, 0:n], M[:, gi, kc, col:col + 1], rs[:, 0:n], ALU.mult, ALU.mult,
                      r=[MO.b(kc), M.b(), rs.b()], w=[tp.b()])
                k.tt(k.POOL, X[:, kc, t0:t1], X[:, kc, t0:t1], tp[:, 0:n], ALU.add, r=[X.b(kc, tbi), tp.b()], w=[X.b(kc, tbi)])

    def mlp(self, s, l):
        k, C = self.k, self.C
        V, A, P = k.DVE, k.ACT, k.POOL
        X, U, M = self.X, self.U, self.MOD[l]
        last = l == self.depth - 1
        with k.scope() as sc:
            H = sc.tile("mlH", [128, 32, 512], BF16)
            MO = Tile(H.t[:, 0:16, :].rearrange("p a b -> p (a b)").bitcast(F32).rearrange("p (a b) -> p a b", b=512), "mlMO")
            MO.b = lambda *key: H.b("mo")
            sqt = sc.tile("mlsq", [128, 8, 512], BF16)
            SQ = [sc.tile(f"mlsqr{i}", [128, 512], F32) for i in range(2)]
            W1 = [sc.tile(f"mlw1{i}", [128, 8, 512], BF16) for i in range(2)]
            W2 = [sc.tile(f"mlw2{i}", [128, 2, 1024], BF16) for i in range(2)]
            hb_all = [H.b(j) for j in range(32)] + [H.b("mo")]
            for tbi, (t0, t1) in enumerate(TB):
                if last and tbi == 0:
                    continue
                n = t1 - t0
                col = 2 if tbi == 0 else s
                if True:
                    sc2 = sc
                    rs = self.rstd_of(sc2, X[:, :, t0:t1], [X.b(kc, tbi) for kc in range(8)], n, C["ones_d"], sq=sqt)
                    tmp = [sc2.ctile(f"mltmp{i}", [128, 512], F32) for i in range(2)]
                    for kc in range(8):
                        tp = tmp[kc % 2]
                        k.stt(tp[:, 0:n], X[:, kc, t0:t1], M[:, 4, kc, col:col + 1], rs[:, 0:n], ALU.mult, ALU.mult,
                              r=[X.b(kc, tbi), M.b(), rs.b()], w=[tp.b()])
                        k.act(U[:, kc, t0:t1], tp[:, 0:n], AF.Identity, r=[tp.b(), M.b()], w=[U.b(kc, tbi)], bias=M[:, 3, kc, col:col + 1])
                for jg in range(8):
                    w1 = W1[jg % 2]
                    self.w_tile(w1, "w_mlp1", l, jg * 512, 8, ncol=512)
                    for jj in range(4):
                        j = jg * 4 + jj
                        bk = k.bank()
                        for kc in range(8):
                            k.mm(bk[:, 0:n], w1[:, kc, jj * 128:(jj + 1) * 128], U[:, kc, t0:t1], kc == 0, kc == 7,
                                 r=[w1.b(), U.b(kc, tbi)], w=[bk.b()], sig=(kc == 7))
                        sq = SQ[j % 2]
                        k.act(sq[:, 0:n], bk[:, 0:n], AF.Square, r=[bk.b()], w=[sq.b()])
                        k.stt(H[:, j, 0:n], bk[:, 0:n], 0.0, sq[:, 0:n], ALU.is_gt, ALU.mult, r=[bk.b(), sq.b()], w=[H.b(j), H.b("mo")])
                bks = [k.bank() for _ in range(8)]
                for jg in range(16):
                    w2 = W2[jg % 2]
                    self.w_tile(w2, "w_mlp2", l, 0, 2, ncol=1024, k0=jg * 2)
                    for jj in range(2):
                        j = jg * 2 + jj
                        for oc in range(8):
                            k.mm(bks[oc][:, 0:n], w2[:, jj, oc * 128:(oc + 1) * 128], H[:, j, 0:n], j == 0, j == 31,
                                 r=[w2.b(), H.b(j)], w=[bks[oc].b()], sig=(oc == 7 and jj == 1))
                for oc in range(8):
                    k.cp(A if oc % 2 == 0 else V, MO[:, oc, 0:n], bks[oc][:, 0:n], r=[bks[oc].b()], w=hb_all)
                self.resid_update(sc, MO, sqt, n, tbi, t0, t1, M, 5, col)

    def permute(self, fwd):
        k = self.k
        X = self.X
        engs = [k.ACT, k.DVE, k.POOL]
        with k.scope() as sc:
            TMPS = [sc.tile(f"pmt{i}", [128, SEQ], F32) for i in range(2)]
            for kc in range(8):
                tmp = TMPS[kc % 2]
                xb = [X.b(kc, tbi) for tbi in range(1, 5)]
                src = X[:, kc, CTX:T]
                if fwd:
                    v = src.rearrange("p (r c) -> p c r", c=GRID_W)
                    tv = tmp[:, :].rearrange("p (c r) -> p c r", c=GRID_W)
                else:
                    v = src.rearrange("p (c r) -> p r c", c=GRID_W)
                    tv = tmp[:, :].rearrange("p (r c) -> p r c", c=GRID_W)
                k.cp(engs[kc % 3], tv, v, r=xb, w=[tmp.b()])
                k.cp(engs[(kc + 1) % 3], X[:, kc, CTX:T], tmp[:, :], r=[tmp.b()], w=xb)

    def store_out(self, s):
        k, C = self.k, self.C
        X = self.X
        with k.scope() as sc:
            OT = [sc.tile(f"ot{i}", [128, D], F32) for i in range(3)]
            for tt in range(2, T // 128):
                ot = OT[tt % 3]
                tbi = self.tbi(tt * 128)
                for h in range(2):
                    bk = k.bank()
                    for j in range(4):
                        kc = h * 4 + j
                        k.tr(bk[:, j * 128:(j + 1) * 128], X[:, kc, tt * 128:(tt + 1) * 128], C["ident_f"][:],
                             r=[X.b(kc, tbi), C["ident_f"].b()], w=[bk.b()], sig=(j == 3))
                    k.cp(k.ACT if h == 0 else k.DVE, ot[:, h * 512:(h + 1) * 512], bk[:, :], r=[bk.b()], w=[ot.b(h)])
                k.dma(k.SP, self.out[s, (tt - 2) * 128:(tt - 1) * 128, :], ot[:], r=[ot.b(0), ot.b(1)], w=[])

    def layer(self, s, l):
        import os
        stop = os.environ.get("KSTOP", "")
        self.phase_a(s, l)
        if stop == f"a{l}":
            return True
        self.hyena(s, l)
        if stop == f"hy{l}":
            return True
        self.hgrn(s, l)
        if stop == f"hg{l}":
            return True
        self.lru(s, l)
        if stop == f"lru{l}":
            return True
        self.merge(s, l)
        if stop == f"mg{l}":
            return True
        self.mlp(s, l)
        if stop == f"mlp{l}":
            return True
        return False

    def build_full(self):
        self.prologue()
        for s in range(self.nseq):
            self.load_x(s)
            for l in range(self.depth):
                if l % 2 == 1:
                    self.permute(True)
                if self.layer(s, l):
                    self.store_out(s)
                    self.finish()
                    return
                if l % 2 == 1:
                    self.permute(False)
            self.store_out(s)
        self.finish()

    def spec_mul(self, yre, yim, A_, B_, Kre, Kim, tm, rb, TM, wb):
        k = self.k
        V, P = k.DVE, k.POOL
        k.tt(V, tm[0], A_, Kre, ALU.mult, r=rb, w=[TM[0].b()])
        k.tt(V, tm[1], B_, Kim, ALU.mult, r=rb, w=[TM[1].b()])
        k.tt(V, yre, tm[0], tm[1], ALU.add, r=[TM[0].b(), TM[1].b()], w=wb)
        k.tt(P, tm[2], B_, Kre, ALU.mult, r=rb, w=[TM[2].b()])
        k.tt(P, tm[3], A_, Kim, ALU.mult, r=rb, w=[TM[3].b()])
        k.tt(P, yim, tm[2], tm[3], ALU.subtract, r=[TM[2].b(), TM[3].b()], w=wb)

    def prologue(self):
        k = self.k
        self.precast()
        self.consts()
        self.resident()
        self.vecbank()
        with k.scope() as psc:
            self.FB = psc.tile("FB", [128, self.depth, 2], F32)
            self.SK = psc.tile("SK", [128, self.depth + 1, 512], F32)
            for name in ("negt", "delta"):
                shp = list(make_consts()[name].shape)
                t = psc.tile("c_" + name, shp, CONST_DT[name])
                k.dma(k.SP, t[:], self.cst[name], w=[t.b()])
                self.C[name] = t
            self.derive()
            for l in range(self.depth):
                self.filt(l, False)
            self.filt(0, True)
        self.X = k.tile("X", [128, 8, T], F32)
        self.U = k.tile("U", [128, 8, T], BF16)

    def finish(self):
        k = self.k
        k.barrier()
        k.stack.close()


def make_in_maps(inputs, ncores=NCORES, nseq=NSEQ, j=0):
    cst = make_consts()
    maps = []
    for i in range(ncores):
        m = {}
        for name in IN_SHAPES:
            a = np.asarray(inputs[name])
            if name in ("x", "c", "ctx"):
                a = a[NSEQ * i + j:NSEQ * i + j + nseq]
            m[name] = np.ascontiguousarray(a, dtype=np.float32)
        for name, arr in cst.items():
            m["k_" + name] = arr
        maps.append(m)
    return maps


_NET = None
NLAUNCH = 1


def kernel(**inputs):
    global _NET
    nseq = NSEQ // NLAUNCH
    if _NET is None:
        net = Net(nseq=nseq)
        net.build_full()
        _NET = net
    net = _NET
    outs = []
    for j in range(NLAUNCH):
        maps = make_in_maps(inputs, NCORES, nseq, j * nseq)
        res = run_bass_kernel_spmd(net.nc, maps, core_ids=list(range(NCORES)))
        outs.append([np.asarray(r["out"]) for r in res.results])
    full = np.zeros((NCORES * NSEQ, SEQ, D), np.float32)
    for j in range(NLAUNCH):
        for i in range(NCORES):
            full[NSEQ * i + j * nseq:NSEQ * i + (j + 1) * nseq] = outs[j][i]
    return full
```

```python
import math
from contextlib import ExitStack
import numpy as np
import ml_dtypes
import concourse.bass as bass
import concourse.mybir as mybir
from concourse.bass_utils import run_bass_kernel_spmd

F32 = mybir.dt.float32
BF16 = mybir.dt.bfloat16
AF = mybir.ActivationFunctionType
ALU = mybir.AluOpType

NCORES = 8
NSEQ = 2
D = 1024
KC = 8
SEQ = 2048
CTX = 256
T = SEQ + CTX
DEPTH = 2
GRID_W = 64
EPS = 1e-6
E_HY = 512
INW = 8192
DFF = 4096
PADL = 3
PT = T + 12
TB = [(0, 256), (256, 768), (768, 1280), (1280, 1792), (1792, 2304)]
C_HY, C_LX, C_LG, C_Q, C_FF, C_FB, C_I, C_OG, C_MG = 0, 1536, 2048, 2560, 3072, 3584, 4096, 4608, 5120


def poff(t):
    return t + PADL if t < CTX else t + 3 * PADL


class Ev:
    __slots__ = ("q", "sem", "val", "dma")

    def __init__(self, q, sem, val, dma=False):
        self.q, self.sem, self.val, self.dma = q, sem, val, dma


class Buf:
    __slots__ = ("name", "w", "r")

    def __init__(self, name=""):
        self.name, self.w, self.r = name, None, []


class Tile:
    def __init__(self, t, name):
        self.t, self.name, self.bufs = t, name, {}

    def b(self, *key):
        v = self.bufs.get(key)
        if v is None:
            v = self.bufs[key] = Buf(f"{self.name}{key}")
        return v

    def __getitem__(self, k):
        return self.t[k]


class Q:
    LIM = 3000

    def __init__(self, kern, name, eng, ndma=0):
        self.k, self.name, self.eng = kern, name, eng
        self.sem = kern.new_sem(name)
        self.cnt = 0
        self.seen = {}
        self.last = None
        self.slots = [[kern.new_sem(f"{name}d{i}"), 0] for i in range(ndma)]
        self.nxt = 0
        self.pend = []
        self.collect = None

    def wait(self, ev):
        key = id(ev.sem)
        if self.seen.get(key, 0) >= ev.val:
            return
        self.seen[key] = ev.val
        if self.collect is not None:
            self.collect = [e for e in self.collect if e.sem is not ev.sem] + [ev]
            return
        self.eng.wait_ge(ev.sem, ev.val)
        self.k.nwait += 1

    def flush(self, ins_fn):
        ws, self.collect = self.collect, None
        for e in ws[:-1]:
            self.eng.wait_ge(e.sem, e.val)
            self.k.nwait += 1
        ins = ins_fn()
        if ws:
            ins._wait_ge(ws[-1].sem, ws[-1].val)
        return ins

    def signal(self, ins):
        if self.cnt >= Q.LIM:
            self.sem = self.k.new_sem(self.name + "x")
            self.cnt = 0
        self.cnt += 1
        ins.then_inc(self.sem, 1)
        self.last = Ev(self, self.sem, self.cnt)
        return self.last

    def dma_signal(self, fn):
        slot = self.slots[self.nxt]
        self.nxt = (self.nxt + 1) % len(self.slots)
        if slot[1] > 0:
            self.wait(Ev(self, slot[0], slot[1], True))
        ins = self.flush(fn)
        slot[1] += 16
        ins.then_inc(slot[0], 16)
        return Ev(self, slot[0], slot[1], True)


class Kern:
    def __init__(self):
        self.nc = bass.Bass("TRN2", target_bir_lowering=False)
        self.stack = ExitStack()
        self.nwait = 0
        self.nins = 0
        self.sems = []
        nc = self.nc
        self.PE = Q(self, "pe", nc.tensor)
        self.ACT = Q(self, "act", nc.scalar)
        self.DVE = Q(self, "dve", nc.vector)
        self.POOL = Q(self, "pool", nc.gpsimd, ndma=2)
        self.SP = Q(self, "sp", nc.sync, ndma=16)
        self.queues = [self.PE, self.ACT, self.DVE, self.POOL, self.SP]
        self.banks = []
        for i in range(8):
            t = nc.alloc_psum_tensor(f"bank{i}", [128, 512], F32)
            self.banks.append(Tile(t, f"bank{i}"))
        self.bank_i = 0
        self.dbg = []

    def new_sem(self, name):
        s = self.stack.enter_context(self.nc.semaphore(f"s_{name}_{len(self.sems)}"))
        self.sems.append(s)
        return s

    def bank(self):
        b = self.banks[self.bank_i]
        self.bank_i = (self.bank_i + 1) % 8
        return b

    def _deps(self, q, r, w):
        for b in r:
            if b.w is not None:
                self._dep(q, b.w, 0)
        for b in w:
            if b.w is not None:
                self._dep(q, b.w, 1)
            for e in b.r:
                self._dep(q, e, 2)

    def _dep(self, q, ev, kind):
        if ev.q is q and not ev.dma:
            if q is self.PE or kind == 2:
                return
        q.wait(ev)

    def _record(self, ev, r, w):
        for b in r:
            if not ev.dma:
                b.r = [e for e in b.r if e.dma or e.q is not ev.q]
            b.r.append(ev)
        for b in w:
            b.w = ev
            b.r = []

    def op(self, q, fn, r=(), w=(), sig=True):
        q.collect = []
        self._deps(q, r, w)
        ins = q.flush(fn)
        self.nins += 1
        if sig:
            ev = q.signal(ins)
            for (pr, pw) in q.pend:
                self._record(ev, pr, pw)
            q.pend = []
            self._record(ev, r, w)
        else:
            q.pend.append((list(r), list(w)))
        return ins

    def dma(self, q, out, in_, r=(), w=(), **kw):
        assert not q.pend
        q.collect = []
        self._deps(q, r, w)
        ev = q.dma_signal(lambda: q.eng.dma_start(out=out, in_=in_, **kw))
        self.nins += 1
        self._record(ev, r, w)

    def barrier(self):
        evs = []
        for q in self.queues:
            assert not q.pend, q.name
            if q.last is not None:
                evs.append(q.last)
            for s in q.slots:
                if s[1] > 0:
                    evs.append(Ev(q, s[0], s[1], True))
        for q in self.queues:
            for e in evs:
                if e.q is q and not e.dma:
                    continue
                q.wait(e)

    def tile(self, name, shape, dtype, stack=None):
        st = stack if stack is not None else self.stack
        self.ntile = getattr(self, "ntile", 0) + 1
        name = f"{name}_{self.ntile}"
        t = st.enter_context(self.nc.sbuf_tensor(name, list(shape), dtype))
        return Tile(t, name)

    class Scope:
        def __init__(self, k):
            self.k = k
            self.st = ExitStack()
            self.cache = {}

        def ctile(self, name, shape, dtype):
            key = (name, tuple(shape), str(dtype))
            t = self.cache.get(key)
            if t is None:
                t = self.cache[key] = self.tile(name, shape, dtype)
            return t

        def __enter__(self):
            self.st.__enter__()
            return self

        def tile(self, name, shape, dtype):
            return self.k.tile(name, shape, dtype, self.st)

        def __exit__(self, *a):
            self.k.barrier()
            return self.st.__exit__(*a)

    def scope(self):
        return Kern.Scope(self)

    def mm(self, out, lhsT, rhs, start, stop, r=(), w=(), sig=False):
        nc = self.nc
        return self.op(self.PE, lambda: nc.tensor.matmul(out, lhsT, rhs, start=start, stop=stop), r, w, sig)

    def tr(self, out, in_, ident, r=(), w=(), sig=True):
        nc = self.nc
        return self.op(self.PE, lambda: nc.tensor.transpose(out, in_, ident), r, w, sig)

    def act(self, out, in_, func, r=(), w=(), bias=None, scale=None):
        nc = self.nc
        kw = {}
        if bias is not None:
            kw["bias"] = bias
        if scale is not None:
            kw["scale"] = scale
        return self.op(self.ACT, lambda: nc.scalar.activation(out=out, in_=in_, func=func, **kw), r, w)

    def tt(self, q, out, in0, in1, op, r=(), w=()):
        return self.op(q, lambda: q.eng.tensor_tensor(out=out, in0=in0, in1=in1, op=op), r, w)

    def ts(self, q, out, in0, s1, op0, s2=None, op1=None, r=(), w=()):
        if op1 is None:
            return self.op(q, lambda: q.eng.tensor_scalar(out=out, in0=in0, scalar1=s1, scalar2=None, op0=op0), r, w)
        return self.op(q, lambda: q.eng.tensor_scalar(out=out, in0=in0, scalar1=s1, scalar2=s2, op0=op0, op1=op1), r, w)

    def stt(self, out, in0, scalar, in1, op0, op1, r=(), w=()):
        nc = self.nc
        return self.op(self.DVE, lambda: nc.vector.scalar_tensor_tensor(out=out, in0=in0, scalar=scalar, in1=in1, op0=op0, op1=op1), r, w)

    def cp(self, q, out, in_, r=(), w=()):
        if q is self.ACT:
            return self.op(q, lambda: q.eng.copy(out=out, in_=in_), r, w)
        return self.op(q, lambda: q.eng.tensor_copy(out=out, in_=in_), r, w)

    def memset(self, q, ap, val, w=()):
        return self.op(q, lambda: q.eng.memset(ap, val), (), w)


_CONSTS = None


def make_consts():
    global _CONSTS
    if _CONSTS is not None:
        return _CONSTS
    bf = ml_dtypes.bfloat16
    c = {}
    c["ident_f"] = np.eye(128, dtype=np.float32)
    c["ident_b"] = np.eye(128, dtype=np.float32).astype(bf)
    c["ones_d"] = np.full((128, 128), 1.0 / D, np.float32).astype(bf)
    c["ones_v"] = np.full((128, 128), 1.0 / 128, np.float32).astype(bf)
    c["onesrow"] = np.ones((128, 512), np.float32).astype(bf)

    def dft(L):
        N = 2 * L
        s = np.arange(L, dtype=np.float64)[:, None]
        f = np.arange(L, dtype=np.float64)[None, :]
        ang = 2.0 * np.pi * np.mod((2 * f + 1) * s, 2 * N) / (2 * N)
        return np.cos(ang), np.sin(ang)

    cs, sn = dft(SEQ)
    fw = np.stack([cs, sn], 0).reshape(2, 16, 128, 16, 128)
    c["fwdL"] = np.ascontiguousarray(fw.transpose(3, 2, 1, 0, 4)).astype(bf)
    iv = np.stack([cs.T, sn.T], 0).reshape(2, 16, 128, SEQ)
    c["invL"] = np.ascontiguousarray(iv.transpose(1, 2, 0, 3)).astype(bf)
    cs, sn = dft(CTX)
    fw = np.stack([cs, sn], 0).reshape(2, 2, 128, CTX)
    c["fwdC"] = np.ascontiguousarray(fw.transpose(2, 1, 0, 3)).astype(bf)
    iv = np.stack([cs.T, sn.T], 0).reshape(2, 2, 128, CTX)
    c["invC"] = np.ascontiguousarray(iv.transpose(2, 1, 0, 3)).astype(bf)

    def zemb(L):
        pos = np.arange(L, dtype=np.float32)
        t = pos / np.float32(max(L - 1, 1))
        bands = np.linspace(1e-4, 7, 8, dtype=np.float32)
        ang = bands[None, :] * (np.float32(2.0 * math.pi) * pos / np.float32(L))[:, None]
        z = np.concatenate([t[:, None], np.cos(ang), -np.sin(ang)], -1).astype(np.float32)
        return np.ascontiguousarray(z.T), t

    c["zL"], tl = zemb(SEQ)
    c["zC"], tc = zemb(CTX)
    negt = np.zeros((128, 18), np.float32)
    negt[:, :16] = -tl.reshape(16, 128).T
    negt[:, 16:] = -tc.reshape(2, 128).T
    c["negt"] = negt
    deltas = np.abs(np.linspace(math.log(1e-2) / 1.5, math.log(1e-2) / 0.3, E_HY, dtype=np.float32))
    c["delta"] = np.ascontiguousarray(np.broadcast_to(deltas[None, :], (128, E_HY))).astype(np.float32)
    s = np.arange(128)[:, None]
    t = np.arange(128)[None, :]
    same = (s // 64) == (t // 64)
    c["maskF"] = (same & (s <= t)).astype(np.uint16)
    c["maskB"] = (same & (s >= t)).astype(np.uint16)
    _CONSTS = c
    return c


CONST_DT = {"ident_f": F32, "ident_b": BF16, "ones_d": BF16, "ones_v": BF16, "onesrow": BF16, "fwdL": BF16,
            "invL": BF16, "fwdC": BF16, "invC": BF16, "zL": F32, "zC": F32, "negt": F32, "delta": F32,
            "maskF": mybir.dt.uint16, "maskB": mybir.dt.uint16}

IN_SHAPES = {
    "x": [NSEQ, SEQ, D], "c": [NSEQ, D], "ctx": [NSEQ, CTX, D], "c_ctx": [D],
    "w_ada": [DEPTH, D, 6 * D], "b_ada": [DEPTH, 6 * D], "norm_gains": [DEPTH, 4, D], "w_in": [DEPTH, D, INW],
    "hy_short_w": [DEPTH, 3, 1536], "hy_short_b": [DEPTH, 1536], "hy_ff_w1": [DEPTH, 17, 64],
    "hy_ff_b1": [DEPTH, 64], "hy_ff_w2": [DEPTH, 64, 64], "hy_ff_b2": [DEPTH, 64], "hy_ff_w3": [DEPTH, 64, 1024],
    "hy_freq": [DEPTH, 2, 64], "hy_skip": [DEPTH, 512], "lru_conv_w": [DEPTH, 2, 4, 512],
    "lru_conv_b": [DEPTH, 2, 512], "lru_wr": [DEPTH, 2, 8, 64, 64], "lru_br": [DEPTH, 2, 512],
    "lru_wi": [DEPTH, 2, 8, 64, 64], "lru_bi": [DEPTH, 2, 512], "lru_lambda": [DEPTH, 2, 512],
    "hg_lower_bounds": [DEPTH, 512], "hg_norm_g": [DEPTH, 128], "w_proj_hy": [DEPTH, 512, D],
    "w_proj_lru": [DEPTH, 512, D], "w_proj_hg": [DEPTH, 512, D], "w_out": [DEPTH, D, D],
    "w_mlp1": [DEPTH, D, DFF], "w_mlp2": [DEPTH, DFF, D],
}


VEC_LIST = [
    ("b_ada", 6144), ("norm_gains", 4096), ("hy_short_w", 4608), ("hy_short_b", 1536), ("lru_conv_w", 4096),
    ("lru_conv_b", 1024), ("lru_br", 1024), ("lru_bi", 1024), ("lru_lambda", 1024), ("hg_lower_bounds", 512),
    ("hg_norm_g", 128), ("hy_ff_b1", 64), ("hy_ff_b2", 64), ("hy_freq0", 64), ("hy_freq1", 64),
]


class Net:
    def __init__(self, nseq=NSEQ, depth=DEPTH, stage="full", dbg=()):
        self.k = k = Kern()
        self.nc = nc = k.nc
        self.nseq, self.depth, self.stage = nseq, depth, stage
        self.dbg_names = dbg
        self.dbg_out = {}
        self.inp = {}
        for name, shp in IN_SHAPES.items():
            shp = list(shp)
            if name in ("x", "c", "ctx"):
                shp[0] = nseq
            self.inp[name] = nc.dram_tensor(name, shp, F32, kind="ExternalInput").ap()
        self.cst = {}
        for name, arr in make_consts().items():
            self.cst[name] = nc.dram_tensor("k_" + name, list(arr.shape), CONST_DT[name], kind="ExternalInput").ap()
        self.out = nc.dram_tensor("out", [nseq, SEQ, D], F32, kind="ExternalOutput").ap()

        def scratch(name, shape, dt=BF16):
            return Tile(nc.dram_tensor(name, list(shape), dt, kind="Internal").ap(), name)

        self.WB = {
            "w_in": scratch("wb_in", [DEPTH, D, INW]), "w_ada": scratch("wb_ada", [DEPTH, D, 6 * D]),
            "w_proj_hy": scratch("wb_phy", [DEPTH, 512, D]), "w_proj_lru": scratch("wb_plru", [DEPTH, 512, D]),
            "w_proj_hg": scratch("wb_phg", [DEPTH, 512, D]), "w_out": scratch("wb_out", [DEPTH, D, D]),
            "w_mlp1": scratch("wb_m1", [DEPTH, D, DFF]), "w_mlp2": scratch("wb_m2", [DEPTH, DFF, D]),
        }
        self.KS = scratch("ks", [DEPTH, 16, 128, 2, 512])
        self.KSC = scratch("ksc", [2, 128, 2, 512])
        self.YP = scratch("yp", [3, 128, 4, T])

    def dump(self, name, tile_ap, shape, dtype=F32):
        if name not in self.dbg_names:
            return
        k, nc = self.k, self.nc
        k.barrier()
        d = nc.dram_tensor("dbg_" + name, list(shape), dtype, kind="ExternalOutput").ap()
        k.dma(k.SP, d, tile_ap)
        k.barrier()
        self.dbg_out[name] = "dbg_" + name

    def precast(self):
        k = self.k
        order = []
        for l in range(self.depth):
            order.append(("w_ada", l))
        for l in range(self.depth):
            for n in ("w_in", "w_proj_hy", "w_proj_hg", "w_proj_lru", "w_out", "w_mlp1", "w_mlp2"):
                order.append((n, l))
        for (n, l) in order:
            src = self.inp[n][l]
            dst = self.WB[n]
            k.dma(k.POOL, dst[l], src, w=[dst.b(l)], max_dma_last_dim=4096)

    def consts(self):
        k, nc = self.k, self.nc
        C = {}
        for name in ("ident_f", "ident_b", "ones_d", "ones_v", "onesrow", "fwdC", "invC", "maskF", "maskB"):
            shp = list(make_consts()[name].shape)
            t = k.tile("c_" + name, shp, CONST_DT[name])
            k.dma(k.SP, t[:], self.cst[name], w=[t.b()])
            C[name] = t
        self.C = C

    def vecbank(self):
        k, nc = self.k, self.nc
        rows = []
        for l in range(self.depth):
            for name, ln in VEC_LIST:
                if name == "hy_freq0":
                    src = self.inp["hy_freq"][l, 0]
                elif name == "hy_freq1":
                    src = self.inp["hy_freq"][l, 1]
                else:
                    src = self.inp[name][l]
                    if len(src.shape) > 1:
                        src = src.flatten() if hasattr(src, "flatten") else src
                rows.append(((name, l), src, ln))
        for i in range(self.nseq):
            rows.append((("c", i), self.inp["c"][i], D))
        rows.append((("c_ctx", 0), self.inp["c_ctx"], D))
        place = {}
        tile_i, r = 0, 0
        for key, src, ln in rows:
            nr = (ln + 127) // 128
            if r + nr > 128:
                tile_i, r = tile_i + 1, 0
            place[key] = (tile_i, r, nr, ln)
            r += nr
        ntile = tile_i + 1
        self.VB = VB = k.tile("VB", [128, ntile, 128], F32)
        with k.scope() as sc:
            RT = sc.tile("rowtile", [128, ntile, 128], F32)
            k.memset(k.DVE, RT[:], 0.0, w=[RT.b()])
            rkeys = {ti: [] for ti in range(ntile)}
            for ri, (key, src, ln) in enumerate(rows):
                ti, r0, nr, _ = place[key]
                rb = RT.b("row", ri)
                rb.w = RT.b().w
                rkeys[ti].append(rb)
                if ln >= 128:
                    k.dma(k.SP, RT[r0:r0 + nr, ti, :], src.rearrange("(r c) -> r c", c=128), w=[rb])
                else:
                    k.dma(k.SP, RT[r0:r0 + 1, ti, 0:ln], src.rearrange("(r c) -> r c", r=1), w=[rb])
            for ti in range(ntile):
                bk = k.bank()
                k.tr(bk[:, 0:128], RT[:, ti, :], self.C["ident_f"][:], r=[RT.b(), self.C["ident_f"].b()] + rkeys[ti], w=[bk.b()])
                k.cp(k.DVE, VB[:, ti, :], bk[:, 0:128], r=[bk.b()], w=[VB.b()])
        self.place = place

    def vec(self, name, l, i0=0, n=1):
        ti, r0, nr, ln = self.place[(name, l)]
        return self.VB[:, ti, r0 + i0:r0 + i0 + n]

    def derive(self):
        k, nc, C = self.k, self.nc, self.C
        V, A, P = k.DVE, k.ACT, k.POOL
        VBb = self.VB.b()
        with k.scope() as sc:
            scT = sc.tile("scT", [128, 8, 3], BF16)
            for i in range(3):
                src = self.vec("c", min(i, self.nseq - 1), 0, 8) if i < 2 else self.vec("c_ctx", 0, 0, 8)
                k.act(scT[:, :, i], src, AF.Silu, r=[VBb], w=[scT.b()])
            WA = [sc.tile(f"wada{i}", [128, 8, 512], BF16) for i in range(2)]
            for l in range(self.depth):
                ADA = sc.tile(f"ADA{l}", [128, 48, 3], F32)
                bk = k.bank()
                wsrc = self.WB["w_ada"]
                wv = wsrc.t[l].rearrange("(kc p) n -> p kc n", p=128)
                for g in range(12):
                    wt = WA[g % 2]
                    k.dma(k.SP, wt[:], wv[:, :, g * 512:(g + 1) * 512], r=[wsrc.b(l)], w=[wt.b()])
                    for jj in range(4):
                        j = g * 4 + jj
                        for kc in range(8):
                            k.mm(bk[:, j * 3:j * 3 + 3], wt[:, kc, jj * 128:(jj + 1) * 128], scT[:, kc, :],
                                 kc == 0, kc == 7, r=[wt.b(), scT.b()], w=[bk.b()], sig=(kc == 7))
                bada = self.vec("b_ada", l, 0, 48)
                k.tt(V, ADA[:], bk[:, 0:144].rearrange("p (j c) -> p j c", c=3),
                     bada.unsqueeze(2).to_broadcast([128, 48, 3]), ALU.add, r=[bk.b(), VBb], w=[ADA.b()])
                self.dump(f"ada{l}", ADA[:], [128, 48, 3])
                M = self.MOD[l]
                g = [self.vec("norm_gains", l, 8 * i, 8).unsqueeze(2).to_broadcast([128, 8, 3]) for i in range(4)]
                tmp = sc.tile(f"modtmp{l}", [128, 8, 3], F32)
                k.cp(V, M[:, 0], ADA[:, 0:8, :], r=[ADA.b()], w=[M.b()])
                k.ts(V, tmp[:], ADA[:, 8:16, :], 1.0, ALU.add, r=[ADA.b()], w=[tmp.b()])
                k.tt(V, M[:, 1], tmp[:], g[0], ALU.mult, r=[tmp.b(), VBb], w=[M.b()])
                k.tt(V, M[:, 2], ADA[:, 16:24, :], g[1], ALU.mult, r=[ADA.b(), VBb], w=[M.b()])
                k.cp(V, M[:, 3], ADA[:, 24:32, :], r=[ADA.b()], w=[M.b()])
                tmp2 = sc.tile(f"modtmp2{l}", [128, 8, 3], F32)
                k.ts(V, tmp2[:], ADA[:, 32:40, :], 1.0, ALU.add, r=[ADA.b()], w=[tmp2.b()])
                k.tt(V, M[:, 4], tmp2[:], g[2], ALU.mult, r=[tmp2.b(), VBb], w=[M.b()])
                k.tt(V, M[:, 5], ADA[:, 40:48, :], g[3], ALU.mult, r=[ADA.b(), VBb], w=[M.b()])
            for l in range(self.depth):
                e = sc.tile(f"c1e{l}", [128, 8], F32)
                k.act(e[:], self.vec("lru_lambda", l, 0, 8), AF.Exp, r=[VBb], w=[e.b()], scale=-1.0)
                k.ts(V, e[:], e[:], 1.0, ALU.add, r=[e.b()], w=[e.b()])
                k.act(e[:], e[:], AF.Ln, r=[e.b()], w=[e.b()])
                k.ts(V, self.C1[:, l, 0, :], e[:], -8.0, ALU.mult, r=[e.b()], w=[self.C1.b()])
                k.ts(V, self.C1[:, l, 1, :], e[:], -16.0, ALU.mult, r=[e.b()], w=[self.C1.b()])
            k.memset(V, self.LB[:, 0, 0, :], 0.0, w=[self.LB.b()])
            k.memset(V, self.LB[:, 0, 1, :], 1.0, w=[self.LB.b()])
            k.memset(V, self.LB[:, 0, 2, :], -1.0, w=[self.LB.b()])
            if self.depth > 1:
                dlt = sc.tile("lbd", [128, 4], F32)
                k.tt(V, dlt[:], self.vec("hg_lower_bounds", 1, 0, 4), self.vec("hg_lower_bounds", 0, 0, 4),
                     ALU.subtract, r=[VBb], w=[dlt.b()])
                k.act(self.LB[:, 1, 0, :], dlt[:], AF.Sigmoid, r=[dlt.b()], w=[self.LB.b()])
                k.act(self.LB[:, 1, 1, :], dlt[:], AF.Sigmoid, r=[dlt.b()], w=[self.LB.b()], scale=-1.0)
                k.ts(V, self.LB[:, 1, 2, :], self.LB[:, 1, 1, :], -1.0, ALU.mult, r=[self.LB.b()], w=[self.LB.b()])
            for l in range(self.depth):
                k.tt(V, self.FB[:, l, 0:1], self.vec("hy_freq0", l), self.vec("hy_ff_b1", l), ALU.mult, r=[VBb], w=[self.FB.b()])
                k.tt(V, self.FB[:, l, 1:2], self.vec("hy_freq1", l), self.vec("hy_ff_b2", l), ALU.mult, r=[VBb], w=[self.FB.b()])
            skr = sc.tile("skraw", [128, self.depth, 512], F32)
            for l in range(self.depth):
                k.dma(k.SP, skr[:, l, :], self.inp["hy_skip"][l].partition_broadcast(128), w=[skr.b()])
                k.ts(V, self.SK[:, l, :], skr[:, l, :], 2.0 / (2 * SEQ), ALU.mult, r=[skr.b()], w=[self.SK.b()])
            k.ts(V, self.SK[:, self.depth, :], skr[:, 0, :], 2.0 / (2 * CTX), ALU.mult, r=[skr.b()], w=[self.SK.b()])

    def wbd(self, l, d, ri, cc):
        return self.WBD[:, (d * 2 + ri) * 4 + cc, :]

    def build_wbd(self, sc, l):
        k = self.k
        self.WBD = sc.tile("WBD", [128, 16, 128], BF16)
        with k.scope() as sc2:
            stg = sc2.tile("wbdstage", [128, 16, 128], F32)
            k.memset(k.POOL, stg[:], 0.0, w=[stg.b()])
            for d in range(2):
                for ri, nm in enumerate(("lru_wr", "lru_wi")):
                    for cc in range(4):
                        idx = (d * 2 + ri) * 4 + cc
                        for h in range(2):
                            k.dma(k.SP, stg[h * 64:(h + 1) * 64, idx, h * 64:(h + 1) * 64],
                                  self.inp[nm][l, d, 2 * cc + h], w=[stg.b()])
            k.cp(k.DVE, self.WBD[:], stg[:], r=[stg.b()], w=[self.WBD.b()])

    def filt(self, l, ctx):
        k, nc, C = self.k, self.nc, self.C
        V, A, P = k.DVE, k.ACT, k.POOL
        VBb = self.VB.b()
        L = CTX if ctx else SEQ
        nt = L // 128
        blocks = [(0, 256)] if ctx else [(i * 512, (i + 1) * 512) for i in range(4)]
        sk = self.SK[:, self.depth if ctx else l, :]
        scale = 2.0 / (2 * L)
        PI = math.pi
        with k.scope() as sc:
            zT = sc.tile("zT", [17, L], F32)
            k.dma(k.SP, zT[:], self.cst["zC" if ctx else "zL"], w=[zT.b()])
            w1 = sc.tile("fw1", [17, 64], F32)
            w2 = sc.tile("fw2", [64, 64], F32)
            w3 = sc.tile("fw3", [64, 1024], F32)
            k.dma(k.SP, w1[:], self.inp["hy_ff_w1"][l], w=[w1.b()])
            k.dma(k.SP, w2[:], self.inp["hy_ff_w2"][l], w=[w2.b()])
            k.dma(k.SP, w3[:], self.inp["hy_ff_w3"][l], w=[w3.b()])
            h1 = sc.tile("fh1", [64, L], F32)
            h2 = sc.tile("fh2", [64, L], F32)
            arg = sc.tile("farg", [64, 512], F32)
            m1 = sc.tile("fm1", [64, 512], F32)
            for stage in range(2):
                wt, src, dst = (w1, zT, h1) if stage == 0 else (w2, h1, h2)
                kk = 17 if stage == 0 else 64
                fr = self.vec("hy_freq0" if stage == 0 else "hy_freq1", l)[0:64, :]
                fb = self.FB[0:64, l, stage:stage + 1]
                for (b0, b1) in blocks:
                    n = b1 - b0
                    bk = k.bank()
                    k.mm(bk[0:64, 0:n], wt[0:kk, :], src[0:kk, b0:b1], True, True, r=[wt.b(), src.b()], w=[bk.b()], sig=True)
                    k.ts(V, arg[:, 0:n], bk[0:64, 0:n], fr, ALU.mult, fb, ALU.add, r=[bk.b(), VBb, self.FB.b()], w=[arg.b()])
                    k.ts(V, m1[:, 0:n], arg[:, 0:n], -PI, ALU.is_lt, 2 * PI, ALU.mult, r=[arg.b()], w=[m1.b()])
                    k.tt(V, m1[:, 0:n], m1[:, 0:n], arg[:, 0:n], ALU.add, r=[m1.b(), arg.b()], w=[m1.b()])
                    k.ts(V, arg[:, 0:n], arg[:, 0:n], PI, ALU.is_gt, -2 * PI, ALU.mult, r=[arg.b()], w=[arg.b()])
                    k.tt(V, arg[:, 0:n], arg[:, 0:n], m1[:, 0:n], ALU.add, r=[m1.b(), arg.b()], w=[arg.b()])
                    k.act(dst[:, b0:b1], arg[:, 0:n], AF.Sin, r=[arg.b()], w=[dst.b()])
            self.dump(f"fh2_{l}_{int(ctx)}", h2[:], [64, L])
            hs = sc.tile("fhs", [128, nt, 512], BF16)
            hd = sc.tile("fhd", [128, nt, 512], BF16)
            dec = sc.tile("fdec", [128, 512], F32)
            hf = sc.tile("fhf", [128, 512], F32)
            hb = sc.tile("fhb", [128, 512], F32)
            for jt in range(nt):
                b0, b1 = k.bank(), k.bank()
                k.mm(b0[:, :], h2[:, jt * 128:(jt + 1) * 128], w3[:, 0:512], True, True, r=[h2.b(), w3.b()], w=[b0.b()], sig=True)
                k.mm(b1[:, :], h2[:, jt * 128:(jt + 1) * 128], w3[:, 512:1024], True, True, r=[h2.b(), w3.b()], w=[b1.b()], sig=True)
                col = (16 if ctx else 0) + jt
                k.act(dec[:], C["delta"][:], AF.Exp, r=[C["delta"].b(), C["negt"].b()], w=[dec.b()], scale=C["negt"][:, col:col + 1])
                k.tt(V, hf[:], b0[:, :], dec[:], ALU.mult, r=[b0.b(), dec.b()], w=[hf.b()])
                k.tt(V, hb[:], b1[:, :], dec[:], ALU.mult, r=[b1.b(), dec.b()], w=[hb.b()])
                if jt == 0:
                    k.memset(V, hb[0:1, :], 0.0, w=[hb.b()])
                k.tt(V, hs[:, jt, :], hf[:], hb[:], ALU.add, r=[hf.b(), hb.b()], w=[hs.b(jt)])
                k.tt(V, hd[:, jt, :], hb[:], hf[:], ALU.subtract, r=[hf.b(), hb.b()], w=[hd.b(jt)])
            FW = None if ctx else [sc.tile(f"ffw{i}", [128, 16, 2, 128], BF16) for i in range(2)]
            KT = [sc.tile(f"fkt{i}", [128, 2, 512], BF16) for i in range(2)]
            dst = self.KSC if ctx else self.KS
            for ft in range(nt):
                if ctx:
                    fw = lambda jt, cs: C["fwdC"][:, jt, cs, ft * 128:(ft + 1) * 128]
                    fwb = C["fwdC"].b()
                else:
                    fwt = FW[ft % 2]
                    k.dma(k.SP, fwt[:], self.cst["fwdL"][ft], w=[fwt.b()])
                    fw = lambda jt, cs: fwt[:, jt, cs, :]
                    fwb = fwt.b()
                bA, bB = k.bank(), k.bank()
                for jt in range(nt):
                    k.mm(bA[:, :], fw(jt, 0), hs[:, jt, :], jt == 0, jt == nt - 1, r=[fwb, hs.b(jt)], w=[bA.b()], sig=(jt == nt - 1))
                for jt in range(nt):
                    k.mm(bB[:, :], fw(jt, 1), hd[:, jt, :], jt == 0, jt == nt - 1, r=[fwb, hd.b(jt)], w=[bB.b()], sig=(jt == nt - 1))
                kt = KT[ft % 2]
                k.stt(kt[:, 0, :], bA[:, :], scale, sk, ALU.mult, ALU.add, r=[bA.b(), self.SK.b()], w=[kt.b()])
                k.act(kt[:, 1, :], bB[:, :], AF.Copy, r=[bB.b()], w=[kt.b()], scale=scale)
                if ctx:
                    k.dma(k.SP, dst.t[ft], kt[:], r=[kt.b()], w=[dst.b(ft)])
                else:
                    k.dma(k.SP, dst.t[l, ft], kt[:], r=[kt.b()], w=[dst.b(l, ft)])

    def resident(self):
        k = self.k
        self.EPSC = k.tile("epsc", [128, 1], F32)
        k.memset(k.DVE, self.EPSC[:], EPS, w=[self.EPSC.b()])
        self.MOD = [k.tile(f"MOD{l}", [128, 6, 8, 3], F32) for l in range(self.depth)]
        self.C1 = k.tile("C1", [128, self.depth, 2, 8], F32)
        self.LB = k.tile("LB", [128, self.depth, 3, 4], F32)

    def load_x(self, s):
        k, C = self.k, self.C
        X = self.X
        with k.scope() as sc:
            XT = [sc.tile(f"xt{i}", [128, D], F32) for i in range(3)]
            for tt in range(T // 128):
                xt = XT[tt % 3]
                src = self.inp["ctx"][s, tt * 128:(tt + 1) * 128, :] if tt < 2 else self.inp["x"][s, (tt - 2) * 128:(tt - 1) * 128, :]
                k.dma(k.SP, xt[:], src, w=[xt.b()])
                tbi = self.tbi(tt * 128)
                for h in range(2):
                    bk = k.bank()
                    for j in range(4):
                        kc = h * 4 + j
                        k.tr(bk[:, j * 128:(j + 1) * 128], xt[:, kc * 128:(kc + 1) * 128], C["ident_f"][:],
                             r=[xt.b(), C["ident_f"].b()], w=[bk.b()], sig=(j == 3))
                    q = k.ACT if h == 0 else k.DVE
                    k.cp(q, X[:, h * 4:(h + 1) * 4, tt * 128:(tt + 1) * 128], bk[:, :].rearrange("p (a b) -> p a b", b=128),
                         r=[bk.b()], w=[X.b(kc2, tbi) for kc2 in range(h * 4, h * 4 + 4)])

    @staticmethod
    def tbi(t):
        for i, (a, b) in enumerate(TB):
            if a <= t < b:
                return i
        raise ValueError(t)

    def rstd_of(self, sc, src3, rbufs, n, ones, nk=8, tag="", sq=None, sqb=None):
        k = self.k
        if sq is None:
            sq = sc.ctile("sq" + tag, [128, nk, 512], BF16)
        if sqb is None:
            sqb = [sq.b()]
        k.act(sq[:, :, 0:n], src3, AF.Square, r=rbufs, w=sqb)
        bk = k.bank()
        for kc in range(nk):
            k.mm(bk[:, 0:n], ones[:], sq[:, kc, 0:n], kc == 0, kc == nk - 1, r=sqb + [ones.b()], w=[bk.b()], sig=(kc == nk - 1))
        rs = sc.ctile("rstd" + tag, [128, 512], F32)
        k.act(rs[:, 0:n], bk[:, 0:n], AF.Ln, r=[bk.b(), self.EPSC.b()], w=[rs.b()], bias=self.EPSC[:, 0:1])
        k.act(rs[:, 0:n], rs[:, 0:n], AF.Exp, r=[rs.b()], w=[rs.b()], scale=-0.5)
        return rs

    def phase_a(self, s, l):
        k, C = self.k, self.C
        X, U, M = self.X, self.U, self.MOD[l]
        with k.scope() as sc:
            for tbi, (t0, t1) in enumerate(TB):
                n = t1 - t0
                col = 2 if tbi == 0 else s
                if True:
                    sc2 = sc
                    rs = self.rstd_of(sc2, X[:, :, t0:t1], [X.b(kc, tbi) for kc in range(8)], n, C["ones_d"])
                    tmp = [sc2.ctile(f"natmp{i}", [128, 512], F32) for i in range(2)]
                    for kc in range(8):
                        tp = tmp[kc % 2]
                        k.stt(tp[:, 0:n], X[:, kc, t0:t1], M[:, 1, kc, col:col + 1], rs[:, 0:n], ALU.mult, ALU.mult,
                              r=[X.b(kc, tbi), M.b(), rs.b()], w=[tp.b()])
                        k.act(U[:, kc, t0:t1], tp[:, 0:n], AF.Identity, r=[tp.b(), M.b()], w=[U.b(kc, tbi)],
                              bias=M[:, 0, kc, col:col + 1])

    def win_tile(self, wt, l, c0, ncol=128):
        k = self.k
        src = self.WB["w_in"]
        k.dma(k.SP, wt[:, :, 0:ncol], src.t[l].rearrange("(kc p) n -> p kc n", p=128)[:, :, c0:c0 + ncol],
              r=[src.b(l)], w=[wt.b()])

    def proj_fm(self, wt, tbi, bk, ncol0=0):
        k, U = self.k, self.U
        t0, t1 = TB[tbi]
        n = t1 - t0
        for kc in range(8):
            k.mm(bk[:, 0:n], wt[:, kc, ncol0:ncol0 + 128], U[:, kc, t0:t1], kc == 0, kc == 7,
                 r=[wt.b(), U.b(kc, tbi)], w=[bk.b()], sig=(kc == 7))
        return n

    def hyena(self, s, l):
        k, C, nc = self.k, self.C, self.nc
        V, A, P = k.DVE, k.ACT, k.POOL
        VBb = self.VB.b()
        need_ctx = l < self.depth - 1
        with k.scope() as sc:
            Utok = sc.tile("Utok", [128, 18, 512], BF16)
            with k.scope() as sc2:
                P1 = [sc2.tile(f"hyP{i}", [128, PT], F32) for i in range(2)]
                acc = sc2.tile("hyacc", [128, PT], F32)
                Zv = sc2.tile("hyZv", [128, PT], F32)
                ubf = [sc2.tile(f"hyu{i}", [128, PT], BF16) for i in range(2)]
                X0S = [sc2.tile(f"hyx0s{i}", [128, T], BF16) for i in range(2)]
                WT = [sc2.tile(f"hyw{i}", [128, 8, 128], BF16) for i in range(3)]
                for p in P1:
                    k.memset(P, p[:], 0.0, w=[p.b()])
                segs = [(0, CTX), (CTX, T)]
                it = 0
                for cc in range(4):
                    ub = ubf[cc % 2]
                    x0s = X0S[cc % 2]
                    for comp in (0, 2, 1):
                        wt = WT[it % 3]
                        p1 = P1[it % 2]
                        it += 1
                        self.win_tile(wt, l, C_HY + comp * 512 + cc * 128)
                        for tbi, (t0, t1) in enumerate(TB):
                            bk = k.bank()
                            n = self.proj_fm(wt, tbi, bk)
                            k.cp(A, p1[:, poff(t0):poff(t0) + n], bk[:, 0:n], r=[bk.b()], w=[p1.b()])
                        ch = comp * 4 + cc
                        w0, w1, w2 = (self.vec("hy_short_w", l, kk * 12 + ch) for kk in range(3))
                        bb = self.vec("hy_short_b", l, ch)
                        for (a0, a1) in segs:
                            q0, q1 = poff(a0), poff(a0) + (a1 - a0)
                            k.act(acc[:, q0:q1], p1[:, q0:q1], AF.Identity, r=[p1.b(), VBb], w=[acc.b()], bias=bb, scale=w1)
                            k.stt(acc[:, q0:q1], p1[:, q0 - 1:q1 - 1], w0, acc[:, q0:q1], ALU.mult, ALU.add,
                                  r=[p1.b(), acc.b(), VBb], w=[acc.b()])
                            if comp == 0:
                                k.stt(Zv[:, q0:q1], p1[:, q0 + 1:q1 + 1], w2, acc[:, q0:q1], ALU.mult, ALU.add,
                                      r=[p1.b(), acc.b(), VBb], w=[Zv.b()])
                            elif comp == 2:
                                k.stt(acc[:, q0:q1], p1[:, q0 + 1:q1 + 1], w2, acc[:, q0:q1], ALU.mult, ALU.add,
                                      r=[p1.b(), acc.b(), VBb], w=[acc.b()])
                                k.tt(V, ub[:, q0:q1], acc[:, q0:q1], Zv[:, q0:q1], ALU.mult, r=[acc.b(), Zv.b()], w=[ub.b()])
                            else:
                                k.stt(x0s[:, a0:a1], p1[:, q0 + 1:q1 + 1], w2, acc[:, q0:q1], ALU.mult, ALU.add,
                                      r=[p1.b(), acc.b(), VBb], w=[x0s.b()])
                    k.dma(k.SP, self.YP.t[0, :, cc, :], x0s[:], r=[x0s.b()], w=[self.YP.b(0, cc, i) for i in range(5)])
                    for g0 in range(0, 18, 8):
                        g1 = min(18, g0 + 8)
                        bk = k.bank()
                        bv = bk[:, :].bitcast(BF16)
                        for tt in range(g0, g1):
                            k.tr(bv[:, (tt - g0) * 128:(tt - g0 + 1) * 128], ub[:, poff(tt * 128):poff(tt * 128) + 128], C["ident_b"][:],
                                 r=[ub.b(), C["ident_b"].b()], w=[bk.b()], sig=(tt == g1 - 1))
                        k.cp(A, Utok[:, g0:g1, cc * 128:(cc + 1) * 128],
                             bv[:, 0:(g1 - g0) * 128].rearrange("p (a b) -> p a b", b=128), r=[bk.b()], w=[Utok.b(cc)])
            self.dump("utok", Utok[:], [128, 18, 512], BF16)
            ut_bufs = [Utok.b(cc) for cc in range(4)]
            YP = self.YP
            if need_ctx:
                with k.scope() as sc4:
                    KT = [sc4.tile(f"hykt{i}", [128, 2, 512], BF16) for i in range(2)]
                    AB = [sc4.tile(f"hyab{i}", [128, 2, 512], BF16) for i in range(2)]
                    TM = [sc4.tile(f"hytm{i}", [128, 512], F32) for i in range(4)]
                    Yc = sc4.tile("hyYc", [128, 2, 2, 512], BF16)
                    x0c = sc4.tile("hyx0c", [128, 4, CTX], BF16)
                    k.dma(k.SP, x0c[:], YP.t[0, :, :, 0:CTX], r=[YP.b(0, cc, 0) for cc in range(4)], w=[x0c.b()])
                    for ft in range(2):
                        kt, ab = KT[ft % 2], AB[ft % 2]
                        k.dma(k.SP, kt[:], self.KSC.t[ft], r=[self.KSC.b(ft)], w=[kt.b()])
                        bA, bB = k.bank(), k.bank()
                        for cs, bk in ((0, bA), (1, bB)):
                            for st in range(2):
                                k.mm(bk[:, :], C["fwdC"][:, st, cs, ft * 128:(ft + 1) * 128], Utok[:, st, :], st == 0, st == 1,
                                     r=[C["fwdC"].b()] + ut_bufs, w=[bk.b()], sig=(st == 1))
                        k.cp(A, ab[:, 0, :], bA[:, :], r=[bA.b()], w=[ab.b(0)])
                        k.cp(A, ab[:, 1, :], bB[:, :], r=[bB.b()], w=[ab.b(1)])
                        self.spec_mul(Yc[:, ft, 0, :], Yc[:, ft, 1, :], ab[:, 0, :], ab[:, 1, :], kt[:, 0, :], kt[:, 1, :],
                                      [t[:] for t in TM], [ab.b(0), ab.b(1), kt.b()], TM, [Yc.b(ft)])
                    for cc in range(4):
                        bk = k.bank()
                        i = 0
                        for ft in range(2):
                            for cs in range(2):
                                k.mm(bk[:, 0:CTX], Yc[:, ft, cs, cc * 128:(cc + 1) * 128], C["invC"][:, ft, cs, :], i == 0, i == 3,
                                     r=[Yc.b(ft), C["invC"].b()], w=[bk.b()], sig=(i == 3))
                                i += 1
                        k.tt(V, x0c[:, cc, :], bk[:, 0:CTX], x0c[:, cc, :], ALU.mult, r=[bk.b(), x0c.b()], w=[x0c.b()])
                    k.dma(k.SP, YP.t[0, :, :, 0:CTX], x0c[:], r=[x0c.b()], w=[YP.b(0, cc, 0) for cc in range(4)])
            with k.scope() as sc3:
                DB = [sc3.tile(f"hydb{i}", [128, 4096], BF16) for i in range(2)]
                KT = [sc3.tile(f"hykt{i}", [128, 2, 512], BF16) for i in range(2)]
                AB = [sc3.tile(f"hyab{i}", [128, 2, 256], BF16) for i in range(2)]
                TM = [sc3.tile(f"hytm{i}", [128, 256], F32) for i in range(4)]
                Yf = sc3.tile("hyYf", [128, 16, 2, 256], BF16)
                XB = [sc3.tile(f"hyxb{i}", [128, 512], BF16) for i in range(4)]
                dbi = 0
                xbi = 0
                for half in range(2):
                    h0 = half * 256
                    for ft in range(16):
                        fwt = DB[dbi % 2]
                        dbi += 1
                        fw = fwt[:, :].rearrange("p (a b c) -> p a b c", a=16, b=2)
                        kt, ab = KT[ft % 2], AB[ft % 2]
                        k.dma(k.SP, fwt[:, :], self.cst["fwdL"][ft].rearrange("p a b c -> p (a b c)"), w=[fwt.b()])
                        k.dma(k.SP, kt[:], self.KS.t[l, ft], r=[self.KS.b(l, ft)], w=[kt.b()])
                        bk = k.bank()
                        for cs in range(2):
                            for st in range(16):
                                k.mm(bk[:, cs * 256:(cs + 1) * 256], fw[:, st, cs, :], Utok[:, 2 + st, h0:h0 + 256], st == 0, st == 15,
                                     r=[fwt.b(), Utok.b(half * 2), Utok.b(half * 2 + 1)], w=[bk.b()], sig=(st == 15))
                        k.cp(A, ab[:, :, :], bk[:, :].rearrange("p (a b) -> p a b", a=2), r=[bk.b()], w=[ab.b()])
                        self.spec_mul(Yf[:, ft, 0, :], Yf[:, ft, 1, :], ab[:, 0, :], ab[:, 1, :], kt[:, 0, h0:h0 + 256], kt[:, 1, h0:h0 + 256],
                                      [t[:] for t in TM], [ab.b(), kt.b()], TM, [Yf.b(ft)])
                    bks = [[k.bank() for tb in range(4)] for c2 in range(2)]
                    for ft in range(16):
                        ivt = DB[dbi % 2]
                        dbi += 1
                        iv = ivt[:, :].rearrange("p (a b) -> p a b", a=2)
                        k.dma(k.SP, ivt[:, :], self.cst["invL"][ft].rearrange("p a b -> p (a b)"), w=[ivt.b()])
                        for c2 in range(2):
                            for tb in range(4):
                                bk = bks[c2][tb]
                                for cs in range(2):
                                    k.mm(bk[:, :], Yf[:, ft, cs, c2 * 128:(c2 + 1) * 128], iv[:, cs, tb * 512:(tb + 1) * 512],
                                         ft == 0 and cs == 0, ft == 15 and cs == 1, r=[Yf.b(ft), ivt.b()], w=[bk.b()],
                                         sig=(cs == 1 and tb == 3 and c2 == 1))
                    for c2 in range(2):
                        cc = half * 2 + c2
                        for tb in range(4):
                            t0 = CTX + tb * 512
                            xb = XB[xbi % 4]
                            xbi += 1
                            k.dma(k.SP, xb[:], YP.t[0, :, cc, t0:t0 + 512], r=[YP.b(0, cc, tb + 1)], w=[xb.b()])
                            k.tt(V, xb[:], bks[c2][tb][:, :], xb[:], ALU.mult, r=[bks[c2][tb].b(), xb.b()], w=[xb.b()])
                            k.dma(k.SP, YP.t[0, :, cc, t0:t0 + 512], xb[:], r=[xb.b()], w=[YP.b(0, cc, tb + 1)])

    def lru(self, s, l):
        k, C = self.k, self.C
        V, A, P = k.DVE, k.ACT, k.POOL
        VBb = self.VB.b()
        YP = self.YP
        with k.scope() as sc:
            self.build_wbd(sc, l)
            xp = sc.tile("lrxp", [128, PT], F32)
            k.memset(P, xp[:], 0.0, w=[xp.b()])
            hsum = sc.tile("lrhs", [128, T], F32)
            xc = sc.tile("lrxc", [128, T], F32)
            xcb = sc.tile("lrxcb", [128, T], BF16)
            R = sc.tile("lrR", [128, T], F32)
            I = sc.tile("lrI", [128, T], F32)
            Aa = sc.tile("lrA", [128, T], F32)
            YS = [sc.tile(f"lrys{i}", [128, T], BF16) for i in range(2)]
            GG = [sc.tile(f"lrgg{i}", [128, 512], BF16) for i in range(2)]
            WX = [sc.tile(f"lrwx{i}", [128, 8, 128], BF16) for i in range(2)]
            WG = [sc.tile(f"lrwg{i}", [128, 8, 128], BF16) for i in range(2)]
            segs = [(0, CTX), (CTX, T)]
            for cc in range(4):
                wx, wg, ys = WX[cc % 2], WG[cc % 2], YS[cc % 2]
                self.win_tile(wx, l, C_LX + cc * 128)
                self.win_tile(wg, l, C_LG + cc * 128)
                for tbi, (t0, t1) in enumerate(TB):
                    bk = k.bank()
                    n = self.proj_fm(wx, tbi, bk)
                    k.cp(A, xp[:, poff(t0):poff(t0) + n], bk[:, 0:n], r=[bk.b()], w=[xp.b()])
                for d in range(2):
                    sg = -1 if d == 0 else 1
                    ch = d * 4 + cc
                    wk = [self.vec("lru_conv_w", l, (d * 4 + kk) * 4 + cc) for kk in range(4)]
                    bb = self.vec("lru_conv_b", l, ch)
                    for (a0, a1) in segs:
                        q0, q1 = poff(a0), poff(a0) + (a1 - a0)
                        k.act(xc[:, a0:a1], xp[:, q0:q1], AF.Identity, r=[xp.b(), VBb], w=[xc.b()], bias=bb, scale=wk[3])
                        for j in (1, 2, 3):
                            k.stt(xc[:, a0:a1], xp[:, q0 + sg * j:q1 + sg * j], wk[3 - j], xc[:, a0:a1], ALU.mult, ALU.add,
                                  r=[xp.b(), xc.b(), VBb], w=[xc.b()])
                    k.cp(P, xcb[:], xc[:], r=[xc.b()], w=[xcb.b()])
                    for tbi, (t0, t1) in enumerate(TB):
                        n = t1 - t0
                        b0, b1 = k.bank(), k.bank()
                        k.mm(b0[:, 0:n], self.wbd(l, d, 0, cc), xcb[:, t0:t1], True, True, r=[self.WBD.b(), xcb.b()], w=[b0.b()], sig=True)
                        k.mm(b1[:, 0:n], self.wbd(l, d, 1, cc), xcb[:, t0:t1], True, True, r=[self.WBD.b(), xcb.b()], w=[b1.b()], sig=True)
                        k.act(R[:, t0:t1], b0[:, 0:n], AF.Sigmoid, r=[b0.b(), VBb], w=[R.b()], bias=self.vec("lru_br", l, ch))
                        k.act(I[:, t0:t1], b1[:, 0:n], AF.Sigmoid, r=[b1.b(), VBb], w=[I.b()], bias=self.vec("lru_bi", l, ch))
                    k.act(Aa[:], R[:], AF.Exp, r=[R.b(), self.C1.b()], w=[Aa.b()], scale=self.C1[:, l, 0, ch:ch + 1])
                    k.act(R[:], R[:], AF.Exp, r=[R.b(), self.C1.b()], w=[R.b()], scale=self.C1[:, l, 1, ch:ch + 1])
                    k.act(R[:], R[:], AF.Sqrt, r=[R.b()], w=[R.b()], bias=1.0, scale=-1.0)
                    k.tt(P, I[:], I[:], xc[:], ALU.mult, r=[I.b(), xc.b()], w=[I.b()])
                    k.tt(V, I[:], I[:], R[:], ALU.mult, r=[I.b(), R.b()], w=[I.b()])
                    if d == 0:
                        k.op(V, lambda: V.eng.tensor_tensor_scan(out=hsum[:], data0=Aa[:], data1=I[:], initial=0.0, op0=ALU.mult, op1=ALU.add),
                             r=[Aa.b(), I.b()], w=[hsum.b()])
                    else:
                        k.op(V, lambda: V.eng.tensor_tensor_scan(out=I[:, 0:CTX][:, ::-1], data0=Aa[:, 0:CTX][:, ::-1], data1=I[:, 0:CTX][:, ::-1],
                                                                 initial=0.0, op0=ALU.mult, op1=ALU.add), r=[Aa.b(), I.b()], w=[I.b()])
                        k.op(V, lambda: V.eng.tensor_tensor_scan(out=I[:, CTX:T][:, ::-1], data0=Aa[:, CTX:T][:, ::-1], data1=I[:, CTX:T][:, ::-1],
                                                                 initial=I[:, 0:1], op0=ALU.mult, op1=ALU.add), r=[Aa.b(), I.b()], w=[I.b()])
                        k.tt(P, hsum[:], hsum[:], I[:], ALU.add, r=[hsum.b(), I.b()], w=[hsum.b()])
                for tbi, (t0, t1) in enumerate(TB):
                    n = t1 - t0
                    bk = k.bank()
                    self.proj_fm(wg, tbi, bk)
                    gg = GG[tbi % 2]
                    k.act(gg[:, 0:n], bk[:, 0:n], AF.Gelu_apprx_tanh, r=[bk.b()], w=[gg.b()])
                    k.tt(V, ys[:, t0:t1], hsum[:, t0:t1], gg[:, 0:n], ALU.mult, r=[hsum.b(), gg.b()], w=[ys.b()])
                k.dma(k.SP, YP.t[1, :, cc, :], ys[:], r=[ys.b()], w=[YP.b(1, cc, i) for i in range(5)])

    def hgrn(self, s, l):
        k, C = self.k, self.C
        V, A, P = k.DVE, k.ACT, k.POOL
        VBb = self.VB.b()
        YP, U = self.YP, self.U
        NCH = T // 64
        with k.scope() as sc:
            qb = sc.tile("hgq", [128, T], BF16)
            Vtok = sc.tile("hgV", [128, 18, 128], BF16)
            O = sc.tile("hgO", [128, T], F32)
            Bc = sc.tile("hgB", [128, T], F32)
            kkb = sc.tile("hgkk", [128, T], BF16)
            CH = sc.tile("hgCH", [128, 6, NCH], F32)
            S = sc.tile("hgS", [128, 128], F32)
            SB = [sc.tile(f"hgSb{i}", [128, 128], BF16) for i in range(4)]
            Dt = sc.tile("hgD", [128, 512], F32)
            E1 = sc.tile("hgE1", [128, 512], F32)
            E2 = sc.tile("hgE2", [128, 512], F32)
            QT = [sc.tile(f"hgqt{i}", [128, 512], BF16) for i in range(2)]
            KT = [sc.tile(f"hgkt{i}", [128, 512], BF16) for i in range(2)]
            QH = [sc.tile(f"hgqh{i}", [128, 512], BF16) for i in range(2)]
            KH = [sc.tile(f"hgkh{i}", [128, 512], BF16) for i in range(2)]
            PM = [sc.tile(f"hgpm{i}", [128, 128], BF16) for i in range(3)]
            KK = [sc.tile(f"hgkhk{i}", [128, 128], BF16) for i in range(3)]
            ys = sc.tile("hgys", [128, T], BF16)
            SG = [sc.tile(f"hgsg{i}", [128, 512], F32) for i in range(2)]
            TMP = [sc.tile(f"hgtmp{i}", [128, 512], F32) for i in range(2)]
            WT = [sc.tile(f"hgw{i}", [128, 8, 128], BF16) for i in range(6)]
            wi_ = 0
            blk_i = 0
            pr_i = 0
            for hd in range(4):
                ws = {}
                for nm, c0 in (("q", C_Q), ("ff", C_FF), ("fb", C_FB), ("i", C_I), ("og", C_OG)):
                    ws[nm] = WT[wi_ % 6]
                    wi_ += 1
                    self.win_tile(ws[nm], l, c0 + hd * 128)
                for tbi, (t0, t1) in enumerate(TB):
                    bk = k.bank()
                    n = self.proj_fm(ws["q"], tbi, bk)
                    k.act(qb[:, t0:t1], bk[:, 0:n], AF.Silu, r=[bk.b()], w=[qb.b()])
                for g0 in range(0, 18, 4):
                    g1 = min(18, g0 + 4)
                    bk = k.bank()
                    for tt in range(g0, g1):
                        tbi = self.tbi(tt * 128)
                        for kc in range(8):
                            k.mm(bk[:, (tt - g0) * 128:(tt - g0 + 1) * 128], U[:, kc, tt * 128:(tt + 1) * 128], ws["i"][:, kc, :],
                                 kc == 0, kc == 7, r=[U.b(kc, tbi), ws["i"].b()], w=[bk.b()], sig=(kc == 7))
                    k.cp(A, Vtok[:, g0:g1, :], bk[:, 0:(g1 - g0) * 128].rearrange("p (a b) -> p a b", b=128), r=[bk.b()], w=[Vtok.b()])
                if getattr(self, "hg_cut", 0) == 1:
                    return
                for d in range(2):
                    wf = ws["ff"] if d == 0 else ws["fb"]
                    for tbi, (t0, t1) in enumerate(TB):
                        bk = k.bank()
                        n = self.proj_fm(wf, tbi, bk)
                        k.act(Bc[:, t0:t1], bk[:, 0:n], AF.Sigmoid, r=[bk.b()], w=[Bc.b()])
                    lbv, omlv, nomlv = (self.LB[:, l, i, hd:hd + 1] for i in range(3))
                    k.ts(V, kkb[:], Bc[:], nomlv, ALU.mult, omlv, ALU.add, r=[Bc.b(), self.LB.b()], w=[kkb.b()])
                    k.ts(V, Bc[:], Bc[:], omlv, ALU.mult, lbv, ALU.add, r=[Bc.b(), self.LB.b()], w=[Bc.b()])
                    k.act(Bc[:], Bc[:], AF.Ln, r=[Bc.b()], w=[Bc.b()])
                    ones = C["onesrow"]
                    if d == 0:
                        order = [0, 1, 2, 3, 4]
                    else:
                        order = [0, 4, 3, 2, 1]
                    prev = None
                    for tbi in order:
                        t0, t1 = TB[tbi]
                        n = t1 - t0
                        seg = Bc[:, t0:t1] if d == 0 else Bc[:, t0:t1][:, ::-1]
                        init = 0.0 if prev is None else prev
                        k.op(V, lambda seg=seg, n=n, init=init: V.eng.tensor_tensor_scan(out=seg, data0=ones[:, 0:n], data1=seg, initial=init,
                                                                                      op0=ALU.mult, op1=ALU.add), r=[Bc.b(), ones.b()], w=[Bc.b()])
                        prev = Bc[:, t1 - 1:t1] if d == 0 else Bc[:, t0:t0 + 1]
                    if d == 0:
                        k.cp(V, CH[:, 0, :], Bc[:, 32::64], r=[Bc.b()], w=[CH.b()])
                        k.cp(V, CH[:, 1, :], Bc[:, 63::64], r=[Bc.b()], w=[CH.b()])
                        k.memset(V, CH[:, 2, 0:1], 0.0, w=[CH.b()])
                        k.cp(V, CH[:, 2, 1:NCH], CH[:, 1, 0:NCH - 1], r=[CH.b()], w=[CH.b()])
                    else:
                        k.cp(V, CH[:, 0, :], Bc[:, 31::64], r=[Bc.b()], w=[CH.b()])
                        k.cp(V, CH[:, 1, :], Bc[:, 0::64], r=[Bc.b()], w=[CH.b()])
                        k.cp(V, CH[:, 2, 0:NCH - 1], CH[:, 1, 1:NCH], r=[CH.b()], w=[CH.b()])
                        k.memset(V, CH[:, 2, 3:4], 0.0, w=[CH.b()])
                        k.cp(V, CH[:, 2, NCH - 1:NCH], CH[:, 1, 0:1], r=[CH.b()], w=[CH.b()])
                    k.tt(V, CH[:, 3, :], CH[:, 0, :], CH[:, 2, :], ALU.subtract, r=[CH.b()], w=[CH.b()])
                    k.tt(V, CH[:, 4, :], CH[:, 1, :], CH[:, 0, :], ALU.subtract, r=[CH.b()], w=[CH.b()])
                    k.tt(V, CH[:, 5, :], CH[:, 1, :], CH[:, 2, :], ALU.subtract, r=[CH.b()], w=[CH.b()])
                    k.act(CH[:, 3:6, :], CH[:, 3:6, :], AF.Exp, r=[CH.b()], w=[CH.b()])
                    if getattr(self, "hg_cut", 0) == 2:
                        return
                    k.memset(V, S[:], 0.0, w=[S.b()])
                    sbi = 0
                    sb_cur = SB[sbi % 4]
                    k.memset(V, sb_cur[:], 0.0, w=[sb_cur.b()])
                    mask = C["maskF"] if d == 0 else C["maskB"]
                    for tbi in order:
                        t0, t1 = TB[tbi]
                        n = t1 - t0
                        c0, nch = t0 // 64, n // 64
                        qt, kt_, qh, kh = QT[blk_i % 2], KT[blk_i % 2], QH[blk_i % 2], KH[blk_i % 2]
                        blk_i += 1

                        def bc(row):
                            return CH[:, row, c0:c0 + nch].unsqueeze(2).to_broadcast([128, nch, 64])

                        def v3(ap):
                            return ap.rearrange("p (a b) -> p a b", b=64)
                        k.tt(V, v3(Dt[:, 0:n]), v3(Bc[:, t0:t1]), bc(0), ALU.subtract, r=[Bc.b(), CH.b()], w=[Dt.b()])
                        k.act(E1[:, 0:n], Dt[:, 0:n], AF.Exp, r=[Dt.b()], w=[E1.b()])
                        k.act(E2[:, 0:n], Dt[:, 0:n], AF.Exp, r=[Dt.b()], w=[E2.b()], scale=-1.0)
                        k.tt(V, qt[:, 0:n], qb[:, t0:t1], E1[:, 0:n], ALU.mult, r=[qb.b(), E1.b()], w=[qt.b()])
                        k.tt(P, kt_[:, 0:n], kkb[:, t0:t1], E2[:, 0:n], ALU.mult, r=[kkb.b(), E2.b()], w=[kt_.b()])
                        k.tt(P, v3(E1[:, 0:n]), v3(E1[:, 0:n]), bc(3), ALU.mult, r=[E1.b(), CH.b()], w=[E1.b()])
                        k.tt(V, v3(E2[:, 0:n]), v3(E2[:, 0:n]), bc(4), ALU.mult, r=[E2.b(), CH.b()], w=[E2.b()])
                        k.tt(V, qh[:, 0:n], qb[:, t0:t1], E1[:, 0:n], ALU.mult, r=[qb.b(), E1.b()], w=[qh.b()])
                        k.tt(P, kh[:, 0:n], kkb[:, t0:t1], E2[:, 0:n], ALU.mult, r=[kkb.b(), E2.b()], w=[kh.b()])
                        if getattr(self, "hg_cut", 0) == 3:
                            return
                        npair = n // 128
                        pairs = list(range(npair)) if d == 0 else list(range(npair - 1, -1, -1))
                        def make_pair(j, t0=t0, qt=qt, kt_=kt_, qh=qh, kh=kh):
                            nonlocal pr_i
                            o = j * 128
                            tp = t0 + o
                            tt = tp // 128
                            pm, kk_ = PM[pr_i % 3], KK[pr_i % 3]
                            pr_i += 1
                            st = {}

                            def front():
                                b_sc, b_tr, b_kv, b_kv2 = k.bank(), k.bank(), k.bank(), k.bank()
                                st["kvb"] = [b_kv, b_kv2]
                                k.mm(b_sc[:, 0:128], kt_[:, o:o + 128], qt[:, o:o + 128], True, True, r=[kt_.b(), qt.b()], w=[b_sc.b()], sig=True)
                                k.memset(P, pm[:], 0.0, w=[pm.b()])
                                k.op(V, lambda: V.eng.copy_predicated(out=pm[:], mask=mask[:], data=b_sc[:, 0:128]),
                                     r=[b_sc.b(), mask.b(), pm.b()], w=[pm.b()])
                                btv = b_tr[:, :].bitcast(BF16)
                                k.tr(btv[:, 0:128], kh[:, o:o + 128], C["ident_b"][:], r=[kh.b(), C["ident_b"].b()], w=[b_tr.b()])
                                k.cp(A, kk_[:], btv[:, 0:128], r=[b_tr.b()], w=[kk_.b()])
                                k.mm(b_kv[:, 0:128], kk_[0:64, :], Vtok[0:64, tt, :], True, True, r=[kk_.b(), Vtok.b()], w=[b_kv.b()], sig=True)
                                k.mm(b_kv2[:, 0:128], kk_[64:128, :], Vtok[64:128, tt, :], True, True, r=[kk_.b(), Vtok.b()], w=[b_kv2.b()], sig=True)

                            def back():
                                nonlocal sbi, sb_cur
                                kvb = st["kvb"]
                                halves = [0, 1] if d == 0 else [1, 0]
                                sb_start = {}
                                for hh in halves:
                                    ch = (tp // 64) + hh
                                    sb_start[hh] = sb_cur
                                    k.stt(S[:], S[:], CH[:, 5, ch:ch + 1], kvb[hh][:, 0:128], ALU.mult, ALU.add,
                                          r=[S.b(), CH.b(), kvb[hh].b()], w=[S.b()])
                                    sbi += 1
                                    sb_cur = SB[sbi % 4]
                                    k.cp(A, sb_cur[:], S[:], r=[S.b()], w=[sb_cur.b()])
                                b_o = k.bank()
                                k.mm(b_o[:, 0:128], Vtok[:, tt, :], pm[:], True, False, r=[Vtok.b(), pm.b()], w=[b_o.b()], sig=False)
                                k.mm(b_o[:, 0:64], sb_start[0][:], qh[:, o:o + 64], False, False, r=[sb_start[0].b(), qh.b()], w=[b_o.b()], sig=False)
                                k.mm(b_o[:, 64:128], sb_start[1][:], qh[:, o + 64:o + 128], False, True, r=[sb_start[1].b(), qh.b()], w=[b_o.b()], sig=True)
                                if d == 0:
                                    k.cp(A, O[:, tp:tp + 128], b_o[:, 0:128], r=[b_o.b()], w=[O.b(tt)])
                                else:
                                    k.tt(V, O[:, tp:tp + 128], b_o[:, 0:128], O[:, tp:tp + 128], ALU.add, r=[b_o.b(), O.b(tt)], w=[O.b(tt)])
                            return front, back
                        prs = [make_pair(j) for j in pairs]
                        prs[0][0]()
                        for pi in range(len(prs)):
                            if pi + 1 < len(prs):
                                prs[pi + 1][0]()
                            prs[pi][1]()
                ng = self.vec("hg_norm_g", l)
                for tbi, (t0, t1) in enumerate(TB):
                    n = t1 - t0
                    if True:
                        sc2 = sc
                        obufs = [O.b(tt) for tt in range(t0 // 128, t1 // 128)]
                        rs = self.rstd_of(sc2, O[:, t0:t1].unsqueeze(1), obufs, n, C["ones_v"], nk=1, tag="hg")
                        bk = k.bank()
                        self.proj_fm(ws["og"], tbi, bk)
                        sg, tmp = SG[tbi % 2], TMP[tbi % 2]
                        k.act(sg[:, 0:n], bk[:, 0:n], AF.Silu, r=[bk.b()], w=[sg.b()])
                        k.stt(tmp[:, 0:n], O[:, t0:t1], ng, rs[:, 0:n], ALU.mult, ALU.mult, r=obufs + [VBb, rs.b()], w=[tmp.b()])
                        k.tt(V, ys[:, t0:t1], tmp[:, 0:n], sg[:, 0:n], ALU.mult, r=[tmp.b(), sg.b()], w=[ys.b()])
                k.dma(k.SP, YP.t[2, :, hd, :], ys[:], r=[ys.b()], w=[YP.b(2, hd, i) for i in range(5)])

    def w_tile(self, wt, name, l, c0, nkc, ncol=128, k0=0):
        k = self.k
        src = self.WB[name]
        k.dma(k.SP, wt[:, 0:nkc, 0:ncol],
              src.t[l].rearrange("(kc p) n -> p kc n", p=128)[:, k0:k0 + nkc, c0:c0 + ncol], r=[src.b(l)], w=[wt.b()])

    def merge(self, s, l):
        k, C = self.k, self.C
        V, A, P = k.DVE, k.ACT, k.POOL
        X, U, M, YP = self.X, self.U, self.MOD[l], self.YP
        last = l == self.depth - 1
        pnames = ("w_proj_hy", "w_proj_lru", "w_proj_hg")
        with k.scope() as sc:
            YB = [sc.tile(f"mgy{i}", [128, 4, 512], BF16) for i in range(3)]
            mbf = sc.tile("mgm", [128, 8, 512], BF16)
            MO = sc.tile("mgmo", [128, 8, 512], F32)
            G = [sc.tile(f"mgg{i}", [128, 512], F32) for i in range(2)]
            macc = sc.tile("mgacc", [128, 512], F32)
            T2 = [sc.tile(f"mgt{i}", [128, 512], F32) for i in range(2)]
            WG = [sc.tile(f"mgwg{i}", [128, 8, 128], BF16) for i in range(6)]
            WP = [sc.tile(f"mgwp{i}", [128, 4, 128], BF16) for i in range(6)]
            WO = [sc.tile(f"mgwo{i}", [128, 8, 128], BF16) for i in range(2)]
            it = 0
            for tbi, (t0, t1) in enumerate(TB):
                if last and tbi == 0:
                    continue
                n = t1 - t0
                col = 2 if tbi == 0 else s
                for br in range(3):
                    k.dma(k.SP, YB[br][:, :, 0:n], YP.t[br, :, :, t0:t1], r=[YP.b(br, cc, tbi) for cc in range(4)], w=[YB[br].b()])
                for oc in range(8):
                    for br in range(3):
                        wg, wp = WG[it % 6], WP[it % 6]
                        g, t2 = G[it % 2], T2[it % 2]
                        it += 1
                        self.win_tile(wg, l, C_MG + br * 1024 + oc * 128)
                        self.w_tile(wp, pnames[br], l, oc * 128, 4)
                        bp, bg = k.bank(), k.bank()
                        for kc in range(4):
                            k.mm(bp[:, 0:n], wp[:, kc, :], YB[br][:, kc, 0:n], kc == 0, kc == 3, r=[wp.b(), YB[br].b()], w=[bp.b()], sig=(kc == 3))
                        self.proj_fm(wg, tbi, bg)
                        k.act(g[:, 0:n], bg[:, 0:n], AF.Sigmoid, r=[bg.b()], w=[g.b()])
                        if br == 0:
                            k.tt(V, macc[:, 0:n], bp[:, 0:n], g[:, 0:n], ALU.mult, r=[bp.b(), g.b()], w=[macc.b()])
                        elif br == 1:
                            k.tt(V, t2[:, 0:n], bp[:, 0:n], g[:, 0:n], ALU.mult, r=[bp.b(), g.b()], w=[t2.b()])
                            k.tt(P, macc[:, 0:n], macc[:, 0:n], t2[:, 0:n], ALU.add, r=[macc.b(), t2.b()], w=[macc.b()])
                        else:
                            k.tt(V, t2[:, 0:n], bp[:, 0:n], g[:, 0:n], ALU.mult, r=[bp.b(), g.b()], w=[t2.b()])
                            k.tt(P, mbf[:, oc, 0:n], macc[:, 0:n], t2[:, 0:n], ALU.add, r=[macc.b(), t2.b()], w=[mbf.b(oc)])
                for oc in range(8):
                    wo = WO[oc % 2]
                    self.w_tile(wo, "w_out", l, oc * 128, 8)
                    bk = k.bank()
                    for kc in range(8):
                        k.mm(bk[:, 0:n], wo[:, kc, :], mbf[:, kc, 0:n], kc == 0, kc == 7, r=[wo.b(), mbf.b(kc)], w=[bk.b()], sig=(kc == 7))
                    k.cp(A, MO[:, oc, 0:n], bk[:, 0:n], r=[bk.b()], w=[MO.b(oc)])
                self.resid_update(sc, MO, mbf, n, tbi, t0, t1, M, 2, col, sqb=[mbf.b(oc) for oc in range(8)])

    def resid_update(self, sc, MO, sqt, n, tbi, t0, t1, M, gi, col, sqb=None):
        k, C = self.k, self.C
        X = self.X
        if True:
            sc2 = sc
            rs = self.rstd_of(sc2, MO[:, 0:8, 0:n], [MO.b(kc) for kc in range(8)], n, C["ones_d"], sq=sqt, sqb=sqb)
            tmp = [sc2.ctile(f"rutmp{i}", [128, 512], F32) for i in range(2)]
            for kc in range(8):
                tp = tmp[kc % 2]
                k.stt(tp[:, 0:n], MO[:, kc, 0:n], M[:, gi, kc, col:col + 1], rs[:, 0:n], ALU.mult, ALU.mult,
                      r=[MO.b(kc), M.b(), rs.b()], w=[tp.b()])
                k.tt(k.POOL, X[:, kc, t0:t1], X[:, kc, t0:t1], tp[:, 0:n], ALU.add, r=[X.b(kc, tbi), tp.b()], w=[X.b(kc, tbi)])

    def mlp(self, s, l):
        k, C = self.k, self.C
        V, A, P = k.DVE, k.ACT, k.POOL
        X, U, M = self.X, self.U, self.MOD[l]
        last = l == self.depth - 1
        with k.scope() as sc:
            H = sc.tile("mlH", [128, 32, 512], BF16)
            MO = Tile(H.t[:, 0:16, :].rearrange("p a b -> p (a b)").bitcast(F32).rearrange("p (a b) -> p a b", b=512), "mlMO")
            MO.b = lambda *key: H.b("mo")
            sqt = sc.tile("mlsq", [128, 8, 512], BF16)
            SQ = [sc.tile(f"mlsqr{i}", [128, 512], F32) for i in range(2)]
            W1 = [sc.tile(f"mlw1{i}", [128, 8, 512], BF16) for i in range(2)]
            W2 = [sc.tile(f"mlw2{i}", [128, 2, 1024], BF16) for i in range(2)]
            hb_all = [H.b(j) for j in range(32)] + [H.b("mo")]
            for tbi, (t0, t1) in enumerate(TB):
                if last and tbi == 0:
                    continue
                n = t1 - t0
                col = 2 if tbi == 0 else s
                if True:
                    sc2 = sc
                    rs = self.rstd_of(sc2, X[:, :, t0:t1], [X.b(kc, tbi) for kc in range(8)], n, C["ones_d"], sq=sqt)
                    tmp = [sc2.ctile(f"mltmp{i}", [128, 512], F32) for i in range(2)]
                    for kc in range(8):
                        tp = tmp[kc % 2]
                        k.stt(tp[:, 0:n], X[:, kc, t0:t1], M[:, 4, kc, col:col + 1], rs[:, 0:n], ALU.mult, ALU.mult,
                              r=[X.b(kc, tbi), M.b(), rs.b()], w=[tp.b()])
                        k.act(U[:, kc, t0:t1], tp[:, 0:n], AF.Identity, r=[tp.b(), M.b()], w=[U.b(kc, tbi)], bias=M[:, 3, kc, col:col + 1])
                for jg in range(8):
                    w1 = W1[jg % 2]
                    self.w_tile(w1, "w_mlp1", l, jg * 512, 8, ncol=512)
                    for jj in range(4):
                        j = jg * 4 + jj
                        bk = k.bank()
                        for kc in range(8):
                            k.mm(bk[:, 0:n], w1[:, kc, jj * 128:(jj + 1) * 128], U[:, kc, t0:t1], kc == 0, kc == 7,
                                 r=[w1.b(), U.b(kc, tbi)], w=[bk.b()], sig=(kc == 7))
                        sq = SQ[j % 2]
                        k.act(sq[:, 0:n], bk[:, 0:n], AF.Square, r=[bk.b()], w=[sq.b()])
                        k.stt(H[:, j, 0:n], bk[:, 0:n], 0.0, sq[:, 0:n], ALU.is_gt, ALU.mult, r=[bk.b(), sq.b()], w=[H.b(j), H.b("mo")])
                bks = [k.bank() for _ in range(8)]
                for jg in range(16):
                    w2 = W2[jg % 2]
                    self.w_tile(w2, "w_mlp2", l, 0, 2, ncol=1024, k0=jg * 2)
                    for jj in range(2):
                        j = jg * 2 + jj
                        for oc in range(8):
                            k.mm(bks[oc][:, 0:n], w2[:, jj, oc * 128:(oc + 1) * 128], H[:, j, 0:n], j == 0, j == 31,
                                 r=[w2.b(), H.b(j)], w=[bks[oc].b()], sig=(oc == 7 and jj == 1))
                for oc in range(8):
                    k.cp(A if oc % 2 == 0 else V, MO[:, oc, 0:n], bks[oc][:, 0:n], r=[bks[oc].b()], w=hb_all)
                self.resid_update(sc, MO, sqt, n, tbi, t0, t1, M, 5, col)

    def permute(self, fwd):
        k = self.k
        X = self.X
        engs = [k.ACT, k.DVE, k.POOL]
        with k.scope() as sc:
            TMPS = [sc.tile(f"pmt{i}", [128, SEQ], F32) for i in range(2)]
            for kc in range(8):
                tmp = TMPS[kc % 2]
                xb = [X.b(kc, tbi) for tbi in range(1, 5)]
                src = X[:, kc, CTX:T]
                if fwd:
                    v = src.rearrange("p (r c) -> p c r", c=GRID_W)
                    tv = tmp[:, :].rearrange("p (c r) -> p c r", c=GRID_W)
                else:
                    v = src.rearrange("p (c r) -> p r c", c=GRID_W)
                    tv = tmp[:, :].rearrange("p (r c) -> p r c", c=GRID_W)
                k.cp(engs[kc % 3], tv, v, r=xb, w=[tmp.b()])
                k.cp(engs[(kc + 1) % 3], X[:, kc, CTX:T], tmp[:, :], r=[tmp.b()], w=xb)

    def store_out(self, s):
        k, C = self.k, self.C
        X = self.X
        with k.scope() as sc:
            OT = [sc.tile(f"ot{i}", [128, D], F32) for i in range(3)]
            for tt in range(2, T // 128):
                ot = OT[tt % 3]
                tbi = self.tbi(tt * 128)
                for h in range(2):
                    bk = k.bank()
                    for j in range(4):
                        kc = h * 4 + j
                        k.tr(bk[:, j * 128:(j + 1) * 128], X[:, kc, tt * 128:(tt + 1) * 128], C["ident_f"][:],
                             r=[X.b(kc, tbi), C["ident_f"].b()], w=[bk.b()], sig=(j == 3))
                    k.cp(k.ACT if h == 0 else k.DVE, ot[:, h * 512:(h + 1) * 512], bk[:, :], r=[bk.b()], w=[ot.b(h)])
                k.dma(k.SP, self.out[s, (tt - 2) * 128:(tt - 1) * 128, :], ot[:], r=[ot.b(0), ot.b(1)], w=[])

    def layer(self, s, l):
        import os
        stop = os.environ.get("KSTOP", "")
        self.phase_a(s, l)
        if stop == f"a{l}":
            return True
        self.hyena(s, l)
        if stop == f"hy{l}":
            return True
        self.hgrn(s, l)
        if stop == f"hg{l}":
            return True
        self.lru(s, l)
        if stop == f"lru{l}":
            return True
        self.merge(s, l)
        if stop == f"mg{l}":
            return True
        self.mlp(s, l)
        if stop == f"mlp{l}":
            return True
        return False

    def build_full(self):
        self.prologue()
        for s in range(self.nseq):
            self.load_x(s)
            for l in range(self.depth):
                if l % 2 == 1:
                    self.permute(True)
                if self.layer(s, l):
                    self.store_out(s)
                    self.finish()
                    return
                if l % 2 == 1:
                    self.permute(False)
            self.store_out(s)
        self.finish()

    def spec_mul(self, yre, yim, A_, B_, Kre, Kim, tm, rb, TM, wb):
        k = self.k
        V, P = k.DVE, k.POOL
        k.tt(V, tm[0], A_, Kre, ALU.mult, r=rb, w=[TM[0].b()])
        k.tt(V, tm[1], B_, Kim, ALU.mult, r=rb, w=[TM[1].b()])
        k.tt(V, yre, tm[0], tm[1], ALU.add, r=[TM[0].b(), TM[1].b()], w=wb)
        k.tt(P, tm[2], B_, Kre, ALU.mult, r=rb, w=[TM[2].b()])
        k.tt(P, tm[3], A_, Kim, ALU.mult, r=rb, w=[TM[3].b()])
        k.tt(P, yim, tm[2], tm[3], ALU.subtract, r=[TM[2].b(), TM[3].b()], w=wb)

    def prologue(self):
        k = self.k
        self.precast()
        self.consts()
        self.resident()
        self.vecbank()
        with k.scope() as psc:
            self.FB = psc.tile("FB", [128, self.depth, 2], F32)
            self.SK = psc.tile("SK", [128, self.depth + 1, 512], F32)
            for name in ("negt", "delta"):
                shp = list(make_consts()[name].shape)
                t = psc.tile("c_" + name, shp, CONST_DT[name])
                k.dma(k.SP, t[:], self.cst[name], w=[t.b()])
                self.C[name] = t
            self.derive()
            for l in range(self.depth):
                self.filt(l, False)
            self.filt(0, True)
        self.X = k.tile("X", [128, 8, T], F32)
        self.U = k.tile("U", [128, 8, T], BF16)

    def finish(self):
        k = self.k
        k.barrier()
        k.stack.close()


def make_in_maps(inputs, ncores=NCORES, nseq=NSEQ, j=0):
    cst = make_consts()
    maps = []
    for i in range(ncores):
        m = {}
        for name in IN_SHAPES:
            a = np.asarray(inputs[name])
            if name in ("x", "c", "ctx"):
                a = a[NSEQ * i + j:NSEQ * i + j + nseq]
            m[name] = np.ascontiguousarray(a, dtype=np.float32)
        for name, arr in cst.items():
            m["k_" + name] = arr
        maps.append(m)
    return maps


_NET = None
NLAUNCH = 1


def kernel(**inputs):
    global _NET
    nseq = NSEQ // NLAUNCH
    if _NET is None:
        net = Net(nseq=nseq)
        net.build_full()
        _NET = net
    net = _NET
    outs = []
    for j in range(NLAUNCH):
        maps = make_in_maps(inputs, NCORES, nseq, j * nseq)
        res = run_bass_kernel_spmd(net.nc, maps, core_ids=list(range(NCORES)))
        outs.append([np.asarray(r["out"]) for r in res.results])
    full = np.zeros((NCORES * NSEQ, SEQ, D), np.float32)
    for j in range(NLAUNCH):
        for i in range(NCORES):
            full[NSEQ * i + j * nseq:NSEQ * i + (j + 1) * nseq] = outs[j][i]
    return full
```

```python
import math
from contextlib import ExitStack
import numpy as np
import ml_dtypes
import concourse.bass as bass
import concourse.mybir as mybir
from concourse.bass_utils import run_bass_kernel_spmd

F32 = mybir.dt.float32
BF16 = mybir.dt.bfloat16
AF = mybir.ActivationFunctionType
ALU = mybir.AluOpType

NCORES = 8
NSEQ = 2
D = 1024
KC = 8
SEQ = 2048
CTX = 256
T = SEQ + CTX
DEPTH = 2
GRID_W = 64
EPS = 1e-6
E_HY = 512
INW = 8192
DFF = 4096
PADL = 3
PT = T + 12
TB = [(0, 256), (256, 768), (768, 1280), (1280, 1792), (1792, 2304)]
C_HY, C_LX, C_LG, C_Q, C_FF, C_FB, C_I, C_OG, C_MG = 0, 1536, 2048, 2560, 3072, 3584, 4096, 4608, 5120


def poff(t):
    return t + PADL if t < CTX else t + 3 * PADL


class Ev:
    __slots__ = ("q", "sem", "val", "dma")

    def __init__(self, q, sem, val, dma=False):
        self.q, self.sem, self.val, self.dma = q, sem, val, dma


class Buf:
    __slots__ = ("name", "w", "r")

    def __init__(self, name=""):
        self.name, self.w, self.r = name, None, []


class Tile:
    def __init__(self, t, name):
        self.t, self.name, self.bufs = t, name, {}

    def b(self, *key):
        v = self.bufs.get(key)
        if v is None:
            v = self.bufs[key] = Buf(f"{self.name}{key}")
        return v

    def __getitem__(self, k):
        return self.t[k]


class Q:
    LIM = 3000

    def __init__(self, kern, name, eng, ndma=0):
        self.k, self.name, self.eng = kern, name, eng
        self.sem = kern.new_sem(name)
        self.cnt = 0
        self.seen = {}
        self.last = None
        self.slots = [[kern.new_sem(f"{name}d{i}"), 0] for i in range(ndma)]
        self.nxt = 0
        self.pend = []
        self.collect = None

    def wait(self, ev):
        key = id(ev.sem)
        if self.seen.get(key, 0) >= ev.val:
            return
        self.seen[key] = ev.val
        if self.collect is not None:
            self.collect = [e for e in self.collect if e.sem is not ev.sem] + [ev]
            return
        self.eng.wait_ge(ev.sem, ev.val)
        self.k.nwait += 1

    def flush(self, ins_fn):
        ws, self.collect = self.collect, None
        for e in ws[:-1]:
            self.eng.wait_ge(e.sem, e.val)
            self.k.nwait += 1
        ins = ins_fn()
        if ws:
            ins._wait_ge(ws[-1].sem, ws[-1].val)
        return ins

    def signal(self, ins):
        if self.cnt >= Q.LIM:
            self.sem = self.k.new_sem(self.name + "x")
            self.cnt = 0
        self.cnt += 1
        ins.then_inc(self.sem, 1)
        self.last = Ev(self, self.sem, self.cnt)
        return self.last

    def dma_signal(self, fn):
        slot = self.slots[self.nxt]
        self.nxt = (self.nxt + 1) % len(self.slots)
        if slot[1] > 0:
            self.wait(Ev(self, slot[0], slot[1], True))
        ins = self.flush(fn)
        slot[1] += 16
        ins.then_inc(slot[0], 16)
        return Ev(self, slot[0], slot[1], True)


class Kern:
    def __init__(self):
        self.nc = bass.Bass("TRN2", target_bir_lowering=False)
        self.stack = ExitStack()
        self.nwait = 0
        self.nins = 0
        self.sems = []
        nc = self.nc
        self.PE = Q(self, "pe", nc.tensor)
        self.ACT = Q(self, "act", nc.scalar)
        self.DVE = Q(self, "dve", nc.vector)
        self.POOL = Q(self, "pool", nc.gpsimd, ndma=2)
        self.SP = Q(self, "sp", nc.sync, ndma=16)
        self.queues = [self.PE, self.ACT, self.DVE, self.POOL, self.SP]
        self.banks = []
        for i in range(8):
            t = nc.alloc_psum_tensor(f"bank{i}", [128, 512], F32)
            self.banks.append(Tile(t, f"bank{i}"))
        self.bank_i = 0
        self.dbg = []

    def new_sem(self, name):
        s = self.stack.enter_context(self.nc.semaphore(f"s_{name}_{len(self.sems)}"))
        self.sems.append(s)
        return s

    def bank(self):
        b = self.banks[self.bank_i]
        self.bank_i = (self.bank_i + 1) % 8
        return b

    def _deps(self, q, r, w):
        for b in r:
            if b.w is not None:
                self._dep(q, b.w, 0)
        for b in w:
            if b.w is not None:
                self._dep(q, b.w, 1)
            for e in b.r:
                self._dep(q, e, 2)

    def _dep(self, q, ev, kind):
        if ev.q is q and not ev.dma:
            if q is self.PE:
                return
        q.wait(ev)

    def _record(self, ev, r, w):
        for b in r:
            if not ev.dma:
                b.r = [e for e in b.r if e.dma or e.q is not ev.q]
            b.r.append(ev)
        for b in w:
            b.w = ev
            b.r = []

    def op(self, q, fn, r=(), w=(), sig=True):
        q.collect = []
        self._deps(q, r, w)
        ins = q.flush(fn)
        self.nins += 1
        if sig:
            ev = q.signal(ins)
            for (pr, pw) in q.pend:
                self._record(ev, pr, pw)
            q.pend = []
            self._record(ev, r, w)
        else:
            q.pend.append((list(r), list(w)))
        return ins

    def dma(self, q, out, in_, r=(), w=(), **kw):
        assert not q.pend
        q.collect = []
        self._deps(q, r, w)
        ev = q.dma_signal(lambda: q.eng.dma_start(out=out, in_=in_, **kw))
        self.nins += 1
        self._record(ev, r, w)

    def barrier(self):
        evs = []
        for q in self.queues:
            assert not q.pend, q.name
            if q.last is not None:
                evs.append(q.last)
            for s in q.slots:
                if s[1] > 0:
                    evs.append(Ev(q, s[0], s[1], True))
        for q in self.queues:
            for e in evs:
                if e.q is q and not e.dma:
                    continue
                q.wait(e)

    def tile(self, name, shape, dtype, stack=None):
        st = stack if stack is not None else self.stack
        self.ntile = getattr(self, "ntile", 0) + 1
        name = f"{name}_{self.ntile}"
        t = st.enter_context(self.nc.sbuf_tensor(name, list(shape), dtype))
        return Tile(t, name)

    class Scope:
        def __init__(self, k):
            self.k = k
            self.st = ExitStack()
            self.cache = {}

        def ctile(self, name, shape, dtype):
            key = (name, tuple(shape), str(dtype))
            t = self.cache.get(key)
            if t is None:
                t = self.cache[key] = self.tile(name, shape, dtype)
            return t

        def __enter__(self):
            self.st.__enter__()
            return self

        def tile(self, name, shape, dtype):
            return self.k.tile(name, shape, dtype, self.st)

        def __exit__(self, *a):
            self.k.barrier()
            return self.st.__exit__(*a)

    def scope(self):
        return Kern.Scope(self)

    def mm(self, out, lhsT, rhs, start, stop, r=(), w=(), sig=False):
        nc = self.nc
        return self.op(self.PE, lambda: nc.tensor.matmul(out, lhsT, rhs, start=start, stop=stop), r, w, sig)

    def tr(self, out, in_, ident, r=(), w=(), sig=True):
        nc = self.nc
        return self.op(self.PE, lambda: nc.tensor.transpose(out, in_, ident), r, w, sig)

    def act(self, out, in_, func, r=(), w=(), bias=None, scale=None):
        nc = self.nc
        kw = {}
        if bias is not None:
            kw["bias"] = bias
        if scale is not None:
            kw["scale"] = scale
        return self.op(self.ACT, lambda: nc.scalar.activation(out=out, in_=in_, func=func, **kw), r, w)

    def tt(self, q, out, in0, in1, op, r=(), w=()):
        return self.op(q, lambda: q.eng.tensor_tensor(out=out, in0=in0, in1=in1, op=op), r, w)

    def ts(self, q, out, in0, s1, op0, s2=None, op1=None, r=(), w=()):
        if op1 is None:
            return self.op(q, lambda: q.eng.tensor_scalar(out=out, in0=in0, scalar1=s1, scalar2=None, op0=op0), r, w)
        return self.op(q, lambda: q.eng.tensor_scalar(out=out, in0=in0, scalar1=s1, scalar2=s2, op0=op0, op1=op1), r, w)

    def stt(self, out, in0, scalar, in1, op0, op1, r=(), w=()):
        nc = self.nc
        return self.op(self.DVE, lambda: nc.vector.scalar_tensor_tensor(out=out, in0=in0, scalar=scalar, in1=in1, op0=op0, op1=op1), r, w)

    def cp(self, q, out, in_, r=(), w=()):
        if q is self.ACT:
            return self.op(q, lambda: q.eng.copy(out=out, in_=in_), r, w)
        return self.op(q, lambda: q.eng.tensor_copy(out=out, in_=in_), r, w)

    def memset(self, q, ap, val, w=()):
        return self.op(q, lambda: q.eng.memset(ap, val), (), w)


_CONSTS = None


def make_consts():
    global _CONSTS
    if _CONSTS is not None:
        return _CONSTS
    bf = ml_dtypes.bfloat16
    c = {}
    c["ident_f"] = np.eye(128, dtype=np.float32)
    c["ident_b"] = np.eye(128, dtype=np.float32).astype(bf)
    c["ones_d"] = np.full((128, 128), 1.0 / D, np.float32).astype(bf)
    c["ones_v"] = np.full((128, 128), 1.0 / 128, np.float32).astype(bf)
    c["onesrow"] = np.ones((128, 512), np.float32).astype(bf)

    def dft(L):
        N = 2 * L
        s = np.arange(L, dtype=np.float64)[:, None]
        f = np.arange(L, dtype=np.float64)[None, :]
        ang = 2.0 * np.pi * np.mod((2 * f + 1) * s, 2 * N) / (2 * N)
        return np.cos(ang), np.sin(ang)

    cs, sn = dft(SEQ)
    fw = np.stack([cs, sn], 0).reshape(2, 16, 128, 16, 128)
    c["fwdL"] = np.ascontiguousarray(fw.transpose(3, 2, 1, 0, 4)).astype(bf)
    iv = np.stack([cs.T, sn.T], 0).reshape(2, 16, 128, SEQ)
    c["invL"] = np.ascontiguousarray(iv.transpose(1, 2, 0, 3)).astype(bf)
    cs, sn = dft(CTX)
    fw = np.stack([cs, sn], 0).reshape(2, 2, 128, CTX)
    c["fwdC"] = np.ascontiguousarray(fw.transpose(2, 1, 0, 3)).astype(bf)
    iv = np.stack([cs.T, sn.T], 0).reshape(2, 2, 128, CTX)
    c["invC"] = np.ascontiguousarray(iv.transpose(2, 1, 0, 3)).astype(bf)

    def zemb(L):
        pos = np.arange(L, dtype=np.float32)
        t = pos / np.float32(max(L - 1, 1))
        bands = np.linspace(1e-4, 7, 8, dtype=np.float32)
        ang = bands[None, :] * (np.float32(2.0 * math.pi) * pos / np.float32(L))[:, None]
        z = np.concatenate([t[:, None], np.cos(ang), -np.sin(ang)], -1).astype(np.float32)
        return np.ascontiguousarray(z.T), t

    c["zL"], tl = zemb(SEQ)
    c["zC"], tc = zemb(CTX)
    negt = np.zeros((128, 18), np.float32)
    negt[:, :16] = -tl.reshape(16, 128).T
    negt[:, 16:] = -tc.reshape(2, 128).T
    c["negt"] = negt
    deltas = np.abs(np.linspace(math.log(1e-2) / 1.5, math.log(1e-2) / 0.3, E_HY, dtype=np.float32))
    c["delta"] = np.ascontiguousarray(np.broadcast_to(deltas[None, :], (128, E_HY))).astype(np.float32)
    s = np.arange(128)[:, None]
    t = np.arange(128)[None, :]
    same = (s // 64) == (t // 64)
    c["maskF"] = (same & (s <= t)).astype(np.uint16)
    c["maskB"] = (same & (s >= t)).astype(np.uint16)
    _CONSTS = c
    return c


CONST_DT = {"ident_f": F32, "ident_b": BF16, "ones_d": BF16, "ones_v": BF16, "onesrow": BF16, "fwdL": BF16,
            "invL": BF16, "fwdC": BF16, "invC": BF16, "zL": F32, "zC": F32, "negt": F32, "delta": F32,
            "maskF": mybir.dt.uint16, "maskB": mybir.dt.uint16}

IN_SHAPES = {
    "x": [NSEQ, SEQ, D], "c": [NSEQ, D], "ctx": [NSEQ, CTX, D], "c_ctx": [D],
    "w_ada": [DEPTH, D, 6 * D], "b_ada": [DEPTH, 6 * D], "norm_gains": [DEPTH, 4, D], "w_in": [DEPTH, D, INW],
    "hy_short_w": [DEPTH, 3, 1536], "hy_short_b": [DEPTH, 1536], "hy_ff_w1": [DEPTH, 17, 64],
    "hy_ff_b1": [DEPTH, 64], "hy_ff_w2": [DEPTH, 64, 64], "hy_ff_b2": [DEPTH, 64], "hy_ff_w3": [DEPTH, 64, 1024],
    "hy_freq": [DEPTH, 2, 64], "hy_skip": [DEPTH, 512], "lru_conv_w": [DEPTH, 2, 4, 512],
    "lru_conv_b": [DEPTH, 2, 512], "lru_wr": [DEPTH, 2, 8, 64, 64], "lru_br": [DEPTH, 2, 512],
    "lru_wi": [DEPTH, 2, 8, 64, 64], "lru_bi": [DEPTH, 2, 512], "lru_lambda": [DEPTH, 2, 512],
    "hg_lower_bounds": [DEPTH, 512], "hg_norm_g": [DEPTH, 128], "w_proj_hy": [DEPTH, 512, D],
    "w_proj_lru": [DEPTH, 512, D], "w_proj_hg": [DEPTH, 512, D], "w_out": [DEPTH, D, D],
    "w_mlp1": [DEPTH, D, DFF], "w_mlp2": [DEPTH, DFF, D],
}


VEC_LIST = [
    ("b_ada", 6144), ("norm_gains", 4096), ("hy_short_w", 4608), ("hy_short_b", 1536), ("lru_conv_w", 4096),
    ("lru_conv_b", 1024), ("lru_br", 1024), ("lru_bi", 1024), ("lru_lambda", 1024), ("hg_lower_bounds", 512),
    ("hg_norm_g", 128), ("hy_ff_b1", 64), ("hy_ff_b2", 64), ("hy_freq0", 64), ("hy_freq1", 64),
]


class Net:
    def __init__(self, nseq=NSEQ, depth=DEPTH, stage="full", dbg=()):
        self.k = k = Kern()
        self.nc = nc = k.nc
        self.nseq, self.depth, self.stage = nseq, depth, stage
        self.dbg_names = dbg
        self.dbg_out = {}
        self.inp = {}
        for name, shp in IN_SHAPES.items():
            shp = list(shp)
            if name in ("x", "c", "ctx"):
                shp[0] = nseq
            self.inp[name] = nc.dram_tensor(name, shp, F32, kind="ExternalInput").ap()
        self.cst = {}
        for name, arr in make_consts().items():
            self.cst[name] = nc.dram_tensor("k_" + name, list(arr.shape), CONST_DT[name], kind="ExternalInput").ap()
        self.out = nc.dram_tensor("out", [nseq, SEQ, D], F32, kind="ExternalOutput").ap()

        def scratch(name, shape, dt=BF16):
            return Tile(nc.dram_tensor(name, list(shape), dt, kind="Internal").ap(), name)

        self.WB = {
            "w_in": scratch("wb_in", [DEPTH, D, INW]), "w_ada": scratch("wb_ada", [DEPTH, D, 6 * D]),
            "w_proj_hy": scratch("wb_phy", [DEPTH, 512, D]), "w_proj_lru": scratch("wb_plru", [DEPTH, 512, D]),
            "w_proj_hg": scratch("wb_phg", [DEPTH, 512, D]), "w_out": scratch("wb_out", [DEPTH, D, D]),
            "w_mlp1": scratch("wb_m1", [DEPTH, D, DFF]), "w_mlp2": scratch("wb_m2", [DEPTH, DFF, D]),
        }
        self.KS = scratch("ks", [DEPTH, 16, 128, 2, 512])
        self.KSC = scratch("ksc", [2, 128, 2, 512])
        self.YP = scratch("yp", [3, 128, 4, T])

    def dump(self, name, tile_ap, shape, dtype=F32):
        if name not in self.dbg_names:
            return
        k, nc = self.k, self.nc
        k.barrier()
        d = nc.dram_tensor("dbg_" + name, list(shape), dtype, kind="ExternalOutput").ap()
        k.dma(k.SP, d, tile_ap)
        k.barrier()
        self.dbg_out[name] = "dbg_" + name

    def precast(self):
        k = self.k
        order = []
        for l in range(self.depth):
            order.append(("w_ada", l))
        for l in range(self.depth):
            for n in ("w_in", "w_proj_hy", "w_proj_hg", "w_proj_lru", "w_out", "w_mlp1", "w_mlp2"):
                order.append((n, l))
        for (n, l) in order:
            src = self.inp[n][l]
            dst = self.WB[n]
            k.dma(k.POOL, dst[l], src, w=[dst.b(l)], max_dma_last_dim=4096)

    def consts(self):
        k, nc = self.k, self.nc
        C = {}
        for name in ("ident_f", "ident_b", "ones_d", "ones_v", "onesrow", "fwdC", "invC", "maskF", "maskB"):
            shp = list(make_consts()[name].shape)
            t = k.tile("c_" + name, shp, CONST_DT[name])
            k.dma(k.SP, t[:], self.cst[name], w=[t.b()])
            C[name] = t
        self.C = C

    def vecbank(self):
        k, nc = self.k, self.nc
        rows = []
        for l in range(self.depth):
            for name, ln in VEC_LIST:
                if name == "hy_freq0":
                    src = self.inp["hy_freq"][l, 0]
                elif name == "hy_freq1":
                    src = self.inp["hy_freq"][l, 1]
                else:
                    src = self.inp[name][l]
                    if len(src.shape) > 1:
                        src = src.flatten() if hasattr(src, "flatten") else src
                rows.append(((name, l), src, ln))
        for i in range(self.nseq):
            rows.append((("c", i), self.inp["c"][i], D))
        rows.append((("c_ctx", 0), self.inp["c_ctx"], D))
        place = {}
        tile_i, r = 0, 0
        for key, src, ln in rows:
            nr = (ln + 127) // 128
            if r + nr > 128:
                tile_i, r = tile_i + 1, 0
            place[key] = (tile_i, r, nr, ln)
            r += nr
        ntile = tile_i + 1
        self.VB = VB = k.tile("VB", [128, ntile, 128], F32)
        with k.scope() as sc:
            RT = sc.tile("rowtile", [128, ntile, 128], F32)
            k.memset(k.DVE, RT[:], 0.0, w=[RT.b()])
            for key, src, ln in rows:
                ti, r0, nr, _ = place[key]
                if ln >= 128:
                    k.dma(k.SP, RT[r0:r0 + nr, ti, :], src.rearrange("(r c) -> r c", c=128), w=[RT.b()])
                else:
                    k.dma(k.SP, RT[r0:r0 + 1, ti, 0:ln], src.rearrange("(r c) -> r c", r=1), w=[RT.b()])
            for ti in range(ntile):
                bk = k.bank()
                k.tr(bk[:, 0:128], RT[:, ti, :], self.C["ident_f"][:], r=[RT.b(), self.C["ident_f"].b()], w=[bk.b()])
                k.cp(k.DVE, VB[:, ti, :], bk[:, 0:128], r=[bk.b()], w=[VB.b()])
        self.place = place

    def vec(self, name, l, i0=0, n=1):
        ti, r0, nr, ln = self.place[(name, l)]
        return self.VB[:, ti, r0 + i0:r0 + i0 + n]

    def derive(self):
        k, nc, C = self.k, self.nc, self.C
        V, A, P = k.DVE, k.ACT, k.POOL
        VBb = self.VB.b()
        with k.scope() as sc:
            scT = sc.tile("scT", [128, 8, 3], BF16)
            for i in range(3):
                src = self.vec("c", min(i, self.nseq - 1), 0, 8) if i < 2 else self.vec("c_ctx", 0, 0, 8)
                k.act(scT[:, :, i], src, AF.Silu, r=[VBb], w=[scT.b()])
            WA = [sc.tile(f"wada{i}", [128, 8, 512], BF16) for i in range(2)]
            for l in range(self.depth):
                ADA = sc.tile(f"ADA{l}", [128, 48, 3], F32)
                bk = k.bank()
                wsrc = self.WB["w_ada"]
                wv = wsrc.t[l].rearrange("(kc p) n -> p kc n", p=128)
                for g in range(12):
                    wt = WA[g % 2]
                    k.dma(k.SP, wt[:], wv[:, :, g * 512:(g + 1) * 512], r=[wsrc.b(l)], w=[wt.b()])
                    for jj in range(4):
                        j = g * 4 + jj
                        for kc in range(8):
                            k.mm(bk[:, j * 3:j * 3 + 3], wt[:, kc, jj * 128:(jj + 1) * 128], scT[:, kc, :],
                                 kc == 0, kc == 7, r=[wt.b(), scT.b()], w=[bk.b()], sig=(kc == 7))
                bada = self.vec("b_ada", l, 0, 48)
                k.tt(V, ADA[:], bk[:, 0:144].rearrange("p (j c) -> p j c", c=3),
                     bada.unsqueeze(2).to_broadcast([128, 48, 3]), ALU.add, r=[bk.b(), VBb], w=[ADA.b()])
                self.dump(f"ada{l}", ADA[:], [128, 48, 3])
                M = self.MOD[l]
                g = [self.vec("norm_gains", l, 8 * i, 8).unsqueeze(2).to_broadcast([128, 8, 3]) for i in range(4)]
                tmp = sc.tile(f"modtmp{l}", [128, 8, 3], F32)
                k.cp(V, M[:, 0], ADA[:, 0:8, :], r=[ADA.b()], w=[M.b()])
                k.ts(V, tmp[:], ADA[:, 8:16, :], 1.0, ALU.add, r=[ADA.b()], w=[tmp.b()])
                k.tt(V, M[:, 1], tmp[:], g[0], ALU.mult, r=[tmp.b(), VBb], w=[M.b()])
                k.tt(V, M[:, 2], ADA[:, 16:24, :], g[1], ALU.mult, r=[ADA.b(), VBb], w=[M.b()])
                k.cp(V, M[:, 3], ADA[:, 24:32, :], r=[ADA.b()], w=[M.b()])
                tmp2 = sc.tile(f"modtmp2{l}", [128, 8, 3], F32)
                k.ts(V, tmp2[:], ADA[:, 32:40, :], 1.0, ALU.add, r=[ADA.b()], w=[tmp2.b()])
                k.tt(V, M[:, 4], tmp2[:], g[2], ALU.mult, r=[tmp2.b(), VBb], w=[M.b()])
                k.tt(V, M[:, 5], ADA[:, 40:48, :], g[3], ALU.mult, r=[ADA.b(), VBb], w=[M.b()])
            for l in range(self.depth):
                e = sc.tile(f"c1e{l}", [128, 8], F32)
                k.act(e[:], self.vec("lru_lambda", l, 0, 8), AF.Exp, r=[VBb], w=[e.b()], scale=-1.0)
                k.ts(V, e[:], e[:], 1.0, ALU.add, r=[e.b()], w=[e.b()])
                k.act(e[:], e[:], AF.Ln, r=[e.b()], w=[e.b()])
                k.ts(V, self.C1[:, l, 0, :], e[:], -8.0, ALU.mult, r=[e.b()], w=[self.C1.b()])
                k.ts(V, self.C1[:, l, 1, :], e[:], -16.0, ALU.mult, r=[e.b()], w=[self.C1.b()])
            k.memset(V, self.LB[:, 0, 0, :], 0.0, w=[self.LB.b()])
            k.memset(V, self.LB[:, 0, 1, :], 1.0, w=[self.LB.b()])
            k.memset(V, self.LB[:, 0, 2, :], -1.0, w=[self.LB.b()])
            if self.depth > 1:
                dlt = sc.tile("lbd", [128, 4], F32)
                k.tt(V, dlt[:], self.vec("hg_lower_bounds", 1, 0, 4), self.vec("hg_lower_bounds", 0, 0, 4),
                     ALU.subtract, r=[VBb], w=[dlt.b()])
                k.act(self.LB[:, 1, 0, :], dlt[:], AF.Sigmoid, r=[dlt.b()], w=[self.LB.b()])
                k.act(self.LB[:, 1, 1, :], dlt[:], AF.Sigmoid, r=[dlt.b()], w=[self.LB.b()], scale=-1.0)
                k.ts(V, self.LB[:, 1, 2, :], self.LB[:, 1, 1, :], -1.0, ALU.mult, r=[self.LB.b()], w=[self.LB.b()])
            for l in range(self.depth):
                k.tt(V, self.FB[:, l, 0:1], self.vec("hy_freq0", l), self.vec("hy_ff_b1", l), ALU.mult, r=[VBb], w=[self.FB.b()])
                k.tt(V, self.FB[:, l, 1:2], self.vec("hy_freq1", l), self.vec("hy_ff_b2", l), ALU.mult, r=[VBb], w=[self.FB.b()])
            skr = sc.tile("skraw", [128, self.depth, 512], F32)
            for l in range(self.depth):
                k.dma(k.SP, skr[:, l, :], self.inp["hy_skip"][l].partition_broadcast(128), w=[skr.b()])
                k.ts(V, self.SK[:, l, :], skr[:, l, :], 2.0 / (2 * SEQ), ALU.mult, r=[skr.b()], w=[self.SK.b()])
            k.ts(V, self.SK[:, self.depth, :], skr[:, 0, :], 2.0 / (2 * CTX), ALU.mult, r=[skr.b()], w=[self.SK.b()])

    def wbd(self, l, d, ri, cc):
        return self.WBD[:, (d * 2 + ri) * 4 + cc, :]

    def build_wbd(self, sc, l):
        k = self.k
        self.WBD = sc.tile("WBD", [128, 16, 128], BF16)
        with k.scope() as sc2:
            stg = sc2.tile("wbdstage", [128, 16, 128], F32)
            k.memset(k.POOL, stg[:], 0.0, w=[stg.b()])
            for d in range(2):
                for ri, nm in enumerate(("lru_wr", "lru_wi")):
                    for cc in range(4):
                        idx = (d * 2 + ri) * 4 + cc
                        for h in range(2):
                            k.dma(k.SP, stg[h * 64:(h + 1) * 64, idx, h * 64:(h + 1) * 64],
                                  self.inp[nm][l, d, 2 * cc + h], w=[stg.b()])
            k.cp(k.DVE, self.WBD[:], stg[:], r=[stg.b()], w=[self.WBD.b()])

    def filt(self, l, ctx):
        k, nc, C = self.k, self.nc, self.C
        V, A, P = k.DVE, k.ACT, k.POOL
        VBb = self.VB.b()
        L = CTX if ctx else SEQ
        nt = L // 128
        blocks = [(0, 256)] if ctx else [(i * 512, (i + 1) * 512) for i in range(4)]
        sk = self.SK[:, self.depth if ctx else l, :]
        scale = 2.0 / (2 * L)
        PI = math.pi
        with k.scope() as sc:
            zT = sc.tile("zT", [17, L], F32)
            k.dma(k.SP, zT[:], self.cst["zC" if ctx else "zL"], w=[zT.b()])
            w1 = sc.tile("fw1", [17, 64], F32)
            w2 = sc.tile("fw2", [64, 64], F32)
            w3 = sc.tile("fw3", [64, 1024], F32)
            k.dma(k.SP, w1[:], self.inp["hy_ff_w1"][l], w=[w1.b()])
            k.dma(k.SP, w2[:], self.inp["hy_ff_w2"][l], w=[w2.b()])
            k.dma(k.SP, w3[:], self.inp["hy_ff_w3"][l], w=[w3.b()])
            h1 = sc.tile("fh1", [64, L], F32)
            h2 = sc.tile("fh2", [64, L], F32)
            arg = sc.tile("farg", [64, 512], F32)
            m1 = sc.tile("fm1", [64, 512], F32)
            for stage in range(2):
                wt, src, dst = (w1, zT, h1) if stage == 0 else (w2, h1, h2)
                kk = 17 if stage == 0 else 64
                fr = self.vec("hy_freq0" if stage == 0 else "hy_freq1", l)[0:64, :]
                fb = self.FB[0:64, l, stage:stage + 1]
                for (b0, b1) in blocks:
                    n = b1 - b0
                    bk = k.bank()
                    k.mm(bk[0:64, 0:n], wt[0:kk, :], src[0:kk, b0:b1], True, True, r=[wt.b(), src.b()], w=[bk.b()], sig=True)
                    k.ts(V, arg[:, 0:n], bk[0:64, 0:n], fr, ALU.mult, fb, ALU.add, r=[bk.b(), VBb, self.FB.b()], w=[arg.b()])
                    k.ts(V, m1[:, 0:n], arg[:, 0:n], -PI, ALU.is_lt, 2 * PI, ALU.mult, r=[arg.b()], w=[m1.b()])
                    k.tt(V, m1[:, 0:n], m1[:, 0:n], arg[:, 0:n], ALU.add, r=[m1.b(), arg.b()], w=[m1.b()])
                    k.ts(V, arg[:, 0:n], arg[:, 0:n], PI, ALU.is_gt, -2 * PI, ALU.mult, r=[arg.b()], w=[arg.b()])
                    k.tt(V, arg[:, 0:n], arg[:, 0:n], m1[:, 0:n], ALU.add, r=[m1.b(), arg.b()], w=[arg.b()])
                    k.act(dst[:, b0:b1], arg[:, 0:n], AF.Sin, r=[arg.b()], w=[dst.b()])
            self.dump(f"fh2_{l}_{int(ctx)}", h2[:], [64, L])
            hs = sc.tile("fhs", [128, nt, 512], BF16)
            hd = sc.tile("fhd", [128, nt, 512], BF16)
            dec = sc.tile("fdec", [128, 512], F32)
            hf = sc.tile("fhf", [128, 512], F32)
            hb = sc.tile("fhb", [128, 512], F32)
            for jt in range(nt):
                b0, b1 = k.bank(), k.bank()
                k.mm(b0[:, :], h2[:, jt * 128:(jt + 1) * 128], w3[:, 0:512], True, True, r=[h2.b(), w3.b()], w=[b0.b()], sig=True)
                k.mm(b1[:, :], h2[:, jt * 128:(jt + 1) * 128], w3[:, 512:1024], True, True, r=[h2.b(), w3.b()], w=[b1.b()], sig=True)
                col = (16 if ctx else 0) + jt
                k.act(dec[:], C["delta"][:], AF.Exp, r=[C["delta"].b(), C["negt"].b()], w=[dec.b()], scale=C["negt"][:, col:col + 1])
                k.tt(V, hf[:], b0[:, :], dec[:], ALU.mult, r=[b0.b(), dec.b()], w=[hf.b()])
                k.tt(V, hb[:], b1[:, :], dec[:], ALU.mult, r=[b1.b(), dec.b()], w=[hb.b()])
                if jt == 0:
                    k.memset(V, hb[0:1, :], 0.0, w=[hb.b()])
                k.tt(V, hs[:, jt, :], hf[:], hb[:], ALU.add, r=[hf.b(), hb.b()], w=[hs.b(jt)])
                k.tt(V, hd[:, jt, :], hb[:], hf[:], ALU.subtract, r=[hf.b(), hb.b()], w=[hd.b(jt)])
            FW = None if ctx else [sc.tile(f"ffw{i}", [128, 16, 2, 128], BF16) for i in range(2)]
            KT = [sc.tile(f"fkt{i}", [128, 2, 512], BF16) for i in range(2)]
            dst = self.KSC if ctx else self.KS
            for ft in range(nt):
                if ctx:
                    fw = lambda jt, cs: C["fwdC"][:, jt, cs, ft * 128:(ft + 1) * 128]
                    fwb = C["fwdC"].b()
                else:
                    fwt = FW[ft % 2]
                    k.dma(k.SP, fwt[:], self.cst["fwdL"][ft], w=[fwt.b()])
                    fw = lambda jt, cs: fwt[:, jt, cs, :]
                    fwb = fwt.b()
                bA, bB = k.bank(), k.bank()
                for jt in range(nt):
                    k.mm(bA[:, :], fw(jt, 0), hs[:, jt, :], jt == 0, jt == nt - 1, r=[fwb, hs.b(jt)], w=[bA.b()], sig=(jt == nt - 1))
                for jt in range(nt):
                    k.mm(bB[:, :], fw(jt, 1), hd[:, jt, :], jt == 0, jt == nt - 1, r=[fwb, hd.b(jt)], w=[bB.b()], sig=(jt == nt - 1))
                kt = KT[ft % 2]
                k.stt(kt[:, 0, :], bA[:, :], scale, sk, ALU.mult, ALU.add, r=[bA.b(), self.SK.b()], w=[kt.b()])
                k.act(kt[:, 1, :], bB[:, :], AF.Copy, r=[bB.b()], w=[kt.b()], scale=scale)
                if ctx:
                    k.dma(k.SP, dst.t[ft], kt[:], r=[kt.b()], w=[dst.b(ft)])
                else:
                    k.dma(k.SP, dst.t[l, ft], kt[:], r=[kt.b()], w=[dst.b(l, ft)])

    def resident(self):
        k = self.k
        self.EPSC = k.tile("epsc", [128, 1], F32)
        k.memset(k.DVE, self.EPSC[:], EPS, w=[self.EPSC.b()])
        self.MOD = [k.tile(f"MOD{l}", [128, 6, 8, 3], F32) for l in range(self.depth)]
        self.C1 = k.tile("C1", [128, self.depth, 2, 8], F32)
        self.LB = k.tile("LB", [128, self.depth, 3, 4], F32)

    def load_x(self, s):
        k, C = self.k, self.C
        X = self.X
        with k.scope() as sc:
            XT = [sc.tile(f"xt{i}", [128, D], F32) for i in range(3)]
            for tt in range(T // 128):
                xt = XT[tt % 3]
                src = self.inp["ctx"][s, tt * 128:(tt + 1) * 128, :] if tt < 2 else self.inp["x"][s, (tt - 2) * 128:(tt - 1) * 128, :]
                k.dma(k.SP, xt[:], src, w=[xt.b()])
                tbi = self.tbi(tt * 128)
                for h in range(2):
                    bk = k.bank()
                    for j in range(4):
                        kc = h * 4 + j
                        k.tr(bk[:, j * 128:(j + 1) * 128], xt[:, kc * 128:(kc + 1) * 128], C["ident_f"][:],
                             r=[xt.b(), C["ident_f"].b()], w=[bk.b()], sig=(j == 3))
                    q = k.ACT if h == 0 else k.DVE
                    k.cp(q, X[:, h * 4:(h + 1) * 4, tt * 128:(tt + 1) * 128], bk[:, :].rearrange("p (a b) -> p a b", b=128),
                         r=[bk.b()], w=[X.b(kc2, tbi) for kc2 in range(h * 4, h * 4 + 4)])

    @staticmethod
    def tbi(t):
        for i, (a, b) in enumerate(TB):
            if a <= t < b:
                return i
        raise ValueError(t)

    def rstd_of(self, sc, src3, rbufs, n, ones, nk=8, tag="", sq=None, sqb=None):
        k = self.k
        if sq is None:
            sq = sc.ctile("sq" + tag, [128, nk, 512], BF16)
        if sqb is None:
            sqb = [sq.b()]
        k.act(sq[:, :, 0:n], src3, AF.Square, r=rbufs, w=sqb)
        bk = k.bank()
        for kc in range(nk):
            k.mm(bk[:, 0:n], ones[:], sq[:, kc, 0:n], kc == 0, kc == nk - 1, r=sqb + [ones.b()], w=[bk.b()], sig=(kc == nk - 1))
        rs = sc.ctile("rstd" + tag, [128, 512], F32)
        k.act(rs[:, 0:n], bk[:, 0:n], AF.Ln, r=[bk.b(), self.EPSC.b()], w=[rs.b()], bias=self.EPSC[:, 0:1])
        k.act(rs[:, 0:n], rs[:, 0:n], AF.Exp, r=[rs.b()], w=[rs.b()], scale=-0.5)
        return rs

    def phase_a(self, s, l):
        k, C = self.k, self.C
        X, U, M = self.X, self.U, self.MOD[l]
        with k.scope() as sc:
            for tbi, (t0, t1) in enumerate(TB):
                n = t1 - t0
                col = 2 if tbi == 0 else s
                if True:
                    sc2 = sc
                    rs = self.rstd_of(sc2, X[:, :, t0:t1], [X.b(kc, tbi) for kc in range(8)], n, C["ones_d"])
                    tmp = [sc2.ctile(f"natmp{i}", [128, 512], F32) for i in range(2)]
                    for kc in range(8):
                        tp = tmp[kc % 2]
                        k.stt(tp[:, 0:n], X[:, kc, t0:t1], M[:, 1, kc, col:col + 1], rs[:, 0:n], ALU.mult, ALU.mult,
                              r=[X.b(kc, tbi), M.b(), rs.b()], w=[tp.b()])
                        k.act(U[:, kc, t0:t1], tp[:, 0:n], AF.Identity, r=[tp.b(), M.b()], w=[U.b(kc, tbi)],
                              bias=M[:, 0, kc, col:col + 1])

    def win_tile(self, wt, l, c0, ncol=128):
        k = self.k
        src = self.WB["w_in"]
        k.dma(k.SP, wt[:, :, 0:ncol], src.t[l].rearrange("(kc p) n -> p kc n", p=128)[:, :, c0:c0 + ncol],
              r=[src.b(l)], w=[wt.b()])

    def proj_fm(self, wt, tbi, bk, ncol0=0):
        k, U = self.k, self.U
        t0, t1 = TB[tbi]
        n = t1 - t0
        for kc in range(8):
            k.mm(bk[:, 0:n], wt[:, kc, ncol0:ncol0 + 128], U[:, kc, t0:t1], kc == 0, kc == 7,
                 r=[wt.b(), U.b(kc, tbi)], w=[bk.b()], sig=(kc == 7))
        return n

    def hyena(self, s, l):
        k, C, nc = self.k, self.C, self.nc
        V, A, P = k.DVE, k.ACT, k.POOL
        VBb = self.VB.b()
        need_ctx = l < self.depth - 1
        with k.scope() as sc:
            Utok = sc.tile("Utok", [128, 18, 512], BF16)
            with k.scope() as sc2:
                P1 = [sc2.tile(f"hyP{i}", [128, PT], F32) for i in range(1)]
                acc = sc2.tile("hyacc", [128, PT], F32)
                Zv = sc2.tile("hyZv", [128, PT], F32)
                ubf = [sc2.tile(f"hyu{i}", [128, PT], BF16) for i in range(1)]
                X0S = [sc2.tile(f"hyx0s{i}", [128, T], BF16) for i in range(2)]
                WT = [sc2.tile(f"hyw{i}", [128, 8, 128], BF16) for i in range(3)]
                for p in P1:
                    k.memset(P, p[:], 0.0, w=[p.b()])
                segs = [(0, CTX), (CTX, T)]
                it = 0
                for cc in range(4):
                    ub = ubf[0]
                    x0s = X0S[cc % 2]
                    for comp in (0, 2, 1):
                        wt = WT[it % 3]
                        p1 = P1[0]
                        it += 1
                        self.win_tile(wt, l, C_HY + comp * 512 + cc * 128)
                        for tbi, (t0, t1) in enumerate(TB):
                            bk = k.bank()
                            n = self.proj_fm(wt, tbi, bk)
                            k.cp(A, p1[:, poff(t0):poff(t0) + n], bk[:, 0:n], r=[bk.b()], w=[p1.b()])
                        ch = comp * 4 + cc
                        w0, w1, w2 = (self.vec("hy_short_w", l, kk * 12 + ch) for kk in range(3))
                        bb = self.vec("hy_short_b", l, ch)
                        for (a0, a1) in segs:
                            q0, q1 = poff(a0), poff(a0) + (a1 - a0)
                            k.act(acc[:, q0:q1], p1[:, q0:q1], AF.Identity, r=[p1.b(), VBb], w=[acc.b()], bias=bb, scale=w1)
                            k.stt(acc[:, q0:q1], p1[:, q0 - 1:q1 - 1], w0, acc[:, q0:q1], ALU.mult, ALU.add,
                                  r=[p1.b(), acc.b(), VBb], w=[acc.b()])
                            if comp == 0:
                                k.stt(Zv[:, q0:q1], p1[:, q0 + 1:q1 + 1], w2, acc[:, q0:q1], ALU.mult, ALU.add,
                                      r=[p1.b(), acc.b(), VBb], w=[Zv.b()])
                            elif comp == 2:
                                k.stt(acc[:, q0:q1], p1[:, q0 + 1:q1 + 1], w2, acc[:, q0:q1], ALU.mult, ALU.add,
                                      r=[p1.b(), acc.b(), VBb], w=[acc.b()])
                                k.tt(V, ub[:, q0:q1], acc[:, q0:q1], Zv[:, q0:q1], ALU.mult, r=[acc.b(), Zv.b()], w=[ub.b()])
                            else:
                                k.stt(x0s[:, a0:a1], p1[:, q0 + 1:q1 + 1], w2, acc[:, q0:q1], ALU.mult, ALU.add,
                                      r=[p1.b(), acc.b(), VBb], w=[x0s.b()])
                    k.dma(k.SP, self.YP.t[0, :, cc, :], x0s[:], r=[x0s.b()], w=[self.YP.b(0, cc, i) for i in range(5)])
                    for g0 in range(0, 18, 8):
                        g1 = min(18, g0 + 8)
                        bk = k.bank()
                        bv = bk[:, :].bitcast(BF16)
                        for tt in range(g0, g1):
                            k.tr(bv[:, (tt - g0) * 128:(tt - g0 + 1) * 128], ub[:, poff(tt * 128):poff(tt * 128) + 128], C["ident_b"][:],
                                 r=[ub.b(), C["ident_b"].b()], w=[bk.b()], sig=(tt == g1 - 1))
                        k.cp(A, Utok[:, g0:g1, cc * 128:(cc + 1) * 128],
                             bv[:, 0:(g1 - g0) * 128].rearrange("p (a b) -> p a b", b=128), r=[bk.b()], w=[Utok.b(cc)])
            self.dump("utok", Utok[:], [128, 18, 512], BF16)
            ut_bufs = [Utok.b(cc) for cc in range(4)]
            YP = self.YP
            if need_ctx:
                with k.scope() as sc4:
                    KT = [sc4.tile(f"hykt{i}", [128, 2, 512], BF16) for i in range(2)]
                    AB = [sc4.tile(f"hyab{i}", [128, 2, 512], BF16) for i in range(2)]
                    TM = [sc4.tile(f"hytm{i}", [128, 512], F32) for i in range(4)]
                    Yc = sc4.tile("hyYc", [128, 2, 2, 512], BF16)
                    x0c = sc4.tile("hyx0c", [128, 4, CTX], BF16)
                    k.dma(k.SP, x0c[:], YP.t[0, :, :, 0:CTX], r=[YP.b(0, cc, 0) for cc in range(4)], w=[x0c.b()])
                    for ft in range(2):
                        kt, ab = KT[ft % 2], AB[ft % 2]
                        k.dma(k.SP, kt[:], self.KSC.t[ft], r=[self.KSC.b(ft)], w=[kt.b()])
                        bA, bB = k.bank(), k.bank()
                        for cs, bk in ((0, bA), (1, bB)):
                            for st in range(2):
                                k.mm(bk[:, :], C["fwdC"][:, st, cs, ft * 128:(ft + 1) * 128], Utok[:, st, :], st == 0, st == 1,
                                     r=[C["fwdC"].b()] + ut_bufs, w=[bk.b()], sig=(st == 1))
                        k.cp(A, ab[:, 0, :], bA[:, :], r=[bA.b()], w=[ab.b(0)])
                        k.cp(A, ab[:, 1, :], bB[:, :], r=[bB.b()], w=[ab.b(1)])
                        self.spec_mul(Yc[:, ft, 0, :], Yc[:, ft, 1, :], ab[:, 0, :], ab[:, 1, :], kt[:, 0, :], kt[:, 1, :],
                                      [t[:] for t in TM], [ab.b(0), ab.b(1), kt.b()], TM, [Yc.b(ft)])
                    for cc in range(4):
                        bk = k.bank()
                        i = 0
                        for ft in range(2):
                            for cs in range(2):
                                k.mm(bk[:, 0:CTX], Yc[:, ft, cs, cc * 128:(cc + 1) * 128], C["invC"][:, ft, cs, :], i == 0, i == 3,
                                     r=[Yc.b(ft), C["invC"].b()], w=[bk.b()], sig=(i == 3))
                                i += 1
                        k.tt(V, x0c[:, cc, :], bk[:, 0:CTX], x0c[:, cc, :], ALU.mult, r=[bk.b(), x0c.b()], w=[x0c.b()])
                    k.dma(k.SP, YP.t[0, :, :, 0:CTX], x0c[:], r=[x0c.b()], w=[YP.b(0, cc, 0) for cc in range(4)])
            with k.scope() as sc3:
                DB = [sc3.tile(f"hydb{i}", [128, 4096], BF16) for i in range(2)]
                KT = [sc3.tile(f"hykt{i}", [128, 2, 512], BF16) for i in range(2)]
                AB = [sc3.tile(f"hyab{i}", [128, 2, 256], BF16) for i in range(2)]
                TM = [sc3.tile(f"hytm{i}", [128, 256], F32) for i in range(4)]
                Yf = sc3.tile("hyYf", [128, 16, 2, 256], BF16)
                XB = [sc3.tile(f"hyxb{i}", [128, 512], BF16) for i in range(4)]
                dbi = 0
                xbi = 0
                for half in range(2):
                    h0 = half * 256
                    for ft in range(16):
                        fwt = DB[dbi % 2]
                        dbi += 1
                        fw = fwt[:, :].rearrange("p (a b c) -> p a b c", a=16, b=2)
                        kt, ab = KT[ft % 2], AB[ft % 2]
                        k.dma(k.SP, fwt[:, :], self.cst["fwdL"][ft].rearrange("p a b c -> p (a b c)"), w=[fwt.b()])
                        k.dma(k.SP, kt[:], self.KS.t[l, ft], r=[self.KS.b(l, ft)], w=[kt.b()])
                        bk = k.bank()
                        for cs in range(2):
                            for st in range(16):
                                k.mm(bk[:, cs * 256:(cs + 1) * 256], fw[:, st, cs, :], Utok[:, 2 + st, h0:h0 + 256], st == 0, st == 15,
                                     r=[fwt.b(), Utok.b(half * 2), Utok.b(half * 2 + 1)], w=[bk.b()], sig=(st == 15))
                        k.cp(A, ab[:, :, :], bk[:, :].rearrange("p (a b) -> p a b", a=2), r=[bk.b()], w=[ab.b()])
                        self.spec_mul(Yf[:, ft, 0, :], Yf[:, ft, 1, :], ab[:, 0, :], ab[:, 1, :], kt[:, 0, h0:h0 + 256], kt[:, 1, h0:h0 + 256],
                                      [t[:] for t in TM], [ab.b(), kt.b()], TM, [Yf.b(ft)])
                    bks = [[k.bank() for tb in range(4)] for c2 in range(2)]
                    for ft in range(16):
                        ivt = DB[dbi % 2]
                        dbi += 1
                        iv = ivt[:, :].rearrange("p (a b) -> p a b", a=2)
                        k.dma(k.SP, ivt[:, :], self.cst["invL"][ft].rearrange("p a b -> p (a b)"), w=[ivt.b()])
                        for c2 in range(2):
                            for tb in range(4):
                                bk = bks[c2][tb]
                                for cs in range(2):
                                    k.mm(bk[:, :], Yf[:, ft, cs, c2 * 128:(c2 + 1) * 128], iv[:, cs, tb * 512:(tb + 1) * 512],
                                         ft == 0 and cs == 0, ft == 15 and cs == 1, r=[Yf.b(ft), ivt.b()], w=[bk.b()],
                                         sig=(cs == 1 and tb == 3 and c2 == 1))
                    for c2 in range(2):
                        cc = half * 2 + c2
                        for tb in range(4):
                            t0 = CTX + tb * 512
                            xb = XB[xbi % 4]
                            xbi += 1
                            k.dma(k.SP, xb[:], YP.t[0, :, cc, t0:t0 + 512], r=[YP.b(0, cc, tb + 1)], w=[xb.b()])
                            k.tt(V, xb[:], bks[c2][tb][:, :], xb[:], ALU.mult, r=[bks[c2][tb].b(), xb.b()], w=[xb.b()])
                            k.dma(k.SP, YP.t[0, :, cc, t0:t0 + 512], xb[:], r=[xb.b()], w=[YP.b(0, cc, tb + 1)])

    def lru(self, s, l):
        k, C = self.k, self.C
        V, A, P = k.DVE, k.ACT, k.POOL
        VBb = self.VB.b()
        YP = self.YP
        with k.scope() as sc:
            self.build_wbd(sc, l)
            xp = sc.tile("lrxp", [128, PT], F32)
            k.memset(P, xp[:], 0.0, w=[xp.b()])
            hsum = sc.tile("lrhs", [128, T], F32)
            xc = sc.tile("lrxc", [128, T], F32)
            xcb = sc.tile("lrxcb", [128, T], BF16)
            R = sc.tile("lrR", [128, T], F32)
            I = sc.tile("lrI", [128, T], F32)
            Aa = sc.tile("lrA", [128, T], F32)
            YS = [sc.tile(f"lrys{i}", [128, T], BF16) for i in range(2)]
            GG = [sc.tile(f"lrgg{i}", [128, 512], BF16) for i in range(2)]
            WX = [sc.tile(f"lrwx{i}", [128, 8, 128], BF16) for i in range(2)]
            WG = [sc.tile(f"lrwg{i}", [128, 8, 128], BF16) for i in range(2)]
            segs = [(0, CTX), (CTX, T)]
            for cc in range(4):
                wx, wg, ys = WX[cc % 2], WG[cc % 2], YS[cc % 2]
                self.win_tile(wx, l, C_LX + cc * 128)
                self.win_tile(wg, l, C_LG + cc * 128)
                for tbi, (t0, t1) in enumerate(TB):
                    bk = k.bank()
                    n = self.proj_fm(wx, tbi, bk)
                    k.cp(A, xp[:, poff(t0):poff(t0) + n], bk[:, 0:n], r=[bk.b()], w=[xp.b()])
                for d in range(2):
                    sg = -1 if d == 0 else 1
                    ch = d * 4 + cc
                    wk = [self.vec("lru_conv_w", l, (d * 4 + kk) * 4 + cc) for kk in range(4)]
                    bb = self.vec("lru_conv_b", l, ch)
                    for (a0, a1) in segs:
                        q0, q1 = poff(a0), poff(a0) + (a1 - a0)
                        k.act(xc[:, a0:a1], xp[:, q0:q1], AF.Identity, r=[xp.b(), VBb], w=[xc.b()], bias=bb, scale=wk[3])
                        for j in (1, 2, 3):
                            k.stt(xc[:, a0:a1], xp[:, q0 + sg * j:q1 + sg * j], wk[3 - j], xc[:, a0:a1], ALU.mult, ALU.add,
                                  r=[xp.b(), xc.b(), VBb], w=[xc.b()])
                    k.cp(P, xcb[:], xc[:], r=[xc.b()], w=[xcb.b()])
                    for tbi, (t0, t1) in enumerate(TB):
                        n = t1 - t0
                        b0, b1 = k.bank(), k.bank()
                        k.mm(b0[:, 0:n], self.wbd(l, d, 0, cc), xcb[:, t0:t1], True, True, r=[self.WBD.b(), xcb.b()], w=[b0.b()], sig=True)
                        k.mm(b1[:, 0:n], self.wbd(l, d, 1, cc), xcb[:, t0:t1], True, True, r=[self.WBD.b(), xcb.b()], w=[b1.b()], sig=True)
                        k.act(R[:, t0:t1], b0[:, 0:n], AF.Sigmoid, r=[b0.b(), VBb], w=[R.b()], bias=self.vec("lru_br", l, ch))
                        k.act(I[:, t0:t1], b1[:, 0:n], AF.Sigmoid, r=[b1.b(), VBb], w=[I.b()], bias=self.vec("lru_bi", l, ch))
                    k.act(Aa[:], R[:], AF.Exp, r=[R.b(), self.C1.b()], w=[Aa.b()], scale=self.C1[:, l, 0, ch:ch + 1])
                    k.act(R[:], R[:], AF.Exp, r=[R.b(), self.C1.b()], w=[R.b()], scale=self.C1[:, l, 1, ch:ch + 1])
                    k.act(R[:], R[:], AF.Sqrt, r=[R.b()], w=[R.b()], bias=1.0, scale=-1.0)
                    k.tt(P, I[:], I[:], xc[:], ALU.mult, r=[I.b(), xc.b()], w=[I.b()])
                    k.tt(V, I[:], I[:], R[:], ALU.mult, r=[I.b(), R.b()], w=[I.b()])
                    if d == 0:
                        k.op(V, lambda: V.eng.tensor_tensor_scan(out=hsum[:], data0=Aa[:], data1=I[:], initial=0.0, op0=ALU.mult, op1=ALU.add),
                             r=[Aa.b(), I.b()], w=[hsum.b()])
                    else:
                        k.op(V, lambda: V.eng.tensor_tensor_scan(out=I[:, 0:CTX][:, ::-1], data0=Aa[:, 0:CTX][:, ::-1], data1=I[:, 0:CTX][:, ::-1],
                                                                 initial=0.0, op0=ALU.mult, op1=ALU.add), r=[Aa.b(), I.b()], w=[I.b()])
                        k.op(V, lambda: V.eng.tensor_tensor_scan(out=I[:, CTX:T][:, ::-1], data0=Aa[:, CTX:T][:, ::-1], data1=I[:, CTX:T][:, ::-1],
                                                                 initial=I[:, 0:1], op0=ALU.mult, op1=ALU.add), r=[Aa.b(), I.b()], w=[I.b()])
                        k.tt(P, hsum[:], hsum[:], I[:], ALU.add, r=[hsum.b(), I.b()], w=[hsum.b()])
                for tbi, (t0, t1) in enumerate(TB):
                    n = t1 - t0
                    bk = k.bank()
                    self.proj_fm(wg, tbi, bk)
                    gg = GG[tbi % 2]
                    k.act(gg[:, 0:n], bk[:, 0:n], AF.Gelu_apprx_tanh, r=[bk.b()], w=[gg.b()])
                    k.tt(V, ys[:, t0:t1], hsum[:, t0:t1], gg[:, 0:n], ALU.mult, r=[hsum.b(), gg.b()], w=[ys.b()])
                k.dma(k.SP, YP.t[1, :, cc, :], ys[:], r=[ys.b()], w=[YP.b(1, cc, i) for i in range(5)])

    def hgrn(self, s, l):
        k, C = self.k, self.C
        V, A, P = k.DVE, k.ACT, k.POOL
        VBb = self.VB.b()
        YP, U = self.YP, self.U
        NCH = T // 64
        with k.scope() as sc:
            qb = sc.tile("hgq", [128, T], BF16)
            Vtok = sc.tile("hgV", [128, 18, 128], BF16)
            O = sc.tile("hgO", [128, T], F32)
            Bc = sc.tile("hgB", [128, T], F32)
            kkb = sc.tile("hgkk", [128, T], BF16)
            CH = sc.tile("hgCH", [128, 6, NCH], F32)
            S = sc.tile("hgS", [128, 128], F32)
            SB = [sc.tile(f"hgSb{i}", [128, 128], BF16) for i in range(4)]
            Dt = sc.tile("hgD", [128, 512], F32)
            E1 = sc.tile("hgE1", [128, 512], F32)
            E2 = sc.tile("hgE2", [128, 512], F32)
            QT = [sc.tile(f"hgqt{i}", [128, 512], BF16) for i in range(2)]
            KT = [sc.tile(f"hgkt{i}", [128, 512], BF16) for i in range(2)]
            QH = [sc.tile(f"hgqh{i}", [128, 512], BF16) for i in range(2)]
            KH = [sc.tile(f"hgkh{i}", [128, 512], BF16) for i in range(2)]
            PM = [sc.tile(f"hgpm{i}", [128, 128], BF16) for i in range(3)]
            KK = [sc.tile(f"hgkhk{i}", [128, 128], BF16) for i in range(3)]
            ys = sc.tile("hgys", [128, T], BF16)
            SG = [sc.tile(f"hgsg{i}", [128, 512], F32) for i in range(2)]
            TMP = [sc.tile(f"hgtmp{i}", [128, 512], F32) for i in range(2)]
            WT = [sc.tile(f"hgw{i}", [128, 8, 128], BF16) for i in range(6)]
            wi_ = 0
            blk_i = 0
            pr_i = 0
            for hd in range(4):
                ws = {}
                for nm, c0 in (("q", C_Q), ("ff", C_FF), ("fb", C_FB), ("i", C_I), ("og", C_OG)):
                    ws[nm] = WT[wi_ % 6]
                    wi_ += 1
                    self.win_tile(ws[nm], l, c0 + hd * 128)
                for tbi, (t0, t1) in enumerate(TB):
                    bk = k.bank()
                    n = self.proj_fm(ws["q"], tbi, bk)
                    k.act(qb[:, t0:t1], bk[:, 0:n], AF.Silu, r=[bk.b()], w=[qb.b()])
                for g0 in range(0, 18, 4):
                    g1 = min(18, g0 + 4)
                    bk = k.bank()
                    for tt in range(g0, g1):
                        tbi = self.tbi(tt * 128)
                        for kc in range(8):
                            k.mm(bk[:, (tt - g0) * 128:(tt - g0 + 1) * 128], U[:, kc, tt * 128:(tt + 1) * 128], ws["i"][:, kc, :],
                                 kc == 0, kc == 7, r=[U.b(kc, tbi), ws["i"].b()], w=[bk.b()], sig=(kc == 7))
                    k.cp(A, Vtok[:, g0:g1, :], bk[:, 0:(g1 - g0) * 128].rearrange("p (a b) -> p a b", b=128), r=[bk.b()], w=[Vtok.b()])
                if getattr(self, "hg_cut", 0) == 1:
                    return
                for d in range(2):
                    wf = ws["ff"] if d == 0 else ws["fb"]
                    for tbi, (t0, t1) in enumerate(TB):
                        bk = k.bank()
                        n = self.proj_fm(wf, tbi, bk)
                        k.act(Bc[:, t0:t1], bk[:, 0:n], AF.Sigmoid, r=[bk.b()], w=[Bc.b()])
                    lbv, omlv, nomlv = (self.LB[:, l, i, hd:hd + 1] for i in range(3))
                    k.ts(V, kkb[:], Bc[:], nomlv, ALU.mult, omlv, ALU.add, r=[Bc.b(), self.LB.b()], w=[kkb.b()])
                    k.ts(V, Bc[:], Bc[:], omlv, ALU.mult, lbv, ALU.add, r=[Bc.b(), self.LB.b()], w=[Bc.b()])
                    k.act(Bc[:], Bc[:], AF.Ln, r=[Bc.b()], w=[Bc.b()])
                    ones = C["onesrow"]
                    if d == 0:
                        order = [0, 1, 2, 3, 4]
                    else:
                        order = [0, 4, 3, 2, 1]
                    prev = None
                    for tbi in order:
                        t0, t1 = TB[tbi]
                        n = t1 - t0
                        seg = Bc[:, t0:t1] if d == 0 else Bc[:, t0:t1][:, ::-1]
                        init = 0.0 if prev is None else prev
                        k.op(V, lambda seg=seg, n=n, init=init: V.eng.tensor_tensor_scan(out=seg, data0=ones[:, 0:n], data1=seg, initial=init,
                                                                                      op0=ALU.mult, op1=ALU.add), r=[Bc.b(), ones.b()], w=[Bc.b()])
                        prev = Bc[:, t1 - 1:t1] if d == 0 else Bc[:, t0:t0 + 1]
                    if d == 0:
                        k.cp(V, CH[:, 0, :], Bc[:, 32::64], r=[Bc.b()], w=[CH.b()])
                        k.cp(V, CH[:, 1, :], Bc[:, 63::64], r=[Bc.b()], w=[CH.b()])
                        k.memset(V, CH[:, 2, 0:1], 0.0, w=[CH.b()])
                        k.cp(V, CH[:, 2, 1:NCH], CH[:, 1, 0:NCH - 1], r=[CH.b()], w=[CH.b()])
                    else:
                        k.cp(V, CH[:, 0, :], Bc[:, 31::64], r=[Bc.b()], w=[CH.b()])
                        k.cp(V, CH[:, 1, :], Bc[:, 0::64], r=[Bc.b()], w=[CH.b()])
                        k.cp(V, CH[:, 2, 0:NCH - 1], CH[:, 1, 1:NCH], r=[CH.b()], w=[CH.b()])
                        k.memset(V, CH[:, 2, 3:4], 0.0, w=[CH.b()])
                        k.cp(V, CH[:, 2, NCH - 1:NCH], CH[:, 1, 0:1], r=[CH.b()], w=[CH.b()])
                    k.tt(V, CH[:, 3, :], CH[:, 0, :], CH[:, 2, :], ALU.subtract, r=[CH.b()], w=[CH.b()])
                    k.tt(V, CH[:, 4, :], CH[:, 1, :], CH[:, 0, :], ALU.subtract, r=[CH.b()], w=[CH.b()])
                    k.tt(V, CH[:, 5, :], CH[:, 1, :], CH[:, 2, :], ALU.subtract, r=[CH.b()], w=[CH.b()])
                    k.act(CH[:, 3:6, :], CH[:, 3:6, :], AF.Exp, r=[CH.b()], w=[CH.b()])
                    if getattr(self, "hg_cut", 0) == 2:
                        return
                    k.memset(V, S[:], 0.0, w=[S.b()])
                    sbi = 0
                    sb_cur = SB[sbi % 4]
                    k.memset(V, sb_cur[:], 0.0, w=[sb_cur.b()])
                    mask = C["maskF"] if d == 0 else C["maskB"]
                    for tbi in order:
                        t0, t1 = TB[tbi]
                        n = t1 - t0
                        c0, nch = t0 // 64, n // 64
                        qt, kt_, qh, kh = QT[blk_i % 2], KT[blk_i % 2], QH[blk_i % 2], KH[blk_i % 2]
                        blk_i += 1

                        def bc(row):
                            return CH[:, row, c0:c0 + nch].unsqueeze(2).to_broadcast([128, nch, 64])

                        def v3(ap):
                            return ap.rearrange("p (a b) -> p a b", b=64)
                        k.tt(V, v3(Dt[:, 0:n]), v3(Bc[:, t0:t1]), bc(0), ALU.subtract, r=[Bc.b(), CH.b()], w=[Dt.b()])
                        k.act(E1[:, 0:n], Dt[:, 0:n], AF.Exp, r=[Dt.b()], w=[E1.b()])
                        k.act(E2[:, 0:n], Dt[:, 0:n], AF.Exp, r=[Dt.b()], w=[E2.b()], scale=-1.0)
                        k.tt(V, qt[:, 0:n], qb[:, t0:t1], E1[:, 0:n], ALU.mult, r=[qb.b(), E1.b()], w=[qt.b()])
                        k.tt(P, kt_[:, 0:n], kkb[:, t0:t1], E2[:, 0:n], ALU.mult, r=[kkb.b(), E2.b()], w=[kt_.b()])
                        k.tt(P, v3(E1[:, 0:n]), v3(E1[:, 0:n]), bc(3), ALU.mult, r=[E1.b(), CH.b()], w=[E1.b()])
                        k.tt(V, v3(E2[:, 0:n]), v3(E2[:, 0:n]), bc(4), ALU.mult, r=[E2.b(), CH.b()], w=[E2.b()])
                        k.tt(V, qh[:, 0:n], qb[:, t0:t1], E1[:, 0:n], ALU.mult, r=[qb.b(), E1.b()], w=[qh.b()])
                        k.tt(P, kh[:, 0:n], kkb[:, t0:t1], E2[:, 0:n], ALU.mult, r=[kkb.b(), E2.b()], w=[kh.b()])
                        if getattr(self, "hg_cut", 0) == 3:
                            return
                        npair = n // 128
                        pairs = list(range(npair)) if d == 0 else list(range(npair - 1, -1, -1))
                        def make_pair(j, t0=t0, qt=qt, kt_=kt_, qh=qh, kh=kh):
                            nonlocal pr_i
                            o = j * 128
                            tp = t0 + o
                            tt = tp // 128
                            pm, kk_ = PM[pr_i % 3], KK[pr_i % 3]
                            pr_i += 1
                            st = {}

                            def front():
                                b_sc, b_tr, b_kv, b_kv2 = k.bank(), k.bank(), k.bank(), k.bank()
                                st["kvb"] = [b_kv, b_kv2]
                                k.mm(b_sc[:, 0:128], kt_[:, o:o + 128], qt[:, o:o + 128], True, True, r=[kt_.b(), qt.b()], w=[b_sc.b()], sig=True)
                                k.memset(P, pm[:], 0.0, w=[pm.b()])
                                k.op(V, lambda: V.eng.copy_predicated(out=pm[:], mask=mask[:], data=b_sc[:, 0:128]),
                                     r=[b_sc.b(), mask.b(), pm.b()], w=[pm.b()])
                                btv = b_tr[:, :].bitcast(BF16)
                                k.tr(btv[:, 0:128], kh[:, o:o + 128], C["ident_b"][:], r=[kh.b(), C["ident_b"].b()], w=[b_tr.b()])
                                k.cp(A, kk_[:], btv[:, 0:128], r=[b_tr.b()], w=[kk_.b()])
                                k.mm(b_kv[:, 0:128], kk_[0:64, :], Vtok[0:64, tt, :], True, True, r=[kk_.b(), Vtok.b()], w=[b_kv.b()], sig=True)
                                k.mm(b_kv2[:, 0:128], kk_[64:128, :], Vtok[64:128, tt, :], True, True, r=[kk_.b(), Vtok.b()], w=[b_kv2.b()], sig=True)

                            def back():
                                nonlocal sbi, sb_cur
                                kvb = st["kvb"]
                                halves = [0, 1] if d == 0 else [1, 0]
                                sb_start = {}
                                for hh in halves:
                                    ch = (tp // 64) + hh
                                    sb_start[hh] = sb_cur
                                    k.stt(S[:], S[:], CH[:, 5, ch:ch + 1], kvb[hh][:, 0:128], ALU.mult, ALU.add,
                                          r=[S.b(), CH.b(), kvb[hh].b()], w=[S.b()])
                                    sbi += 1
                                    sb_cur = SB[sbi % 4]
                                    k.cp(A, sb_cur[:], S[:], r=[S.b()], w=[sb_cur.b()])
                                b_o = k.bank()
                                k.mm(b_o[:, 0:128], Vtok[:, tt, :], pm[:], True, False, r=[Vtok.b(), pm.b()], w=[b_o.b()], sig=False)
                                k.mm(b_o[:, 0:64], sb_start[0][:], qh[:, o:o + 64], False, False, r=[sb_start[0].b(), qh.b()], w=[b_o.b()], sig=False)
                                k.mm(b_o[:, 64:128], sb_start[1][:], qh[:, o + 64:o + 128], False, True, r=[sb_start[1].b(), qh.b()], w=[b_o.b()], sig=True)
                                if d == 0:
                                    k.cp(A, O[:, tp:tp + 128], b_o[:, 0:128], r=[b_o.b()], w=[O.b(tt)])
                                else:
                                    k.tt(V, O[:, tp:tp + 128], b_o[:, 0:128], O[:, tp:tp + 128], ALU.add, r=[b_o.b(), O.b(tt)], w=[O.b(tt)])
                            return front, back
                        prs = [make_pair(j) for j in pairs]
                        prs[0][0]()
                        for pi in range(len(prs)):
                            if pi + 1 < len(prs):
                                prs[pi + 1][0]()
                            prs[pi][1]()
                ng = self.vec("hg_norm_g", l)
                for tbi, (t0, t1) in enumerate(TB):
                    n = t1 - t0
                    if True:
                        sc2 = sc
                        obufs = [O.b(tt) for tt in range(t0 // 128, t1 // 128)]
                        rs = self.rstd_of(sc2, O[:, t0:t1].unsqueeze(1), obufs, n, C["ones_v"], nk=1, tag="hg")
                        bk = k.bank()
                        self.proj_fm(ws["og"], tbi, bk)
                        sg, tmp = SG[tbi % 2], TMP[tbi % 2]
                        k.act(sg[:, 0:n], bk[:, 0:n], AF.Silu, r=[bk.b()], w=[sg.b()])
                        k.stt(tmp[:, 0:n], O[:, t0:t1], ng, rs[:, 0:n], ALU.mult, ALU.mult, r=obufs + [VBb, rs.b()], w=[tmp.b()])
                        k.tt(V, ys[:, t0:t1], tmp[:, 0:n], sg[:, 0:n], ALU.mult, r=[tmp.b(), sg.b()], w=[ys.b()])
                k.dma(k.SP, YP.t[2, :, hd, :], ys[:], r=[ys.b()], w=[YP.b(2, hd, i) for i in range(5)])

    def w_tile(self, wt, name, l, c0, nkc, ncol=128, k0=0):
        k = self.k
        src = self.WB[name]
        k.dma(k.SP, wt[:, 0:nkc, 0:ncol],
              src.t[l].rearrange("(kc p) n -> p kc n", p=128)[:, k0:k0 + nkc, c0:c0 + ncol], r=[src.b(l)], w=[wt.b()])

    def merge(self, s, l):
        k, C = self.k, self.C
        V, A, P = k.DVE, k.ACT, k.POOL
        X, U, M, YP = self.X, self.U, self.MOD[l], self.YP
        last = l == self.depth - 1
        pnames = ("w_proj_hy", "w_proj_lru", "w_proj_hg")
        with k.scope() as sc:
            YB = [sc.tile(f"mgy{i}", [128, 4, 512], BF16) for i in range(3)]
            mbf = sc.tile("mgm", [128, 8, 512], BF16)
            MO = sc.tile("mgmo", [128, 8, 512], F32)
            G = [sc.tile(f"mgg{i}", [128, 512], F32) for i in range(2)]
            macc = sc.tile("mgacc", [128, 512], F32)
            T2 = [sc.tile(f"mgt{i}", [128, 512], F32) for i in range(2)]
            WG = [sc.tile(f"mgwg{i}", [128, 8, 128], BF16) for i in range(6)]
            WP = [sc.tile(f"mgwp{i}", [128, 4, 128], BF16) for i in range(6)]
            WO = [sc.tile(f"mgwo{i}", [128, 8, 128], BF16) for i in range(2)]
            it = 0
            for tbi, (t0, t1) in enumerate(TB):
                if last and tbi == 0:
                    continue
                n = t1 - t0
                col = 2 if tbi == 0 else s
                for br in range(3):
                    k.dma(k.SP, YB[br][:, :, 0:n], YP.t[br, :, :, t0:t1], r=[YP.b(br, cc, tbi) for cc in range(4)], w=[YB[br].b()])
                for oc in range(8):
                    for br in range(3):
                        wg, wp = WG[it % 6], WP[it % 6]
                        g, t2 = G[it % 2], T2[it % 2]
                        it += 1
                        self.win_tile(wg, l, C_MG + br * 1024 + oc * 128)
                        self.w_tile(wp, pnames[br], l, oc * 128, 4)
                        bp, bg = k.bank(), k.bank()
                        for kc in range(4):
                            k.mm(bp[:, 0:n], wp[:, kc, :], YB[br][:, kc, 0:n], kc == 0, kc == 3, r=[wp.b(), YB[br].b()], w=[bp.b()], sig=(kc == 3))
                        self.proj_fm(wg, tbi, bg)
                        k.act(g[:, 0:n], bg[:, 0:n], AF.Sigmoid, r=[bg.b()], w=[g.b()])
                        if br == 0:
                            k.tt(V, macc[:, 0:n], bp[:, 0:n], g[:, 0:n], ALU.mult, r=[bp.b(), g.b()], w=[macc.b()])
                        elif br == 1:
                            k.tt(V, t2[:, 0:n], bp[:, 0:n], g[:, 0:n], ALU.mult, r=[bp.b(), g.b()], w=[t2.b()])
                            k.tt(P, macc[:, 0:n], macc[:, 0:n], t2[:, 0:n], ALU.add, r=[macc.b(), t2.b()], w=[macc.b()])
                        else:
                            k.tt(V, t2[:, 0:n], bp[:, 0:n], g[:, 0:n], ALU.mult, r=[bp.b(), g.b()], w=[t2.b()])
                            k.tt(P, mbf[:, oc, 0:n], macc[:, 0:n], t2[:, 0:n], ALU.add, r=[macc.b(), t2.b()], w=[mbf.b(oc)])
                for oc in range(8):
                    wo = WO[oc % 2]
                    self.w_tile(wo, "w_out", l, oc * 128, 8)
                    bk = k.bank()
                    for kc in range(8):
                        k.mm(bk[:, 0:n], wo[:, kc, :], mbf[:, kc, 0:n], kc == 0, kc == 7, r=[wo.b(), mbf.b(kc)], w=[bk.b()], sig=(kc == 7))
                    k.cp(A, MO[:, oc, 0:n], bk[:, 0:n], r=[bk.b()], w=[MO.b(oc)])
                self.resid_update(sc, MO, mbf, n, tbi, t0, t1, M, 2, col, sqb=[mbf.b(oc) for oc in range(8)])

    def resid_update(self, sc, MO, sqt, n, tbi, t0, t1, M, gi, col, sqb=None):
        k, C = self.k, self.C
        X = self.X
        if True:
            sc2 = sc
            rs = self.rstd_of(sc2, MO[:, 0:8, 0:n], [MO.b(kc) for kc in range(8)], n, C["ones_d"], sq=sqt, sqb=sqb)
            tmp = [sc2.ctile(f"rutmp{i}", [128, 512], F32) for i in range(2)]
            for kc in range(8):
                tp = tmp[kc % 2]
                k.stt(tp[:, 0:n], MO[:, kc, 0:n], M[:, gi, kc, col:col + 1], rs[:, 0:n], ALU.mult, ALU.mult,
                      r=[MO.b(kc), M.b(), rs.b()], w=[tp.b()])
                k.tt(k.POOL, X[:, kc, t0:t1], X[:, kc, t0:t1], tp[:, 0:n], ALU.add, r=[X.b(kc, tbi), tp.b()], w=[X.b(kc, tbi)])

    def mlp(self, s, l):
        k, C = self.k, self.C
        V, A, P = k.DVE, k.ACT, k.POOL
        X, U, M = self.X, self.U, self.MOD[l]
        last = l == self.depth - 1
        with k.scope() as sc:
            H = sc.tile("mlH", [128, 32, 512], BF16)
            MO = Tile(H.t[:, 0:16, :].rearrange("p a b -> p (a b)").bitcast(F32).rearrange("p (a b) -> p a b", b=512), "mlMO")
            MO.b = lambda *key: H.b("mo")
            sqt = sc.tile("mlsq", [128, 8, 512], BF16)
            SQ = [sc.tile(f"mlsqr{i}", [128, 512], F32) for i in range(2)]
            W1 = [sc.tile(f"mlw1{i}", [128, 8, 512], BF16) for i in range(2)]
            W2 = [sc.tile(f"mlw2{i}", [128, 2, 1024], BF16) for i in range(2)]
            hb_all = [H.b(j) for j in range(32)] + [H.b("mo")]
            for tbi, (t0, t1) in enumerate(TB):
                if last and tbi == 0:
                    continue
                n = t1 - t0
                col = 2 if tbi == 0 else s
                if True:
                    sc2 = sc
                    rs = self.rstd_of(sc2, X[:, :, t0:t1], [X.b(kc, tbi) for kc in range(8)], n, C["ones_d"], sq=sqt)
                    tmp = [sc2.ctile(f"mltmp{i}", [128, 512], F32) for i in range(2)]
                    for kc in range(8):
                        tp = tmp[kc % 2]
                        k.stt(tp[:, 0:n], X[:, kc, t0:t1], M[:, 4, kc, col:col + 1], rs[:, 0:n], ALU.mult, ALU.mult,
                              r=[X.b(kc, tbi), M.b(), rs.b()], w=[tp.b()])
                        k.act(U[:, kc, t0:t1], tp[:, 0:n], AF.Identity, r=[tp.b(), M.b()], w=[U.b(kc, tbi)], bias=M[:, 3, kc, col:col + 1])
                for jg in range(8):
                    w1 = W1[jg % 2]
                    self.w_tile(w1, "w_mlp1", l, jg * 512, 8, ncol=512)
                    for jj in range(4):
                        j = jg * 4 + jj
                        bk = k.bank()
                        for kc in range(8):
                            k.mm(bk[:, 0:n], w1[:, kc, jj * 128:(jj + 1) * 128], U[:, kc, t0:t1], kc == 0, kc == 7,
                                 r=[w1.b(), U.b(kc, tbi)], w=[bk.b()], sig=(kc == 7))
                        sq = SQ[j % 2]
                        k.act(sq[:, 0:n], bk[:, 0:n], AF.Square, r=[bk.b()], w=[sq.b()])
                        k.stt(H[:, j, 0:n], bk[:, 0:n], 0.0, sq[:, 0:n], ALU.is_gt, ALU.mult, r=[bk.b(), sq.b()], w=[H.b(j), H.b("mo")])
                bks = [k.bank() for _ in range(8)]
                for jg in range(16):
                    w2 = W2[jg % 2]
                    self.w_tile(w2, "w_mlp2", l, 0, 2, ncol=1024, k0=jg * 2)
                    for jj in range(2):
                        j = jg * 2 + jj
                        for oc in range(8):
                            k.mm(bks[oc][:, 0:n], w2[:, jj, oc * 128:(oc + 1) * 128], H[:, j, 0:n], j == 0, j == 31,
                                 r=[w2.b(), H.b(j)], w=[bks[oc].b()], sig=(oc == 7 and jj == 1))
                for oc in range(8):
                    k.cp(A if oc % 2 == 0 else V, MO[:, oc, 0:n], bks[oc][:, 0:n], r=[bks[oc].b()], w=hb_all)
                self.resid_update(sc, MO, sqt, n, tbi, t0, t1, M, 5, col)

    def permute(self, fwd):
        k = self.k
        X = self.X
        engs = [k.ACT, k.DVE, k.POOL]
        with k.scope() as sc:
            TMPS = [sc.tile(f"pmt{i}", [128, SEQ], F32) for i in range(2)]
            for kc in range(8):
                tmp = TMPS[kc % 2]
                xb = [X.b(kc, tbi) for tbi in range(1, 5)]
                src = X[:, kc, CTX:T]
                if fwd:
                    v = src.rearrange("p (r c) -> p c r", c=GRID_W)
                    tv = tmp[:, :].rearrange("p (c r) -> p c r", c=GRID_W)
                else:
                    v = src.rearrange("p (c r) -> p r c", c=GRID_W)
                    tv = tmp[:, :].rearrange("p (r c) -> p r c", c=GRID_W)
                k.cp(engs[kc % 3], tv, v, r=xb, w=[tmp.b()])
                k.cp(engs[(kc + 1) % 3], X[:, kc, CTX:T], tmp[:, :], r=[tmp.b()], w=xb)

    def store_out(self, s):
        k, C = self.k, self.C
        X = self.X
        with k.scope() as sc:
            OT = [sc.tile(f"ot{i}", [128, D], F32) for i in range(3)]
            for tt in range(2, T // 128):
                ot = OT[tt % 3]
                tbi = self.tbi(tt * 128)
                for h in range(2):
                    bk = k.bank()
                    for j in range(4):
                        kc = h * 4 + j
                        k.tr(bk[:, j * 128:(j + 1) * 128], X[:, kc, tt * 128:(tt + 1) * 128], C["ident_f"][:],
                             r=[X.b(kc, tbi), C["ident_f"].b()], w=[bk.b()], sig=(j == 3))
                    k.cp(k.ACT if h == 0 else k.DVE, ot[:, h * 512:(h + 1) * 512], bk[:, :], r=[bk.b()], w=[ot.b(h)])
                k.dma(k.SP, self.out[s, (tt - 2) * 128:(tt - 1) * 128, :], ot[:], r=[ot.b(0), ot.b(1)], w=[])

    def layer(self, s, l):
        import os
        stop = os.environ.get("KSTOP", "")
        self.phase_a(s, l)
        if stop == f"a{l}":
            return True
        self.hyena(s, l)
        if stop == f"hy{l}":
            return True
        self.hgrn(s, l)
        if stop == f"hg{l}":
            return True
        self.lru(s, l)
        if stop == f"lru{l}":
            return True
        self.merge(s, l)
        if stop == f"mg{l}":
            return True
        self.mlp(s, l)
        if stop == f"mlp{l}":
            return True
        return False

    def build_full(self):
        self.prologue()
        for s in range(self.nseq):
            self.load_x(s)
            for l in range(self.depth):
                if l % 2 == 1:
                    self.permute(True)
                if self.layer(s, l):
                    self.store_out(s)
                    self.finish()
                    return
                if l % 2 == 1:
                    self.permute(False)
            self.store_out(s)
        self.finish()

    def spec_mul(self, yre, yim, A_, B_, Kre, Kim, tm, rb, TM, wb):
        k = self.k
        V, P = k.DVE, k.POOL
        k.tt(V, tm[0], A_, Kre, ALU.mult, r=rb, w=[TM[0].b()])
        k.tt(V, tm[1], B_, Kim, ALU.mult, r=rb, w=[TM[1].b()])
        k.tt(V, yre, tm[0], tm[1], ALU.add, r=[TM[0].b(), TM[1].b()], w=wb)
        k.tt(P, tm[2], B_, Kre, ALU.mult, r=rb, w=[TM[2].b()])
        k.tt(P, tm[3], A_, Kim, ALU.mult, r=rb, w=[TM[3].b()])
        k.tt(P, yim, tm[2], tm[3], ALU.subtract, r=[TM[2].b(), TM[3].b()], w=wb)

    def prologue(self):
        k = self.k
        self.precast()
        self.consts()
        self.resident()
        self.vecbank()
        with k.scope() as psc:
            self.FB = psc.tile("FB", [128, self.depth, 2], F32)
            self.SK = psc.tile("SK", [128, self.depth + 1, 512], F32)
            for name in ("negt", "delta"):
                shp = list(make_consts()[name].shape)
                t = psc.tile("c_" + name, shp, CONST_DT[name])
                k.dma(k.SP, t[:], self.cst[name], w=[t.b()])
                self.C[name] = t
            self.derive()
            for l in range(self.depth):
                self.filt(l, False)
            self.filt(0, True)
        self.X = k.tile("X", [128, 8, T], F32)
        self.U = k.tile("U", [128, 8, T], BF16)

    def finish(self):
        k = self.k
        k.barrier()
        k.stack.close()


def make_in_maps(inputs, ncores=NCORES, nseq=NSEQ, j=0):
    cst = make_consts()
    maps = []
    for i in range(ncores):
        m = {}
        for name in IN_SHAPES:
            a = np.asarray(inputs[name])
            if name in ("x", "c", "ctx"):
                a = a[NSEQ * i + j:NSEQ * i + j + nseq]
            m[name] = np.ascontiguousarray(a, dtype=np.float32)
        for name, arr in cst.items():
            m["k_" + name] = arr
        maps.append(m)
    return maps


_NET = None
NLAUNCH = 1


def kernel(**inputs):
    global _NET
    nseq = NSEQ // NLAUNCH
    if _NET is None:
        net = Net(nseq=nseq)
        net.build_full()
        _NET = net
    net = _NET
    outs = []
    for j in range(NLAUNCH):
        maps = make_in_maps(inputs, NCORES, nseq, j * nseq)
        res = run_bass_kernel_spmd(net.nc, maps, core_ids=list(range(NCORES)))
        outs.append([np.asarray(r["out"]) for r in res.results])
    full = np.zeros((NCORES * NSEQ, SEQ, D), np.float32)
    for j in range(NLAUNCH):
        for i in range(NCORES):
            full[NSEQ * i + j * nseq:NSEQ * i + (j + 1) * nseq] = outs[j][i]
    return full
```
